# Optimizing a Trainium2 kernel written in Bass

```python
import math
import jax
import jax.numpy as jnp
from jax import lax
import numpy as np

D_MODEL = 1024
BATCH = 2
SEQ = 16384
DEPTH = 4

N_MIXERS = 4
F32 = jnp.float32
ROPE_THETA = 10000.0
DEEPNORM_ALPHA = (2.0 * DEPTH) ** 0.25
DEEPNORM_BETA = (8.0 * DEPTH) ** -0.25
LN_EPS = 1e-5
NEG_INF = -1e30
D_FF = ((8 * D_MODEL + 3 * 256 - 1) // (3 * 256)) * 256

RWKV_HEAD_DIM = 64
RWKV_HEADS = D_MODEL // RWKV_HEAD_DIM
RWKV_LORA_DECAY = 64
RWKV_LORA_ICLR = 64
RWKV_LORA_GATE = 128
RWKV_GN_EPS = 64e-5
GDN_HEAD_DIM = 128
GDN_HEADS = D_MODEL // GDN_HEAD_DIM
GDN_CONV = 4
GDN_CHUNK = 64
GDN_QKV = 3 * GDN_HEADS * GDN_HEAD_DIM
GDN_PROJ = 4 * GDN_HEADS * GDN_HEAD_DIM + 2 * GDN_HEADS
DIFF_HEADS = 8
DIFF_HEAD_DIM = D_MODEL // (2 * DIFF_HEADS)
ATTN_Q_BLOCK = 128
SWA_HEAD_DIM = 64
SWA_Q_HEADS = D_MODEL // SWA_HEAD_DIM
SWA_KV_HEADS = SWA_Q_HEADS // 8
SWA_WINDOW = 128
SWA_QKV = (SWA_Q_HEADS + 2 * SWA_KV_HEADS) * SWA_HEAD_DIM

kernel_name = 'hybrid_interleaved_rwkv7_gdn_diffattn_swa'


def _layer_norm(x, g, b):
    xf = x.astype(F32)
    mu = jnp.mean(xf, -1, keepdims=True)
    var = jnp.mean(jnp.square(xf - mu), -1, keepdims=True)
    return ((xf - mu) * lax.rsqrt(var + LN_EPS) * g + b).astype(x.dtype)


def _rms_norm(x, g, eps):
    xf = x.astype(F32)
    return xf * lax.rsqrt(jnp.mean(xf * xf, -1, keepdims=True) + eps) * g


def _l2norm(x, eps=1e-6):
    xf = x.astype(F32)
    return (xf * lax.rsqrt(jnp.sum(xf * xf, -1, keepdims=True) + eps)).astype(x.dtype)


def _rope(x, pos):
    d = x.shape[-1]
    inv = ROPE_THETA ** (-jnp.arange(0, d, 2, dtype=F32) / d)
    ang = pos.astype(F32)[:, None] * inv[None, :]
    cos = jnp.cos(ang)[None, :, None, :]
    sin = jnp.sin(ang)[None, :, None, :]
    x1, x2 = jnp.split(x.astype(F32), 2, axis=-1)
    return jnp.concatenate([x1 * cos - x2 * sin, x2 * cos + x1 * sin], -1).astype(x.dtype)


def _token_shift(x):
    return jnp.pad(x, ((0, 0), (1, 0), (0, 0)))[:, :-1]


def _causal_depthwise_conv(x, w):
    K, C = w.shape
    return lax.conv_general_dilated(x, w[:, None, :].astype(x.dtype), window_strides=(1,),
                                    padding=[(K - 1, 0)], dimension_numbers=('NWC', 'WIO', 'NWC'),
                                    feature_group_count=C)


def _rwkv7_mixer(u, mu, w_rkv, w0, w1, w2, a0, a1, a2, g1, g2, k_k, k_a, r_k, gn_g, gn_b, w_out):
    B, T, D = u.shape
    H, N = RWKV_HEADS, RWKV_HEAD_DIM
    xx = _token_shift(u) - u
    lerp = lambda i: u + xx * mu[i]
    r = lerp(0) @ w_rkv[0]
    k = lerp(1) @ w_rkv[1]
    v = lerp(2) @ w_rkv[2]
    w_log = -jax.nn.softplus(-(w0 + jnp.tanh(lerp(3) @ w1) @ w2)) - 0.5
    a = jax.nn.sigmoid(a0 + (lerp(4) @ a1) @ a2)
    g = jax.nn.sigmoid(lerp(5) @ g1) @ g2
    heads = lambda t: t.reshape(B, T, H, N)
    kk = _l2norm(heads(k * k_k))
    k = k * (1 + (a - 1) * k_a)
    r, k, v, a = heads(r), heads(k), heads(v), heads(a)
    decay = jnp.exp(-jnp.exp(heads(w_log).astype(F32)))
    tm = lambda t: jnp.moveaxis(t.astype(F32), 1, 0)

    def step(S, inp):
        r_t, k_t, v_t, w_t, a_t, b_t = inp
        sa = jnp.einsum('bhvk,bhk->bhv', S, a_t)
        S = S * w_t[:, :, None, :] + sa[..., None] * b_t[:, :, None, :] + v_t[..., None] * k_t[:, :, None, :]
        return S, jnp.einsum('bhvk,bhk->bhv', S, r_t)

    S0 = jnp.zeros((B, H, N, N), F32)
    _, y = lax.scan(step, S0, (tm(r), tm(k), tm(v), jnp.moveaxis(decay, 1, 0), tm(-kk), tm(kk * a)))
    y = jnp.moveaxis(y, 0, 1)
    mean = jnp.mean(y, -1, keepdims=True)
    var = jnp.mean(jnp.square(y - mean), -1, keepdims=True)
    y = ((y - mean) * lax.rsqrt(var + RWKV_GN_EPS)).reshape(B, T, D) * gn_g + gn_b
    bonus = jnp.sum((r * k * r_k).astype(F32), -1, keepdims=True) * v.astype(F32)
    y = (y + bonus.reshape(B, T, D)).astype(u.dtype)
    return (y * g) @ w_out


def _chunk_gated_delta_rule(q, k, v, g, beta):
    B, T, H, dk = q.shape
    dv = v.shape[-1]
    C = GDN_CHUNK
    n = T // C
    chunks = lambda t: jnp.moveaxis(t.astype(F32).reshape(B, n, C, H, -1), 3, 1)
    q, k, v = chunks(q), chunks(k), chunks(v)
    beta = chunks(beta[..., None])
    gc = jnp.cumsum(chunks(g[..., None])[..., 0], axis=-1)
    k_beta, v_beta = k * beta, v * beta
    idx = jnp.arange(C)
    causal = idx[:, None] >= idx[None, :]
    strict = idx[:, None] > idx[None, :]
    decay = jnp.exp(jnp.where(causal, gc[..., :, None] - gc[..., None, :], -jnp.inf))
    L = jnp.where(strict, jnp.einsum('bhnid,bhnjd->bhnij', k_beta, k) * decay, 0.0)
    eye = jnp.eye(C, dtype=F32)
    tinv = lax.linalg.triangular_solve(eye + L, jnp.broadcast_to(eye, L.shape), left_side=True, lower=True)
    u_c = tinv @ v_beta
    w_c = tinv @ (k_beta * jnp.exp(gc)[..., None])
    a_intra = jnp.where(causal, jnp.einsum('bhnid,bhnjd->bhnij', q, k) * decay, 0.0)
    q_dec = q * jnp.exp(gc)[..., None]
    k_dec = k * jnp.exp(gc[..., -1:] - gc)[..., None]
    g_last = jnp.exp(gc[..., -1])

    def step(S, inp):
        q_i, k_i, u_i, w_i, a_i, gl = inp
        v_new = u_i - w_i @ S
        o = q_i @ S + a_i @ v_new
        S = S * gl[..., None, None] + jnp.einsum('bhck,bhcv->bhkv', k_i, v_new)
        return S, o

    xs = tuple(jnp.moveaxis(t, 2, 0) for t in (q_dec, k_dec, u_c, w_c, a_intra, g_last))
    _, o = lax.scan(step, jnp.zeros((B, H, dk, dv), F32), xs)
    return jnp.transpose(o, (1, 0, 3, 2, 4)).reshape(B, T, H, dv)


def _gated_deltanet_mixer(u, w_in, conv_w, a_log, dt_bias, norm_g, w_out):
    B, T, _ = u.shape
    H, d = GDN_HEADS, GDN_HEAD_DIM
    hd = H * d
    proj = u @ w_in
    qkv = jax.nn.silu(_causal_depthwise_conv(proj[..., :3 * hd], conv_w))
    z = proj[..., 3 * hd:4 * hd].reshape(B, T, H, d)
    b_in = proj[..., 4 * hd:4 * hd + H]
    a_in = proj[..., 4 * hd + H:]
    q = _l2norm(qkv[..., :hd].reshape(B, T, H, d)) * d ** -0.5
    k = _l2norm(qkv[..., hd:2 * hd].reshape(B, T, H, d))
    v = qkv[..., 2 * hd:].reshape(B, T, H, d)
    beta = jax.nn.sigmoid(b_in.astype(F32))
    g = -jnp.exp(a_log.astype(F32)) * jax.nn.softplus(a_in.astype(F32) + dt_bias)
    o = _chunk_gated_delta_rule(q, k, v, g, beta)
    o = _rms_norm(o, norm_g, 1e-6) * jax.nn.silu(z.astype(F32))
    return o.reshape(B, T, hd).astype(u.dtype) @ w_out


def _diff_attention_mixer(u, pos, w_in, lam_p, subln_g, w_out, lam_init):
    B, T, D = u.shape
    H, d, Qb = DIFF_HEADS, DIFF_HEAD_DIM, ATTN_Q_BLOCK
    proj = u @ w_in
    q = _rope(proj[..., :D].reshape(B, T, 2 * H, d), pos) * d ** -0.5
    k = _rope(proj[..., D:2 * D].reshape(B, T, 2 * H, d), pos)
    v = proj[..., 2 * D:].reshape(B, T, H, 2 * d)
    lp = lam_p.astype(F32)
    lam = jnp.exp(jnp.sum(lp[0] * lp[1])) - jnp.exp(jnp.sum(lp[2] * lp[3])) + lam_init
    nb = T // Qb
    qb = jnp.moveaxis(q.reshape(B, nb, Qb, 2 * H, d), 1, 0)
    kf, vf = k.astype(F32), v.astype(F32)
    kpos = jnp.arange(T)

    def block(args):
        q_blk, i = args
        s = jnp.einsum('bqhd,bkhd->bhqk', q_blk.astype(F32), kf)
        qpos = i * Qb + jnp.arange(Qb)
        s = jnp.where(kpos[None, :] <= qpos[:, None], s, NEG_INF)
        p = jax.nn.softmax(s, axis=-1).reshape(B, H, 2, Qb, T)
        attn = p[:, :, 0] - lam * p[:, :, 1]
        return jnp.einsum('bhqk,bkhe->bqhe', attn, vf)

    o = lax.map(block, (qb, jnp.arange(nb)))
    o = jnp.moveaxis(o, 0, 1).reshape(B, T, H, 2 * d)
    o = _rms_norm(o, subln_g, 1e-5) * (1.0 - lam_init)
    return o.reshape(B, T, D).astype(u.dtype) @ w_out


def _swa_sink_mixer(u, pos, w_qkv, b_qkv, sinks, w_out, b_out):
    B, T, _ = u.shape
    Hq, Hkv, d, W = SWA_Q_HEADS, SWA_KV_HEADS, SWA_HEAD_DIM, SWA_WINDOW
    G = Hq // Hkv
    nb = T // W
    proj = u @ w_qkv + b_qkv
    q = _rope(proj[..., :Hq * d].reshape(B, T, Hq, d), pos) * d ** -0.5
    k = _rope(proj[..., Hq * d:(Hq + Hkv) * d].reshape(B, T, Hkv, d), pos)
    v = proj[..., (Hq + Hkv) * d:].reshape(B, T, Hkv, d)
    qb = q.astype(F32).reshape(B, nb, W, Hkv, G, d)

    def band(t):
        tb = t.astype(F32).reshape(B, nb, W, Hkv, d)
        prev = jnp.pad(tb, ((0, 0), (1, 0), (0, 0), (0, 0), (0, 0)))[:, :-1]
        return jnp.concatenate([prev, tb], axis=2)

    kb, vb = band(k), band(v)
    s = jnp.einsum('bnqhgd,bnkhd->bnhgqk', qb, kb)
    qi = jnp.arange(W)[:, None]
    kj = jnp.arange(2 * W)[None, :]
    rel = W + qi - kj
    in_window = (rel >= 0) & (rel < W)
    key_pos = jnp.arange(nb)[:, None] * W + jnp.arange(2 * W)[None, :] - W
    valid = in_window[None, :, :] & (key_pos >= 0)[:, None, :]
    s = jnp.where(valid[None, :, None, None], s, NEG_INF)
    sink = sinks.astype(F32).reshape(Hkv, G)[None, None, :, :, None, None]
    m = jnp.maximum(jnp.max(s, -1, keepdims=True), sink)
    p = jnp.exp(s - m)
    p = p / (jnp.sum(p, -1, keepdims=True) + jnp.exp(sink - m))
    o = jnp.einsum('bnhgqk,bnkhd->bnqhgd', p, vb).reshape(B, T, Hq * d)
    return o.astype(u.dtype) @ w_out + b_out


def _count(m):
    return len(range(m, DEPTH, N_MIXERS))


def setup_inputs(seed: int = 0) -> dict:
    key = jax.random.key(seed)
    keys = iter(jax.random.split(key, 48))

    def nrm(shape, std):
        return jax.random.normal(next(keys), shape, F32) * std

    def unif(shape, lo, hi):
        return jax.random.uniform(next(keys), shape, F32, lo, hi)

    D = D_MODEL
    nA, nB, nC, nD = (_count(m) for m in range(N_MIXERS))
    beta = DEEPNORM_BETA
    gdn_hd = GDN_HEADS * GDN_HEAD_DIM
    swa_hd = SWA_Q_HEADS * SWA_HEAD_DIM
    dt = jnp.exp(unif((nB, GDN_HEADS), math.log(1e-3), math.log(1e-1)))
    return {
        'x': nrm((BATCH, SEQ, D), 1.0),
        'c': nrm((BATCH, D), 1.0),
        'ada_w': nrm((DEPTH, D, 6 * D), 0.1 * D ** -0.5),
        'ada_b': nrm((DEPTH, 6 * D), 0.02),
        'ln_g': 1.0 + nrm((DEPTH, 2, D), 0.02),
        'ln_b': nrm((DEPTH, 2, D), 0.02),
        'ffn_w_in': nrm((DEPTH, D, 2 * D_FF), D ** -0.5),
        'ffn_w_out': nrm((DEPTH, D_FF, D), beta * D_FF ** -0.5),
        'rwkv_mu': unif((nA, 6, D), 0.0, 1.0),
        'rwkv_w_rkv': nrm((nA, 3, D, D), D ** -0.5),
        'rwkv_w0': unif((nA, D), -6.0, 1.0),
        'rwkv_w1': nrm((nA, D, RWKV_LORA_DECAY), D ** -0.5),
        'rwkv_w2': nrm((nA, RWKV_LORA_DECAY, D), 0.5 * RWKV_LORA_DECAY ** -0.5),
        'rwkv_a0': nrm((nA, D), 0.1),
        'rwkv_a1': nrm((nA, D, RWKV_LORA_ICLR), D ** -0.5),
        'rwkv_a2': nrm((nA, RWKV_LORA_ICLR, D), 0.5 * RWKV_LORA_ICLR ** -0.5),
        'rwkv_g1': nrm((nA, D, RWKV_LORA_GATE), D ** -0.5),
        'rwkv_g2': nrm((nA, RWKV_LORA_GATE, D), RWKV_LORA_GATE ** -0.5),
        'rwkv_k_k': 1.0 + nrm((nA, D), 0.1),
        'rwkv_k_a': 1.0 + nrm((nA, D), 0.1),
        'rwkv_r_k': nrm((nA, RWKV_HEADS, RWKV_HEAD_DIM), 0.1),
        'rwkv_gn_g': 1.0 + nrm((nA, D), 0.02),
        'rwkv_gn_b': nrm((nA, D), 0.02),
        'rwkv_w_out': nrm((nA, D, D), beta * D ** -0.5),
        'gdn_w_in': nrm((nB, D, GDN_PROJ), D ** -0.5),
        'gdn_conv': nrm((nB, GDN_CONV, GDN_QKV), GDN_CONV ** -0.5),
        'gdn_a_log': jnp.log(unif((nB, GDN_HEADS), 1.0, 16.0)),
        'gdn_dt_bias': dt + jnp.log(-jnp.expm1(-dt)),
        'gdn_norm_g': 1.0 + nrm((nB, GDN_HEAD_DIM), 0.02),
        'gdn_w_out': nrm((nB, gdn_hd, D), beta * gdn_hd ** -0.5),
        'diff_w_in': nrm((nC, D, 3 * D), D ** -0.5),
        'diff_lambda': nrm((nC, 4, DIFF_HEAD_DIM), 0.1),
        'diff_subln_g': 1.0 + nrm((nC, 2 * DIFF_HEAD_DIM), 0.02),
        'diff_w_out': nrm((nC, D, D), beta * D ** -0.5),
        'swa_w_qkv': nrm((nD, D, SWA_QKV), D ** -0.5),
        'swa_b_qkv': nrm((nD, SWA_QKV), 0.02),
        'swa_sinks': nrm((nD, SWA_Q_HEADS), 1.0),
        'swa_w_out': nrm((nD, swa_hd, D), beta * swa_hd ** -0.5),
        'swa_b_out': nrm((nD, D), 0.02),
    }


def reference(x, c, ada_w, ada_b, ln_g, ln_b, ffn_w_in, ffn_w_out,
              rwkv_mu, rwkv_w_rkv, rwkv_w0, rwkv_w1, rwkv_w2, rwkv_a0, rwkv_a1, rwkv_a2,
              rwkv_g1, rwkv_g2, rwkv_k_k, rwkv_k_a, rwkv_r_k, rwkv_gn_g, rwkv_gn_b, rwkv_w_out,
              gdn_w_in, gdn_conv, gdn_a_log, gdn_dt_bias, gdn_norm_g, gdn_w_out,
              diff_w_in, diff_lambda, diff_subln_g, diff_w_out,
              swa_w_qkv, swa_b_qkv, swa_sinks, swa_w_out, swa_b_out):
    T = x.shape[1]
    pos = jnp.arange(T)
    cond = jax.nn.silu(c)
    for i in range(DEPTH):
        mod = (cond @ ada_w[i] + ada_b[i])[:, None, :]
        sh1, sc1, ga1, sh2, sc2, ga2 = jnp.split(mod, 6, axis=-1)
        u = x * (1 + sc1) + sh1
        m, j = i % N_MIXERS, i // N_MIXERS
        if m == 0:
            y = _rwkv7_mixer(u, rwkv_mu[j], rwkv_w_rkv[j], rwkv_w0[j], rwkv_w1[j], rwkv_w2[j],
                             rwkv_a0[j], rwkv_a1[j], rwkv_a2[j], rwkv_g1[j], rwkv_g2[j],
                             rwkv_k_k[j], rwkv_k_a[j], rwkv_r_k[j], rwkv_gn_g[j], rwkv_gn_b[j],
                             rwkv_w_out[j])
        elif m == 1:
            y = _gated_deltanet_mixer(u, gdn_w_in[j], gdn_conv[j], gdn_a_log[j], gdn_dt_bias[j],
                                      gdn_norm_g[j], gdn_w_out[j])
        elif m == 2:
            lam_init = 0.8 - 0.6 * math.exp(-0.3 * i)
            y = _diff_attention_mixer(u, pos, diff_w_in[j], diff_lambda[j], diff_subln_g[j],
                                      diff_w_out[j], lam_init)
        else:
            y = _swa_sink_mixer(u, pos, swa_w_qkv[j], swa_b_qkv[j], swa_sinks[j], swa_w_out[j],
                                swa_b_out[j])
        x = _layer_norm(DEEPNORM_ALPHA * x + (1 + ga1) * y, ln_g[i, 0], ln_b[i, 0])
        u = x * (1 + sc2) + sh2
        gate, up = jnp.split(u @ ffn_w_in[i], 2, axis=-1)
        y = (jax.nn.silu(gate) * up) @ ffn_w_out[i]
        x = _layer_norm(DEEPNORM_ALPHA * x + (1 + ga2) * y, ln_g[i, 1], ln_b[i, 1])
    return x
```

```python
import numpy as np
import ml_dtypes
from contextlib import ExitStack
import concourse.bass as bass
import concourse.mybir as mybir
from concourse.bass_utils import run_bass_kernel_spmd

F32 = mybir.dt.float32
BF16 = mybir.dt.bfloat16
AF = mybir.ActivationFunctionType
ALU = mybir.AluOpType
AX = mybir.AxisListType
NPBF16 = ml_dtypes.bfloat16

D = 1024
B = 2
T = 16384
DEPTH = 4
DFF = 2816
NCORES = 8
TPC = T * B // NCORES
ALPHA = (2.0 * DEPTH) ** 0.25
LN_EPS = 1e-5


class Tile:
    def __init__(self, h, name, psum=False):
        self.h = h
        self.name = name
        self.psum = psum
        self.w = None
        self.r = {}

    def __getitem__(self, idx):
        return V(self, self.h[idx])

    @property
    def ap(self):
        return self.h[:]


class V:
    def __init__(self, t, ap):
        self.t = t
        self.ap = ap

    def __getitem__(self, idx):
        return V(self.t, self.ap[idx])

    def re(self, s, **kw):
        return V(self.t, self.ap.rearrange(s, **kw))

    def bc(self, shape):
        return V(self.t, self.ap.to_broadcast(shape))


def _t(v):
    if isinstance(v, Tile):
        return v
    if isinstance(v, V):
        return v.t
    return None


def _ap(v):
    if isinstance(v, Tile):
        return v.h[:]
    if isinstance(v, V):
        return v.ap
    return v


class Em:
    NDS = 20

    def __init__(self, nc, st):
        self.nc, self.st = nc, st
        self.engs = {'pe': nc.tensor, 'dve': nc.vector, 'act': nc.scalar,
                     'pool': nc.gpsimd, 'sp': nc.sync}
        self.sems = {k: st.enter_context(nc.semaphore('sem_' + k)) for k in self.engs}
        self.cnt = {k: 0 for k in self.engs}
        self.seen = {k: {} for k in self.engs}
        self.dsem = [st.enter_context(nc.semaphore('dsem%d' % i)) for i in range(self.NDS)]
        self.dcnt = [0] * self.NDS
        self.dpool = {'sp': list(range(0, 10)), 'pool': list(range(10, 16)), 'act': list(range(16, 20))}
        self.dnext = {'sp': 0, 'pool': 0, 'act': 0}
        self.uid = 0
        self.psum_banks = None

    def sb(self, shape, dtype, name=None):
        self.uid += 1
        name = (name or 't') + '_%d' % self.uid
        h = self.st.enter_context(self.nc.sbuf_tensor(name, list(shape), dtype))
        return Tile(h, name)

    def ps(self, shape, dtype=F32, name=None):
        self.uid += 1
        name = (name or 'p') + '_%d' % self.uid
        h = self.st.enter_context(self.nc.psum_tensor(name, list(shape), dtype))
        return Tile(h, name, psum=True)

    def dram(self, name, shape, dtype, kind):
        h = self.nc.dram_tensor(name, list(shape), dtype, kind=kind)
        return Tile(h.ap(), name)

    def _sem(self, key):
        if isinstance(key, tuple):
            return self.dsem[key[1]]
        return self.sems[key]

    def _wait(self, E, dep):
        if dep is None:
            return
        key, val = dep
        if key == E and E == 'pe':
            return
        if self.seen[E].get(key, 0) >= val:
            return
        self.seen[E][key] = val
        self.engs[E].wait_ge(self._sem(key), val)

    def _deps(self, E, reads, writes):
        for v in reads:
            t = _t(v)
            if t is not None:
                self._wait(E, t.w)
                if t.psum:
                    for k, dep in list(t.r.items()):
                        if k != E:
                            self._wait(E, dep)
        for v in writes:
            t = _t(v)
            if t is not None:
                self._wait(E, t.w)
                for dep in list(t.r.values()):
                    self._wait(E, dep)

    def _mark(self, dep, reads, writes):
        for v in reads:
            t = _t(v)
            if t is not None:
                t.r[dep[0]] = dep
        for v in writes:
            t = _t(v)
            if t is not None:
                t.w = dep
                t.r = {}

    def op(self, E, fn, reads=(), writes=()):
        self._deps(E, reads, writes)
        ins = fn(self.engs[E])
        self.cnt[E] += 1
        ins.then_inc(self.sems[E], 1)
        self._mark((E, self.cnt[E]), reads, writes)

    def dma(self, out, in_, Q='sp', **kw):
        pool = self.dpool[Q]
        i = pool[self.dnext[Q]]
        self.dnext[Q] = (self.dnext[Q] + 1) % len(pool)
        key = ('d', i)
        if self.dcnt[i] > 0:
            self._wait(Q, (key, self.dcnt[i]))
        self._deps(Q, [in_], [out])
        ins = self.engs[Q].dma_start(out=_ap(out), in_=_ap(in_), **kw)
        self.dcnt[i] += 16
        ins.then_inc(self.dsem[i], 16)
        self._mark((key, self.dcnt[i]), [in_], [out])

    def finish(self):
        for i in range(self.NDS):
            if self.dcnt[i] > 0:
                self._wait('sp', (('d', i), self.dcnt[i]))
        for E in ('pe', 'dve', 'act', 'pool'):
            if self.cnt[E] > 0:
                self._wait('sp', (E, self.cnt[E]))

    def mm(self, out, lhsT, rhs, start=True, stop=True, sgc=False):
        kw = {'skip_group_check': True} if sgc else {}
        self.op('pe', lambda e: e.matmul(_ap(out), lhsT=_ap(lhsT), rhs=_ap(rhs), start=start, stop=stop, **kw),
                reads=[lhsT, rhs], writes=[out])

    def transpose(self, out, in_, ident):
        self.op('pe', lambda e: e.transpose(_ap(out), _ap(in_), _ap(ident)),
                reads=[in_, ident], writes=[out])

    def act(self, out, in_, func, bias=None, scale=None, accum_out=None, E='act'):
        kw = {}
        rd = [in_]
        wr = [out]
        if bias is not None:
            kw['bias'] = _ap(bias)
            rd.append(bias)
        if scale is not None:
            kw['scale'] = _ap(scale)
            rd.append(scale)
        if accum_out is not None:
            kw['accum_out'] = _ap(accum_out)
            wr.append(accum_out)
        self.op('act', lambda e: e.activation(out=_ap(out), in_=_ap(in_), func=func, **kw),
                reads=rd, writes=wr)

    def tt(self, out, a, b, op, E='dve'):
        self.op(E, lambda e: e.tensor_tensor(out=_ap(out), in0=_ap(a), in1=_ap(b), op=op),
                reads=[a, b], writes=[out])

    def ts(self, out, a, s1, s2=None, op0=ALU.mult, op1=None, E='dve', accum_out=None):
        kw = {}
        wr = [out]
        if op1 is not None:
            kw['op1'] = op1
        if accum_out is not None:
            kw['accum_out'] = _ap(accum_out)
            wr.append(accum_out)
        self.op(E, lambda e: e.tensor_scalar(out=_ap(out), in0=_ap(a), scalar1=_ap(s1), scalar2=_ap(s2),
                                             op0=op0, **kw),
                reads=[a, s1, s2], writes=wr)

    def stt(self, out, a, s, b, op0, op1, E='dve'):
        E = 'dve'
        self.op(E, lambda e: e.scalar_tensor_tensor(out=_ap(out), in0=_ap(a), scalar=_ap(s), in1=_ap(b),
                                                    op0=op0, op1=op1),
                reads=[a, s, b], writes=[out])

    def copy(self, out, in_, E='dve'):
        if E == 'act':
            self.op('act', lambda e: e.copy(out=_ap(out), in_=_ap(in_)), reads=[in_], writes=[out])
        else:
            self.op(E, lambda e: e.tensor_copy(out=_ap(out), in_=_ap(in_)), reads=[in_], writes=[out])

    def memset(self, out, val, E='pool'):
        self.op(E, lambda e: e.memset(_ap(out), val), reads=[], writes=[out])

    def recip(self, out, in_):
        self.op('dve', lambda e: e.reciprocal(out=_ap(out), in_=_ap(in_)), reads=[in_], writes=[out])


class Rot:
    def __init__(self, tiles):
        self.tiles = tiles
        self.i = 0

    def next(self):
        t = self.tiles[self.i]
        self.i = (self.i + 1) % len(self.tiles)
        return t


def new_nc():
    return bass.Bass("TRN2", target_bir_lowering=False)


def run(nc, in_maps):
    import time as _time
    t0 = _time.time()
    res = run_bass_kernel_spmd(nc, in_maps, core_ids=list(range(NCORES)))
    print("[launch] %.1fs" % (_time.time() - t0), flush=True)
    return res.results


def build_mod():
    nc = new_nc()
    with ExitStack() as st:
        em = Em(nc, st)
        cT = em.dram("cT", [128, 8, 2], F32, "ExternalInput")
        w = em.dram("w", [D, 3072], F32, "ExternalInput")
        bias = em.dram("bias", [128, 24], F32, "ExternalInput")
        out = em.dram("modT", [128, 24, 2], F32, "ExternalOutput")
        ct = em.sb([128, 8, 2], F32, "ct")
        cs = em.sb([128, 8, 2], F32, "cs")
        bt = em.sb([128, 24], F32, "bt")
        ot = em.sb([128, 24, 2], F32, "ot")
        em.dma(ct, cT)
        em.dma(bt, bias)
        em.act(cs, ct, AF.Silu)
        wt = [em.sb([128, 3072], F32, "w%d" % k) for k in range(8)]
        wv = w[:].rearrange("(k p) n -> k p n", p=128) if False else None
        for k in range(8):
            em.dma(wt[k], w[k * 128:(k + 1) * 128, :])
        pp = em.ps([128, 24, 2], F32, "pp")
        for j in range(24):
            for k in range(8):
                em.mm(pp[:, j, :], wt[k][:, j * 128:(j + 1) * 128], cs[:, k, :], start=(k == 0), stop=(k == 7))
        for b in range(2):
            em.tt(ot[:, :, b], pp[:, :, b], bt, ALU.add)
        em.dma(out, ot)
        em.finish()
    return nc


def run_mod(c, ada_w, ada_b):
    nc = build_mod()
    cT = np.ascontiguousarray(c.T.reshape(8, 128, 2).transpose(1, 0, 2))
    in_maps = []
    for core in range(NCORES):
        l, half = core % 4, core // 4
        in_maps.append({
            "cT": cT,
            "w": np.ascontiguousarray(ada_w[l][:, half * 3072:(half + 1) * 3072]),
            "bias": np.ascontiguousarray(ada_b[l][half * 3072:(half + 1) * 3072].reshape(24, 128).T),
        })
    res = run(nc, in_maps)
    mod = np.zeros((DEPTH, B, 128, 48), np.float32)
    for core in range(NCORES):
        l, half = core % 4, core // 4
        m = res[core]["modT"]
        for b in range(2):
            mod[l, b, :, half * 24:(half + 1) * 24] = m[:, :, b]
    return mod


def emit_ln(em, z, N, ones_bf, g, bvec, out, ps1, ps2, tmp, gb=None):
    zb, zsq, mean, rstd, t1 = tmp
    em.copy(zb, z, E='pool')
    em.act(zsq, z, AF.Square)
    for j in range(8):
        em.mm(ps1, ones_bf, zb[:, j, :], start=(j == 0), stop=(j == 7))
    for j in range(8):
        em.mm(ps2, ones_bf, zsq[:, j, :], start=(j == 0), stop=(j == 7))
    eps = LN_EPS / (ALPHA * ALPHA)
    em.ts(mean, ps1, 1.0 / D, None, op0=ALU.mult)
    em.tt(t1, mean, mean, ALU.mult)
    em.stt(rstd, ps2, 1.0 / D, t1, ALU.mult, ALU.subtract)
    em.ts(rstd, rstd, eps, None, op0=ALU.add)
    em.act(rstd, rstd, AF.Sqrt)
    em.recip(rstd, rstd)
    mb = V(mean, mean.ap.unsqueeze(1).to_broadcast([128, 8, N]))
    rb = V(rstd, rstd.ap.unsqueeze(1).to_broadcast([128, 8, N]))
    if gb is not None:
        Gt, Bt = gb
        h = 4
        em.tt(out[:, 0:h, :], z[:, 0:h, :], mb[:, 0:h, :], ALU.subtract, E='pool')
        em.tt(out[:, h:8, :], z[:, h:8, :], mb[:, h:8, :], ALU.subtract, E='dve')
        em.tt(out[:, 0:h, :], out[:, 0:h, :], rb[:, 0:h, :], ALU.mult, E='dve')
        em.tt(out[:, h:8, :], out[:, h:8, :], rb[:, h:8, :], ALU.mult, E='pool')
        em.tt(out[:, 0:h, :], out[:, 0:h, :], Gt[:, 0:h, :], ALU.mult, E='pool')
        em.tt(out[:, h:8, :], out[:, h:8, :], Gt[:, h:8, :], ALU.mult, E='dve')
        em.tt(out[:, 0:h, :], out[:, 0:h, :], Bt[:, 0:h, :], ALU.add, E='dve')
        em.tt(out[:, h:8, :], out[:, h:8, :], Bt[:, h:8, :], ALU.add, E='pool')
    else:
        h = 4
        em.tt(out[:, 0:h, :], z[:, 0:h, :], mb[:, 0:h, :], ALU.subtract, E='pool')
        em.tt(out[:, h:8, :], z[:, h:8, :], mb[:, h:8, :], ALU.subtract, E='dve')
        em.tt(out[:, 0:h, :], out[:, 0:h, :], rb[:, 0:h, :], ALU.mult, E='dve')
        em.tt(out[:, h:8, :], out[:, h:8, :], rb[:, h:8, :], ALU.mult, E='pool')
        for j in range(8):
            em.act(out[:, j, :], out[:, j, :], AF.Identity, bias=bvec[:, j:j + 1], scale=g[:, j:j + 1])


def make_gb(em, g, bvec, N):
    Gt = em.sb([128, 8, N], F32, "Gt")
    Bt = em.sb([128, 8, N], F32, "Bt")
    em.memset(Gt, 1.0)
    em.memset(Bt, 0.0)
    for j in range(8):
        em.ts(Gt[:, j, :], Gt[:, j, :], g[:, j:j + 1], None, op0=ALU.mult, E='pool')
        em.ts(Bt[:, j, :], Bt[:, j, :], bvec[:, j:j + 1], None, op0=ALU.add, E='pool')
    return Gt, Bt


def build_ffn(ntok=TPC):
    NT = 256
    nc = new_nc()
    with ExitStack() as st:
        em = Em(nc, st)
        xT = em.dram("xT", [D, ntok], F32, "ExternalInput")
        w_in = em.dram("w_in", [D, 2 * DFF], F32, "ExternalInput")
        w_out = em.dram("w_out", [DFF, D], F32, "ExternalInput")
        scal = em.dram("scal", [128, 40], F32, "ExternalInput")
        yT = em.dram("yT", [D, ntok], F32, "ExternalOutput")

        sc = em.sb([128, 40], F32, "sc")
        em.dma(sc, scal)
        sc2p = em.sb([128, 8], F32, "sc2p")
        gs = em.sb([128, 8], F32, "gs")
        em.ts(sc2p, sc[:, 0:8], 1.0, None, op0=ALU.add)
        em.ts(gs, sc[:, 16:24], 1.0, 1.0 / ALPHA, op0=ALU.add, op1=ALU.mult)
        ones_bf = em.sb([128, 128], BF16, "ones")
        em.memset(ones_bf, 1.0)

        win = [em.sb([128, 2 * DFF], BF16, "win%d" % k) for k in range(8)]
        wout = [em.sb([128, D], BF16, "wout%d" % m) for m in range(22)]
        for k in range(8):
            em.dma(win[k], w_in[k * 128:(k + 1) * 128, :], Q='pool')
        for m in range(22):
            em.dma(wout[m], w_out[m * 128:(m + 1) * 128, :], Q='pool')

        xt_r = Rot([em.sb([128, 8, NT], F32, "xt") for _ in range(2)])
        ub_r = Rot([em.sb([128, 8, NT], BF16, "ub") for _ in range(2)])
        hT = em.sb([128, 22, NT], BF16, "hT")
        sg_r = Rot([em.sb([128, NT], F32, "sg") for _ in range(3)])
        z_r = Rot([em.sb([128, 8, NT], F32, "z") for _ in range(2)])
        zb_r = Rot([em.sb([128, 8, NT], BF16, "zb") for _ in range(2)])
        zsq_r = Rot([em.sb([128, 8, NT], BF16, "zsq") for _ in range(2)])
        mean = em.sb([128, NT], F32, "mean")
        rstd = em.sb([128, NT], F32, "rstd")
        t1 = em.sb([128, NT], F32, "t1")
        pbank = Rot([em.ps([128, 512], F32, "pb") for _ in range(6)])
        ps1 = em.ps([128, 512], F32, "ps1")
        ps2 = em.ps([128, 512], F32, "ps2")

        xv = xT[:].ap.rearrange("(c p) t -> p c t", p=128)
        yv = yT[:].ap.rearrange("(c p) t -> p c t", p=128)
        def load_tile(tt):
            xt = xt_r.next()
            ub = ub_r.next()
            em.dma(xt, V(xT, xv[:, :, tt * NT:(tt + 1) * NT]))
            for c in range(8):
                em.ts(ub[:, c, :], xt[:, c, :], sc2p[:, c:c + 1], sc[:, 8 + c:9 + c], op0=ALU.mult, op1=ALU.add,
                      E=('dve' if c % 2 == 0 else 'pool'))
            return xt, ub

        nxt_tile = load_tile(0)
        for tt in range(ntok // NT):
            tsl = slice(tt * NT, (tt + 1) * NT)
            xt, ub = nxt_tile
            z, zb, zsq = z_r.next(), zb_r.next(), zsq_r.next()
            for m in range(22):
                pg = pbank.next()
                pu = pbank.next()
                for k in range(8):
                    em.mm(pg[:, 0:NT], win[k][:, m * 128:(m + 1) * 128], ub[:, k, :], start=(k == 0), stop=(k == 7))
                for k in range(8):
                    em.mm(pu[:, 0:NT], win[k][:, DFF + m * 128:DFF + (m + 1) * 128], ub[:, k, :],
                          start=(k == 0), stop=(k == 7))
                sg = sg_r.next()
                em.act(sg, pg[:, 0:NT], AF.Silu)
                em.tt(hT[:, m, :], sg, pu[:, 0:NT], ALU.mult)
            for j in range(8):
                py = pbank.next()
                for m in range(22):
                    em.mm(py[:, 0:NT], wout[m][:, j * 128:(j + 1) * 128], hT[:, m, :], start=(m == 0), stop=(m == 21))
                em.stt(z[:, j, :], py[:, 0:NT], gs[:, j:j + 1], xt[:, j, :], ALU.mult, ALU.add)
            if tt + 1 < ntok // NT:
                nxt_tile = load_tile(tt + 1)
            emit_ln(em, z, NT, ones_bf, sc[:, 24:32], sc[:, 32:40], z, ps1[:, 0:NT], ps2[:, 0:NT],
                    (zb, zsq, mean, rstd, t1))
            em.dma(V(yT, yv[:, :, tsl]), z, Q='pool')
        em.finish()
    return nc


def chunk_masks():
    i = np.arange(128)
    same = (i[:, None] // 64) == (i[None, :] // 64)
    mS = (same & (i[:, None] < i[None, :])).astype(np.float32)
    mI = (same & (i[:, None] <= i[None, :])).astype(np.float32)
    mL = mS.T.copy()
    return mS, mI, mL


def build_rwkv_scan(Tn=T, SEG=1024, PI=2):
    NP = SEG // 128
    NCH = SEG // 64
    nc = new_nc()
    with ExitStack() as st:
        em = Em(nc, st)
        din = {n: em.dram(n, [2, 128, Tn], F32, "ExternalInput") for n in ("r", "k", "kk", "a", "lw")}
        vin = em.dram("v", [Tn, 256], BF16, "ExternalInput")
        cst = em.dram("cst", [128, 128 * 5], F32, "ExternalInput")
        yT = em.dram("yT", [2, 128, Tn], F32, "ExternalOutput")

        cf = em.sb([128, 640], F32, "cf")
        em.dma(cf, cst)
        mSI = cf[:, 0:256]
        mL = cf[:, 256:384]
        If = cf[:, 384:512]
        Ib = em.sb([128, 128], BF16, "Ib")
        em.copy(Ib, cf[:, 384:512])
        reset = em.sb([128, SEG], F32, "reset")
        for q in range(SEG // 128):
            em.copy(reset[:, q * 128:(q + 1) * 128], cf[:, 512:640], E='pool')

        STh = [em.sb([64, 64], F32, "ST%d" % h) for h in range(4)]
        for h in range(4):
            em.memset(STh[h], 0.0)
        dW_r = Rot([em.sb([64, 64], F32, "dW") for _ in range(8)])

        banks = Rot([em.ps([128, 512], F32, "bk") for _ in range(8)])

        def seg_tiles(nm, dt=F32):
            return [em.sb([128, SEG], dt, nm + "%d" % hp) for hp in range(2)]
        tin = {n: seg_tiles("in_" + n) for n in din}
        cum = seg_tiles("cum")
        tmp = seg_tiles("tmp")
        tmp2 = seg_tiles("tmp2")
        kka = seg_tiles("kka")
        at = seg_tiles("at", BF16)
        bt = seg_tiles("bt", BF16)
        kt = seg_tiles("kt", BF16)
        rt = seg_tiles("rt", BF16)
        bh = seg_tiles("bh", BF16)
        kh = seg_tiles("kh", BF16)
        WC = [em.sb([128, NCH], F32, "WC%d" % hp) for hp in range(2)]
        vt = em.sb([128, NP, 256], BF16, "vt")
        yo = [em.sb([128, SEG], F32, "yo%d" % hp) for hp in range(2)]

        def rot(nm, shape, dt, n):
            return Rot([em.sb(shape, dt, nm) for _ in range(n)])
        NI = 4 * PI
        AB_r = rot("AB", [128, 256], BF16, 2 * NI)
        AK_r = rot("AK", [128, 256], BF16, 2 * NI)
        N_r = rot("N", [128, 128], BF16, 2 * NI + 4)
        Z_r = rot("Z", [128, 128], BF16, 2 * NI + 4)
        IZ_r = rot("IZ", [128, 128], BF16, 2 * NI + 4)
        X_r = rot("X", [128, 128], BF16, 2 * NI + 4)
        BK_r = rot("BK", [128, 128], BF16, NI + 4)
        Q1_r = rot("Q1", [64, 128], F32, NI + 4)
        Q2_r = rot("Q2", [64, 128], F32, NI + 4)
        MT_r = rot("MT", [64, 128], F32, NI + 4)
        G_r = rot("G", [64, 128], F32, NI + 4)

        for seg in range(Tn // SEG):
            s0 = seg * SEG
            for hp in range(2):
                for n in din:
                    em.dma(tin[n][hp], din[n][hp, :, s0:s0 + SEG])
            em.dma(vt, V(vin, vin[s0:s0 + SEG, :].ap.rearrange("(p t) c -> t p c", t=128)))
            for hp in range(2):
                r_, k_, kk_, a_, lw_ = (tin[n][hp] for n in ("r", "k", "kk", "a", "lw"))
                c_ = cum[hp]
                em.op('dve', lambda e: e.tensor_tensor_scan(out=c_.ap, data0=reset.ap, data1=lw_.ap, initial=0.0,
                                                            op0=ALU.mult, op1=ALU.add),
                      reads=[reset, lw_], writes=[c_])
                em.act(tmp[hp], c_, AF.Exp)
                em.tt(rt[hp], r_, tmp[hp], ALU.mult, E='pool')
                em.act(tmp2[hp], c_, AF.Exp, scale=-1.0)
                em.tt(kka[hp], kk_, a_, ALU.mult, E='pool')
                em.tt(bt[hp], kka[hp], tmp2[hp], ALU.mult)
                em.tt(kt[hp], k_, tmp2[hp], ALU.mult, E='pool')
                em.tt(tmp[hp], c_, lw_, ALU.subtract)
                em.act(tmp[hp], tmp[hp], AF.Exp)
                em.stt(at[hp], kk_, -1.0, tmp[hp], ALU.mult, ALU.mult)
                c3 = c_[:].re("p (c t) -> p c t", t=64)
                cC = c3[:, :, 63:64]
                em.tt(tmp2[hp][:].re("p (c t) -> p c t", t=64), cC.bc([128, NCH, 64]), c3, ALU.subtract, E='pool')
                em.act(tmp2[hp], tmp2[hp], AF.Exp)
                em.tt(bh[hp], kka[hp], tmp2[hp], ALU.mult)
                em.tt(kh[hp], k_, tmp2[hp], ALU.mult, E='pool')
                em.act(WC[hp][:].re("p (c o) -> p c o", o=1), cC, AF.Exp)

            for p0 in range(0, NP, PI):
                items = [(p, h4 // 2, slice(64 * (h4 % 2), 64 * (h4 % 2) + 64), h4)
                         for p in range(p0, min(NP, p0 + PI)) for h4 in range(4)]
                AB, AK, Nn, Z, IZ, X = {}, {}, {}, {}, {}, {}
                Q1d, Q2d, MTd, Gd = {}, {}, {}, {}
                for p, hp, ps, h4 in items:
                    key = (p, h4)
                    tsl = slice(p * 128, (p + 1) * 128)
                    pa = banks.next()
                    em.mm(pa[:, 0:128], bt[hp][ps, tsl], at[hp][ps, tsl])
                    em.mm(pa[:, 128:256], bt[hp][ps, tsl], rt[hp][ps, tsl])
                    em.mm(pa[:, 256:384], kt[hp][ps, tsl], at[hp][ps, tsl])
                    em.mm(pa[:, 384:512], kt[hp][ps, tsl], rt[hp][ps, tsl])
                    AB[key] = AB_r.next()
                    AK[key] = AK_r.next()
                    em.tt(AB[key], pa[:, 0:256], mSI, ALU.mult)
                    em.tt(AK[key], pa[:, 256:512], mSI, ALU.mult)
                    pn = banks.next()
                    em.mm(pn[:, 0:128], at[hp][ps, tsl], bt[hp][ps, tsl])
                    em.mm(pn[:, 128:192], at[hp][ps, tsl], Ib[ps, ps])
                    pn2 = banks.next()
                    em.mm(pn2[:, 0:64], AK[key][:, 0:128], vt[:, p, h4 * 64:(h4 + 1) * 64])
                    Nn[key] = N_r.next()
                    em.tt(Nn[key], pn[:, 0:128], mL, ALU.mult)
                    Z[key] = AB[key][:, 0:128]
                    IZ[key] = IZ_r.next()
                    em.tt(IZ[key], AB[key][:, 0:128], Ib, ALU.add, E='pool')
                    X[key] = X_r.next()
                    em.copy(X[key][:, 0:64], pn[:, 128:192], E='act')
                    em.copy(X[key][:, 64:128], pn2[:, 0:64], E='act')
                for j in range(6):
                    for p, hp, ps, h4 in items:
                        key = (p, h4)
                        px = banks.next()
                        em.mm(px[:, 0:128], IZ[key], X[key])
                        if j < 5:
                            em.mm(px[:, 128:256], Nn[key], Z[key])
                            if j < 4:
                                em.mm(px[:, 256:384], Z[key], Nn[key])
                        X[key] = X_r.next()
                        em.copy(X[key], px[:, 0:128], E='act')
                        if j < 5:
                            Zn = Z_r.next()
                            IZ[key] = IZ_r.next()
                            em.tt(IZ[key], px[:, 128:256], Ib, ALU.add)
                            if j < 4:
                                em.copy(Zn, px[:, 128:256], E='act')
                                Nx = N_r.next()
                                em.copy(Nx, px[:, 256:384], E='dve')
                                Nn[key] = Nx
                                Z[key] = Zn
                for p, hp, ps, h4 in items:
                    key = (p, h4)
                    tsl = slice(p * 128, (p + 1) * 128)
                    pb = banks.next()
                    em.mm(pb[:, 0:64], bh[hp][ps, tsl], Ib[ps, ps])
                    em.mm(pb[:, 64:128], kh[hp][ps, tsl], Ib[ps, ps])
                    em.mm(pb[0:64, 128:256], X[key][:, 0:64], AB[key][:, 128:256], start=True, stop=False)
                    em.mm(pb[0:64, 128:256], Ib[ps, ps], rt[hp][ps, tsl], start=False, stop=True)
                    em.mm(pb[0:64, 256:384], X[key][:, 64:128], AB[key][:, 128:256], start=True, stop=False)
                    em.mm(pb[0:64, 256:384], vt[:, p, h4 * 64:(h4 + 1) * 64], AK[key][:, 128:256], start=False, stop=True)
                    BK = BK_r.next()
                    em.copy(BK, pb[:, 0:128], E='act')
                    Q1d[key] = Q1_r.next()
                    Q2d[key] = Q2_r.next()
                    em.copy(Q1d[key], pb[0:64, 128:256], E='dve')
                    em.copy(Q2d[key], pb[0:64, 256:384], E='act')
                    pms = [banks.next(), banks.next()]
                    for c in range(2):
                        cs = slice(64 * c, 64 * c + 64)
                        pm = pms[c]
                        em.mm(pm[0:64, 0:64], X[key][cs, 0:64], BK[cs, 0:64])
                        em.mm(pm[0:64, 64:128], BK[cs, 0:64], X[key][cs, 64:128], start=True, stop=False)
                        em.mm(pm[0:64, 64:128], BK[cs, 64:128], vt[cs, p, h4 * 64:(h4 + 1) * 64], start=False, stop=True)
                    MTd[key] = MT_r.next()
                    Gd[key] = G_r.next()
                    for c in range(2):
                        ch = p * 2 + c
                        dW = dW_r.next()
                        em.ts(dW, If[ps, ps], WC[hp][ps, ch:ch + 1], None, op0=ALU.mult, E='pool')
                        em.tt(MTd[key][:, 64 * c:64 * c + 64], pms[c][0:64, 0:64], dW, ALU.add)
                        em.copy(Gd[key][:, 64 * c:64 * c + 64], pms[c][0:64, 64:128], E='dve')
                for p in range(p0, min(NP, p0 + PI)):
                    for c in range(2):
                        for h4 in range(4):
                            key = (p, h4)
                            hp, ps = h4 // 2, slice(64 * (h4 % 2), 64 * (h4 % 2) + 64)
                            pc = banks.next()
                            em.mm(pc[0:64, 0:64], STh[h4], Q1d[key][:, 64 * c:64 * c + 64])
                            em.tt(yo[hp][ps, p * 128 + 64 * c:p * 128 + 64 * c + 64], pc[0:64, 0:64],
                                  Q2d[key][:, 64 * c:64 * c + 64], ALU.add)
                            em.mm(pc[0:64, 64:128], MTd[key][:, 64 * c:64 * c + 64], STh[h4])
                            em.tt(STh[h4], pc[0:64, 64:128], Gd[key][:, 64 * c:64 * c + 64], ALU.add)
            for hp in range(2):
                em.dma(yT[hp, :, s0:s0 + SEG], yo[hp])
        em.finish()
    return nc


def build_rwkv_pre(ntok=TPC):
    NT = 256
    nc = new_nc()
    with ExitStack() as st:
        em = Em(nc, st)
        xT = em.dram("xT", [D, ntok + 1], F32, "ExternalInput")
        scal = em.dram("scal", [128, 105], F32, "ExternalInput")
        cst = em.dram("cst", [128, 128], F32, "ExternalInput")
        w_rkv = em.dram("w_rkv", [3, D, D], F32, "ExternalInput")
        w1 = em.dram("w1", [D, 64], F32, "ExternalInput")
        a1 = em.dram("a1", [D, 64], F32, "ExternalInput")
        g1 = em.dram("g1", [D, 128], F32, "ExternalInput")
        w2 = em.dram("w2", [64, D], F32, "ExternalInput")
        a2 = em.dram("a2", [64, D], F32, "ExternalInput")
        g2 = em.dram("g2", [128, D], F32, "ExternalInput")
        outs = {n: em.dram(n, [D, ntok], F32, "ExternalOutput") for n in ("r", "k", "kk", "a", "lw")}
        outs["g"] = em.dram("g", [D, ntok], BF16, "ExternalOutput")
        outs["bonus"] = em.dram("bonus", [D, ntok], BF16, "ExternalOutput")
        vtok = em.dram("vtok", [ntok, D], BF16, "ExternalOutput")

        sc = em.sb([128, 105], F32, "sc")
        em.dma(sc, scal)
        col = lambda i: sc[:, 8 * i:8 * i + 8]
        sc1p = em.sb([128, 8], F32, "sc1p")
        em.ts(sc1p, col(0), 1.0, None, op0=ALU.add)
        sh1, w0, a0, k_k, k_a, r_k = col(1), col(8), col(9), col(10), col(11), col(12)
        hv = sc[:, 104:105]
        bd_f = em.sb([128, 128], F32, "bd_f")
        em.dma(bd_f, cst)
        bd = em.sb([128, 128], BF16, "bd")
        em.copy(bd, bd_f)

        W = [[em.sb([128, D], BF16, "W%d_%d" % (i, k)) for k in range(8)] for i in range(3)]
        for i in range(3):
            for k in range(8):
                em.dma(W[i][k], w_rkv[i, k * 128:(k + 1) * 128, :], Q='pool')
        w1s = em.sb([128, 8, 64], BF16, "w1s")
        a1s = em.sb([128, 8, 64], BF16, "a1s")
        g1s = em.sb([128, 8, 128], BF16, "g1s")
        em.dma(w1s, V(w1, w1[:].ap.rearrange("(k p) n -> p k n", p=128)), Q='pool')
        em.dma(a1s, V(a1, a1[:].ap.rearrange("(k p) n -> p k n", p=128)), Q='pool')
        em.dma(g1s, V(g1, g1[:].ap.rearrange("(k p) n -> p k n", p=128)), Q='pool')
        w2s = em.sb([64, D], BF16, "w2s")
        a2s = em.sb([64, D], BF16, "a2s")
        g2s = em.sb([128, D], BF16, "g2s")
        em.dma(w2s, w2, Q='pool')
        em.dma(a2s, a2, Q='pool')
        em.dma(g2s, g2, Q='pool')

        xt_r = Rot([em.sb([128, 8, NT + 1], F32, "xt") for _ in range(2)])
        u = em.sb([128, 8, NT + 1], F32, "u")
        xx = em.sb([128, 8, NT], F32, "xx")
        lerp = [em.sb([128, 8, NT], BF16, "lerp%d" % i) for i in range(6)]
        hw = em.sb([64, NT], BF16, "hw")
        ha = em.sb([64, NT], BF16, "ha")
        hg = em.sb([128, NT], BF16, "hg")
        banks = Rot([em.ps([128, 512], F32, "bk") for _ in range(8)])

        def rot(nm, dt, n=2):
            return Rot([em.sb([128, NT], dt, nm) for _ in range(n)])
        r_r, k_r, v_r, kp_r, kk_r, a_r, lw_r = (rot(n, F32, 3) for n in ("r", "k", "v", "kp", "kk", "a", "lw"))
        g_r, bo_r, sq_r, t_r = rot("g", BF16, 3), rot("bo", BF16, 3), rot("sq", BF16, 3), rot("t", F32, 8)
        tb_r = rot("tb", BF16, 3)
        vt_r = Rot([em.sb([128, D], BF16, "vt") for _ in range(2)])

        xv = xT[:].ap.rearrange("(c p) t -> p c t", p=128)
        def load_x(tt_):
            xt_ = xt_r.next()
            em.dma(xt_, V(xT, xv[:, :, tt_ * NT:tt_ * NT + NT + 1]))
            return xt_

        xt_next = load_x(0)
        for tt in range(ntok // NT):
            t0 = tt * NT
            xt = xt_next
            for c in range(8):
                em.ts(u[:, c, :], xt[:, c, :], sc1p[:, c:c + 1], sh1[:, c:c + 1], op0=ALU.mult, op1=ALU.add,
                      E=('dve' if c % 2 == 0 else 'pool'))
            if tt + 1 < ntok // NT:
                xt_next = load_x(tt + 1)
            if tt == 0:
                em.ts(u[:, :, 0:1], u[:, :, 0:1], hv, None, op0=ALU.mult)
            for c in range(8):
                em.tt(xx[:, c, :], u[:, c, 0:NT], u[:, c, 1:NT + 1], ALU.subtract, E=('pool' if c % 2 == 0 else 'dve'))
            for i in range(6):
                for c in range(8):
                    em.stt(lerp[i][:, c, :], xx[:, c, :], sc[:, 8 * (2 + i) + c:8 * (2 + i) + c + 1], u[:, c, 1:NT + 1],
                           ALU.mult, ALU.add, E=('dve' if (c + i) % 2 == 0 else 'pool'))
            p1 = banks.next()
            for k in range(8):
                em.mm(p1[0:64, 0:NT], w1s[:, k, :], lerp[3][:, k, :], start=(k == 0), stop=(k == 7))
            em.act(hw, p1[0:64, 0:NT], AF.Tanh)
            p2 = banks.next()
            for k in range(8):
                em.mm(p2[0:64, 0:NT], a1s[:, k, :], lerp[4][:, k, :], start=(k == 0), stop=(k == 7))
            em.copy(ha, p2[0:64, 0:NT], E='dve')
            p3 = banks.next()
            for k in range(8):
                em.mm(p3[:, 0:NT], g1s[:, k, :], lerp[5][:, k, :], start=(k == 0), stop=(k == 7))
            em.act(hg, p3[:, 0:NT], AF.Sigmoid)
            for tb in range(NT // 128):
                vt = vt_r.next()
                for half in range(2):
                    pv = banks.next()
                    for k in range(8):
                        em.mm(pv[:, 0:512], lerp[2][:, k, tb * 128:(tb + 1) * 128], W[2][k][:, half * 512:(half + 1) * 512],
                              start=(k == 0), stop=(k == 7))
                    em.copy(vt[:, half * 512:(half + 1) * 512], pv[:, 0:512], E=('act' if half == 0 else 'dve'))
                em.dma(vtok[t0 + tb * 128:t0 + (tb + 1) * 128, :], vt)
            def stage1(j):
                js = slice(j * 128, (j + 1) * 128)
                jc = slice(j, j + 1)
                pr, pk, pv = banks.next(), banks.next(), banks.next()
                for (pp, i) in ((pr, 0), (pk, 1), (pv, 2)):
                    for k in range(8):
                        em.mm(pp[:, 0:NT], W[i][k][:, js], lerp[i][:, k, :], start=(k == 0), stop=(k == 7))
                r_, k_, v_ = r_r.next(), k_r.next(), v_r.next()
                em.copy(r_, pr[:, 0:NT], E='act')
                em.copy(k_, pk[:, 0:NT], E='dve')
                em.copy(v_, pv[:, 0:NT], E='act')
                pl = banks.next()
                em.mm(pl[:, 0:NT], w2s[:, js], hw)
                lw_ = lw_r.next()
                em.act(lw_, pl[:, 0:NT], AF.Sigmoid, bias=w0[:, jc])
                em.ts(lw_, lw_, -0.6065306597126334, None, op0=ALU.mult, E='pool')
                pa = banks.next()
                em.mm(pa[:, 0:NT], a2s[:, js], ha)
                a_ = a_r.next()
                em.act(a_, pa[:, 0:NT], AF.Sigmoid, bias=a0[:, jc])
                pg = banks.next()
                em.mm(pg[:, 0:NT], g2s[:, js], hg)
                g_ = g_r.next()
                em.copy(g_, pg[:, 0:NT], E='dve')
                kkr = t_r.next()
                em.ts(kkr, k_, k_k[:, jc], None, op0=ALU.mult, E='pool')
                sq = sq_r.next()
                em.act(sq, kkr, AF.Square)
                tk = t_r.next()
                em.ts(tk, a_, -1.0, k_a[:, jc], op0=ALU.add, op1=ALU.mult, E='pool')
                kp = kp_r.next()
                em.stt(kp, tk, 1.0, k_, ALU.add, ALU.mult)
                tb_ = tb_r.next()
                em.stt(tb_, r_, r_k[:, jc], kp, ALU.mult, ALU.mult)
                return dict(js=js, r_=r_, v_=v_, lw_=lw_, a_=a_, g_=g_, kkr=kkr, sq=sq, kp=kp, tb_=tb_)

            def stage2(d):
                ps_ = banks.next()
                em.mm(ps_[:, 0:NT], bd, d["sq"])
                rn = t_r.next()
                em.ts(rn, ps_[:, 0:NT], 1e-6, None, op0=ALU.add)
                em.act(rn, rn, AF.Sqrt)
                em.recip(rn, rn)
                kk_ = kk_r.next()
                em.tt(kk_, d["kkr"], rn, ALU.mult, E='pool')
                pb = banks.next()
                em.mm(pb[:, 0:NT], bd, d["tb_"])
                bo = bo_r.next()
                em.tt(bo, pb[:, 0:NT], d["v_"], ALU.mult)
                for (nm, tl) in (("r", d["r_"]), ("k", d["kp"]), ("kk", kk_), ("a", d["a_"]), ("lw", d["lw_"]),
                                 ("g", d["g_"]), ("bonus", bo)):
                    em.dma(outs[nm][d["js"], t0:t0 + NT], tl)

            prev = stage1(0)
            for j in range(8):
                nxt = stage1(j + 1) if j + 1 < 8 else None
                stage2(prev)
                prev = nxt
        em.finish()
    return nc


def build_post(mode, ntok=TPC, hscale=1.0, has_bias=True):
    NT = 256
    nc = new_nc()
    with ExitStack() as st:
        em = Em(nc, st)
        xT = em.dram("xT", [D, ntok], F32, "ExternalInput")
        w_o = em.dram("w_o", [D, D], F32, "ExternalInput")
        scal = em.dram("scal", [128, 56], F32, "ExternalInput")
        if mode == 'rwkv':
            yin = em.dram("yin", [D, ntok], F32, "ExternalInput")
            gin = em.dram("g", [D, ntok], BF16, "ExternalInput")
            bin_ = em.dram("bonus", [D, ntok], BF16, "ExternalInput")
            cst = em.dram("cst", [128, 128], F32, "ExternalInput")
        elif mode in ('gdn', 'diff'):
            yin = em.dram("yin", [D, ntok], F32, "ExternalInput")
            if mode == 'gdn':
                zin = em.dram("zs", [D, ntok], BF16, "ExternalInput")
        else:
            oin = em.dram("oT", [D, ntok], BF16, "ExternalInput")
        yT = em.dram("yT", [D, ntok], F32, "ExternalOutput")

        sc = em.sb([128, 56], F32, "sc")
        em.dma(sc, scal)
        col = lambda i: sc[:, 8 * i:8 * i + 8]
        gs = em.sb([128, 8], F32, "gs")
        em.ts(gs, col(0), 1.0, 1.0 / ALPHA, op0=ALU.add, op1=ALU.mult)
        gsb = em.sb([128, 8], F32, "gsb")
        em.tt(gsb, gs, col(3), ALU.mult)
        if hscale != 1.0:
            em.ts(sc[:, 32:33], sc[:, 32:33], float(hscale), None, op0=ALU.mult)
        ones_bf = em.sb([128, 128], BF16, "ones")
        em.memset(ones_bf, 1.0)
        if mode == 'rwkv':
            bd_f = em.sb([128, 128], F32, "bd_f")
            em.dma(bd_f, cst)
            bd = em.sb([128, 128], BF16, "bd")
            em.copy(bd, bd_f)
        Wo = [em.sb([128, D], BF16, "Wo%d" % k) for k in range(8)]
        for k in range(8):
            em.dma(Wo[k], w_o[k * 128:(k + 1) * 128, :], Q='pool')

        xt_r = Rot([em.sb([128, 8, NT], F32, "xt") for _ in range(2)])
        ot_r = Rot([em.sb([128, 8, NT], BF16, "ot") for _ in range(2)])
        z_r = Rot([em.sb([128, 8, NT], F32, "z") for _ in range(2)])
        zb_r = Rot([em.sb([128, 8, NT], BF16, "zb") for _ in range(2)])
        zsq_r = Rot([em.sb([128, 8, NT], BF16, "zsq") for _ in range(2)])
        mean = em.sb([128, NT], F32, "mean")
        rstd = em.sb([128, NT], F32, "rstd")
        t1 = em.sb([128, NT], F32, "t1")
        pbank = Rot([em.ps([128, 512], F32, "pb") for _ in range(2)])
        ps1 = em.ps([128, 512], F32, "ps1")
        ps2 = em.ps([128, 512], F32, "ps2")
        if mode != 'plain':
            pm2 = em.ps([128, 4, NT], F32, "pm2")
            pq2 = em.ps([128, 4, NT], F32, "pq2")
            hb = lambda nm, dt=F32: Rot([em.sb([128, 4, NT], dt, nm) for _ in range(2)])
            gm_h, gr_h, g1_h = hb("gm_h"), hb("gr_h"), hb("g1_h")
            ybig = em.sb([128, 8, NT], BF16, "ybig")
            ysqb = em.sb([128, 8, NT], BF16, "ysqb")
        if mode in ('gdn', 'diff'):
            yt_r = Rot([em.sb([128, 8, NT], F32, "yt") for _ in range(2)])
            zt_r = Rot([em.sb([128, 8, NT], BF16, "zt") for _ in range(2)])
            ysq_r = Rot([em.sb([128, NT], BF16, "ysq") for _ in range(2)])
            gr_r = Rot([em.sb([128, NT], F32, "gr") for _ in range(2)])
            gt1_r = Rot([em.sb([128, NT], F32, "gt1") for _ in range(2)])
        if mode == 'rwkv':
            yt_r = Rot([em.sb([128, 8, NT], F32, "yt") for _ in range(2)])
            gt_r = Rot([em.sb([128, 8, NT], BF16, "gt") for _ in range(2)])
            bt_r = Rot([em.sb([128, 8, NT], BF16, "bt") for _ in range(2)])
            yb_r = Rot([em.sb([128, NT], BF16, "yb") for _ in range(2)])
            ysq_r = Rot([em.sb([128, NT], BF16, "ysq") for _ in range(2)])
            gm_r = Rot([em.sb([128, NT], F32, "gm") for _ in range(2)])
            gr_r = Rot([em.sb([128, NT], F32, "gr") for _ in range(2)])
            gt1_r = Rot([em.sb([128, NT], F32, "gt1") for _ in range(2)])

        fm = lambda tl: tl[:].ap.rearrange("(c p) t -> p c t", p=128)
        xv, yv = fm(xT), fm(yT)
        for tt in range(ntok // NT):
            tsl = slice(tt * NT, (tt + 1) * NT)
            xt = xt_r.next()
            ot = ot_r.next()
            z, zb, zsq = z_r.next(), zb_r.next(), zsq_r.next()
            em.dma(xt, V(xT, xv[:, :, tsl]))
            if mode == 'rwkv':
                yt, gt, bt = yt_r.next(), gt_r.next(), bt_r.next()
                em.dma(yt, V(yin, fm(yin)[:, :, tsl]))
                em.dma(gt, V(gin, fm(gin)[:, :, tsl]))
                em.dma(bt, V(bin_, fm(bin_)[:, :, tsl]))
                em.copy(ybig, yt, E='pool')
                em.act(ysqb, yt, AF.Square)
                for hf in range(2):
                    hs_ = slice(4 * hf, 4 * hf + 4)
                    for j in range(4):
                        em.mm(pm2[:, j, :], bd, ybig[:, 4 * hf + j, :])
                    for j in range(4):
                        em.mm(pq2[:, j, :], bd, ysqb[:, 4 * hf + j, :])
                    gm, gr, g1 = gm_h.next(), gr_h.next(), g1_h.next()
                    em.ts(gm, pm2, 1.0 / 64, None, op0=ALU.mult)
                    em.tt(g1, gm, gm, ALU.mult, E='pool')
                    em.stt(gr, pq2, 1.0 / 64, g1, ALU.mult, ALU.subtract)
                    em.ts(gr, gr, 64e-5, None, op0=ALU.add, E='pool')
                    em.act(gr, gr, AF.Sqrt)
                    em.recip(gr, gr)
                    em.tt(g1, yt[:, hs_, :], gm, ALU.subtract, E='pool')
                    em.tt(g1, g1, gr, ALU.mult)
                    for j in range(4):
                        jj = 4 * hf + j
                        em.act(g1[:, j, :], g1[:, j, :], AF.Identity, bias=sc[:, 40 + jj:41 + jj], scale=sc[:, 32 + jj:33 + jj])
                    em.tt(g1, g1, bt[:, hs_, :], ALU.add)
                    em.tt(ot[:, hs_, :], g1, gt[:, hs_, :], ALU.mult, E='pool')
            elif mode in ('gdn', 'diff'):
                yt = yt_r.next()
                em.dma(yt, V(yin, fm(yin)[:, :, tsl]))
                if mode == 'gdn':
                    zt = zt_r.next()
                    em.dma(zt, V(zin, fm(zin)[:, :, tsl]))
                heps = 1e-6 if mode == 'gdn' else 1e-5
                em.act(ysqb, yt, AF.Square)
                for hf in range(2):
                    hs_ = slice(4 * hf, 4 * hf + 4)
                    for j in range(4):
                        em.mm(pq2[:, j, :], ones_bf, ysqb[:, 4 * hf + j, :])
                    gr, g1 = gr_h.next(), g1_h.next()
                    em.ts(gr, pq2, 1.0 / 128, heps, op0=ALU.mult, op1=ALU.add)
                    em.act(gr, gr, AF.Sqrt)
                    em.recip(gr, gr)
                    if mode == 'gdn':
                        em.stt(g1, yt[:, hs_, :], sc[:, 32:33], gr, ALU.mult, ALU.mult)
                        em.tt(ot[:, hs_, :], g1, zt[:, hs_, :], ALU.mult, E='pool')
                    else:
                        em.stt(ot[:, hs_, :], yt[:, hs_, :], sc[:, 32:33], gr, ALU.mult, ALU.mult)
            else:
                em.dma(ot, V(oin, fm(oin)[:, :, tsl]))
            for j in range(8):
                py = pbank.next()
                for k in range(8):
                    em.mm(py[:, 0:NT], Wo[k][:, j * 128:(j + 1) * 128], ot[:, k, :], start=(k == 0), stop=(k == 7))
                em.stt(z[:, j, :], py[:, 0:NT], gs[:, j:j + 1], xt[:, j, :], ALU.mult, ALU.add)
                if has_bias:
                    em.ts(z[:, j, :], z[:, j, :], gsb[:, j:j + 1], None, op0=ALU.add, E='pool')
            emit_ln(em, z, NT, ones_bf, col(1), col(2), z, ps1[:, 0:NT], ps2[:, 0:NT], (zb, zsq, mean, rstd, t1))
            em.dma(V(yT, yv[:, :, tsl]), z, Q='pool')
        em.finish()
    return nc


def gdn_level_masks():
    i = np.arange(128)
    out = []
    for li in range(7):
        m = 1 << li
        out.append((((i[:, None] // (2 * m)) == (i[None, :] // (2 * m))) &
                    ((i[:, None] // m) < (i[None, :] // m))).astype(np.float32))
    return out


def build_gdn_scan(Tn=T, SEG=1024, PI=4):
    NCK = SEG // 128
    nc = new_nc()
    with ExitStack() as st:
        em = Em(nc, st)
        qin = em.dram("q", [2, 128, Tn], BF16, "ExternalInput")
        kin = em.dram("k", [2, 128, Tn], BF16, "ExternalInput")
        vin = em.dram("v", [2, 128, Tn], BF16, "ExternalInput")
        bin_ = em.dram("beta", [2, Tn], F32, "ExternalInput")
        gin = em.dram("g", [2, Tn], F32, "ExternalInput")
        cst = em.dram("cst", [128, 512 + 7 * 128], F32, "ExternalInput")
        oT = em.dram("oT", [2, 128, Tn], F32, "ExternalOutput")

        cf = em.sb([128, 512 + 7 * 128], F32, "cf")
        em.dma(cf, cst)
        lvl = [cf[:, 512 + 128 * i:512 + 128 * (i + 1)] for i in range(7)]
        mnegI = cf[:, 0:128]
        nmS = cf[:, 128:256]
        If = cf[:, 256:384]
        Ib = em.sb([128, 128], BF16, "Ib")
        em.copy(Ib, cf[:, 256:384])
        e0 = em.sb([128, 2], F32, "e0")
        em.copy(e0[:, 0:1], cf[:, 256:257])
        em.copy(e0[:, 1:2], cf[:, 256:257])
        reset = em.sb([128, SEG], F32, "reset")
        for c in range(NCK):
            em.copy(reset[:, c * 128:(c + 1) * 128], cf[:, 384:512], E='pool')
        S = [em.sb([128, 128], F32, "S%d" % h) for h in range(2)]
        for h in range(2):
            em.memset(S[h], 0.0)
        banks = Rot([em.ps([128, 512], F32, "bk") for _ in range(8)])

        qt = [em.sb([128, SEG], BF16, "qt%d" % h) for h in range(2)]
        kt = [em.sb([128, SEG], BF16, "kt%d" % h) for h in range(2)]
        vt = [em.sb([128, SEG], BF16, "vt%d" % h) for h in range(2)]
        bb = [em.sb([128, SEG], F32, "bb%d" % h) for h in range(2)]
        gb = [em.sb([128, SEG], F32, "gb%d" % h) for h in range(2)]
        gc = [em.sb([128, SEG], F32, "gc%d" % h) for h in range(2)]
        eg = [em.sb([128, SEG], F32, "eg%d" % h) for h in range(2)]
        oo = [em.sb([128, SEG], F32, "oo%d" % h) for h in range(2)]

        def rot(nm, shape, dt, n):
            return Rot([em.sb(shape, dt, nm) for _ in range(n)])
        NI = 2 * PI
        col_r = rot("col", [128, 8], F32, NI + 2)
        DT_r = rot("DT", [128, 128], F32, 4)
        nb_r = rot("nb", [128, 128], F32, 4)
        t_r = rot("tt", [128, 128], F32, 4)
        Aqk_r = rot("Aqk", [128, 128], BF16, NI + 2)
        Z_r = rot("Z", [128, 128], F32, NI + 2)
        Zl_r = rot("Zl", [128, 128], F32, NI + 2)
        W_r = rot("W", [128, 128], F32, NI + 2)
        T_r = rot("T", [128, 128], F32, 2 * NI + 2)
        U_r = rot("U", [128, 128], F32, 2 * NI + 2)
        X0_r = rot("X0", [128, 256], F32, NI + 2)
        X_r = rot("X", [128, 256], BF16, NI + 2)
        kd_r = rot("kd", [128, 128], BF16, NI + 2)
        MT_r = rot("MT", [128, 128], F32, NI + 2)
        G_r = rot("G", [128, 128], F32, NI + 2)
        Q1_r = rot("Q1", [128, 128], F32, NI + 2)
        Q2_r = rot("Q2", [128, 128], F32, NI + 2)
        gI_r = rot("gI", [128, 128], F32, 4)

        for seg in range(Tn // SEG):
            s0 = seg * SEG
            for h in range(2):
                em.dma(qt[h], qin[h, :, s0:s0 + SEG])
                em.dma(kt[h], kin[h, :, s0:s0 + SEG])
                em.dma(vt[h], vin[h, :, s0:s0 + SEG])
                em.dma(bb[h], V(bin_, bin_[h:h + 1, s0:s0 + SEG].ap.partition_broadcast(128)))
                em.dma(gb[h], V(gin, gin[h:h + 1, s0:s0 + SEG].ap.partition_broadcast(128)))
                g_, c_ = gb[h], gc[h]
                em.op('dve', lambda e, c_=c_, g_=g_: e.tensor_tensor_scan(out=c_.ap, data0=reset.ap, data1=g_.ap,
                                                                          initial=0.0, op0=ALU.mult, op1=ALU.add),
                      reads=[reset, g_], writes=[c_])
                em.act(eg[h], c_, AF.Exp)
            for ck0 in range(0, NCK, PI):
                items = [(ck, h) for ck in range(ck0, min(NCK, ck0 + PI)) for h in range(2)]
                Z, X, cols, Aqk, kd = {}, {}, {}, {}, {}
                MTd, Gd, Q1d, Q2d = {}, {}, {}, {}
                for ck, h in items:
                    key = (ck, h)
                    tsl = slice(ck * 128, (ck + 1) * 128)
                    pcol = banks.next()
                    em.mm(pcol[:, 0:1], gc[h][:, tsl], e0[:, 0:1])
                    em.mm(pcol[:, 1:2], bb[h][:, tsl], e0[:, 0:1])
                    cl = col_r.next()
                    cols[key] = cl
                    em.copy(cl[:, 0:2], pcol[:, 0:2], E='dve')
                    em.ts(cl[:, 2:3], cl[:, 0:1], -1.0, None, op0=ALU.mult)
                    em.act(cl[:, 3:4], cl[:, 0:1], AF.Exp)
                    em.tt(cl[:, 4:5], cl[:, 3:4], cl[:, 1:2], ALU.mult)
                    em.act(cl[:, 5:6], cl[:, 2:3], AF.Exp, bias=gc[h][:, ck * 128 + 127:ck * 128 + 128])
                    em.copy(cl[:, 6:7], eg[h][:, ck * 128 + 127:ck * 128 + 128], E='pool')
                    DT = DT_r.next()
                    em.tt(DT, gc[h][:, tsl], mnegI, ALU.add, E='pool')
                    em.act(DT, DT, AF.Exp, bias=cl[:, 2:3])
                    nb = nb_r.next()
                    em.tt(nb, bb[h][:, tsl], nmS, ALU.mult, E='pool')
                    pk = banks.next()
                    em.mm(pk[:, 0:128], kt[h][:, tsl], kt[h][:, tsl])
                    em.mm(pk[:, 128:256], kt[h][:, tsl], qt[h][:, tsl])
                    em.mm(pk[:, 256:384], kt[h][:, tsl], Ib)
                    em.mm(pk[:, 384:512], vt[h][:, tsl], Ib)
                    t1 = t_r.next()
                    em.tt(t1, pk[:, 0:128], DT, ALU.mult)
                    Z[key] = Z_r.next()
                    em.tt(Z[key], t1, nb, ALU.mult, E='pool')
                    Aqk[key] = Aqk_r.next()
                    em.tt(Aqk[key], pk[:, 128:256], DT, ALU.mult)
                    X0 = X0_r.next()
                    X[key] = X0
                    em.ts(X0[:, 0:128], pk[:, 384:512], cl[:, 1:2], None, op0=ALU.mult)
                    em.ts(X0[:, 128:256], pk[:, 256:384], cl[:, 4:5], None, op0=ALU.mult)
                    kd[key] = kd_r.next()
                    em.act(kd[key], pk[:, 256:384], AF.Copy, scale=cl[:, 5:6])
                Tm, Um = {}, {}
                for ck, h in items:
                    key = (ck, h)
                    Um[key] = U_r.next()
                    em.tt(Um[key], Z[key], lvl[0], ALU.mult, E='pool')
                    em.tt(Um[key], Um[key], If, ALU.add, E='pool')
                    pt = banks.next()
                    em.mm(pt[:, 0:128], Um[key], If)
                    Tm[key] = T_r.next()
                    em.copy(Tm[key], pt[:, 0:128], E='act')
                for li in range(1, 7):
                    pws, Wts = {}, {}
                    for ck, h in items:
                        key = (ck, h)
                        Zl = Zl_r.next()
                        em.tt(Zl, Z[key], lvl[li], ALU.mult, E='pool')
                        pw = banks.next()
                        em.mm(pw[:, 0:128], Zl, Tm[key])
                        Wt = W_r.next()
                        em.copy(Wt, pw[:, 0:128], E='act')
                        pws[key], Wts[key] = pw, Wt
                    for ck, h in items:
                        key = (ck, h)
                        pw, Wt = pws[key], Wts[key]
                        if li < 6:
                            em.mm(pw[:, 128:256], Um[key], Wt)
                        em.mm(pw[:, 256:384], Wt, Um[key])
                        Un = U_r.next()
                        em.tt(Un, pw[:, 256:384], Um[key], ALU.add)
                        if li < 6:
                            Tn_ = T_r.next()
                            em.tt(Tn_, pw[:, 128:256], Tm[key], ALU.add)
                            Tm[key] = Tn_
                        Um[key] = Un
                for ck, h in items:
                    key = (ck, h)
                    tsl = slice(ck * 128, (ck + 1) * 128)
                    px = banks.next()
                    em.mm(px[:, 0:256], Um[key], X[key])
                    Xb = X_r.next()
                    em.copy(Xb, px[:, 0:256], E='act')
                    cl = cols[key]
                    uc, wc = Xb[:, 0:128], Xb[:, 128:256]
                    pm = banks.next()
                    em.mm(pm[:, 0:128], wc, kd[key])
                    em.mm(pm[:, 128:256], kd[key], uc)
                    em.mm(pm[:, 256:384], wc, Aqk[key])
                    em.mm(pm[:, 384:512], uc, Aqk[key])
                    gI = gI_r.next()
                    em.ts(gI, If, cl[:, 6:7], None, op0=ALU.mult, E='pool')
                    MTd[key], Gd[key], Q1d[key], Q2d[key] = MT_r.next(), G_r.next(), Q1_r.next(), Q2_r.next()
                    em.stt(MTd[key], pm[:, 0:128], -1.0, gI, ALU.mult, ALU.add)
                    em.copy(Gd[key], pm[:, 128:256], E='act')
                    qd = t_r.next()
                    em.tt(qd, qt[h][:, tsl], eg[h][:, tsl], ALU.mult, E='pool')
                    em.stt(Q1d[key], pm[:, 256:384], -1.0, qd, ALU.mult, ALU.add)
                    em.copy(Q2d[key], pm[:, 384:512], E='act')
                for ck, h in items:
                    key = (ck, h)
                    tsl = slice(ck * 128, (ck + 1) * 128)
                    pc = banks.next()
                    em.mm(pc[:, 0:128], S[h], Q1d[key])
                    em.tt(oo[h][:, tsl], pc[:, 0:128], Q2d[key], ALU.add)
                    em.mm(pc[:, 128:256], MTd[key], S[h])
                    em.tt(S[h], pc[:, 128:256], Gd[key], ALU.add)
            for h in range(2):
                em.dma(oT[h, :, s0:s0 + SEG], oo[h])
        em.finish()
    return nc


def build_gdn_pre(ntok=TPC):
    NT = 256
    NH = NT + 3
    nc = new_nc()
    with ExitStack() as st:
        em = Em(nc, st)
        xT = em.dram("xT", [D, ntok + 3], F32, "ExternalInput")
        scal = em.dram("scal", [128, 113], F32, "ExternalInput")
        hsc = em.dram("hsc", [8, 2], F32, "ExternalInput")
        w_in = em.dram("w_in", [D, 4112], F32, "ExternalInput")
        outs = {n: em.dram(n, [D, ntok], BF16, "ExternalOutput") for n in ("q", "k", "v", "zs")}
        beta_o = em.dram("beta", [8, ntok], F32, "ExternalOutput")
        g_o = em.dram("g", [8, ntok], F32, "ExternalOutput")

        sc = em.sb([128, 113], F32, "sc")
        em.dma(sc, scal)
        sc1p = em.sb([128, 8], F32, "sc1p")
        em.ts(sc1p, sc[:, 0:8], 1.0, None, op0=ALU.add)
        hv = sc[:, 112:113]
        hs = em.sb([8, 2], F32, "hs")
        em.dma(hs, hsc)
        nea = em.sb([8, 1], F32, "nea")
        em.act(nea, hs[:, 0:1], AF.Exp)
        em.ts(nea, nea, -1.0, None, op0=ALU.mult)
        ones_bf = em.sb([128, 128], BF16, "ones")
        em.memset(ones_bf, 1.0)
        W = [em.sb([128, 4112], BF16, "W%d" % k) for k in range(8)]
        for k in range(8):
            em.dma(W[k], w_in[k * 128:(k + 1) * 128, :], Q='pool')

        xt_r = Rot([em.sb([128, 8, NH], F32, "xt") for _ in range(2)])
        ub = em.sb([128, 8, NH], BF16, "ub")
        banks = Rot([em.ps([128, 512], F32, "bk") for _ in range(8)])

        def rot(nm, n_, dt, n=2):
            return Rot([em.sb([128, n_], dt, nm) for _ in range(n)])
        pp_r, cv_r, s_r, sq_r, rn_r = rot("pp", NH, F32, 3), rot("cv", NT, F32, 3), rot("s", NT, F32, 3), rot("sq", NT, BF16, 3), rot("rn", NT, F32)
        ob_r = rot("ob", NT, BF16, 4)
        bt_r = Rot([em.sb([8, NT], F32, "bt") for _ in range(2)])
        gt_r = Rot([em.sb([8, NT], F32, "gt") for _ in range(2)])

        xv = xT[:].ap.rearrange("(c p) t -> p c t", p=128)
        def load_x(tt_):
            xt_ = xt_r.next()
            em.dma(xt_, V(xT, xv[:, :, tt_ * NT:tt_ * NT + NH]))
            return xt_

        xt_next = load_x(0)
        for tt in range(ntok // NT):
            t0 = tt * NT
            xt = xt_next
            for c in range(8):
                em.ts(ub[:, c, :], xt[:, c, :], sc1p[:, c:c + 1], sc[:, 8 + c:9 + c], op0=ALU.mult, op1=ALU.add,
                      E=('dve' if c % 2 == 0 else 'pool'))
            if tt == 0:
                em.ts(ub[:, :, 0:3], ub[:, :, 0:3], hv, None, op0=ALU.mult)
            if tt + 1 < ntok // NT:
                xt_next = load_x(tt + 1)
            def stage1(j):
                pp = banks.next()
                for k in range(8):
                    em.mm(pp[:, 0:NH], W[k][:, j * 128:(j + 1) * 128], ub[:, k, :], start=(k == 0), stop=(k == 7))
                if j >= 24:
                    ob = ob_r.next()
                    em.act(ob, pp[:, 3:NT + 3], AF.Silu)
                    em.dma(outs["zs"][(j - 24) * 128:(j - 23) * 128, t0:t0 + NT], ob)
                    return None
                ps_ = pp_r.next()
                em.copy(ps_, pp[:, 0:NH], E='act')
                cv = cv_r.next()
                cw = lambda kk: sc[:, 16 + j * 4 + kk:16 + j * 4 + kk + 1]
                e1 = 'dve' if j % 2 == 0 else 'pool'
                em.ts(cv, ps_[:, 3:NT + 3], cw(3), None, op0=ALU.mult, E=e1)
                em.stt(cv, ps_[:, 2:NT + 2], cw(2), cv, ALU.mult, ALU.add)
                em.stt(cv, ps_[:, 1:NT + 1], cw(1), cv, ALU.mult, ALU.add)
                em.stt(cv, ps_[:, 0:NT], cw(0), cv, ALU.mult, ALU.add)
                if j >= 16:
                    ob = ob_r.next()
                    em.act(ob, cv, AF.Silu)
                    em.dma(outs["v"][(j % 8) * 128:(j % 8 + 1) * 128, t0:t0 + NT], ob)
                    return None
                s_ = s_r.next()
                em.act(s_, cv, AF.Silu)
                sq = sq_r.next()
                em.act(sq, s_, AF.Square)
                return dict(j=j, s_=s_, sq=sq)

            def stage2(d):
                if d is None:
                    return
                j = d["j"]
                pn = banks.next()
                em.mm(pn[:, 0:NT], ones_bf, d["sq"])
                rn = rn_r.next()
                em.ts(rn, pn[:, 0:NT], 1e-6, None, op0=ALU.add)
                em.act(rn, rn, AF.Sqrt)
                em.recip(rn, rn)
                ob = ob_r.next()
                if j < 8:
                    em.stt(ob, d["s_"], 128.0 ** -0.5, rn, ALU.mult, ALU.mult)
                else:
                    em.tt(ob, d["s_"], rn, ALU.mult, E='pool')
                nm = "q" if j < 8 else "k"
                em.dma(outs[nm][(j % 8) * 128:(j % 8 + 1) * 128, t0:t0 + NT], ob)

            prev = stage1(0)
            for j in range(32):
                nxt = stage1(j + 1) if j + 1 < 32 else None
                stage2(prev)
                prev = nxt
            pb = banks.next()
            for k in range(8):
                em.mm(pb[0:8, 0:NT], W[k][:, 4096:4104], ub[:, k, 3:NT + 3], start=(k == 0), stop=(k == 7))
            bt = bt_r.next()
            em.act(bt, pb[0:8, 0:NT], AF.Sigmoid)
            em.dma(beta_o[:, t0:t0 + NT], bt)
            pa = banks.next()
            for k in range(8):
                em.mm(pa[0:8, 0:NT], W[k][:, 4104:4112], ub[:, k, 3:NT + 3], start=(k == 0), stop=(k == 7))
            gt = gt_r.next()
            em.act(gt, pa[0:8, 0:NT], AF.Exp, bias=hs[:, 1:2])
            em.act(gt, gt, AF.Ln, bias=1.0)
            em.ts(gt, gt, nea[:, 0:1], None, op0=ALU.mult)
            em.dma(g_o[:, t0:t0 + NT], gt)
        em.finish()
    return nc


def rope_tables(pos):
    d = np.arange(128) % 64
    inv = (10000.0 ** (-(np.arange(0, 64, 2, dtype=np.float32)) / 64)).astype(np.float32)
    ang = pos.astype(np.float32)[None, :] * inv[d % 32][:, None]
    C = np.cos(ang).astype(np.float32)
    S = np.sin(ang).astype(np.float32)
    S = np.where((d < 32)[:, None], -S, S).astype(np.float32)
    return C, S


def rope_perm(ncols):
    c = np.arange(ncols)
    return (c // 64) * 64 + (c % 64 + 32) % 64


def build_diff_pre(ntok=TPC):
    NT = 256
    nc = new_nc()
    with ExitStack() as st:
        em = Em(nc, st)
        xT = em.dram("xT", [D, ntok], F32, "ExternalInput")
        scal = em.dram("scal", [128, 16], F32, "ExternalInput")
        w_in = em.dram("w_in", [D, 3 * D], F32, "ExternalInput")
        w_pm = em.dram("w_pm", [D, 2 * D], F32, "ExternalInput")
        ctab = em.dram("ctab", [128, ntok], F32, "ExternalInput")
        stab = em.dram("stab", [128, ntok], F32, "ExternalInput")
        qo = em.dram("q", [D, ntok], BF16, "ExternalOutput")
        ko = em.dram("k", [D, ntok], BF16, "ExternalOutput")
        vtok = em.dram("vtok", [ntok, D], BF16, "ExternalOutput")
        sc = em.sb([128, 16], F32, "sc")
        em.dma(sc, scal)
        sc1p = em.sb([128, 8], F32, "sc1p")
        em.ts(sc1p, sc[:, 0:8], 1.0, None, op0=ALU.add)
        W = [em.sb([128, 3 * D], BF16, "W%d" % k) for k in range(8)]
        Wp = [em.sb([128, 2 * D], BF16, "Wp%d" % k) for k in range(8)]
        for k in range(8):
            em.dma(W[k], w_in[k * 128:(k + 1) * 128, :], Q='pool')
            em.dma(Wp[k], w_pm[k * 128:(k + 1) * 128, :], Q='pool')
        xt_r = Rot([em.sb([128, 8, NT], F32, "xt") for _ in range(2)])
        ub = em.sb([128, 8, NT], BF16, "ub")
        ct_r = Rot([em.sb([128, NT], F32, "ct") for _ in range(2)])
        st_r = Rot([em.sb([128, NT], F32, "st") for _ in range(2)])
        t1_r = Rot([em.sb([128, NT], F32, "t1") for _ in range(2)])
        t2_r = Rot([em.sb([128, NT], F32, "t2") for _ in range(2)])
        ob_r = Rot([em.sb([128, NT], BF16, "ob") for _ in range(3)])
        vt_r = Rot([em.sb([128, D], BF16, "vt") for _ in range(2)])
        banks = Rot([em.ps([128, 512], F32, "bk") for _ in range(8)])
        xv = xT[:].ap.rearrange("(c p) t -> p c t", p=128)
        for tt in range(ntok // NT):
            t0 = tt * NT
            xt = xt_r.next()
            em.dma(xt, V(xT, xv[:, :, t0:t0 + NT]))
            ct, stb = ct_r.next(), st_r.next()
            em.dma(ct, ctab[:, t0:t0 + NT])
            em.dma(stb, stab[:, t0:t0 + NT])
            for c in range(8):
                em.ts(ub[:, c, :], xt[:, c, :], sc1p[:, c:c + 1], sc[:, 8 + c:9 + c], op0=ALU.mult, op1=ALU.add,
                      E=('dve' if c % 2 == 0 else 'pool'))
            for j in range(16):
                p1, p2 = banks.next(), banks.next()
                for k in range(8):
                    em.mm(p1[:, 0:NT], W[k][:, j * 128:(j + 1) * 128], ub[:, k, :], start=(k == 0), stop=(k == 7))
                for k in range(8):
                    em.mm(p2[:, 0:NT], Wp[k][:, j * 128:(j + 1) * 128], ub[:, k, :], start=(k == 0), stop=(k == 7))
                scl = 0.125 if j < 8 else 1.0
                t1, t2, ob = t1_r.next(), t2_r.next(), ob_r.next()
                em.stt(t1, p1[:, 0:NT], scl, ct, ALU.mult, ALU.mult)
                em.stt(t2, p2[:, 0:NT], scl, stb, ALU.mult, ALU.mult)
                em.tt(ob, t1, t2, ALU.add, E='pool')
                dst = qo if j < 8 else ko
                em.dma(dst[(j % 8) * 128:(j % 8 + 1) * 128, t0:t0 + NT], ob)
            for tb in range(NT // 128):
                vt = vt_r.next()
                for half in range(2):
                    pv = banks.next()
                    for k in range(8):
                        em.mm(pv[:, 0:512], ub[:, k, tb * 128:(tb + 1) * 128],
                              W[k][:, 2 * D + half * 512:2 * D + (half + 1) * 512], start=(k == 0), stop=(k == 7))
                    em.copy(vt[:, half * 512:(half + 1) * 512], pv[:, 0:512], E=('act' if half == 0 else 'dve'))
                em.dma(vtok[t0 + tb * 128:t0 + (tb + 1) * 128, :], vt)
        em.finish()
    return nc


def build_diff_attn(Tn=T, lam_init=0.0):
    NB = Tn // 128
    NG = Tn // 512
    nc = new_nc()
    with ExitStack() as st:
        em = Em(nc, st)
        qin = em.dram("q", [2, 2, 64, Tn], BF16, "ExternalInput")
        kin = em.dram("k", [2, 2, 64, Tn], BF16, "ExternalInput")
        vin = em.dram("v", [2, Tn, 128], BF16, "ExternalInput")
        lin = em.dram("lam", [4, 64], F32, "ExternalInput")
        cst = em.dram("cst", [128, 256], F32, "ExternalInput")
        oT = em.dram("oT", [2, 128, Tn], F32, "ExternalOutput")

        cf = em.sb([128, 256], F32, "cf")
        em.dma(cf, cst)
        If = cf[:, 128:256]
        tri = em.sb([128, 128], BF16, "tri")
        em.copy(tri, cf[:, 0:128])
        sel = em.sb([64, 65], BF16, "sel")
        em.memset(sel, 0.0)
        em.memset(sel[:, 64:65], 1.0)
        lt = em.sb([128, 4, 64], F32, "lt")
        em.dma(lt, V(lin, lin[:].ap.rearrange("a b -> (a b)").partition_broadcast(128)).re("p (a b) -> p a b", a=4))
        lp = em.sb([128, 2, 64], F32, "lp")
        em.tt(lp[:, 0, :], lt[:, 0, :], lt[:, 1, :], ALU.mult)
        em.tt(lp[:, 1, :], lt[:, 2, :], lt[:, 3, :], ALU.mult)
        ls = em.sb([128, 4], F32, "ls")
        em.op('dve', lambda e: e.reduce_sum(out=ls[:, 0:1].ap, in_=lp[:, 0, :].ap, axis=AX.X), reads=[lp], writes=[ls])
        em.op('dve', lambda e: e.reduce_sum(out=ls[:, 1:2].ap, in_=lp[:, 1, :].ap, axis=AX.X), reads=[lp], writes=[ls])
        em.act(ls[:, 0:2], ls[:, 0:2], AF.Exp)
        em.tt(ls[:, 2:3], ls[:, 1:2], ls[:, 0:1], ALU.subtract)
        em.ts(ls[:, 3:4], ls[:, 2:3], -float(lam_init), None, op0=ALU.add)

        kaug = [em.sb([65, Tn], BF16, "kaug%d" % c) for c in range(2)]
        vaug = em.sb([128, NB, 129], BF16, "vaug")
        qaug_r = [Rot([em.sb([65, 512], BF16, "qaug%d" % c) for _ in range(2)]) for c in range(2)]
        ksq = em.sb([64, 512], BF16, "ksq")
        kmx = em.sb([65, 40], F32, "kmx")
        km2 = [em.sb([65, 1], F32, "km2_%d" % c) for c in range(2)]
        qsq_r = Rot([em.sb([64, 512], BF16, "qsq") for _ in range(2)])
        PT_r = Rot([em.sb([128, 512], BF16, "PT") for _ in range(4)])
        sbk = [Rot([em.ps([128, 512], F32, "sb%d" % c) for _ in range(2)]) for c in range(2)]
        obk = [[em.ps([128, 512], F32, "ob%d_%d" % (c, i)) for i in range(2)] for c in range(2)]
        rc_r = Rot([em.sb([128, 2], F32, "rc") for _ in range(4)])
        t_r = Rot([em.sb([128, 128], F32, "tf") for _ in range(3)])
        of_r = Rot([em.sb([128, 128], F32, "of") for _ in range(3)])
        ost_r = Rot([em.sb([128, 512], F32, "ost") for _ in range(2)])

        for u in range(2):
            em.dma(vaug[:, :, 0:128], V(vin, vin[u].ap.rearrange("(n p) c -> p n c", p=128)))
            em.memset(vaug[:, :, 128:129], 1.0)
            for c in range(2):
                em.dma(kaug[c][0:64, :], kin[u, c])
                em.memset(kaug[c][64:65, :], 1.0)
                for g in range(NG):
                    em.act(ksq, kaug[c][0:64, g * 512:(g + 1) * 512], AF.Square)
                    pn = sbk[c].next()
                    em.mm(pn[0:65, 0:512], sel, ksq)
                    em.op('dve', lambda e, pn=pn, g=g: e.reduce_max(out=kmx[64:65, g:g + 1].ap, in_=pn[64:65, 0:512].ap, axis=AX.X),
                          reads=[pn], writes=[kmx])
                em.op('dve', lambda e, c=c: e.reduce_max(out=km2[c][64:65, 0:1].ap, in_=kmx[64:65, 0:NG].ap, axis=AX.X),
                      reads=[kmx], writes=[km2[c]])
                em.ts(km2[c][64:65, :], km2[c][64:65, :], 1.05, None, op0=ALU.mult)
            for G in range(NG):
                qa = []
                for c in range(2):
                    q_ = qaug_r[c].next()
                    qa.append(q_)
                    em.dma(q_[0:64, :], qin[u, c, :, G * 512:(G + 1) * 512])
                    qsq = qsq_r.next()
                    em.act(qsq, q_[0:64, :], AF.Square)
                    pn = sbk[c].next()
                    em.mm(pn[0:65, 0:512], sel, qsq)
                    em.act(q_[64:65, :], pn[64:65, 0:512], AF.Sqrt, scale=km2[c][64:65, 0:1])
                    em.ts(q_[64:65, :], q_[64:65, :], -1.0, None, op0=ALU.mult)
                nkb = 4 * G + 4
                first = [[True, True], [True, True]]

                def s_mm(j):
                    m = max(0, j - 4 * G)
                    ncol = (4 - m) * 128
                    out = []
                    for c in range(2):
                        ps_ = sbk[c].next()
                        em.mm(ps_[:, 0:ncol], kaug[c][:, j * 128:(j + 1) * 128], qa[c][:, m * 128:512])
                        out.append(ps_)
                    return out

                cur = s_mm(0)
                for j in range(nkb):
                    m = max(0, j - 4 * G)
                    ncol = (4 - m) * 128
                    PTs = []
                    for c in range(2):
                        PT = PT_r.next()
                        em.act(PT[:, 0:ncol], cur[c][:, 0:ncol], AF.Exp)
                        if j >= 4 * G:
                            em.tt(PT[:, 0:128], PT[:, 0:128], tri, ALU.mult, E='pool')
                        PTs.append(PT)
                    nxt = s_mm(j + 1) if j + 1 < nkb else None
                    for c in range(2):
                        PT = PTs[c]
                        for qi in range(m, 4):
                            bk = obk[c][qi // 2]
                            o_ = bk[:, (qi % 2) * 129:(qi % 2) * 129 + 129]
                            em.mm(o_, PT[:, (qi - m) * 128:(qi - m + 1) * 128], vaug[:, j, :],
                                  start=first[c][qi // 2], stop=(j == 4 * G + qi), sgc=True)
                            first[c][qi // 2] = False
                    cur = nxt
                ost = ost_r.next()
                for qi in range(4):
                    o1 = obk[0][qi // 2][:, (qi % 2) * 129:(qi % 2) * 129 + 129]
                    o2 = obk[1][qi // 2][:, (qi % 2) * 129:(qi % 2) * 129 + 129]
                    rc = rc_r.next()
                    em.recip(rc[:, 0:1], o1[:, 128:129])
                    em.recip(rc[:, 1:2], o2[:, 128:129])
                    em.tt(rc[:, 1:2], rc[:, 1:2], ls[:, 3:4], ALU.mult)
                    t2 = t_r.next()
                    em.ts(t2, o2[:, 0:128], rc[:, 1:2], None, op0=ALU.mult)
                    of = of_r.next()
                    em.stt(of, o1[:, 0:128], rc[:, 0:1], t2, ALU.mult, ALU.add)
                    pt = sbk[qi % 2].next()
                    em.mm(pt[:, 0:128], of, If)
                    em.copy(ost[:, qi * 128:(qi + 1) * 128], pt[:, 0:128], E='act')
                em.dma(oT[u, :, G * 512:(G + 1) * 512], ost)
        em.finish()
    return nc


def build_swa(ntok=TPC):
    NT = 512
    NBL = NT // 128
    nc = new_nc()
    with ExitStack() as st:
        em = Em(nc, st)
        xT = em.dram("xT", [D, 128 + ntok], F32, "ExternalInput")
        scal = em.dram("scal", [128, 36], F32, "ExternalInput")
        w_in = em.dram("w_in", [D, 1280], F32, "ExternalInput")
        w_pm = em.dram("w_pm", [D, 1152], F32, "ExternalInput")
        bv = em.dram("bv", [1, 128], F32, "ExternalInput")
        sinks = em.dram("sinks", [1, 16], F32, "ExternalInput")
        ctab = em.dram("ctab", [128, 128 + ntok], F32, "ExternalInput")
        stab = em.dram("stab", [128, 128 + ntok], F32, "ExternalInput")
        cst = em.dram("cst", [128, 384], F32, "ExternalInput")
        oT = em.dram("oT", [D, ntok], BF16, "ExternalOutput")

        sc = em.sb([128, 36], F32, "sc")
        em.dma(sc, scal)
        sc1p = em.sb([128, 8], F32, "sc1p")
        em.ts(sc1p, sc[:, 0:8], 1.0, None, op0=ALU.add)
        hv = sc[:, 34:35]
        cf = em.sb([128, 384], F32, "cf")
        em.dma(cf, cst)
        If = cf[:, 256:384]
        mP = em.sb([128, 512], BF16, "mP")
        mC = em.sb([128, 512], BF16, "mC")
        for i in range(4):
            em.copy(mP[:, i * 128:(i + 1) * 128], cf[:, 0:128])
            em.copy(mC[:, i * 128:(i + 1) * 128], cf[:, 128:256])
        bvb = em.sb([128, 128], F32, "bvb")
        em.dma(bvb, V(bv, bv[0:1, :].ap.partition_broadcast(128)))
        esk = em.sb([128, 16], F32, "esk")
        em.dma(esk, V(sinks, sinks[0:1, :].ap.partition_broadcast(128)))
        em.act(esk, esk, AF.Exp)
        sel = em.sb([64, 65], BF16, "sel")
        em.memset(sel, 0.0)
        em.memset(sel[:, 64:65], 1.0)
        vvirt = em.sb([65, 66], BF16, "vvirt")
        em.memset(vvirt, 0.0)
        em.memset(vvirt[64:65, 65:66], 1.0)

        W = [em.sb([128, 1280], BF16, "W%d" % k) for k in range(8)]
        Wp = [em.sb([128, 1152], BF16, "Wp%d" % k) for k in range(8)]
        for k in range(8):
            em.dma(W[k], w_in[k * 128:(k + 1) * 128, :], Q='pool')
            em.dma(Wp[k], w_pm[k * 128:(k + 1) * 128, :], Q='pool')

        xt_r = Rot([em.sb([128, 8, NT], F32, "xt") for _ in range(2)])
        ub = em.sb([128, 8, NT], BF16, "ub")
        ct_r = Rot([em.sb([128, NT], F32, "ct") for _ in range(2)])
        st_r = Rot([em.sb([128, NT], F32, "st") for _ in range(2)])
        t1_r = Rot([em.sb([128, NT], F32, "t1") for _ in range(2)])
        t2_r = Rot([em.sb([128, NT], F32, "t2") for _ in range(2)])
        kaug = [em.sb([65, 128 + NT], BF16, "kaug%d" % g) for g in range(2)]
        vaug = [em.sb([128, NBL + 1, 66], BF16, "vaug%d" % g) for g in range(2)]
        for g in range(2):
            em.memset(kaug[g][64:65, :], 1.0)
            em.memset(vaug[g][:, :, 64:65], 1.0)
            em.memset(vaug[g][:, :, 65:66], 0.0)
        qaug = em.sb([65, 16, NT], BF16, "qaug")
        erow = em.sb([65, 16, NT], BF16, "erow")
        ksq = em.sb([64, 128 + NT], BF16, "ksq")
        qsq_r = Rot([em.sb([64, NT], BF16, "qsq") for _ in range(2)])
        km = em.sb([65, 4], F32, "km")
        km2 = [em.sb([65, 1], F32, "km2_%d" % g) for g in range(2)]
        PT_r = Rot([em.sb([128, 512], BF16, "PT") for _ in range(4)])
        den_r = Rot([em.sb([128, 4], F32, "den") for _ in range(4)])
        otok_r = Rot([em.sb([128, D], F32, "otok") for _ in range(2)])
        ostg_r = Rot([em.sb([128, 8, NT], BF16, "ostg") for _ in range(2)])
        banks = Rot([em.ps([128, 512], F32, "bk") for _ in range(4)])
        sbanks = Rot([em.ps([128, 512], F32, "sbk") for _ in range(4)])

        xv = xT[:].ap.rearrange("(c p) t -> p c t", p=128)
        ov = oT[:].ap.rearrange("(c p) t -> p c t", p=128)

        def project(c0, n, tile_i):
            xt = xt_r.next()
            em.dma(xt[:, :, 0:n], V(xT, xv[:, :, c0:c0 + n]))
            ct, stb = ct_r.next(), st_r.next()
            em.dma(ct[:, 0:n], ctab[:, c0:c0 + n])
            em.dma(stb[:, 0:n], stab[:, c0:c0 + n])
            for c in range(8):
                em.ts(ub[:, c, 0:n], xt[:, c, 0:n], sc1p[:, c:c + 1], sc[:, 8 + c:9 + c], op0=ALU.mult, op1=ALU.add,
                      E=('dve' if c % 2 == 0 else 'pool'))
            koff = 0 if tile_i < 0 else 128
            chunks = [8] if tile_i < 0 else list(range(9))
            for j in chunks:
                p1, p2 = banks.next(), banks.next()
                for k in range(8):
                    em.mm(p1[:, 0:n], W[k][:, j * 128:(j + 1) * 128], ub[:, k, 0:n], start=(k == 0), stop=(k == 7))
                for k in range(8):
                    em.mm(p2[:, 0:n], Wp[k][:, j * 128:(j + 1) * 128], ub[:, k, 0:n], start=(k == 0), stop=(k == 7))
                t1, t2 = t1_r.next(), t2_r.next()
                b1 = sc[:, 16 + j:17 + j] if j < 8 else sc[:, 32:33]
                b2 = sc[:, 24 + j:25 + j] if j < 8 else sc[:, 33:34]
                em.stt(t1[:, 0:n], p1[:, 0:n], b1, ct[:, 0:n], ALU.add, ALU.mult)
                em.stt(t2[:, 0:n], p2[:, 0:n], b2, stb[:, 0:n], ALU.add, ALU.mult)
                for e in range(2):
                    ps = slice(64 * e, 64 * e + 64)
                    if j < 8:
                        em.tt(qaug[0:64, 2 * j + e, 0:n], t1[ps, 0:n], t2[ps, 0:n], ALU.add, E='pool')
                    else:
                        em.stt(kaug[e][0:64, koff:koff + n], t1[ps, 0:n], 1.0, t2[ps, 0:n], ALU.mult, ALU.add, E='pool')
                        em.ts(kaug[e][0:64, koff:koff + n], kaug[e][0:64, koff:koff + n], 0.125, None, op0=ALU.mult, E='pool')
            for bl in range(n // 128):
                pv = banks.next()
                for k in range(8):
                    em.mm(pv[:, 0:128], ub[:, k, bl * 128:(bl + 1) * 128], W[k][:, 1152:1280], start=(k == 0), stop=(k == 7))
                for g in range(2):
                    vb = bl + (0 if tile_i < 0 else 1)
                    em.tt(vaug[g][:, vb, 0:64], pv[:, g * 64:(g + 1) * 64], bvb[:, g * 64:(g + 1) * 64], ALU.add)

        project(0, 128, -1)
        for tt in range(ntok // NT):
            project(128 + tt * NT, NT, tt)
            for g in range(2):
                em.act(ksq, kaug[g][0:64, :], AF.Square)
                for hf in range(2):
                    w_ = (128 + NT) // 2
                    pn = banks.next()
                    em.mm(pn[0:65, 0:w_], sel, ksq[:, hf * w_:(hf + 1) * w_])
                    em.op('dve', lambda e, pn=pn, hf=hf, w_=w_: e.reduce_max(out=km[64:65, hf:hf + 1].ap, in_=pn[64:65, 0:w_].ap, axis=AX.X),
                          reads=[pn], writes=[km])
                em.op('dve', lambda e, g=g: e.reduce_max(out=km2[g][64:65, 0:1].ap, in_=km[64:65, 0:2].ap, axis=AX.X),
                      reads=[km], writes=[km2[g]])
                em.ts(km2[g][64:65, :], km2[g][64:65, :], 1.05, None, op0=ALU.mult)
            for h in range(16):
                g = h // 8
                qsq = qsq_r.next()
                em.act(qsq, qaug[0:64, h, :], AF.Square)
                pn = banks.next()
                em.mm(pn[0:65, 0:NT], sel, qsq)
                em.act(qaug[64:65, h, :], pn[64:65, 0:NT], AF.Sqrt, scale=km2[g][64:65, 0:1])
                em.ts(qaug[64:65, h, :], qaug[64:65, h, :], -1.0, None, op0=ALU.mult)
                em.act(erow[64:65, h, :], qaug[64:65, h, :], AF.Exp)
            ostg = ostg_r.next()
            def s_mm(bl, g, hg):
                h0 = g * 8 + hg * 4
                q4 = qaug[:, h0:h0 + 4, bl * 128:(bl + 1) * 128]
                pp, pc = sbanks.next(), sbanks.next()
                em.mm(pp[:, 0:512], kaug[g][:, bl * 128:(bl + 1) * 128], q4)
                em.mm(pc[:, 0:512], kaug[g][:, (bl + 1) * 128:(bl + 2) * 128], q4)
                return pp, pc

            groups = [(bl, g, hg) for bl in range(NBL) for g in range(2) for hg in range(2)]
            s_next = s_mm(*groups[0])
            for gi, (bl, g, hg) in enumerate(groups):
                bsl = slice(bl * 128, (bl + 1) * 128)
                if g == 0 and hg == 0:
                    otok = otok_r.next()
                if True:
                    if True:
                        h0 = g * 8 + hg * 4
                        pp, pc = s_next
                        Pp, Pc = PT_r.next(), PT_r.next()
                        em.act(Pp, pp[:, 0:512], AF.Exp)
                        em.act(Pc, pc[:, 0:512], AF.Exp)
                        em.tt(Pp, Pp, mP, ALU.mult, E='pool')
                        em.tt(Pc, Pc, mC, ALU.mult, E='dve')
                        if tt == 0 and bl == 0:
                            em.ts(Pp, Pp, hv, None, op0=ALU.mult)
                        if gi + 1 < len(groups):
                            s_next = s_mm(*groups[gi + 1])
                        po = banks.next()
                        for i in range(4):
                            o_ = po[:, i * 66:(i + 1) * 66]
                            em.mm(o_, Pp[:, i * 128:(i + 1) * 128], vaug[g][:, bl, :], start=(i == 0), stop=False, sgc=True)
                            em.mm(o_, Pc[:, i * 128:(i + 1) * 128], vaug[g][:, bl + 1, :], start=False, stop=False, sgc=True)
                            em.mm(o_, erow[64:65, h0 + i, bsl], vvirt[64:65, :], start=False, stop=True, sgc=True)
                        for i in range(4):
                            h = h0 + i
                            o_ = po[:, i * 66:(i + 1) * 66]
                            den = den_r.next()
                            em.copy(den[:, 2:4], o_[:, 64:66], E='dve')
                            em.stt(den[:, 0:1], den[:, 3:4], esk[:, h:h + 1], den[:, 2:3], ALU.mult, ALU.add)
                            em.recip(den[:, 1:2], den[:, 0:1])
                            em.ts(otok[:, h * 64:(h + 1) * 64], o_[:, 0:64], den[:, 1:2], None, op0=ALU.mult)
                if g == 1 and hg == 1:
                    for j in range(8):
                        pt = banks.next()
                        em.mm(pt[:, 0:128], otok[:, j * 128:(j + 1) * 128], If)
                        em.copy(ostg[:, j, bsl], pt[:, 0:128], E='act')
            em.dma(V(oT, ov[:, :, tt * NT:(tt + 1) * NT]), ostg)
            for g in range(2):
                em.copy(kaug[g][0:64, 0:128], kaug[g][0:64, NT:NT + 128], E='pool')
                em.copy(vaug[g][:, 0, 0:64], vaug[g][:, NBL, 0:64], E='pool')
        em.finish()
    return nc


def pcol(v):
    v = np.asarray(v, np.float32)
    return np.ascontiguousarray(v.reshape(-1, 128).T)


def core_bq(c):
    return c // 4, c % 4


def halo_x(x_tok, c, h):
    b, q = core_bq(c)
    if q == 0:
        left = np.zeros((D, h), np.float32)
    else:
        left = x_tok[c - 1][:, TPC - h:]
    return np.ascontiguousarray(np.concatenate([left, x_tok[c]], axis=1))


def tok_to_rows(outs, name, b, r0, r1):
    return np.concatenate([outs[b * 4 + q][name][r0:r1, :] for q in range(4)], axis=1)


_DBG = {}


def kernel(x, c, ada_w, ada_b, ln_g, ln_b, ffn_w_in, ffn_w_out,
           rwkv_mu, rwkv_w_rkv, rwkv_w0, rwkv_w1, rwkv_w2, rwkv_a0, rwkv_a1, rwkv_a2,
           rwkv_g1, rwkv_g2, rwkv_k_k, rwkv_k_a, rwkv_r_k, rwkv_gn_g, rwkv_gn_b, rwkv_w_out,
           gdn_w_in, gdn_conv, gdn_a_log, gdn_dt_bias, gdn_norm_g, gdn_w_out,
           diff_w_in, diff_lambda, diff_subln_g, diff_w_out,
           swa_w_qkv, swa_b_qkv, swa_sinks, swa_w_out, swa_b_out, _layers=DEPTH, _debug=None):
    f32 = lambda a: np.ascontiguousarray(np.asarray(a, np.float32))
    x = f32(x)
    mod = run_mod(f32(c), f32(ada_w), f32(ada_b))
    ms = lambda l, b, w: mod[l, b][:, w * 8:(w + 1) * 8]
    xs = [np.ascontiguousarray(x[cc // 4, (cc % 4) * TPC:(cc % 4 + 1) * TPC, :].T) for cc in range(NCORES)]
    zeros8 = np.zeros((128, 8), np.float32)
    bd = np.kron(np.eye(2), np.ones((64, 64))).astype(np.float32)
    eye = np.eye(128, dtype=np.float32)
    hvcol = lambda cc: np.full((128, 1), 0.0 if cc % 4 == 0 else 1.0, np.float32)

    def post(i, mode, extra, w_o, b_out=None, cols4=None, cols5=None, hscale=1.0):
        nc = build_post(mode, TPC, hscale, has_bias=(b_out is not None))
        ims = []
        for cc in range(NCORES):
            b = cc // 4
            sc = np.concatenate([ms(i, b, 2), pcol(ln_g[i, 0]), pcol(ln_b[i, 0]),
                                 pcol(b_out) if b_out is not None else zeros8,
                                 cols4 if cols4 is not None else zeros8,
                                 cols5 if cols5 is not None else zeros8, zeros8], axis=1)
            im = {"xT": xs[cc], "w_o": f32(w_o), "scal": np.ascontiguousarray(sc)}
            im.update(extra[cc])
            ims.append(im)
        res = run(nc, ims)
        return [res[cc]["yT"] for cc in range(NCORES)]

    def ffn(i, xin):
        nc = build_ffn(TPC)
        ims = []
        for cc in range(NCORES):
            b = cc // 4
            sc = np.concatenate([ms(i, b, 4), ms(i, b, 3), ms(i, b, 5), pcol(ln_g[i, 1]), pcol(ln_b[i, 1])], axis=1)
            ims.append({"xT": xin[cc], "w_in": f32(ffn_w_in[i]), "w_out": f32(ffn_w_out[i]),
                        "scal": np.ascontiguousarray(sc)})
        res = run(nc, ims)
        return [res[cc]["yT"] for cc in range(NCORES)]

    for i in range(_layers):
        m = i % 4
        if m == 0:
            nc = build_rwkv_pre(TPC)
            ims = []
            for cc in range(NCORES):
                b = cc // 4
                sc = np.concatenate([ms(i, b, 1), ms(i, b, 0)] + [pcol(rwkv_mu[0, k]) for k in range(6)] +
                                    [pcol(rwkv_w0[0]), pcol(rwkv_a0[0]), pcol(rwkv_k_k[0]), pcol(rwkv_k_a[0]),
                                     pcol(np.asarray(rwkv_r_k[0]).reshape(-1)), hvcol(cc)], axis=1)
                ims.append({"xT": halo_x(xs, cc, 1), "scal": np.ascontiguousarray(sc), "cst": bd,
                            "w_rkv": f32(rwkv_w_rkv[0]), "w1": f32(rwkv_w1[0]), "a1": f32(rwkv_a1[0]), "g1": f32(rwkv_g1[0]),
                            "w2": f32(rwkv_w2[0]), "a2": f32(rwkv_a2[0]), "g2": f32(rwkv_g2[0])})
            pre = run(nc, ims)
            nc = build_rwkv_scan(T, 1024)
            mS, mI, mL = chunk_masks()
            reset = np.ones((128, 128), np.float32)
            reset[:, 0] = 0
            reset[:, 64] = 0
            cst = np.ascontiguousarray(np.concatenate([mS, mI, mL, eye, reset], axis=1))
            ims = []
            for cc in range(NCORES):
                b, hq = cc // 4, cc % 4
                im = {"cst": cst}
                for n in ("r", "k", "kk", "a", "lw"):
                    im[n] = np.ascontiguousarray(tok_to_rows(pre, n, b, 256 * hq, 256 * hq + 256).reshape(2, 128, T))
                im["v"] = np.ascontiguousarray(np.concatenate([pre[b * 4 + q]["vtok"][:, 256 * hq:256 * hq + 256]
                                                               for q in range(4)], axis=0))
                ims.append(im)
            sres = run(nc, ims)
            extra = []
            for cc in range(NCORES):
                b, q = cc // 4, cc % 4
                yin = np.concatenate([sres[b * 4 + hq]["yT"].reshape(256, T)[:, q * TPC:(q + 1) * TPC] for hq in range(4)], axis=0)
                extra.append({"yin": np.ascontiguousarray(yin), "g": pre[cc]["g"], "bonus": pre[cc]["bonus"], "cst": bd})
            xs = post(i, 'rwkv', extra, rwkv_w_out[0], cols4=pcol(rwkv_gn_g[0]), cols5=pcol(rwkv_gn_b[0]))
        elif m == 1:
            nc = build_gdn_pre(TPC)
            cw = np.asarray(gdn_conv[0], np.float32).reshape(4, 24, 128).transpose(2, 1, 0).reshape(128, 96)
            hsc = np.ascontiguousarray(np.stack([np.asarray(gdn_a_log[0], np.float32), np.asarray(gdn_dt_bias[0], np.float32)], axis=1))
            ims = []
            for cc in range(NCORES):
                b = cc // 4
                sc = np.concatenate([ms(i, b, 1), ms(i, b, 0), cw, hvcol(cc)], axis=1)
                ims.append({"xT": halo_x(xs, cc, 3), "scal": np.ascontiguousarray(sc), "hsc": hsc, "w_in": f32(gdn_w_in[0])})
            pre = run(nc, ims)
            nc = build_gdn_scan(T, 1024)
            ii = np.arange(128)
            mnegI = np.where(ii[:, None] <= ii[None, :], 0.0, -1e4).astype(np.float32)
            nmS = -(ii[:, None] < ii[None, :]).astype(np.float32)
            reset = np.ones((128, 128), np.float32)
            reset[:, 0] = 0
            cst = np.ascontiguousarray(np.concatenate([mnegI, nmS, eye, reset] + gdn_level_masks(), axis=1))
            ims = []
            for cc in range(NCORES):
                b, hq = cc // 4, cc % 4
                im = {"cst": cst}
                for n in ("q", "k", "v"):
                    im[n] = np.ascontiguousarray(tok_to_rows(pre, n, b, 256 * hq, 256 * hq + 256).reshape(2, 128, T))
                for n in ("beta", "g"):
                    im[n] = np.ascontiguousarray(tok_to_rows(pre, n, b, 2 * hq, 2 * hq + 2))
                ims.append(im)
            sres = run(nc, ims)
            extra = []
            ng = np.zeros((128, 8), np.float32)
            ng[:, 0] = np.asarray(gdn_norm_g[0], np.float32)
            for cc in range(NCORES):
                b, q = cc // 4, cc % 4
                yin = np.concatenate([sres[b * 4 + hq]["oT"].reshape(256, T)[:, q * TPC:(q + 1) * TPC] for hq in range(4)], axis=0)
                extra.append({"yin": np.ascontiguousarray(yin), "zs": pre[cc]["zs"]})
            xs = post(i, 'gdn', extra, gdn_w_out[0], cols4=ng)
        elif m == 2:
            lam_init = 0.8 - 0.6 * float(np.exp(-0.3 * i))
            nc = build_diff_pre(TPC)
            perm = rope_perm(2 * D)
            w_in = f32(diff_w_in[0])
            w_pm = np.ascontiguousarray(w_in[:, :2 * D][:, perm])
            ims = []
            for cc in range(NCORES):
                b, q = cc // 4, cc % 4
                Ct, St = rope_tables(np.arange(q * TPC, (q + 1) * TPC))
                sc = np.concatenate([ms(i, b, 1), ms(i, b, 0)], axis=1)
                ims.append({"xT": xs[cc], "scal": np.ascontiguousarray(sc), "w_in": w_in, "w_pm": w_pm, "ctab": Ct, "stab": St})
            pre = run(nc, ims)
            nc = build_diff_attn(T, lam_init)
            ii = np.arange(128)
            cst = np.ascontiguousarray(np.concatenate([(ii[:, None] <= ii[None, :]).astype(np.float32), eye], axis=1))
            ims = []
            for cc in range(NCORES):
                im = {"lam": f32(diff_lambda[0]), "cst": cst}
                qs, ks, vs = [], [], []
                for u in range(2):
                    b, h = (2 * cc + u) // 8, (2 * cc + u) % 8
                    qs.append(tok_to_rows(pre, "q", b, 128 * h, 128 * h + 128).reshape(2, 64, T))
                    ks.append(tok_to_rows(pre, "k", b, 128 * h, 128 * h + 128).reshape(2, 64, T))
                    vs.append(np.concatenate([pre[b * 4 + q]["vtok"][:, 128 * h:128 * h + 128] for q in range(4)], axis=0))
                im["q"] = np.ascontiguousarray(np.stack(qs))
                im["k"] = np.ascontiguousarray(np.stack(ks))
                im["v"] = np.ascontiguousarray(np.stack(vs))
                ims.append(im)
            ares = run(nc, ims)
            extra = []
            sg = np.zeros((128, 8), np.float32)
            sg[:, 0] = np.asarray(diff_subln_g[0], np.float32)
            for cc in range(NCORES):
                b, q = cc // 4, cc % 4
                rows = []
                for h in range(8):
                    unit = b * 8 + h
                    rows.append(ares[unit // 2]["oT"][unit % 2][:, q * TPC:(q + 1) * TPC])
                extra.append({"yin": np.ascontiguousarray(np.concatenate(rows, axis=0))})
            xs = post(i, 'diff', extra, diff_w_out[0], cols4=sg, hscale=1.0 - lam_init)
        else:
            nc = build_swa(TPC)
            perm = rope_perm(1152)
            w_in = f32(swa_w_qkv[0])
            bq = np.asarray(swa_b_qkv[0], np.float32)
            bqp = bq[:1152][perm]
            w_pm = np.ascontiguousarray(w_in[:, :1152][:, perm])
            ii = np.arange(128)
            cst = np.ascontiguousarray(np.concatenate([(ii[:, None] > ii[None, :]).astype(np.float32),
                                                       (ii[:, None] <= ii[None, :]).astype(np.float32), eye], axis=1))
            ims = []
            for cc in range(NCORES):
                b, q = cc // 4, cc % 4
                Ct, St = rope_tables(np.arange(q * TPC - 128, (q + 1) * TPC))
                sc = np.concatenate([ms(i, b, 1), ms(i, b, 0), pcol(bq[:1024]), pcol(bqp[:1024]), pcol(bq[1024:1152]),
                                     pcol(bqp[1024:1152]), hvcol(cc), np.zeros((128, 1), np.float32)], axis=1)
                ims.append({"xT": halo_x(xs, cc, 128), "scal": np.ascontiguousarray(sc), "w_in": w_in, "w_pm": w_pm,
                            "bv": np.ascontiguousarray(bq[None, 1152:]), "sinks": f32(swa_sinks[0])[None, :],
                            "ctab": Ct, "stab": St, "cst": cst})
            ares = run(nc, ims)
            extra = [{"oT": ares[cc]["oT"]} for cc in range(NCORES)]
            xs = post(i, 'plain', extra, swa_w_out[0], b_out=swa_b_out[0])
        if _debug is not None:
            _debug.append(("mix%d" % i, [a.copy() for a in xs]))
        xs = ffn(i, xs)
        if _debug is not None:
            _debug.append(("ffn%d" % i, [a.copy() for a in xs]))
    out = np.empty((B, T, D), np.float32)
    for cc in range(NCORES):
        out[cc // 4, (cc % 4) * TPC:(cc % 4 + 1) * TPC, :] = xs[cc].T
    return out
```

```python
import numpy as np
import ml_dtypes
from contextlib import ExitStack
import concourse.bass as bass
import concourse.mybir as mybir
from concourse.bass_utils import run_bass_kernel_spmd

F32 = mybir.dt.float32
BF16 = mybir.dt.bfloat16
AF = mybir.ActivationFunctionType
ALU = mybir.AluOpType
AX = mybir.AxisListType
NPBF16 = ml_dtypes.bfloat16

D = 1024
B = 2
T = 16384
DEPTH = 4
DFF = 2816
NCORES = 8
TPC = T * B // NCORES
ALPHA = (2.0 * DEPTH) ** 0.25
LN_EPS = 1e-5


class Tile:
    def __init__(self, h, name, psum=False):
        self.h = h
        self.name = name
        self.psum = psum
        self.w = None
        self.r = {}

    def __getitem__(self, idx):
        return V(self, self.h[idx])

    @property
    def ap(self):
        return self.h[:]


class V:
    def __init__(self, t, ap):
        self.t = t
        self.ap = ap

    def __getitem__(self, idx):
        return V(self.t, self.ap[idx])

    def re(self, s, **kw):
        return V(self.t, self.ap.rearrange(s, **kw))

    def bc(self, shape):
        return V(self.t, self.ap.to_broadcast(shape))


def _t(v):
    if isinstance(v, Tile):
        return v
    if isinstance(v, V):
        return v.t
    return None


def _ap(v):
    if isinstance(v, Tile):
        return v.h[:]
    if isinstance(v, V):
        return v.ap
    return v


class Em:
    NDS = 20

    def __init__(self, nc, st):
        self.nc, self.st = nc, st
        self.engs = {'pe': nc.tensor, 'dve': nc.vector, 'act': nc.scalar,
                     'pool': nc.gpsimd, 'sp': nc.sync}
        self.sems = {k: st.enter_context(nc.semaphore('sem_' + k)) for k in self.engs}
        self.cnt = {k: 0 for k in self.engs}
        self.seen = {k: {} for k in self.engs}
        self.dsem = [st.enter_context(nc.semaphore('dsem%d' % i)) for i in range(self.NDS)]
        self.dcnt = [0] * self.NDS
        self.dpool = {'sp': list(range(0, 10)), 'pool': list(range(10, 16)), 'act': list(range(16, 20))}
        self.dnext = {'sp': 0, 'pool': 0, 'act': 0}
        self.uid = 0
        self.psum_banks = None

    def sb(self, shape, dtype, name=None):
        self.uid += 1
        name = (name or 't') + '_%d' % self.uid
        h = self.st.enter_context(self.nc.sbuf_tensor(name, list(shape), dtype))
        return Tile(h, name)

    def ps(self, shape, dtype=F32, name=None):
        self.uid += 1
        name = (name or 'p') + '_%d' % self.uid
        h = self.st.enter_context(self.nc.psum_tensor(name, list(shape), dtype))
        return Tile(h, name, psum=True)

    def dram(self, name, shape, dtype, kind):
        h = self.nc.dram_tensor(name, list(shape), dtype, kind=kind)
        return Tile(h.ap(), name)

    def _sem(self, key):
        if isinstance(key, tuple):
            return self.dsem[key[1]]
        return self.sems[key]

    def _wait(self, E, dep):
        if dep is None:
            return
        key, val = dep
        if key == E and E == 'pe':
            return
        if self.seen[E].get(key, 0) >= val:
            return
        self.seen[E][key] = val
        self.engs[E].wait_ge(self._sem(key), val)

    def _deps(self, E, reads, writes):
        for v in reads:
            t = _t(v)
            if t is not None:
                self._wait(E, t.w)
                if t.psum:
                    for k, dep in list(t.r.items()):
                        if k != E:
                            self._wait(E, dep)
        for v in writes:
            t = _t(v)
            if t is not None:
                self._wait(E, t.w)
                for dep in list(t.r.values()):
                    self._wait(E, dep)

    def _mark(self, dep, reads, writes):
        for v in reads:
            t = _t(v)
            if t is not None:
                t.r[dep[0]] = dep
        for v in writes:
            t = _t(v)
            if t is not None:
                t.w = dep
                t.r = {}

    def op(self, E, fn, reads=(), writes=(), signal=True):
        self._deps(E, reads, writes)
        ins = fn(self.engs[E])
        if signal:
            self.cnt[E] += 1
            ins.then_inc(self.sems[E], 1)
            self._mark((E, self.cnt[E]), reads, writes)
        else:
            self._mark((E, self.cnt[E] + 1), reads, writes)

    def dma(self, out, in_, Q='sp', **kw):
        pool = self.dpool[Q]
        i = pool[self.dnext[Q]]
        self.dnext[Q] = (self.dnext[Q] + 1) % len(pool)
        key = ('d', i)
        if self.dcnt[i] > 0:
            self._wait(Q, (key, self.dcnt[i]))
        self._deps(Q, [in_], [out])
        ins = self.engs[Q].dma_start(out=_ap(out), in_=_ap(in_), **kw)
        self.dcnt[i] += 16
        ins.then_inc(self.dsem[i], 16)
        self._mark((key, self.dcnt[i]), [in_], [out])

    def finish(self):
        for i in range(self.NDS):
            if self.dcnt[i] > 0:
                self._wait('sp', (('d', i), self.dcnt[i]))
        for E in ('pe', 'dve', 'act', 'pool'):
            if self.cnt[E] > 0:
                self._wait('sp', (E, self.cnt[E]))

    def mm(self, out, lhsT, rhs, start=True, stop=True, sgc=False):
        kw = {'skip_group_check': True} if sgc else {}
        self.op('pe', lambda e: e.matmul(_ap(out), lhsT=_ap(lhsT), rhs=_ap(rhs), start=start, stop=stop, **kw),
                reads=[lhsT, rhs], writes=[out], signal=(stop or sgc))

    def transpose(self, out, in_, ident):
        self.op('pe', lambda e: e.transpose(_ap(out), _ap(in_), _ap(ident)),
                reads=[in_, ident], writes=[out])

    def act(self, out, in_, func, bias=None, scale=None, accum_out=None, E='act'):
        kw = {}
        rd = [in_]
        wr = [out]
        if bias is not None:
            kw['bias'] = _ap(bias)
            rd.append(bias)
        if scale is not None:
            kw['scale'] = _ap(scale)
            rd.append(scale)
        if accum_out is not None:
            kw['accum_out'] = _ap(accum_out)
            wr.append(accum_out)
        self.op('act', lambda e: e.activation(out=_ap(out), in_=_ap(in_), func=func, **kw),
                reads=rd, writes=wr)

    def tt(self, out, a, b, op, E='dve'):
        self.op(E, lambda e: e.tensor_tensor(out=_ap(out), in0=_ap(a), in1=_ap(b), op=op),
                reads=[a, b], writes=[out])

    def ts(self, out, a, s1, s2=None, op0=ALU.mult, op1=None, E='dve', accum_out=None):
        kw = {}
        wr = [out]
        if op1 is not None:
            kw['op1'] = op1
        if accum_out is not None:
            kw['accum_out'] = _ap(accum_out)
            wr.append(accum_out)
        self.op(E, lambda e: e.tensor_scalar(out=_ap(out), in0=_ap(a), scalar1=_ap(s1), scalar2=_ap(s2),
                                             op0=op0, **kw),
                reads=[a, s1, s2], writes=wr)

    def stt(self, out, a, s, b, op0, op1, E='dve'):
        E = 'dve'
        self.op(E, lambda e: e.scalar_tensor_tensor(out=_ap(out), in0=_ap(a), scalar=_ap(s), in1=_ap(b),
                                                    op0=op0, op1=op1),
                reads=[a, s, b], writes=[out])

    def copy(self, out, in_, E='dve'):
        if E == 'act':
            self.op('act', lambda e: e.copy(out=_ap(out), in_=_ap(in_)), reads=[in_], writes=[out])
        else:
            self.op(E, lambda e: e.tensor_copy(out=_ap(out), in_=_ap(in_)), reads=[in_], writes=[out])

    def memset(self, out, val, E='pool'):
        self.op(E, lambda e: e.memset(_ap(out), val), reads=[], writes=[out])

    def recip(self, out, in_):
        self.op('dve', lambda e: e.reciprocal(out=_ap(out), in_=_ap(in_)), reads=[in_], writes=[out])


class Rot:
    def __init__(self, tiles):
        self.tiles = tiles
        self.i = 0

    def next(self):
        t = self.tiles[self.i]
        self.i = (self.i + 1) % len(self.tiles)
        return t


def new_nc():
    return bass.Bass("TRN2", target_bir_lowering=False)


def run(nc, in_maps):
    import time as _time
    t0 = _time.time()
    res = run_bass_kernel_spmd(nc, in_maps, core_ids=list(range(NCORES)))
    print("[launch] %.1fs" % (_time.time() - t0), flush=True)
    return res.results


def build_mod():
    nc = new_nc()
    with ExitStack() as st:
        em = Em(nc, st)
        cT = em.dram("cT", [128, 8, 2], F32, "ExternalInput")
        w = em.dram("w", [D, 3072], F32, "ExternalInput")
        bias = em.dram("bias", [128, 24], F32, "ExternalInput")
        out = em.dram("modT", [128, 24, 2], F32, "ExternalOutput")
        ct = em.sb([128, 8, 2], F32, "ct")
        cs = em.sb([128, 8, 2], F32, "cs")
        bt = em.sb([128, 24], F32, "bt")
        ot = em.sb([128, 24, 2], F32, "ot")
        em.dma(ct, cT)
        em.dma(bt, bias)
        em.act(cs, ct, AF.Silu)
        wt = [em.sb([128, 3072], F32, "w%d" % k) for k in range(8)]
        wv = w[:].rearrange("(k p) n -> k p n", p=128) if False else None
        for k in range(8):
            em.dma(wt[k], w[k * 128:(k + 1) * 128, :])
        pp = em.ps([128, 24, 2], F32, "pp")
        for j in range(24):
            for k in range(8):
                em.mm(pp[:, j, :], wt[k][:, j * 128:(j + 1) * 128], cs[:, k, :], start=(k == 0), stop=(k == 7))
        for b in range(2):
            em.tt(ot[:, :, b], pp[:, :, b], bt, ALU.add)
        em.dma(out, ot)
        em.finish()
    return nc


def run_mod(c, ada_w, ada_b):
    nc = build_mod()
    cT = np.ascontiguousarray(c.T.reshape(8, 128, 2).transpose(1, 0, 2))
    in_maps = []
    for core in range(NCORES):
        l, half = core % 4, core // 4
        in_maps.append({
            "cT": cT,
            "w": np.ascontiguousarray(ada_w[l][:, half * 3072:(half + 1) * 3072]),
            "bias": np.ascontiguousarray(ada_b[l][half * 3072:(half + 1) * 3072].reshape(24, 128).T),
        })
    res = run(nc, in_maps)
    mod = np.zeros((DEPTH, B, 128, 48), np.float32)
    for core in range(NCORES):
        l, half = core % 4, core // 4
        m = res[core]["modT"]
        for b in range(2):
            mod[l, b, :, half * 24:(half + 1) * 24] = m[:, :, b]
    return mod


def emit_ln(em, z, N, ones_bf, g, bvec, out, ps1, ps2, tmp, gb=None):
    zb, zsq, mean, rstd, t1 = tmp
    em.copy(zb, z, E='pool')
    em.act(zsq, z, AF.Square)
    for j in range(8):
        em.mm(ps1, ones_bf, zb[:, j, :], start=(j == 0), stop=(j == 7))
    for j in range(8):
        em.mm(ps2, ones_bf, zsq[:, j, :], start=(j == 0), stop=(j == 7))
    eps = LN_EPS / (ALPHA * ALPHA)
    em.ts(mean, ps1, 1.0 / D, None, op0=ALU.mult)
    em.tt(t1, mean, mean, ALU.mult)
    em.stt(rstd, ps2, 1.0 / D, t1, ALU.mult, ALU.subtract)
    em.ts(rstd, rstd, eps, None, op0=ALU.add)
    em.act(rstd, rstd, AF.Sqrt)
    em.recip(rstd, rstd)
    mb = V(mean, mean.ap.unsqueeze(1).to_broadcast([128, 8, N]))
    rb = V(rstd, rstd.ap.unsqueeze(1).to_broadcast([128, 8, N]))
    if gb is not None:
        Gt, Bt = gb
        h = 4
        em.tt(out[:, 0:h, :], z[:, 0:h, :], mb[:, 0:h, :], ALU.subtract, E='pool')
        em.tt(out[:, h:8, :], z[:, h:8, :], mb[:, h:8, :], ALU.subtract, E='dve')
        em.tt(out[:, 0:h, :], out[:, 0:h, :], rb[:, 0:h, :], ALU.mult, E='dve')
        em.tt(out[:, h:8, :], out[:, h:8, :], rb[:, h:8, :], ALU.mult, E='pool')
        em.tt(out[:, 0:h, :], out[:, 0:h, :], Gt[:, 0:h, :], ALU.mult, E='pool')
        em.tt(out[:, h:8, :], out[:, h:8, :], Gt[:, h:8, :], ALU.mult, E='dve')
        em.tt(out[:, 0:h, :], out[:, 0:h, :], Bt[:, 0:h, :], ALU.add, E='dve')
        em.tt(out[:, h:8, :], out[:, h:8, :], Bt[:, h:8, :], ALU.add, E='pool')
    else:
        h = 4
        em.tt(out[:, 0:h, :], z[:, 0:h, :], mb[:, 0:h, :], ALU.subtract, E='pool')
        em.tt(out[:, h:8, :], z[:, h:8, :], mb[:, h:8, :], ALU.subtract, E='dve')
        em.tt(out[:, 0:h, :], out[:, 0:h, :], rb[:, 0:h, :], ALU.mult, E='dve')
        em.tt(out[:, h:8, :], out[:, h:8, :], rb[:, h:8, :], ALU.mult, E='pool')
        for j in range(8):
            em.act(out[:, j, :], out[:, j, :], AF.Identity, bias=bvec[:, j:j + 1], scale=g[:, j:j + 1])


def make_gb(em, g, bvec, N):
    Gt = em.sb([128, 8, N], F32, "Gt")
    Bt = em.sb([128, 8, N], F32, "Bt")
    em.memset(Gt, 1.0)
    em.memset(Bt, 0.0)
    for j in range(8):
        em.ts(Gt[:, j, :], Gt[:, j, :], g[:, j:j + 1], None, op0=ALU.mult, E='pool')
        em.ts(Bt[:, j, :], Bt[:, j, :], bvec[:, j:j + 1], None, op0=ALU.add, E='pool')
    return Gt, Bt


def build_ffn(ntok=TPC):
    NT = 256
    nc = new_nc()
    with ExitStack() as st:
        em = Em(nc, st)
        xT = em.dram("xT", [D, ntok], F32, "ExternalInput")
        w_in = em.dram("w_in", [D, 2 * DFF], F32, "ExternalInput")
        w_out = em.dram("w_out", [DFF, D], F32, "ExternalInput")
        scal = em.dram("scal", [128, 40], F32, "ExternalInput")
        yT = em.dram("yT", [D, ntok], F32, "ExternalOutput")

        sc = em.sb([128, 40], F32, "sc")
        em.dma(sc, scal)
        sc2p = em.sb([128, 8], F32, "sc2p")
        gs = em.sb([128, 8], F32, "gs")
        em.ts(sc2p, sc[:, 0:8], 1.0, None, op0=ALU.add)
        em.ts(gs, sc[:, 16:24], 1.0, 1.0 / ALPHA, op0=ALU.add, op1=ALU.mult)
        ones_bf = em.sb([128, 128], BF16, "ones")
        em.memset(ones_bf, 1.0)

        win = [em.sb([128, 2 * DFF], BF16, "win%d" % k) for k in range(8)]
        wout = [em.sb([128, D], BF16, "wout%d" % m) for m in range(22)]
        for k in range(8):
            em.dma(win[k], w_in[k * 128:(k + 1) * 128, :], Q='pool')
        for m in range(22):
            em.dma(wout[m], w_out[m * 128:(m + 1) * 128, :], Q='pool')

        xt_r = Rot([em.sb([128, 8, NT], F32, "xt") for _ in range(2)])
        ub_r = Rot([em.sb([128, 8, NT], BF16, "ub") for _ in range(2)])
        hT = em.sb([128, 22, NT], BF16, "hT")
        sg_r = Rot([em.sb([128, NT], F32, "sg") for _ in range(3)])
        z_r = Rot([em.sb([128, 8, NT], F32, "z") for _ in range(2)])
        zb_r = Rot([em.sb([128, 8, NT], BF16, "zb") for _ in range(2)])
        zsq_r = Rot([em.sb([128, 8, NT], BF16, "zsq") for _ in range(2)])
        mean = em.sb([128, NT], F32, "mean")
        rstd = em.sb([128, NT], F32, "rstd")
        t1 = em.sb([128, NT], F32, "t1")
        pbank = Rot([em.ps([128, 512], F32, "pb") for _ in range(6)])
        ps1 = em.ps([128, 512], F32, "ps1")
        ps2 = em.ps([128, 512], F32, "ps2")

        xv = xT[:].ap.rearrange("(c p) t -> p c t", p=128)
        yv = yT[:].ap.rearrange("(c p) t -> p c t", p=128)
        def load_tile(tt):
            xt = xt_r.next()
            ub = ub_r.next()
            em.dma(xt, V(xT, xv[:, :, tt * NT:(tt + 1) * NT]))
            for c in range(8):
                em.ts(ub[:, c, :], xt[:, c, :], sc2p[:, c:c + 1], sc[:, 8 + c:9 + c], op0=ALU.mult, op1=ALU.add,
                      E=('dve' if c % 2 == 0 else 'pool'))
            return xt, ub

        nxt_tile = load_tile(0)
        for tt in range(ntok // NT):
            tsl = slice(tt * NT, (tt + 1) * NT)
            xt, ub = nxt_tile
            z, zb, zsq = z_r.next(), zb_r.next(), zsq_r.next()
            for m in range(22):
                pg = pbank.next()
                pu = pbank.next()
                for k in range(8):
                    em.mm(pg[:, 0:NT], win[k][:, m * 128:(m + 1) * 128], ub[:, k, :], start=(k == 0), stop=(k == 7))
                for k in range(8):
                    em.mm(pu[:, 0:NT], win[k][:, DFF + m * 128:DFF + (m + 1) * 128], ub[:, k, :],
                          start=(k == 0), stop=(k == 7))
                sg = sg_r.next()
                em.act(sg, pg[:, 0:NT], AF.Silu)
                em.tt(hT[:, m, :], sg, pu[:, 0:NT], ALU.mult)
            for j in range(8):
                py = pbank.next()
                for m in range(22):
                    em.mm(py[:, 0:NT], wout[m][:, j * 128:(j + 1) * 128], hT[:, m, :], start=(m == 0), stop=(m == 21))
                em.stt(z[:, j, :], py[:, 0:NT], gs[:, j:j + 1], xt[:, j, :], ALU.mult, ALU.add)
            if tt + 1 < ntok // NT:
                nxt_tile = load_tile(tt + 1)
            emit_ln(em, z, NT, ones_bf, sc[:, 24:32], sc[:, 32:40], z, ps1[:, 0:NT], ps2[:, 0:NT],
                    (zb, zsq, mean, rstd, t1))
            em.dma(V(yT, yv[:, :, tsl]), z, Q='pool')
        em.finish()
    return nc


def chunk_masks():
    i = np.arange(128)
    same = (i[:, None] // 64) == (i[None, :] // 64)
    mS = (same & (i[:, None] < i[None, :])).astype(np.float32)
    mI = (same & (i[:, None] <= i[None, :])).astype(np.float32)
    mL = mS.T.copy()
    return mS, mI, mL


def build_rwkv_scan(Tn=T, SEG=1024, PI=2):
    NP = SEG // 128
    NCH = SEG // 64
    nc = new_nc()
    with ExitStack() as st:
        em = Em(nc, st)
        din = {n: em.dram(n, [2, 128, Tn], F32, "ExternalInput") for n in ("r", "k", "kk", "a", "lw")}
        vin = em.dram("v", [Tn, 256], BF16, "ExternalInput")
        cst = em.dram("cst", [128, 128 * 5], F32, "ExternalInput")
        yT = em.dram("yT", [2, 128, Tn], F32, "ExternalOutput")

        cf = em.sb([128, 640], F32, "cf")
        em.dma(cf, cst)
        mSI = cf[:, 0:256]
        mL = cf[:, 256:384]
        If = cf[:, 384:512]
        Ib = em.sb([128, 128], BF16, "Ib")
        em.copy(Ib, cf[:, 384:512])
        reset = em.sb([128, SEG], F32, "reset")
        for q in range(SEG // 128):
            em.copy(reset[:, q * 128:(q + 1) * 128], cf[:, 512:640], E='pool')

        STh = [em.sb([64, 64], F32, "ST%d" % h) for h in range(4)]
        for h in range(4):
            em.memset(STh[h], 0.0)
        dW_r = Rot([em.sb([64, 64], F32, "dW") for _ in range(8)])

        banks = Rot([em.ps([128, 512], F32, "bk") for _ in range(8)])

        def seg_tiles(nm, dt=F32):
            return [em.sb([128, SEG], dt, nm + "%d" % hp) for hp in range(2)]
        tin = {n: seg_tiles("in_" + n) for n in din}
        cum = seg_tiles("cum")
        tmp = seg_tiles("tmp")
        tmp2 = seg_tiles("tmp2")
        kka = seg_tiles("kka")
        at = seg_tiles("at", BF16)
        bt = seg_tiles("bt", BF16)
        kt = seg_tiles("kt", BF16)
        rt = seg_tiles("rt", BF16)
        bh = seg_tiles("bh", BF16)
        kh = seg_tiles("kh", BF16)
        WC = [em.sb([128, NCH], F32, "WC%d" % hp) for hp in range(2)]
        vt = em.sb([128, NP, 256], BF16, "vt")
        yo = [em.sb([128, SEG], F32, "yo%d" % hp) for hp in range(2)]

        def rot(nm, shape, dt, n):
            return Rot([em.sb(shape, dt, nm) for _ in range(n)])
        NI = 4 * PI
        AB_r = rot("AB", [128, 256], BF16, 2 * NI)
        AK_r = rot("AK", [128, 256], BF16, 2 * NI)
        N_r = rot("N", [128, 128], BF16, 2 * NI + 4)
        Z_r = rot("Z", [128, 128], BF16, 2 * NI + 4)
        IZ_r = rot("IZ", [128, 128], BF16, 2 * NI + 4)
        X_r = rot("X", [128, 128], BF16, 2 * NI + 4)
        BK_r = rot("BK", [128, 128], BF16, NI + 4)
        Q1_r = rot("Q1", [64, 128], F32, NI + 4)
        Q2_r = rot("Q2", [64, 128], F32, NI + 4)
        MT_r = rot("MT", [64, 128], F32, NI + 4)
        G_r = rot("G", [64, 128], F32, NI + 4)

        for seg in range(Tn // SEG):
            s0 = seg * SEG
            for hp in range(2):
                for n in din:
                    em.dma(tin[n][hp], din[n][hp, :, s0:s0 + SEG])
            em.dma(vt, V(vin, vin[s0:s0 + SEG, :].ap.rearrange("(p t) c -> t p c", t=128)))
            for hp in range(2):
                r_, k_, kk_, a_, lw_ = (tin[n][hp] for n in ("r", "k", "kk", "a", "lw"))
                c_ = cum[hp]
                em.op('dve', lambda e: e.tensor_tensor_scan(out=c_.ap, data0=reset.ap, data1=lw_.ap, initial=0.0,
                                                            op0=ALU.mult, op1=ALU.add),
                      reads=[reset, lw_], writes=[c_])
                em.act(tmp[hp], c_, AF.Exp)
                em.tt(rt[hp], r_, tmp[hp], ALU.mult, E='pool')
                em.act(tmp2[hp], c_, AF.Exp, scale=-1.0)
                em.tt(kka[hp], kk_, a_, ALU.mult, E='pool')
                em.tt(bt[hp], kka[hp], tmp2[hp], ALU.mult)
                em.tt(kt[hp], k_, tmp2[hp], ALU.mult, E='pool')
                em.tt(tmp[hp], c_, lw_, ALU.subtract)
                em.act(tmp[hp], tmp[hp], AF.Exp)
                em.stt(at[hp], kk_, -1.0, tmp[hp], ALU.mult, ALU.mult)
                c3 = c_[:].re("p (c t) -> p c t", t=64)
                cC = c3[:, :, 63:64]
                em.tt(tmp2[hp][:].re("p (c t) -> p c t", t=64), cC.bc([128, NCH, 64]), c3, ALU.subtract, E='pool')
                em.act(tmp2[hp], tmp2[hp], AF.Exp)
                em.tt(bh[hp], kka[hp], tmp2[hp], ALU.mult)
                em.tt(kh[hp], k_, tmp2[hp], ALU.mult, E='pool')
                em.act(WC[hp][:].re("p (c o) -> p c o", o=1), cC, AF.Exp)

            for p0 in range(0, NP, PI):
                items = [(p, h4 // 2, slice(64 * (h4 % 2), 64 * (h4 % 2) + 64), h4)
                         for p in range(p0, min(NP, p0 + PI)) for h4 in range(4)]
                AB, AK, Nn, Z, IZ, X = {}, {}, {}, {}, {}, {}
                Q1d, Q2d, MTd, Gd = {}, {}, {}, {}
                for p, hp, ps, h4 in items:
                    key = (p, h4)
                    tsl = slice(p * 128, (p + 1) * 128)
                    pa = banks.next()
                    em.mm(pa[:, 0:128], bt[hp][ps, tsl], at[hp][ps, tsl])
                    em.mm(pa[:, 128:256], bt[hp][ps, tsl], rt[hp][ps, tsl])
                    em.mm(pa[:, 256:384], kt[hp][ps, tsl], at[hp][ps, tsl])
                    em.mm(pa[:, 384:512], kt[hp][ps, tsl], rt[hp][ps, tsl])
                    AB[key] = AB_r.next()
                    AK[key] = AK_r.next()
                    em.tt(AB[key], pa[:, 0:256], mSI, ALU.mult)
                    em.tt(AK[key], pa[:, 256:512], mSI, ALU.mult)
                    pn = banks.next()
                    em.mm(pn[:, 0:128], at[hp][ps, tsl], bt[hp][ps, tsl])
                    em.mm(pn[:, 128:192], at[hp][ps, tsl], Ib[ps, ps])
                    pn2 = banks.next()
                    em.mm(pn2[:, 0:64], AK[key][:, 0:128], vt[:, p, h4 * 64:(h4 + 1) * 64])
                    Nn[key] = N_r.next()
                    em.tt(Nn[key], pn[:, 0:128], mL, ALU.mult)
                    Z[key] = AB[key][:, 0:128]
                    IZ[key] = IZ_r.next()
                    em.tt(IZ[key], AB[key][:, 0:128], Ib, ALU.add, E='pool')
                    X[key] = X_r.next()
                    em.copy(X[key][:, 0:64], pn[:, 128:192], E='act')
                    em.copy(X[key][:, 64:128], pn2[:, 0:64], E='act')
                for j in range(6):
                    for p, hp, ps, h4 in items:
                        key = (p, h4)
                        px = banks.next()
                        em.mm(px[:, 0:128], IZ[key], X[key])
                        if j < 5:
                            em.mm(px[:, 128:256], Nn[key], Z[key])
                            if j < 4:
                                em.mm(px[:, 256:384], Z[key], Nn[key])
                        X[key] = X_r.next()
                        em.copy(X[key], px[:, 0:128], E='act')
                        if j < 5:
                            Zn = Z_r.next()
                            IZ[key] = IZ_r.next()
                            em.tt(IZ[key], px[:, 128:256], Ib, ALU.add)
                            if j < 4:
                                em.copy(Zn, px[:, 128:256], E='act')
                                Nx = N_r.next()
                                em.copy(Nx, px[:, 256:384], E='dve')
                                Nn[key] = Nx
                                Z[key] = Zn
                for p, hp, ps, h4 in items:
                    key = (p, h4)
                    tsl = slice(p * 128, (p + 1) * 128)
                    pb = banks.next()
                    em.mm(pb[:, 0:64], bh[hp][ps, tsl], Ib[ps, ps])
                    em.mm(pb[:, 64:128], kh[hp][ps, tsl], Ib[ps, ps])
                    em.mm(pb[0:64, 128:256], X[key][:, 0:64], AB[key][:, 128:256], start=True, stop=False)
                    em.mm(pb[0:64, 128:256], Ib[ps, ps], rt[hp][ps, tsl], start=False, stop=True)
                    em.mm(pb[0:64, 256:384], X[key][:, 64:128], AB[key][:, 128:256], start=True, stop=False)
                    em.mm(pb[0:64, 256:384], vt[:, p, h4 * 64:(h4 + 1) * 64], AK[key][:, 128:256], start=False, stop=True)
                    BK = BK_r.next()
                    em.copy(BK, pb[:, 0:128], E='act')
                    Q1d[key] = Q1_r.next()
                    Q2d[key] = Q2_r.next()
                    em.copy(Q1d[key], pb[0:64, 128:256], E='dve')
                    em.copy(Q2d[key], pb[0:64, 256:384], E='act')
                    pms = [banks.next(), banks.next()]
                    for c in range(2):
                        cs = slice(64 * c, 64 * c + 64)
                        pm = pms[c]
                        em.mm(pm[0:64, 0:64], X[key][cs, 0:64], BK[cs, 0:64])
                        em.mm(pm[0:64, 64:128], BK[cs, 0:64], X[key][cs, 64:128], start=True, stop=False)
                        em.mm(pm[0:64, 64:128], BK[cs, 64:128], vt[cs, p, h4 * 64:(h4 + 1) * 64], start=False, stop=True)
                    MTd[key] = MT_r.next()
                    Gd[key] = G_r.next()
                    for c in range(2):
                        ch = p * 2 + c
                        dW = dW_r.next()
                        em.ts(dW, If[ps, ps], WC[hp][ps, ch:ch + 1], None, op0=ALU.mult, E='pool')
                        em.tt(MTd[key][:, 64 * c:64 * c + 64], pms[c][0:64, 0:64], dW, ALU.add)
                        em.copy(Gd[key][:, 64 * c:64 * c + 64], pms[c][0:64, 64:128], E='dve')
                for p in range(p0, min(NP, p0 + PI)):
                    for c in range(2):
                        for h4 in range(4):
                            key = (p, h4)
                            hp, ps = h4 // 2, slice(64 * (h4 % 2), 64 * (h4 % 2) + 64)
                            pc = banks.next()
                            em.mm(pc[0:64, 0:64], STh[h4], Q1d[key][:, 64 * c:64 * c + 64])
                            em.mm(pc[0:64, 64:128], MTd[key][:, 64 * c:64 * c + 64], STh[h4])
                            em.tt(yo[hp][ps, p * 128 + 64 * c:p * 128 + 64 * c + 64], pc[0:64, 0:64],
                                  Q2d[key][:, 64 * c:64 * c + 64], ALU.add)
                            em.tt(STh[h4], pc[0:64, 64:128], Gd[key][:, 64 * c:64 * c + 64], ALU.add)
            for hp in range(2):
                em.dma(yT[hp, :, s0:s0 + SEG], yo[hp])
        em.finish()
    return nc


def build_rwkv_pre(ntok=TPC):
    NT = 256
    nc = new_nc()
    with ExitStack() as st:
        em = Em(nc, st)
        xT = em.dram("xT", [D, ntok + 1], F32, "ExternalInput")
        scal = em.dram("scal", [128, 105], F32, "ExternalInput")
        cst = em.dram("cst", [128, 128], F32, "ExternalInput")
        w_rkv = em.dram("w_rkv", [3, D, D], F32, "ExternalInput")
        w1 = em.dram("w1", [D, 64], F32, "ExternalInput")
        a1 = em.dram("a1", [D, 64], F32, "ExternalInput")
        g1 = em.dram("g1", [D, 128], F32, "ExternalInput")
        w2 = em.dram("w2", [64, D], F32, "ExternalInput")
        a2 = em.dram("a2", [64, D], F32, "ExternalInput")
        g2 = em.dram("g2", [128, D], F32, "ExternalInput")
        outs = {n: em.dram(n, [D, ntok], F32, "ExternalOutput") for n in ("r", "k", "kk", "a", "lw")}
        outs["g"] = em.dram("g", [D, ntok], BF16, "ExternalOutput")
        outs["bonus"] = em.dram("bonus", [D, ntok], BF16, "ExternalOutput")
        vtok = em.dram("vtok", [ntok, D], BF16, "ExternalOutput")

        sc = em.sb([128, 105], F32, "sc")
        em.dma(sc, scal)
        col = lambda i: sc[:, 8 * i:8 * i + 8]
        sc1p = em.sb([128, 8], F32, "sc1p")
        em.ts(sc1p, col(0), 1.0, None, op0=ALU.add)
        sh1, w0, a0, k_k, k_a, r_k = col(1), col(8), col(9), col(10), col(11), col(12)
        hv = sc[:, 104:105]
        bd_f = em.sb([128, 128], F32, "bd_f")
        em.dma(bd_f, cst)
        bd = em.sb([128, 128], BF16, "bd")
        em.copy(bd, bd_f)

        W = [[em.sb([128, D], BF16, "W%d_%d" % (i, k)) for k in range(8)] for i in range(3)]
        for i in range(3):
            for k in range(8):
                em.dma(W[i][k], w_rkv[i, k * 128:(k + 1) * 128, :], Q='pool')
        w1s = em.sb([128, 8, 64], BF16, "w1s")
        a1s = em.sb([128, 8, 64], BF16, "a1s")
        g1s = em.sb([128, 8, 128], BF16, "g1s")
        em.dma(w1s, V(w1, w1[:].ap.rearrange("(k p) n -> p k n", p=128)), Q='pool')
        em.dma(a1s, V(a1, a1[:].ap.rearrange("(k p) n -> p k n", p=128)), Q='pool')
        em.dma(g1s, V(g1, g1[:].ap.rearrange("(k p) n -> p k n", p=128)), Q='pool')
        w2s = em.sb([64, D], BF16, "w2s")
        a2s = em.sb([64, D], BF16, "a2s")
        g2s = em.sb([128, D], BF16, "g2s")
        em.dma(w2s, w2, Q='pool')
        em.dma(a2s, a2, Q='pool')
        em.dma(g2s, g2, Q='pool')

        xt_r = Rot([em.sb([128, 8, NT + 1], F32, "xt") for _ in range(2)])
        u = em.sb([128, 8, NT + 1], F32, "u")
        xx = em.sb([128, 8, NT], F32, "xx")
        lerp = [em.sb([128, 8, NT], BF16, "lerp%d" % i) for i in range(6)]
        hw = em.sb([64, NT], BF16, "hw")
        ha = em.sb([64, NT], BF16, "ha")
        hg = em.sb([128, NT], BF16, "hg")
        banks = Rot([em.ps([128, 512], F32, "bk") for _ in range(8)])

        def rot(nm, dt, n=2):
            return Rot([em.sb([128, NT], dt, nm) for _ in range(n)])
        r_r, k_r, v_r, kp_r, kk_r, a_r, lw_r = (rot(n, F32, 3) for n in ("r", "k", "v", "kp", "kk", "a", "lw"))
        g_r, bo_r, sq_r, t_r = rot("g", BF16, 3), rot("bo", BF16, 3), rot("sq", BF16, 3), rot("t", F32, 8)
        tb_r = rot("tb", BF16, 3)
        vt_r = Rot([em.sb([128, D], BF16, "vt") for _ in range(2)])

        xv = xT[:].ap.rearrange("(c p) t -> p c t", p=128)
        def load_x(tt_):
            xt_ = xt_r.next()
            em.dma(xt_, V(xT, xv[:, :, tt_ * NT:tt_ * NT + NT + 1]))
            return xt_

        xt_next = load_x(0)
        for tt in range(ntok // NT):
            t0 = tt * NT
            xt = xt_next
            for c in range(8):
                em.ts(u[:, c, :], xt[:, c, :], sc1p[:, c:c + 1], sh1[:, c:c + 1], op0=ALU.mult, op1=ALU.add,
                      E=('dve' if c % 2 == 0 else 'pool'))
            if tt + 1 < ntok // NT:
                xt_next = load_x(tt + 1)
            if tt == 0:
                em.ts(u[:, :, 0:1], u[:, :, 0:1], hv, None, op0=ALU.mult)
            for c in range(8):
                em.tt(xx[:, c, :], u[:, c, 0:NT], u[:, c, 1:NT + 1], ALU.subtract, E=('pool' if c % 2 == 0 else 'dve'))
            for i in range(6):
                for c in range(8):
                    em.stt(lerp[i][:, c, :], xx[:, c, :], sc[:, 8 * (2 + i) + c:8 * (2 + i) + c + 1], u[:, c, 1:NT + 1],
                           ALU.mult, ALU.add, E=('dve' if (c + i) % 2 == 0 else 'pool'))
            p1 = banks.next()
            for k in range(8):
                em.mm(p1[0:64, 0:NT], w1s[:, k, :], lerp[3][:, k, :], start=(k == 0), stop=(k == 7))
            em.act(hw, p1[0:64, 0:NT], AF.Tanh)
            p2 = banks.next()
            for k in range(8):
                em.mm(p2[0:64, 0:NT], a1s[:, k, :], lerp[4][:, k, :], start=(k == 0), stop=(k == 7))
            em.copy(ha, p2[0:64, 0:NT], E='dve')
            p3 = banks.next()
            for k in range(8):
                em.mm(p3[:, 0:NT], g1s[:, k, :], lerp[5][:, k, :], start=(k == 0), stop=(k == 7))
            em.act(hg, p3[:, 0:NT], AF.Sigmoid)
            for tb in range(NT // 128):
                vt = vt_r.next()
                for half in range(2):
                    pv = banks.next()
                    for k in range(8):
                        em.mm(pv[:, 0:512], lerp[2][:, k, tb * 128:(tb + 1) * 128], W[2][k][:, half * 512:(half + 1) * 512],
                              start=(k == 0), stop=(k == 7))
                    em.copy(vt[:, half * 512:(half + 1) * 512], pv[:, 0:512], E=('act' if half == 0 else 'dve'))
                em.dma(vtok[t0 + tb * 128:t0 + (tb + 1) * 128, :], vt)
            def stage1(j):
                js = slice(j * 128, (j + 1) * 128)
                jc = slice(j, j + 1)
                pr, pk, pv = banks.next(), banks.next(), banks.next()
                for (pp, i) in ((pr, 0), (pk, 1), (pv, 2)):
                    for k in range(8):
                        em.mm(pp[:, 0:NT], W[i][k][:, js], lerp[i][:, k, :], start=(k == 0), stop=(k == 7))
                r_, k_, v_ = r_r.next(), k_r.next(), v_r.next()
                em.copy(r_, pr[:, 0:NT], E='act')
                em.copy(k_, pk[:, 0:NT], E='dve')
                em.copy(v_, pv[:, 0:NT], E='act')
                pl = banks.next()
                em.mm(pl[:, 0:NT], w2s[:, js], hw)
                lw_ = lw_r.next()
                em.act(lw_, pl[:, 0:NT], AF.Sigmoid, bias=w0[:, jc])
                em.ts(lw_, lw_, -0.6065306597126334, None, op0=ALU.mult, E='pool')
                pa = banks.next()
                em.mm(pa[:, 0:NT], a2s[:, js], ha)
                a_ = a_r.next()
                em.act(a_, pa[:, 0:NT], AF.Sigmoid, bias=a0[:, jc])
                pg = banks.next()
                em.mm(pg[:, 0:NT], g2s[:, js], hg)
                g_ = g_r.next()
                em.copy(g_, pg[:, 0:NT], E='dve')
                kkr = t_r.next()
                em.ts(kkr, k_, k_k[:, jc], None, op0=ALU.mult, E='pool')
                sq = sq_r.next()
                em.act(sq, kkr, AF.Square)
                tk = t_r.next()
                em.ts(tk, a_, -1.0, k_a[:, jc], op0=ALU.add, op1=ALU.mult, E='pool')
                kp = kp_r.next()
                em.stt(kp, tk, 1.0, k_, ALU.add, ALU.mult)
                tb_ = tb_r.next()
                em.stt(tb_, r_, r_k[:, jc], kp, ALU.mult, ALU.mult)
                return dict(js=js, r_=r_, v_=v_, lw_=lw_, a_=a_, g_=g_, kkr=kkr, sq=sq, kp=kp, tb_=tb_)

            def stage2(d):
                ps_ = banks.next()
                em.mm(ps_[:, 0:NT], bd, d["sq"])
                rn = t_r.next()
                em.ts(rn, ps_[:, 0:NT], 1e-6, None, op0=ALU.add)
                em.act(rn, rn, AF.Sqrt)
                em.recip(rn, rn)
                kk_ = kk_r.next()
                em.tt(kk_, d["kkr"], rn, ALU.mult, E='pool')
                pb = banks.next()
                em.mm(pb[:, 0:NT], bd, d["tb_"])
                bo = bo_r.next()
                em.tt(bo, pb[:, 0:NT], d["v_"], ALU.mult)
                for (nm, tl) in (("r", d["r_"]), ("k", d["kp"]), ("kk", kk_), ("a", d["a_"]), ("lw", d["lw_"]),
                                 ("g", d["g_"]), ("bonus", bo)):
                    em.dma(outs[nm][d["js"], t0:t0 + NT], tl)

            prev = stage1(0)
            for j in range(8):
                nxt = stage1(j + 1) if j + 1 < 8 else None
                stage2(prev)
                prev = nxt
        em.finish()
    return nc


def build_post(mode, ntok=TPC, hscale=1.0, has_bias=True):
    NT = 256
    nc = new_nc()
    with ExitStack() as st:
        em = Em(nc, st)
        xT = em.dram("xT", [D, ntok], F32, "ExternalInput")
        w_o = em.dram("w_o", [D, D], F32, "ExternalInput")
        scal = em.dram("scal", [128, 56], F32, "ExternalInput")
        if mode == 'rwkv':
            yin = em.dram("yin", [D, ntok], F32, "ExternalInput")
            gin = em.dram("g", [D, ntok], BF16, "ExternalInput")
            bin_ = em.dram("bonus", [D, ntok], BF16, "ExternalInput")
            cst = em.dram("cst", [128, 128], F32, "ExternalInput")
        elif mode in ('gdn', 'diff'):
            yin = em.dram("yin", [D, ntok], F32, "ExternalInput")
            if mode == 'gdn':
                zin = em.dram("zs", [D, ntok], BF16, "ExternalInput")
        else:
            oin = em.dram("oT", [D, ntok], BF16, "ExternalInput")
        yT = em.dram("yT", [D, ntok], F32, "ExternalOutput")

        sc = em.sb([128, 56], F32, "sc")
        em.dma(sc, scal)
        col = lambda i: sc[:, 8 * i:8 * i + 8]
        gs = em.sb([128, 8], F32, "gs")
        em.ts(gs, col(0), 1.0, 1.0 / ALPHA, op0=ALU.add, op1=ALU.mult)
        gsb = em.sb([128, 8], F32, "gsb")
        em.tt(gsb, gs, col(3), ALU.mult)
        if hscale != 1.0:
            em.ts(sc[:, 32:33], sc[:, 32:33], float(hscale), None, op0=ALU.mult)
        ones_bf = em.sb([128, 128], BF16, "ones")
        em.memset(ones_bf, 1.0)
        if mode == 'rwkv':
            bd_f = em.sb([128, 128], F32, "bd_f")
            em.dma(bd_f, cst)
            bd = em.sb([128, 128], BF16, "bd")
            em.copy(bd, bd_f)
        Wo = [em.sb([128, D], BF16, "Wo%d" % k) for k in range(8)]
        for k in range(8):
            em.dma(Wo[k], w_o[k * 128:(k + 1) * 128, :], Q='pool')

        xt_r = Rot([em.sb([128, 8, NT], F32, "xt") for _ in range(2)])
        ot_r = Rot([em.sb([128, 8, NT], BF16, "ot") for _ in range(2)])
        z_r = Rot([em.sb([128, 8, NT], F32, "z") for _ in range(2)])
        zb_r = Rot([em.sb([128, 8, NT], BF16, "zb") for _ in range(2)])
        zsq_r = Rot([em.sb([128, 8, NT], BF16, "zsq") for _ in range(2)])
        mean = em.sb([128, NT], F32, "mean")
        rstd = em.sb([128, NT], F32, "rstd")
        t1 = em.sb([128, NT], F32, "t1")
        pbank = Rot([em.ps([128, 512], F32, "pb") for _ in range(2)])
        ps1 = em.ps([128, 512], F32, "ps1")
        ps2 = em.ps([128, 512], F32, "ps2")
        if mode != 'plain':
            pm2 = em.ps([128, 4, NT], F32, "pm2")
            pq2 = em.ps([128, 4, NT], F32, "pq2")
            hb = lambda nm, dt=F32: Rot([em.sb([128, 4, NT], dt, nm) for _ in range(2)])
            gm_h, gr_h, g1_h = hb("gm_h"), hb("gr_h"), hb("g1_h")
            ybig = em.sb([128, 8, NT], BF16, "ybig")
            ysqb = em.sb([128, 8, NT], BF16, "ysqb")
        if mode in ('gdn', 'diff'):
            yt_r = Rot([em.sb([128, 8, NT], F32, "yt") for _ in range(2)])
            zt_r = Rot([em.sb([128, 8, NT], BF16, "zt") for _ in range(2)])
            ysq_r = Rot([em.sb([128, NT], BF16, "ysq") for _ in range(2)])
            gr_r = Rot([em.sb([128, NT], F32, "gr") for _ in range(2)])
            gt1_r = Rot([em.sb([128, NT], F32, "gt1") for _ in range(2)])
        if mode == 'rwkv':
            yt_r = Rot([em.sb([128, 8, NT], F32, "yt") for _ in range(2)])
            gt_r = Rot([em.sb([128, 8, NT], BF16, "gt") for _ in range(2)])
            bt_r = Rot([em.sb([128, 8, NT], BF16, "bt") for _ in range(2)])
            yb_r = Rot([em.sb([128, NT], BF16, "yb") for _ in range(2)])
            ysq_r = Rot([em.sb([128, NT], BF16, "ysq") for _ in range(2)])
            gm_r = Rot([em.sb([128, NT], F32, "gm") for _ in range(2)])
            gr_r = Rot([em.sb([128, NT], F32, "gr") for _ in range(2)])
            gt1_r = Rot([em.sb([128, NT], F32, "gt1") for _ in range(2)])

        fm = lambda tl: tl[:].ap.rearrange("(c p) t -> p c t", p=128)
        xv, yv = fm(xT), fm(yT)
        for tt in range(ntok // NT):
            tsl = slice(tt * NT, (tt + 1) * NT)
            xt = xt_r.next()
            ot = ot_r.next()
            z, zb, zsq = z_r.next(), zb_r.next(), zsq_r.next()
            em.dma(xt, V(xT, xv[:, :, tsl]))
            if mode == 'rwkv':
                yt, gt, bt = yt_r.next(), gt_r.next(), bt_r.next()
                em.dma(yt, V(yin, fm(yin)[:, :, tsl]))
                em.dma(gt, V(gin, fm(gin)[:, :, tsl]))
                em.dma(bt, V(bin_, fm(bin_)[:, :, tsl]))
                em.copy(ybig, yt, E='pool')
                em.act(ysqb, yt, AF.Square)
                for hf in range(2):
                    hs_ = slice(4 * hf, 4 * hf + 4)
                    for j in range(4):
                        em.mm(pm2[:, j, :], bd, ybig[:, 4 * hf + j, :])
                    for j in range(4):
                        em.mm(pq2[:, j, :], bd, ysqb[:, 4 * hf + j, :])
                    gm, gr, g1 = gm_h.next(), gr_h.next(), g1_h.next()
                    em.ts(gm, pm2, 1.0 / 64, None, op0=ALU.mult)
                    em.tt(g1, gm, gm, ALU.mult, E='pool')
                    em.stt(gr, pq2, 1.0 / 64, g1, ALU.mult, ALU.subtract)
                    em.ts(gr, gr, 64e-5, None, op0=ALU.add, E='pool')
                    em.act(gr, gr, AF.Sqrt)
                    em.recip(gr, gr)
                    em.tt(g1, yt[:, hs_, :], gm, ALU.subtract, E='pool')
                    em.tt(g1, g1, gr, ALU.mult)
                    for j in range(4):
                        jj = 4 * hf + j
                        em.act(g1[:, j, :], g1[:, j, :], AF.Identity, bias=sc[:, 40 + jj:41 + jj], scale=sc[:, 32 + jj:33 + jj])
                    em.tt(g1, g1, bt[:, hs_, :], ALU.add)
                    em.tt(ot[:, hs_, :], g1, gt[:, hs_, :], ALU.mult, E='pool')
            elif mode in ('gdn', 'diff'):
                yt = yt_r.next()
                em.dma(yt, V(yin, fm(yin)[:, :, tsl]))
                if mode == 'gdn':
                    zt = zt_r.next()
                    em.dma(zt, V(zin, fm(zin)[:, :, tsl]))
                heps = 1e-6 if mode == 'gdn' else 1e-5
                em.act(ysqb, yt, AF.Square)
                for hf in range(2):
                    hs_ = slice(4 * hf, 4 * hf + 4)
                    for j in range(4):
                        em.mm(pq2[:, j, :], ones_bf, ysqb[:, 4 * hf + j, :])
                    gr, g1 = gr_h.next(), g1_h.next()
                    em.ts(gr, pq2, 1.0 / 128, heps, op0=ALU.mult, op1=ALU.add)
                    em.act(gr, gr, AF.Sqrt)
                    em.recip(gr, gr)
                    if mode == 'gdn':
                        em.stt(g1, yt[:, hs_, :], sc[:, 32:33], gr, ALU.mult, ALU.mult)
                        em.tt(ot[:, hs_, :], g1, zt[:, hs_, :], ALU.mult, E='pool')
                    else:
                        em.stt(ot[:, hs_, :], yt[:, hs_, :], sc[:, 32:33], gr, ALU.mult, ALU.mult)
            else:
                em.dma(ot, V(oin, fm(oin)[:, :, tsl]))
            for j in range(8):
                py = pbank.next()
                for k in range(8):
                    em.mm(py[:, 0:NT], Wo[k][:, j * 128:(j + 1) * 128], ot[:, k, :], start=(k == 0), stop=(k == 7))
                em.stt(z[:, j, :], py[:, 0:NT], gs[:, j:j + 1], xt[:, j, :], ALU.mult, ALU.add)
                if has_bias:
                    em.ts(z[:, j, :], z[:, j, :], gsb[:, j:j + 1], None, op0=ALU.add, E='pool')
            emit_ln(em, z, NT, ones_bf, col(1), col(2), z, ps1[:, 0:NT], ps2[:, 0:NT], (zb, zsq, mean, rstd, t1))
            em.dma(V(yT, yv[:, :, tsl]), z, Q='pool')
        em.finish()
    return nc


def gdn_level_masks():
    i = np.arange(128)
    out = []
    for li in range(7):
        m = 1 << li
        out.append((((i[:, None] // (2 * m)) == (i[None, :] // (2 * m))) &
                    ((i[:, None] // m) < (i[None, :] // m))).astype(np.float32))
    return out


def build_gdn_scan(Tn=T, SEG=1024, PI=4):
    NCK = SEG // 128
    nc = new_nc()
    with ExitStack() as st:
        em = Em(nc, st)
        qin = em.dram("q", [2, 128, Tn], BF16, "ExternalInput")
        kin = em.dram("k", [2, 128, Tn], BF16, "ExternalInput")
        vin = em.dram("v", [2, 128, Tn], BF16, "ExternalInput")
        bin_ = em.dram("beta", [2, Tn], F32, "ExternalInput")
        gin = em.dram("g", [2, Tn], F32, "ExternalInput")
        cst = em.dram("cst", [128, 512 + 7 * 128], F32, "ExternalInput")
        oT = em.dram("oT", [2, 128, Tn], F32, "ExternalOutput")

        cf = em.sb([128, 512 + 7 * 128], F32, "cf")
        em.dma(cf, cst)
        lvl = [cf[:, 512 + 128 * i:512 + 128 * (i + 1)] for i in range(7)]
        mnegI = cf[:, 0:128]
        nmS = cf[:, 128:256]
        If = cf[:, 256:384]
        Ib = em.sb([128, 128], BF16, "Ib")
        em.copy(Ib, cf[:, 256:384])
        e0 = em.sb([128, 2], F32, "e0")
        em.copy(e0[:, 0:1], cf[:, 256:257])
        em.copy(e0[:, 1:2], cf[:, 256:257])
        reset = em.sb([128, SEG], F32, "reset")
        for c in range(NCK):
            em.copy(reset[:, c * 128:(c + 1) * 128], cf[:, 384:512], E='pool')
        S = [em.sb([128, 128], F32, "S%d" % h) for h in range(2)]
        for h in range(2):
            em.memset(S[h], 0.0)
        banks = Rot([em.ps([128, 512], F32, "bk") for _ in range(8)])

        qt = [em.sb([128, SEG], BF16, "qt%d" % h) for h in range(2)]
        kt = [em.sb([128, SEG], BF16, "kt%d" % h) for h in range(2)]
        vt = [em.sb([128, SEG], BF16, "vt%d" % h) for h in range(2)]
        bb = [em.sb([128, SEG], F32, "bb%d" % h) for h in range(2)]
        gb = [em.sb([128, SEG], F32, "gb%d" % h) for h in range(2)]
        gc = [em.sb([128, SEG], F32, "gc%d" % h) for h in range(2)]
        eg = [em.sb([128, SEG], F32, "eg%d" % h) for h in range(2)]
        oo = [em.sb([128, SEG], F32, "oo%d" % h) for h in range(2)]

        def rot(nm, shape, dt, n):
            return Rot([em.sb(shape, dt, nm) for _ in range(n)])
        NI = 2 * PI
        col_r = rot("col", [128, 8], F32, NI + 2)
        DT_r = rot("DT", [128, 128], F32, 4)
        nb_r = rot("nb", [128, 128], F32, 4)
        t_r = rot("tt", [128, 128], F32, 4)
        Aqk_r = rot("Aqk", [128, 128], BF16, NI + 2)
        Z_r = rot("Z", [128, 128], F32, NI + 2)
        Zl_r = rot("Zl", [128, 128], F32, NI + 2)
        W_r = rot("W", [128, 128], F32, NI + 2)
        T_r = rot("T", [128, 128], F32, 2 * NI + 2)
        U_r = rot("U", [128, 128], F32, 2 * NI + 2)
        X0_r = rot("X0", [128, 256], F32, NI + 2)
        X_r = rot("X", [128, 256], BF16, NI + 2)
        kd_r = rot("kd", [128, 128], BF16, NI + 2)
        MT_r = rot("MT", [128, 128], F32, NI + 2)
        G_r = rot("G", [128, 128], F32, NI + 2)
        Q1_r = rot("Q1", [128, 128], F32, NI + 2)
        Q2_r = rot("Q2", [128, 128], F32, NI + 2)
        gI_r = rot("gI", [128, 128], F32, 4)

        for seg in range(Tn // SEG):
            s0 = seg * SEG
            for h in range(2):
                em.dma(qt[h], qin[h, :, s0:s0 + SEG])
                em.dma(kt[h], kin[h, :, s0:s0 + SEG])
                em.dma(vt[h], vin[h, :, s0:s0 + SEG])
                em.dma(bb[h], V(bin_, bin_[h:h + 1, s0:s0 + SEG].ap.partition_broadcast(128)))
                em.dma(gb[h], V(gin, gin[h:h + 1, s0:s0 + SEG].ap.partition_broadcast(128)))
                g_, c_ = gb[h], gc[h]
                em.op('dve', lambda e, c_=c_, g_=g_: e.tensor_tensor_scan(out=c_.ap, data0=reset.ap, data1=g_.ap,
                                                                          initial=0.0, op0=ALU.mult, op1=ALU.add),
                      reads=[reset, g_], writes=[c_])
                em.act(eg[h], c_, AF.Exp)
            for ck0 in range(0, NCK, PI):
                items = [(ck, h) for ck in range(ck0, min(NCK, ck0 + PI)) for h in range(2)]
                Z, X, cols, Aqk, kd = {}, {}, {}, {}, {}
                MTd, Gd, Q1d, Q2d = {}, {}, {}, {}
                for ck, h in items:
                    key = (ck, h)
                    tsl = slice(ck * 128, (ck + 1) * 128)
                    pcol = banks.next()
                    em.mm(pcol[:, 0:1], gc[h][:, tsl], e0[:, 0:1])
                    em.mm(pcol[:, 1:2], bb[h][:, tsl], e0[:, 0:1])
                    cl = col_r.next()
                    cols[key] = cl
                    em.copy(cl[:, 0:2], pcol[:, 0:2], E='dve')
                    em.ts(cl[:, 2:3], cl[:, 0:1], -1.0, None, op0=ALU.mult)
                    em.act(cl[:, 3:4], cl[:, 0:1], AF.Exp)
                    em.tt(cl[:, 4:5], cl[:, 3:4], cl[:, 1:2], ALU.mult)
                    em.act(cl[:, 5:6], cl[:, 2:3], AF.Exp, bias=gc[h][:, ck * 128 + 127:ck * 128 + 128])
                    em.copy(cl[:, 6:7], eg[h][:, ck * 128 + 127:ck * 128 + 128], E='pool')
                    DT = DT_r.next()
                    em.tt(DT, gc[h][:, tsl], mnegI, ALU.add, E='pool')
                    em.act(DT, DT, AF.Exp, bias=cl[:, 2:3])
                    nb = nb_r.next()
                    em.tt(nb, bb[h][:, tsl], nmS, ALU.mult, E='pool')
                    pk = banks.next()
                    em.mm(pk[:, 0:128], kt[h][:, tsl], kt[h][:, tsl])
                    em.mm(pk[:, 128:256], kt[h][:, tsl], qt[h][:, tsl])
                    em.mm(pk[:, 256:384], kt[h][:, tsl], Ib)
                    em.mm(pk[:, 384:512], vt[h][:, tsl], Ib)
                    t1 = t_r.next()
                    em.tt(t1, pk[:, 0:128], DT, ALU.mult)
                    Z[key] = Z_r.next()
                    em.tt(Z[key], t1, nb, ALU.mult, E='pool')
                    Aqk[key] = Aqk_r.next()
                    em.tt(Aqk[key], pk[:, 128:256], DT, ALU.mult)
                    X0 = X0_r.next()
                    X[key] = X0
                    em.ts(X0[:, 0:128], pk[:, 384:512], cl[:, 1:2], None, op0=ALU.mult)
                    em.ts(X0[:, 128:256], pk[:, 256:384], cl[:, 4:5], None, op0=ALU.mult)
                    kd[key] = kd_r.next()
                    em.act(kd[key], pk[:, 256:384], AF.Copy, scale=cl[:, 5:6])
                Tm, Um = {}, {}
                for ck, h in items:
                    key = (ck, h)
                    Um[key] = U_r.next()
                    em.tt(Um[key], Z[key], lvl[0], ALU.mult, E='pool')
                    em.tt(Um[key], Um[key], If, ALU.add, E='pool')
                    pt = banks.next()
                    em.mm(pt[:, 0:128], Um[key], If)
                    Tm[key] = T_r.next()
                    em.copy(Tm[key], pt[:, 0:128], E='act')
                for li in range(1, 7):
                    pws, Wts = {}, {}
                    for ck, h in items:
                        key = (ck, h)
                        Zl = Zl_r.next()
                        em.tt(Zl, Z[key], lvl[li], ALU.mult, E='pool')
                        pw = banks.next()
                        em.mm(pw[:, 0:128], Zl, Tm[key])
                        Wt = W_r.next()
                        em.copy(Wt, pw[:, 0:128], E='act')
                        pws[key], Wts[key] = pw, Wt
                    for ck, h in items:
                        key = (ck, h)
                        pw, Wt = pws[key], Wts[key]
                        if li < 6:
                            em.mm(pw[:, 128:256], Um[key], Wt)
                        em.mm(pw[:, 256:384], Wt, Um[key])
                        Un = U_r.next()
                        em.tt(Un, pw[:, 256:384], Um[key], ALU.add)
                        if li < 6:
                            Tn_ = T_r.next()
                            em.tt(Tn_, pw[:, 128:256], Tm[key], ALU.add)
                            Tm[key] = Tn_
                        Um[key] = Un
                Xbs = {}
                for ck, h in items:
                    key = (ck, h)
                    px = banks.next()
                    em.mm(px[:, 0:256], Um[key], X[key])
                    Xbs[key] = X_r.next()
                    em.copy(Xbs[key], px[:, 0:256], E='act')
                for ck, h in items:
                    key = (ck, h)
                    tsl = slice(ck * 128, (ck + 1) * 128)
                    Xb = Xbs[key]
                    cl = cols[key]
                    uc, wc = Xb[:, 0:128], Xb[:, 128:256]
                    pm = banks.next()
                    em.mm(pm[:, 0:128], wc, kd[key])
                    em.mm(pm[:, 128:256], kd[key], uc)
                    em.mm(pm[:, 256:384], wc, Aqk[key])
                    em.mm(pm[:, 384:512], uc, Aqk[key])
                    gI = gI_r.next()
                    em.ts(gI, If, cl[:, 6:7], None, op0=ALU.mult, E='pool')
                    MTd[key], Gd[key], Q1d[key], Q2d[key] = MT_r.next(), G_r.next(), Q1_r.next(), Q2_r.next()
                    em.stt(MTd[key], pm[:, 0:128], -1.0, gI, ALU.mult, ALU.add)
                    em.copy(Gd[key], pm[:, 128:256], E='act')
                    qd = t_r.next()
                    em.tt(qd, qt[h][:, tsl], eg[h][:, tsl], ALU.mult, E='pool')
                    em.stt(Q1d[key], pm[:, 256:384], -1.0, qd, ALU.mult, ALU.add)
                    em.copy(Q2d[key], pm[:, 384:512], E='act')
                for ck, h in items:
                    key = (ck, h)
                    tsl = slice(ck * 128, (ck + 1) * 128)
                    pc = banks.next()
                    em.mm(pc[:, 0:128], S[h], Q1d[key])
                    em.mm(pc[:, 128:256], MTd[key], S[h])
                    em.tt(oo[h][:, tsl], pc[:, 0:128], Q2d[key], ALU.add)
                    em.tt(S[h], pc[:, 128:256], Gd[key], ALU.add)
            for h in range(2):
                em.dma(oT[h, :, s0:s0 + SEG], oo[h])
        em.finish()
    return nc


def build_gdn_pre(ntok=TPC):
    NT = 256
    NH = NT + 3
    nc = new_nc()
    with ExitStack() as st:
        em = Em(nc, st)
        xT = em.dram("xT", [D, ntok + 3], F32, "ExternalInput")
        scal = em.dram("scal", [128, 113], F32, "ExternalInput")
        hsc = em.dram("hsc", [8, 2], F32, "ExternalInput")
        w_in = em.dram("w_in", [D, 4112], F32, "ExternalInput")
        outs = {n: em.dram(n, [D, ntok], BF16, "ExternalOutput") for n in ("q", "k", "v", "zs")}
        beta_o = em.dram("beta", [8, ntok], F32, "ExternalOutput")
        g_o = em.dram("g", [8, ntok], F32, "ExternalOutput")

        sc = em.sb([128, 113], F32, "sc")
        em.dma(sc, scal)
        sc1p = em.sb([128, 8], F32, "sc1p")
        em.ts(sc1p, sc[:, 0:8], 1.0, None, op0=ALU.add)
        hv = sc[:, 112:113]
        hs = em.sb([8, 2], F32, "hs")
        em.dma(hs, hsc)
        nea = em.sb([8, 1], F32, "nea")
        em.act(nea, hs[:, 0:1], AF.Exp)
        em.ts(nea, nea, -1.0, None, op0=ALU.mult)
        ones_bf = em.sb([128, 128], BF16, "ones")
        em.memset(ones_bf, 1.0)
        W = [em.sb([128, 4112], BF16, "W%d" % k) for k in range(8)]
        for k in range(8):
            em.dma(W[k], w_in[k * 128:(k + 1) * 128, :], Q='pool')

        xt_r = Rot([em.sb([128, 8, NH], F32, "xt") for _ in range(2)])
        ub = em.sb([128, 8, NH], BF16, "ub")
        banks = Rot([em.ps([128, 512], F32, "bk") for _ in range(8)])

        def rot(nm, n_, dt, n=2):
            return Rot([em.sb([128, n_], dt, nm) for _ in range(n)])
        pp_r, cv_r, s_r, sq_r, rn_r = rot("pp", NH, F32, 3), rot("cv", NT, F32, 3), rot("s", NT, F32, 3), rot("sq", NT, BF16, 3), rot("rn", NT, F32)
        ob_r = rot("ob", NT, BF16, 4)
        bt_r = Rot([em.sb([8, NT], F32, "bt") for _ in range(2)])
        gt_r = Rot([em.sb([8, NT], F32, "gt") for _ in range(2)])

        xv = xT[:].ap.rearrange("(c p) t -> p c t", p=128)
        def load_x(tt_):
            xt_ = xt_r.next()
            em.dma(xt_, V(xT, xv[:, :, tt_ * NT:tt_ * NT + NH]))
            return xt_

        xt_next = load_x(0)
        for tt in range(ntok // NT):
            t0 = tt * NT
            xt = xt_next
            for c in range(8):
                em.ts(ub[:, c, :], xt[:, c, :], sc1p[:, c:c + 1], sc[:, 8 + c:9 + c], op0=ALU.mult, op1=ALU.add,
                      E=('dve' if c % 2 == 0 else 'pool'))
            if tt == 0:
                em.ts(ub[:, :, 0:3], ub[:, :, 0:3], hv, None, op0=ALU.mult)
            if tt + 1 < ntok // NT:
                xt_next = load_x(tt + 1)
            def stage1(j):
                pp = banks.next()
                for k in range(8):
                    em.mm(pp[:, 0:NH], W[k][:, j * 128:(j + 1) * 128], ub[:, k, :], start=(k == 0), stop=(k == 7))
                if j >= 24:
                    ob = ob_r.next()
                    em.act(ob, pp[:, 3:NT + 3], AF.Silu)
                    em.dma(outs["zs"][(j - 24) * 128:(j - 23) * 128, t0:t0 + NT], ob)
                    return None
                ps_ = pp_r.next()
                em.copy(ps_, pp[:, 0:NH], E='act')
                cv = cv_r.next()
                cw = lambda kk: sc[:, 16 + j * 4 + kk:16 + j * 4 + kk + 1]
                e1 = 'dve' if j % 2 == 0 else 'pool'
                em.ts(cv, ps_[:, 3:NT + 3], cw(3), None, op0=ALU.mult, E=e1)
                em.stt(cv, ps_[:, 2:NT + 2], cw(2), cv, ALU.mult, ALU.add)
                em.stt(cv, ps_[:, 1:NT + 1], cw(1), cv, ALU.mult, ALU.add)
                em.stt(cv, ps_[:, 0:NT], cw(0), cv, ALU.mult, ALU.add)
                if j >= 16:
                    ob = ob_r.next()
                    em.act(ob, cv, AF.Silu)
                    em.dma(outs["v"][(j % 8) * 128:(j % 8 + 1) * 128, t0:t0 + NT], ob)
                    return None
                s_ = s_r.next()
                em.act(s_, cv, AF.Silu)
                sq = sq_r.next()
                em.act(sq, s_, AF.Square)
                return dict(j=j, s_=s_, sq=sq)

            def stage2(d):
                if d is None:
                    return
                j = d["j"]
                pn = banks.next()
                em.mm(pn[:, 0:NT], ones_bf, d["sq"])
                rn = rn_r.next()
                em.ts(rn, pn[:, 0:NT], 1e-6, None, op0=ALU.add)
                em.act(rn, rn, AF.Sqrt)
                em.recip(rn, rn)
                ob = ob_r.next()
                if j < 8:
                    em.stt(ob, d["s_"], 128.0 ** -0.5, rn, ALU.mult, ALU.mult)
                else:
                    em.tt(ob, d["s_"], rn, ALU.mult, E='pool')
                nm = "q" if j < 8 else "k"
                em.dma(outs[nm][(j % 8) * 128:(j % 8 + 1) * 128, t0:t0 + NT], ob)

            prev = stage1(0)
            for j in range(32):
                nxt = stage1(j + 1) if j + 1 < 32 else None
                stage2(prev)
                prev = nxt
            pb = banks.next()
            for k in range(8):
                em.mm(pb[0:8, 0:NT], W[k][:, 4096:4104], ub[:, k, 3:NT + 3], start=(k == 0), stop=(k == 7))
            bt = bt_r.next()
            em.act(bt, pb[0:8, 0:NT], AF.Sigmoid)
            em.dma(beta_o[:, t0:t0 + NT], bt)
            pa = banks.next()
            for k in range(8):
                em.mm(pa[0:8, 0:NT], W[k][:, 4104:4112], ub[:, k, 3:NT + 3], start=(k == 0), stop=(k == 7))
            gt = gt_r.next()
            em.act(gt, pa[0:8, 0:NT], AF.Exp, bias=hs[:, 1:2])
            em.act(gt, gt, AF.Ln, bias=1.0)
            em.ts(gt, gt, nea[:, 0:1], None, op0=ALU.mult)
            em.dma(g_o[:, t0:t0 + NT], gt)
        em.finish()
    return nc


def rope_tables(pos):
    d = np.arange(128) % 64
    inv = (10000.0 ** (-(np.arange(0, 64, 2, dtype=np.float32)) / 64)).astype(np.float32)
    ang = pos.astype(np.float32)[None, :] * inv[d % 32][:, None]
    C = np.cos(ang).astype(np.float32)
    S = np.sin(ang).astype(np.float32)
    S = np.where((d < 32)[:, None], -S, S).astype(np.float32)
    return C, S


def rope_perm(ncols):
    c = np.arange(ncols)
    return (c // 64) * 64 + (c % 64 + 32) % 64


def build_diff_pre(ntok=TPC):
    NT = 256
    nc = new_nc()
    with ExitStack() as st:
        em = Em(nc, st)
        xT = em.dram("xT", [D, ntok], F32, "ExternalInput")
        scal = em.dram("scal", [128, 16], F32, "ExternalInput")
        w_in = em.dram("w_in", [D, 3 * D], F32, "ExternalInput")
        w_pm = em.dram("w_pm", [D, 2 * D], F32, "ExternalInput")
        ctab = em.dram("ctab", [128, ntok], F32, "ExternalInput")
        stab = em.dram("stab", [128, ntok], F32, "ExternalInput")
        qo = em.dram("q", [D, ntok], BF16, "ExternalOutput")
        ko = em.dram("k", [D, ntok], BF16, "ExternalOutput")
        vtok = em.dram("vtok", [ntok, D], BF16, "ExternalOutput")
        sc = em.sb([128, 16], F32, "sc")
        em.dma(sc, scal)
        sc1p = em.sb([128, 8], F32, "sc1p")
        em.ts(sc1p, sc[:, 0:8], 1.0, None, op0=ALU.add)
        W = [em.sb([128, 3 * D], BF16, "W%d" % k) for k in range(8)]
        Wp = [em.sb([128, 2 * D], BF16, "Wp%d" % k) for k in range(8)]
        for k in range(8):
            em.dma(W[k], w_in[k * 128:(k + 1) * 128, :], Q='pool')
            em.dma(Wp[k], w_pm[k * 128:(k + 1) * 128, :], Q='pool')
        xt_r = Rot([em.sb([128, 8, NT], F32, "xt") for _ in range(2)])
        ub = em.sb([128, 8, NT], BF16, "ub")
        ct_r = Rot([em.sb([128, NT], F32, "ct") for _ in range(2)])
        st_r = Rot([em.sb([128, NT], F32, "st") for _ in range(2)])
        t1_r = Rot([em.sb([128, NT], F32, "t1") for _ in range(2)])
        t2_r = Rot([em.sb([128, NT], F32, "t2") for _ in range(2)])
        ob_r = Rot([em.sb([128, NT], BF16, "ob") for _ in range(3)])
        vt_r = Rot([em.sb([128, D], BF16, "vt") for _ in range(2)])
        banks = Rot([em.ps([128, 512], F32, "bk") for _ in range(8)])
        xv = xT[:].ap.rearrange("(c p) t -> p c t", p=128)
        for tt in range(ntok // NT):
            t0 = tt * NT
            xt = xt_r.next()
            em.dma(xt, V(xT, xv[:, :, t0:t0 + NT]))
            ct, stb = ct_r.next(), st_r.next()
            em.dma(ct, ctab[:, t0:t0 + NT])
            em.dma(stb, stab[:, t0:t0 + NT])
            for c in range(8):
                em.ts(ub[:, c, :], xt[:, c, :], sc1p[:, c:c + 1], sc[:, 8 + c:9 + c], op0=ALU.mult, op1=ALU.add,
                      E=('dve' if c % 2 == 0 else 'pool'))
            for j in range(16):
                p1, p2 = banks.next(), banks.next()
                for k in range(8):
                    em.mm(p1[:, 0:NT], W[k][:, j * 128:(j + 1) * 128], ub[:, k, :], start=(k == 0), stop=(k == 7))
                for k in range(8):
                    em.mm(p2[:, 0:NT], Wp[k][:, j * 128:(j + 1) * 128], ub[:, k, :], start=(k == 0), stop=(k == 7))
                scl = 0.125 if j < 8 else 1.0
                t1, t2, ob = t1_r.next(), t2_r.next(), ob_r.next()
                em.stt(t1, p1[:, 0:NT], scl, ct, ALU.mult, ALU.mult)
                em.stt(t2, p2[:, 0:NT], scl, stb, ALU.mult, ALU.mult)
                em.tt(ob, t1, t2, ALU.add, E='pool')
                dst = qo if j < 8 else ko
                em.dma(dst[(j % 8) * 128:(j % 8 + 1) * 128, t0:t0 + NT], ob)
            for tb in range(NT // 128):
                vt = vt_r.next()
                for half in range(2):
                    pv = banks.next()
                    for k in range(8):
                        em.mm(pv[:, 0:512], ub[:, k, tb * 128:(tb + 1) * 128],
                              W[k][:, 2 * D + half * 512:2 * D + (half + 1) * 512], start=(k == 0), stop=(k == 7))
                    em.copy(vt[:, half * 512:(half + 1) * 512], pv[:, 0:512], E=('act' if half == 0 else 'dve'))
                em.dma(vtok[t0 + tb * 128:t0 + (tb + 1) * 128, :], vt)
        em.finish()
    return nc


def build_diff_attn(Tn=T, lam_init=0.0):
    NB = Tn // 128
    NG = Tn // 512
    nc = new_nc()
    with ExitStack() as st:
        em = Em(nc, st)
        qin = em.dram("q", [2, 2, 64, Tn], BF16, "ExternalInput")
        kin = em.dram("k", [2, 2, 64, Tn], BF16, "ExternalInput")
        vin = em.dram("v", [2, Tn, 128], BF16, "ExternalInput")
        lin = em.dram("lam", [4, 64], F32, "ExternalInput")
        cst = em.dram("cst", [128, 256], F32, "ExternalInput")
        oT = em.dram("oT", [2, 128, Tn], F32, "ExternalOutput")

        cf = em.sb([128, 256], F32, "cf")
        em.dma(cf, cst)
        If = cf[:, 128:256]
        tri = em.sb([128, 128], BF16, "tri")
        em.copy(tri, cf[:, 0:128])
        sel = em.sb([64, 65], BF16, "sel")
        em.memset(sel, 0.0)
        em.memset(sel[:, 64:65], 1.0)
        lt = em.sb([128, 4, 64], F32, "lt")
        em.dma(lt, V(lin, lin[:].ap.rearrange("a b -> (a b)").partition_broadcast(128)).re("p (a b) -> p a b", a=4))
        lp = em.sb([128, 2, 64], F32, "lp")
        em.tt(lp[:, 0, :], lt[:, 0, :], lt[:, 1, :], ALU.mult)
        em.tt(lp[:, 1, :], lt[:, 2, :], lt[:, 3, :], ALU.mult)
        ls = em.sb([128, 4], F32, "ls")
        em.op('dve', lambda e: e.reduce_sum(out=ls[:, 0:1].ap, in_=lp[:, 0, :].ap, axis=AX.X), reads=[lp], writes=[ls])
        em.op('dve', lambda e: e.reduce_sum(out=ls[:, 1:2].ap, in_=lp[:, 1, :].ap, axis=AX.X), reads=[lp], writes=[ls])
        em.act(ls[:, 0:2], ls[:, 0:2], AF.Exp)
        em.tt(ls[:, 2:3], ls[:, 1:2], ls[:, 0:1], ALU.subtract)
        em.ts(ls[:, 3:4], ls[:, 2:3], -float(lam_init), None, op0=ALU.add)

        kaug = [em.sb([65, Tn], BF16, "kaug%d" % c) for c in range(2)]
        vaug = em.sb([128, NB, 129], BF16, "vaug")
        qaug_r = [Rot([em.sb([65, 512], BF16, "qaug%d" % c) for _ in range(2)]) for c in range(2)]
        ksq = em.sb([64, 512], BF16, "ksq")
        kmx = em.sb([65, 40], F32, "kmx")
        km2 = [em.sb([65, 1], F32, "km2_%d" % c) for c in range(2)]
        qsq_r = Rot([em.sb([64, 512], BF16, "qsq") for _ in range(2)])
        PT_r = Rot([em.sb([128, 2, 512], BF16, "PT") for _ in range(3)])
        sb2 = Rot([em.ps([128, 2, 512], F32, "sb2_%d" % i) for i in range(2)])

        class _Half:
            def __init__(self, c):
                self.c = c

            def next(self):
                t = sb2.next()
                return t[:, self.c, :]
        sbk = [_Half(0), _Half(1)]
        obk = [[em.ps([128, 512], F32, "ob%d_%d" % (c, i)) for i in range(2)] for c in range(2)]
        rc_r = Rot([em.sb([128, 2], F32, "rc") for _ in range(4)])
        t_r = Rot([em.sb([128, 128], F32, "tf") for _ in range(3)])
        of_r = Rot([em.sb([128, 128], F32, "of") for _ in range(3)])
        ost_r = Rot([em.sb([128, 512], F32, "ost") for _ in range(2)])

        for u in range(2):
            em.dma(vaug[:, :, 0:128], V(vin, vin[u].ap.rearrange("(n p) c -> p n c", p=128)))
            em.memset(vaug[:, :, 128:129], 1.0)
            for c in range(2):
                em.dma(kaug[c][0:64, :], kin[u, c])
                em.memset(kaug[c][64:65, :], 1.0)
                for g in range(NG):
                    em.act(ksq, kaug[c][0:64, g * 512:(g + 1) * 512], AF.Square)
                    pn = sbk[c].next()
                    em.mm(pn[0:65, 0:512], sel, ksq)
                    em.op('dve', lambda e, pn=pn, g=g: e.reduce_max(out=kmx[64:65, g:g + 1].ap, in_=pn[64:65, 0:512].ap, axis=AX.X),
                          reads=[pn], writes=[kmx])
                em.op('dve', lambda e, c=c: e.reduce_max(out=km2[c][64:65, 0:1].ap, in_=kmx[64:65, 0:NG].ap, axis=AX.X),
                      reads=[kmx], writes=[km2[c]])
                em.ts(km2[c][64:65, :], km2[c][64:65, :], 1.05, None, op0=ALU.mult)
            for G in range(NG):
                qa = []
                for c in range(2):
                    q_ = qaug_r[c].next()
                    qa.append(q_)
                    em.dma(q_[0:64, :], qin[u, c, :, G * 512:(G + 1) * 512])
                    qsq = qsq_r.next()
                    em.act(qsq, q_[0:64, :], AF.Square)
                    pn = sbk[c].next()
                    em.mm(pn[0:65, 0:512], sel, qsq)
                    em.act(q_[64:65, :], pn[64:65, 0:512], AF.Sqrt, scale=km2[c][64:65, 0:1])
                    em.ts(q_[64:65, :], q_[64:65, :], -1.0, None, op0=ALU.mult)
                nkb = 4 * G + 4
                first = [[True, True], [True, True]]

                def s_mm(j):
                    m = max(0, j - 4 * G)
                    ncol = (4 - m) * 128
                    ps2 = sb2.next()
                    for c in range(2):
                        em.mm(ps2[:, c, 0:ncol], kaug[c][:, j * 128:(j + 1) * 128], qa[c][:, m * 128:512])
                    return ps2

                cur = s_mm(0)
                for j in range(nkb):
                    m = max(0, j - 4 * G)
                    ncol = (4 - m) * 128
                    PT2 = PT_r.next()
                    em.act(PT2[:, :, 0:ncol], cur[:, :, 0:ncol], AF.Exp)
                    if j >= 4 * G:
                        for c in range(2):
                            em.tt(PT2[:, c, 0:128], PT2[:, c, 0:128], tri, ALU.mult, E='pool')
                    nxt = s_mm(j + 1) if j + 1 < nkb else None
                    for c in range(2):
                        PT = PT2[:, c, :]
                        for qi in range(m, 4):
                            bk = obk[c][qi // 2]
                            o_ = bk[:, (qi % 2) * 129:(qi % 2) * 129 + 129]
                            em.mm(o_, PT[:, (qi - m) * 128:(qi - m + 1) * 128], vaug[:, j, :],
                                  start=first[c][qi // 2], stop=(j == 4 * G + qi), sgc=True)
                            first[c][qi // 2] = False
                    cur = nxt
                ost = ost_r.next()
                for qi in range(4):
                    o1 = obk[0][qi // 2][:, (qi % 2) * 129:(qi % 2) * 129 + 129]
                    o2 = obk[1][qi // 2][:, (qi % 2) * 129:(qi % 2) * 129 + 129]
                    rc = rc_r.next()
                    em.recip(rc[:, 0:1], o1[:, 128:129])
                    em.recip(rc[:, 1:2], o2[:, 128:129])
                    em.tt(rc[:, 1:2], rc[:, 1:2], ls[:, 3:4], ALU.mult)
                    t2 = t_r.next()
                    em.ts(t2, o2[:, 0:128], rc[:, 1:2], None, op0=ALU.mult)
                    of = of_r.next()
                    em.stt(of, o1[:, 0:128], rc[:, 0:1], t2, ALU.mult, ALU.add)
                    pt = sbk[qi % 2].next()
                    em.mm(pt[:, 0:128], of, If)
                    em.copy(ost[:, qi * 128:(qi + 1) * 128], pt[:, 0:128], E='act')
                em.dma(oT[u, :, G * 512:(G + 1) * 512], ost)
        em.finish()
    return nc


def build_swa(ntok=TPC):
    NT = 512
    NBL = NT // 128
    nc = new_nc()
    with ExitStack() as st:
        em = Em(nc, st)
        xT = em.dram("xT", [D, 128 + ntok], F32, "ExternalInput")
        scal = em.dram("scal", [128, 36], F32, "ExternalInput")
        w_in = em.dram("w_in", [D, 1280], F32, "ExternalInput")
        w_pm = em.dram("w_pm", [D, 1152], F32, "ExternalInput")
        bv = em.dram("bv", [1, 128], F32, "ExternalInput")
        sinks = em.dram("sinks", [1, 16], F32, "ExternalInput")
        ctab = em.dram("ctab", [128, 128 + ntok], F32, "ExternalInput")
        stab = em.dram("stab", [128, 128 + ntok], F32, "ExternalInput")
        cst = em.dram("cst", [128, 384], F32, "ExternalInput")
        oT = em.dram("oT", [D, ntok], BF16, "ExternalOutput")

        sc = em.sb([128, 36], F32, "sc")
        em.dma(sc, scal)
        sc1p = em.sb([128, 8], F32, "sc1p")
        em.ts(sc1p, sc[:, 0:8], 1.0, None, op0=ALU.add)
        hv = sc[:, 34:35]
        cf = em.sb([128, 384], F32, "cf")
        em.dma(cf, cst)
        If = cf[:, 256:384]
        mP = em.sb([128, 512], BF16, "mP")
        mC = em.sb([128, 512], BF16, "mC")
        for i in range(4):
            em.copy(mP[:, i * 128:(i + 1) * 128], cf[:, 0:128])
            em.copy(mC[:, i * 128:(i + 1) * 128], cf[:, 128:256])
        bvb = em.sb([128, 128], F32, "bvb")
        em.dma(bvb, V(bv, bv[0:1, :].ap.partition_broadcast(128)))
        esk = em.sb([128, 16], F32, "esk")
        em.dma(esk, V(sinks, sinks[0:1, :].ap.partition_broadcast(128)))
        em.act(esk, esk, AF.Exp)
        sel = em.sb([64, 65], BF16, "sel")
        em.memset(sel, 0.0)
        em.memset(sel[:, 64:65], 1.0)
        vvirt = em.sb([65, 66], BF16, "vvirt")
        em.memset(vvirt, 0.0)
        em.memset(vvirt[64:65, 65:66], 1.0)

        W = [em.sb([128, 1280], BF16, "W%d" % k) for k in range(8)]
        Wp = [em.sb([128, 1152], BF16, "Wp%d" % k) for k in range(8)]
        for k in range(8):
            em.dma(W[k], w_in[k * 128:(k + 1) * 128, :], Q='pool')
            em.dma(Wp[k], w_pm[k * 128:(k + 1) * 128, :], Q='pool')

        xt_r = Rot([em.sb([128, 8, NT], F32, "xt") for _ in range(2)])
        ub = em.sb([128, 8, NT], BF16, "ub")
        ct_r = Rot([em.sb([128, NT], F32, "ct") for _ in range(2)])
        st_r = Rot([em.sb([128, NT], F32, "st") for _ in range(2)])
        t1_r = Rot([em.sb([128, NT], F32, "t1") for _ in range(2)])
        t2_r = Rot([em.sb([128, NT], F32, "t2") for _ in range(2)])
        kaug = [em.sb([65, 128 + NT], BF16, "kaug%d" % g) for g in range(2)]
        vaug = [em.sb([128, NBL + 1, 66], BF16, "vaug%d" % g) for g in range(2)]
        for g in range(2):
            em.memset(kaug[g][64:65, :], 1.0)
            em.memset(vaug[g][:, :, 64:65], 1.0)
            em.memset(vaug[g][:, :, 65:66], 0.0)
        qaug = em.sb([65, 16, NT], BF16, "qaug")
        erow = em.sb([65, 16, NT], BF16, "erow")
        ksq = em.sb([64, 128 + NT], BF16, "ksq")
        qsq_r = Rot([em.sb([64, NT], BF16, "qsq") for _ in range(2)])
        km = em.sb([65, 4], F32, "km")
        km2 = [em.sb([65, 1], F32, "km2_%d" % g) for g in range(2)]
        PT_r = Rot([em.sb([128, 512], BF16, "PT") for _ in range(4)])
        den_r = Rot([em.sb([128, 4], F32, "den") for _ in range(4)])
        otok_r = Rot([em.sb([128, D], F32, "otok") for _ in range(2)])
        ostg_r = Rot([em.sb([128, 8, NT], BF16, "ostg") for _ in range(2)])
        banks = Rot([em.ps([128, 512], F32, "bk") for _ in range(4)])
        sbanks = Rot([em.ps([128, 512], F32, "sbk") for _ in range(4)])

        xv = xT[:].ap.rearrange("(c p) t -> p c t", p=128)
        ov = oT[:].ap.rearrange("(c p) t -> p c t", p=128)

        def project(c0, n, tile_i):
            xt = xt_r.next()
            em.dma(xt[:, :, 0:n], V(xT, xv[:, :, c0:c0 + n]))
            ct, stb = ct_r.next(), st_r.next()
            em.dma(ct[:, 0:n], ctab[:, c0:c0 + n])
            em.dma(stb[:, 0:n], stab[:, c0:c0 + n])
            for c in range(8):
                em.ts(ub[:, c, 0:n], xt[:, c, 0:n], sc1p[:, c:c + 1], sc[:, 8 + c:9 + c], op0=ALU.mult, op1=ALU.add,
                      E=('dve' if c % 2 == 0 else 'pool'))
            koff = 0 if tile_i < 0 else 128
            chunks = [8] if tile_i < 0 else list(range(9))
            for j in chunks:
                p1, p2 = banks.next(), banks.next()
                for k in range(8):
                    em.mm(p1[:, 0:n], W[k][:, j * 128:(j + 1) * 128], ub[:, k, 0:n], start=(k == 0), stop=(k == 7))
                for k in range(8):
                    em.mm(p2[:, 0:n], Wp[k][:, j * 128:(j + 1) * 128], ub[:, k, 0:n], start=(k == 0), stop=(k == 7))
                t1, t2 = t1_r.next(), t2_r.next()
                b1 = sc[:, 16 + j:17 + j] if j < 8 else sc[:, 32:33]
                b2 = sc[:, 24 + j:25 + j] if j < 8 else sc[:, 33:34]
                em.stt(t1[:, 0:n], p1[:, 0:n], b1, ct[:, 0:n], ALU.add, ALU.mult)
                em.stt(t2[:, 0:n], p2[:, 0:n], b2, stb[:, 0:n], ALU.add, ALU.mult)
                for e in range(2):
                    ps = slice(64 * e, 64 * e + 64)
                    if j < 8:
                        em.tt(qaug[0:64, 2 * j + e, 0:n], t1[ps, 0:n], t2[ps, 0:n], ALU.add, E='pool')
                    else:
                        em.stt(kaug[e][0:64, koff:koff + n], t1[ps, 0:n], 1.0, t2[ps, 0:n], ALU.mult, ALU.add, E='pool')
                        em.ts(kaug[e][0:64, koff:koff + n], kaug[e][0:64, koff:koff + n], 0.125, None, op0=ALU.mult, E='pool')
            for bl in range(n // 128):
                pv = banks.next()
                for k in range(8):
                    em.mm(pv[:, 0:128], ub[:, k, bl * 128:(bl + 1) * 128], W[k][:, 1152:1280], start=(k == 0), stop=(k == 7))
                for g in range(2):
                    vb = bl + (0 if tile_i < 0 else 1)
                    em.tt(vaug[g][:, vb, 0:64], pv[:, g * 64:(g + 1) * 64], bvb[:, g * 64:(g + 1) * 64], ALU.add)

        project(0, 128, -1)
        for tt in range(ntok // NT):
            project(128 + tt * NT, NT, tt)
            for g in range(2):
                em.act(ksq, kaug[g][0:64, :], AF.Square)
                for hf in range(2):
                    w_ = (128 + NT) // 2
                    pn = banks.next()
                    em.mm(pn[0:65, 0:w_], sel, ksq[:, hf * w_:(hf + 1) * w_])
                    em.op('dve', lambda e, pn=pn, hf=hf, w_=w_: e.reduce_max(out=km[64:65, hf:hf + 1].ap, in_=pn[64:65, 0:w_].ap, axis=AX.X),
                          reads=[pn], writes=[km])
                em.op('dve', lambda e, g=g: e.reduce_max(out=km2[g][64:65, 0:1].ap, in_=km[64:65, 0:2].ap, axis=AX.X),
                      reads=[km], writes=[km2[g]])
                em.ts(km2[g][64:65, :], km2[g][64:65, :], 1.05, None, op0=ALU.mult)
            for h in range(16):
                g = h // 8
                qsq = qsq_r.next()
                em.act(qsq, qaug[0:64, h, :], AF.Square)
                pn = banks.next()
                em.mm(pn[0:65, 0:NT], sel, qsq)
                em.act(qaug[64:65, h, :], pn[64:65, 0:NT], AF.Sqrt, scale=km2[g][64:65, 0:1])
                em.ts(qaug[64:65, h, :], qaug[64:65, h, :], -1.0, None, op0=ALU.mult)
                em.act(erow[64:65, h, :], qaug[64:65, h, :], AF.Exp)
            ostg = ostg_r.next()
            def s_mm(bl, g, hg):
                h0 = g * 8 + hg * 4
                q4 = qaug[:, h0:h0 + 4, bl * 128:(bl + 1) * 128]
                pp, pc = sbanks.next(), sbanks.next()
                em.mm(pp[:, 0:512], kaug[g][:, bl * 128:(bl + 1) * 128], q4)
                em.mm(pc[:, 0:512], kaug[g][:, (bl + 1) * 128:(bl + 2) * 128], q4)
                return pp, pc

            groups = [(bl, g, hg) for bl in range(NBL) for g in range(2) for hg in range(2)]
            s_next = s_mm(*groups[0])
            for gi, (bl, g, hg) in enumerate(groups):
                bsl = slice(bl * 128, (bl + 1) * 128)
                if g == 0 and hg == 0:
                    otok = otok_r.next()
                if True:
                    if True:
                        h0 = g * 8 + hg * 4
                        pp, pc = s_next
                        Pp, Pc = PT_r.next(), PT_r.next()
                        em.act(Pp, pp[:, 0:512], AF.Exp)
                        em.act(Pc, pc[:, 0:512], AF.Exp)
                        em.tt(Pp, Pp, mP, ALU.mult, E='pool')
                        em.tt(Pc, Pc, mC, ALU.mult, E='dve')
                        if tt == 0 and bl == 0:
                            em.ts(Pp, Pp, hv, None, op0=ALU.mult)
                        if gi + 1 < len(groups):
                            s_next = s_mm(*groups[gi + 1])
                        po = banks.next()
                        for i in range(4):
                            o_ = po[:, i * 66:(i + 1) * 66]
                            em.mm(o_, Pp[:, i * 128:(i + 1) * 128], vaug[g][:, bl, :], start=(i == 0), stop=False, sgc=True)
                            em.mm(o_, Pc[:, i * 128:(i + 1) * 128], vaug[g][:, bl + 1, :], start=False, stop=False, sgc=True)
                            em.mm(o_, erow[64:65, h0 + i, bsl], vvirt[64:65, :], start=False, stop=True, sgc=True)
                        for i in range(4):
                            h = h0 + i
                            o_ = po[:, i * 66:(i + 1) * 66]
                            den = den_r.next()
                            em.copy(den[:, 2:4], o_[:, 64:66], E='dve')
                            em.stt(den[:, 0:1], den[:, 3:4], esk[:, h:h + 1], den[:, 2:3], ALU.mult, ALU.add)
                            em.recip(den[:, 1:2], den[:, 0:1])
                            em.ts(otok[:, h * 64:(h + 1) * 64], o_[:, 0:64], den[:, 1:2], None, op0=ALU.mult)
                if g == 1 and hg == 1:
                    for j in range(8):
                        pt = banks.next()
                        em.mm(pt[:, 0:128], otok[:, j * 128:(j + 1) * 128], If)
                        em.copy(ostg[:, j, bsl], pt[:, 0:128], E='act')
            em.dma(V(oT, ov[:, :, tt * NT:(tt + 1) * NT]), ostg)
            for g in range(2):
                em.copy(kaug[g][0:64, 0:128], kaug[g][0:64, NT:NT + 128], E='pool')
                em.copy(vaug[g][:, 0, 0:64], vaug[g][:, NBL, 0:64], E='pool')
        em.finish()
    return nc


def pcol(v):
    v = np.asarray(v, np.float32)
    return np.ascontiguousarray(v.reshape(-1, 128).T)


def core_bq(c):
    return c // 4, c % 4


def halo_x(x_tok, c, h):
    b, q = core_bq(c)
    if q == 0:
        left = np.zeros((D, h), np.float32)
    else:
        left = x_tok[c - 1][:, TPC - h:]
    return np.ascontiguousarray(np.concatenate([left, x_tok[c]], axis=1))


def tok_to_rows(outs, name, b, r0, r1):
    return np.concatenate([outs[b * 4 + q][name][r0:r1, :] for q in range(4)], axis=1)


_DBG = {}


def kernel(x, c, ada_w, ada_b, ln_g, ln_b, ffn_w_in, ffn_w_out,
           rwkv_mu, rwkv_w_rkv, rwkv_w0, rwkv_w1, rwkv_w2, rwkv_a0, rwkv_a1, rwkv_a2,
           rwkv_g1, rwkv_g2, rwkv_k_k, rwkv_k_a, rwkv_r_k, rwkv_gn_g, rwkv_gn_b, rwkv_w_out,
           gdn_w_in, gdn_conv, gdn_a_log, gdn_dt_bias, gdn_norm_g, gdn_w_out,
           diff_w_in, diff_lambda, diff_subln_g, diff_w_out,
           swa_w_qkv, swa_b_qkv, swa_sinks, swa_w_out, swa_b_out, _layers=DEPTH, _debug=None):
    f32 = lambda a: np.ascontiguousarray(np.asarray(a, np.float32))
    x = f32(x)
    mod = run_mod(f32(c), f32(ada_w), f32(ada_b))
    ms = lambda l, b, w: mod[l, b][:, w * 8:(w + 1) * 8]
    xs = [np.ascontiguousarray(x[cc // 4, (cc % 4) * TPC:(cc % 4 + 1) * TPC, :].T) for cc in range(NCORES)]
    zeros8 = np.zeros((128, 8), np.float32)
    bd = np.kron(np.eye(2), np.ones((64, 64))).astype(np.float32)
    eye = np.eye(128, dtype=np.float32)
    hvcol = lambda cc: np.full((128, 1), 0.0 if cc % 4 == 0 else 1.0, np.float32)

    def post(i, mode, extra, w_o, b_out=None, cols4=None, cols5=None, hscale=1.0):
        nc = build_post(mode, TPC, hscale, has_bias=(b_out is not None))
        ims = []
        for cc in range(NCORES):
            b = cc // 4
            sc = np.concatenate([ms(i, b, 2), pcol(ln_g[i, 0]), pcol(ln_b[i, 0]),
                                 pcol(b_out) if b_out is not None else zeros8,
                                 cols4 if cols4 is not None else zeros8,
                                 cols5 if cols5 is not None else zeros8, zeros8], axis=1)
            im = {"xT": xs[cc], "w_o": f32(w_o), "scal": np.ascontiguousarray(sc)}
            im.update(extra[cc])
            ims.append(im)
        res = run(nc, ims)
        return [res[cc]["yT"] for cc in range(NCORES)]

    def ffn(i, xin):
        nc = build_ffn(TPC)
        ims = []
        for cc in range(NCORES):
            b = cc // 4
            sc = np.concatenate([ms(i, b, 4), ms(i, b, 3), ms(i, b, 5), pcol(ln_g[i, 1]), pcol(ln_b[i, 1])], axis=1)
            ims.append({"xT": xin[cc], "w_in": f32(ffn_w_in[i]), "w_out": f32(ffn_w_out[i]),
                        "scal": np.ascontiguousarray(sc)})
        res = run(nc, ims)
        return [res[cc]["yT"] for cc in range(NCORES)]

    for i in range(_layers):
        m = i % 4
        if m == 0:
            nc = build_rwkv_pre(TPC)
            ims = []
            for cc in range(NCORES):
                b = cc // 4
                sc = np.concatenate([ms(i, b, 1), ms(i, b, 0)] + [pcol(rwkv_mu[0, k]) for k in range(6)] +
                                    [pcol(rwkv_w0[0]), pcol(rwkv_a0[0]), pcol(rwkv_k_k[0]), pcol(rwkv_k_a[0]),
                                     pcol(np.asarray(rwkv_r_k[0]).reshape(-1)), hvcol(cc)], axis=1)
                ims.append({"xT": halo_x(xs, cc, 1), "scal": np.ascontiguousarray(sc), "cst": bd,
                            "w_rkv": f32(rwkv_w_rkv[0]), "w1": f32(rwkv_w1[0]), "a1": f32(rwkv_a1[0]), "g1": f32(rwkv_g1[0]),
                            "w2": f32(rwkv_w2[0]), "a2": f32(rwkv_a2[0]), "g2": f32(rwkv_g2[0])})
            pre = run(nc, ims)
            nc = build_rwkv_scan(T, 1024)
            mS, mI, mL = chunk_masks()
            reset = np.ones((128, 128), np.float32)
            reset[:, 0] = 0
            reset[:, 64] = 0
            cst = np.ascontiguousarray(np.concatenate([mS, mI, mL, eye, reset], axis=1))
            ims = []
            for cc in range(NCORES):
                b, hq = cc // 4, cc % 4
                im = {"cst": cst}
                for n in ("r", "k", "kk", "a", "lw"):
                    im[n] = np.ascontiguousarray(tok_to_rows(pre, n, b, 256 * hq, 256 * hq + 256).reshape(2, 128, T))
                im["v"] = np.ascontiguousarray(np.concatenate([pre[b * 4 + q]["vtok"][:, 256 * hq:256 * hq + 256]
                                                               for q in range(4)], axis=0))
                ims.append(im)
            sres = run(nc, ims)
            extra = []
            for cc in range(NCORES):
                b, q = cc // 4, cc % 4
                yin = np.concatenate([sres[b * 4 + hq]["yT"].reshape(256, T)[:, q * TPC:(q + 1) * TPC] for hq in range(4)], axis=0)
                extra.append({"yin": np.ascontiguousarray(yin), "g": pre[cc]["g"], "bonus": pre[cc]["bonus"], "cst": bd})
            xs = post(i, 'rwkv', extra, rwkv_w_out[0], cols4=pcol(rwkv_gn_g[0]), cols5=pcol(rwkv_gn_b[0]))
        elif m == 1:
            nc = build_gdn_pre(TPC)
            cw = np.asarray(gdn_conv[0], np.float32).reshape(4, 24, 128).transpose(2, 1, 0).reshape(128, 96)
            hsc = np.ascontiguousarray(np.stack([np.asarray(gdn_a_log[0], np.float32), np.asarray(gdn_dt_bias[0], np.float32)], axis=1))
            ims = []
            for cc in range(NCORES):
                b = cc // 4
                sc = np.concatenate([ms(i, b, 1), ms(i, b, 0), cw, hvcol(cc)], axis=1)
                ims.append({"xT": halo_x(xs, cc, 3), "scal": np.ascontiguousarray(sc), "hsc": hsc, "w_in": f32(gdn_w_in[0])})
            pre = run(nc, ims)
            nc = build_gdn_scan(T, 1024)
            ii = np.arange(128)
            mnegI = np.where(ii[:, None] <= ii[None, :], 0.0, -1e4).astype(np.float32)
            nmS = -(ii[:, None] < ii[None, :]).astype(np.float32)
            reset = np.ones((128, 128), np.float32)
            reset[:, 0] = 0
            cst = np.ascontiguousarray(np.concatenate([mnegI, nmS, eye, reset] + gdn_level_masks(), axis=1))
            ims = []
            for cc in range(NCORES):
                b, hq = cc // 4, cc % 4
                im = {"cst": cst}
                for n in ("q", "k", "v"):
                    im[n] = np.ascontiguousarray(tok_to_rows(pre, n, b, 256 * hq, 256 * hq + 256).reshape(2, 128, T))
                for n in ("beta", "g"):
                    im[n] = np.ascontiguousarray(tok_to_rows(pre, n, b, 2 * hq, 2 * hq + 2))
                ims.append(im)
            sres = run(nc, ims)
            extra = []
            ng = np.zeros((128, 8), np.float32)
            ng[:, 0] = np.asarray(gdn_norm_g[0], np.float32)
            for cc in range(NCORES):
                b, q = cc // 4, cc % 4
                yin = np.concatenate([sres[b * 4 + hq]["oT"].reshape(256, T)[:, q * TPC:(q + 1) * TPC] for hq in range(4)], axis=0)
                extra.append({"yin": np.ascontiguousarray(yin), "zs": pre[cc]["zs"]})
            xs = post(i, 'gdn', extra, gdn_w_out[0], cols4=ng)
        elif m == 2:
            lam_init = 0.8 - 0.6 * float(np.exp(-0.3 * i))
            nc = build_diff_pre(TPC)
            perm = rope_perm(2 * D)
            w_in = f32(diff_w_in[0])
            w_pm = np.ascontiguousarray(w_in[:, :2 * D][:, perm])
            ims = []
            for cc in range(NCORES):
                b, q = cc // 4, cc % 4
                Ct, St = rope_tables(np.arange(q * TPC, (q + 1) * TPC))
                sc = np.concatenate([ms(i, b, 1), ms(i, b, 0)], axis=1)
                ims.append({"xT": xs[cc], "scal": np.ascontiguousarray(sc), "w_in": w_in, "w_pm": w_pm, "ctab": Ct, "stab": St})
            pre = run(nc, ims)
            nc = build_diff_attn(T, lam_init)
            ii = np.arange(128)
            cst = np.ascontiguousarray(np.concatenate([(ii[:, None] <= ii[None, :]).astype(np.float32), eye], axis=1))
            ims = []
            for cc in range(NCORES):
                im = {"lam": f32(diff_lambda[0]), "cst": cst}
                qs, ks, vs = [], [], []
                for u in range(2):
                    b, h = (2 * cc + u) // 8, (2 * cc + u) % 8
                    qs.append(tok_to_rows(pre, "q", b, 128 * h, 128 * h + 128).reshape(2, 64, T))
                    ks.append(tok_to_rows(pre, "k", b, 128 * h, 128 * h + 128).reshape(2, 64, T))
                    vs.append(np.concatenate([pre[b * 4 + q]["vtok"][:, 128 * h:128 * h + 128] for q in range(4)], axis=0))
                im["q"] = np.ascontiguousarray(np.stack(qs))
                im["k"] = np.ascontiguousarray(np.stack(ks))
                im["v"] = np.ascontiguousarray(np.stack(vs))
                ims.append(im)
            ares = run(nc, ims)
            extra = []
            sg = np.zeros((128, 8), np.float32)
            sg[:, 0] = np.asarray(diff_subln_g[0], np.float32)
            for cc in range(NCORES):
                b, q = cc // 4, cc % 4
                rows = []
                for h in range(8):
                    unit = b * 8 + h
                    rows.append(ares[unit // 2]["oT"][unit % 2][:, q * TPC:(q + 1) * TPC])
                extra.append({"yin": np.ascontiguousarray(np.concatenate(rows, axis=0))})
            xs = post(i, 'diff', extra, diff_w_out[0], cols4=sg, hscale=1.0 - lam_init)
        else:
            nc = build_swa(TPC)
            perm = rope_perm(1152)
            w_in = f32(swa_w_qkv[0])
            bq = np.asarray(swa_b_qkv[0], np.float32)
            bqp = bq[:1152][perm]
            w_pm = np.ascontiguousarray(w_in[:, :1152][:, perm])
            ii = np.arange(128)
            cst = np.ascontiguousarray(np.concatenate([(ii[:, None] > ii[None, :]).astype(np.float32),
                                                       (ii[:, None] <= ii[None, :]).astype(np.float32), eye], axis=1))
            ims = []
            for cc in range(NCORES):
                b, q = cc // 4, cc % 4
                Ct, St = rope_tables(np.arange(q * TPC - 128, (q + 1) * TPC))
                sc = np.concatenate([ms(i, b, 1), ms(i, b, 0), pcol(bq[:1024]), pcol(bqp[:1024]), pcol(bq[1024:1152]),
                                     pcol(bqp[1024:1152]), hvcol(cc), np.zeros((128, 1), np.float32)], axis=1)
                ims.append({"xT": halo_x(xs, cc, 128), "scal": np.ascontiguousarray(sc), "w_in": w_in, "w_pm": w_pm,
                            "bv": np.ascontiguousarray(bq[None, 1152:]), "sinks": f32(swa_sinks[0])[None, :],
                            "ctab": Ct, "stab": St, "cst": cst})
            ares = run(nc, ims)
            extra = [{"oT": ares[cc]["oT"]} for cc in range(NCORES)]
            xs = post(i, 'plain', extra, swa_w_out[0], b_out=swa_b_out[0])
        if _debug is not None:
            _debug.append(("mix%d" % i, [a.copy() for a in xs]))
        xs = ffn(i, xs)
        if _debug is not None:
            _debug.append(("ffn%d" % i, [a.copy() for a in xs]))
    out = np.empty((B, T, D), np.float32)
    for cc in range(NCORES):
        out[cc // 4, (cc % 4) * TPC:(cc % 4 + 1) * TPC, :] = xs[cc].T
    return out
```

```python
import numpy as np
import ml_dtypes
from contextlib import ExitStack
import concourse.bass as bass
import concourse.mybir as mybir
from concourse.bass_utils import run_bass_kernel_spmd

F32 = mybir.dt.float32
BF16 = mybir.dt.bfloat16
AF = mybir.ActivationFunctionType
ALU = mybir.AluOpType
AX = mybir.AxisListType
NPBF16 = ml_dtypes.bfloat16

D = 1024
B = 2
T = 16384
DEPTH = 4
DFF = 2816
NCORES = 8
TPC = T * B // NCORES
ALPHA = (2.0 * DEPTH) ** 0.25
LN_EPS = 1e-5


class Tile:
    def __init__(self, h, name, psum=False):
        self.h = h
        self.name = name
        self.psum = psum
        self.w = None
        self.r = {}

    def __getitem__(self, idx):
        return V(self, self.h[idx])

    @property
    def ap(self):
        return self.h[:]


class V:
    def __init__(self, t, ap):
        self.t = t
        self.ap = ap

    def __getitem__(self, idx):
        return V(self.t, self.ap[idx])

    def re(self, s, **kw):
        return V(self.t, self.ap.rearrange(s, **kw))

    def bc(self, shape):
        return V(self.t, self.ap.to_broadcast(shape))


def _t(v):
    if isinstance(v, Tile):
        return v
    if isinstance(v, V):
        return v.t
    return None


def _ap(v):
    if isinstance(v, Tile):
        return v.h[:]
    if isinstance(v, V):
        return v.ap
    return v


class Em:
    NDS = 20

    def __init__(self, nc, st):
        self.nc, self.st = nc, st
        self.engs = {'pe': nc.tensor, 'dve': nc.vector, 'act': nc.scalar,
                     'pool': nc.gpsimd, 'sp': nc.sync}
        self.sems = {k: st.enter_context(nc.semaphore('sem_' + k)) for k in self.engs}
        self.cnt = {k: 0 for k in self.engs}
        self.seen = {k: {} for k in self.engs}
        self.dsem = [st.enter_context(nc.semaphore('dsem%d' % i)) for i in range(self.NDS)]
        self.dcnt = [0] * self.NDS
        self.dpool = {'sp': list(range(0, 10)), 'pool': list(range(10, 16)), 'act': list(range(16, 20))}
        self.dnext = {'sp': 0, 'pool': 0, 'act': 0}
        self.uid = 0
        self.psum_banks = None

    def sb(self, shape, dtype, name=None):
        self.uid += 1
        name = (name or 't') + '_%d' % self.uid
        h = self.st.enter_context(self.nc.sbuf_tensor(name, list(shape), dtype))
        return Tile(h, name)

    def ps(self, shape, dtype=F32, name=None):
        self.uid += 1
        name = (name or 'p') + '_%d' % self.uid
        h = self.st.enter_context(self.nc.psum_tensor(name, list(shape), dtype))
        return Tile(h, name, psum=True)

    def dram(self, name, shape, dtype, kind):
        h = self.nc.dram_tensor(name, list(shape), dtype, kind=kind)
        return Tile(h.ap(), name)

    def _sem(self, key):
        if isinstance(key, tuple):
            return self.dsem[key[1]]
        return self.sems[key]

    def _wait(self, E, dep):
        if dep is None:
            return
        key, val = dep
        if key == E and E == 'pe':
            return
        if self.seen[E].get(key, 0) >= val:
            return
        self.seen[E][key] = val
        self.engs[E].wait_ge(self._sem(key), val)

    def _deps(self, E, reads, writes):
        for v in reads:
            t = _t(v)
            if t is not None:
                self._wait(E, t.w)
                if t.psum:
                    for k, dep in list(t.r.items()):
                        if k != E:
                            self._wait(E, dep)
        for v in writes:
            t = _t(v)
            if t is not None:
                self._wait(E, t.w)
                for dep in list(t.r.values()):
                    self._wait(E, dep)

    def _mark(self, dep, reads, writes):
        for v in reads:
            t = _t(v)
            if t is not None:
                t.r[dep[0]] = dep
        for v in writes:
            t = _t(v)
            if t is not None:
                t.w = dep
                t.r = {}

    def op(self, E, fn, reads=(), writes=(), signal=True):
        self._deps(E, reads, writes)
        ins = fn(self.engs[E])
        if signal:
            self.cnt[E] += 1
            ins.then_inc(self.sems[E], 1)
            self._mark((E, self.cnt[E]), reads, writes)
        else:
            self._mark((E, self.cnt[E] + 1), reads, writes)

    def dma(self, out, in_, Q='sp', **kw):
        pool = self.dpool[Q]
        i = pool[self.dnext[Q]]
        self.dnext[Q] = (self.dnext[Q] + 1) % len(pool)
        key = ('d', i)
        if self.dcnt[i] > 0:
            self._wait(Q, (key, self.dcnt[i]))
        self._deps(Q, [in_], [out])
        ins = self.engs[Q].dma_start(out=_ap(out), in_=_ap(in_), **kw)
        self.dcnt[i] += 16
        ins.then_inc(self.dsem[i], 16)
        self._mark((key, self.dcnt[i]), [in_], [out])

    def finish(self):
        for i in range(self.NDS):
            if self.dcnt[i] > 0:
                self._wait('sp', (('d', i), self.dcnt[i]))
        for E in ('pe', 'dve', 'act', 'pool'):
            if self.cnt[E] > 0:
                self._wait('sp', (E, self.cnt[E]))

    def mm(self, out, lhsT, rhs, start=True, stop=True, sgc=False):
        kw = {'skip_group_check': True} if sgc else {}
        self.op('pe', lambda e: e.matmul(_ap(out), lhsT=_ap(lhsT), rhs=_ap(rhs), start=start, stop=stop, **kw),
                reads=[lhsT, rhs], writes=[out], signal=(stop or sgc))

    def transpose(self, out, in_, ident):
        self.op('pe', lambda e: e.transpose(_ap(out), _ap(in_), _ap(ident)),
                reads=[in_, ident], writes=[out])

    def act(self, out, in_, func, bias=None, scale=None, accum_out=None, E='act'):
        kw = {}
        rd = [in_]
        wr = [out]
        if bias is not None:
            kw['bias'] = _ap(bias)
            rd.append(bias)
        if scale is not None:
            kw['scale'] = _ap(scale)
            rd.append(scale)
        if accum_out is not None:
            kw['accum_out'] = _ap(accum_out)
            wr.append(accum_out)
        self.op('act', lambda e: e.activation(out=_ap(out), in_=_ap(in_), func=func, **kw),
                reads=rd, writes=wr)

    def tt(self, out, a, b, op, E='dve'):
        self.op(E, lambda e: e.tensor_tensor(out=_ap(out), in0=_ap(a), in1=_ap(b), op=op),
                reads=[a, b], writes=[out])

    def ts(self, out, a, s1, s2=None, op0=ALU.mult, op1=None, E='dve', accum_out=None):
        kw = {}
        wr = [out]
        if op1 is not None:
            kw['op1'] = op1
        if accum_out is not None:
            kw['accum_out'] = _ap(accum_out)
            wr.append(accum_out)
        self.op(E, lambda e: e.tensor_scalar(out=_ap(out), in0=_ap(a), scalar1=_ap(s1), scalar2=_ap(s2),
                                             op0=op0, **kw),
                reads=[a, s1, s2], writes=wr)

    def stt(self, out, a, s, b, op0, op1, E='dve'):
        E = 'dve'
        self.op(E, lambda e: e.scalar_tensor_tensor(out=_ap(out), in0=_ap(a), scalar=_ap(s), in1=_ap(b),
                                                    op0=op0, op1=op1),
                reads=[a, s, b], writes=[out])

    def copy(self, out, in_, E='dve'):
        if E == 'act':
            self.op('act', lambda e: e.copy(out=_ap(out), in_=_ap(in_)), reads=[in_], writes=[out])
        else:
            self.op(E, lambda e: e.tensor_copy(out=_ap(out), in_=_ap(in_)), reads=[in_], writes=[out])

    def memset(self, out, val, E='pool'):
        self.op(E, lambda e: e.memset(_ap(out), val), reads=[], writes=[out])

    def recip(self, out, in_):
        self.op('dve', lambda e: e.reciprocal(out=_ap(out), in_=_ap(in_)), reads=[in_], writes=[out])


class Rot:
    def __init__(self, tiles):
        self.tiles = tiles
        self.i = 0

    def next(self):
        t = self.tiles[self.i]
        self.i = (self.i + 1) % len(self.tiles)
        return t


def new_nc():
    return bass.Bass("TRN2", target_bir_lowering=False)


def run(nc, in_maps):
    import time as _time
    t0 = _time.time()
    res = run_bass_kernel_spmd(nc, in_maps, core_ids=list(range(NCORES)))
    print("[launch] %.1fs" % (_time.time() - t0), flush=True)
    return res.results


def build_mod():
    nc = new_nc()
    with ExitStack() as st:
        em = Em(nc, st)
        cT = em.dram("cT", [128, 8, 2], F32, "ExternalInput")
        w = em.dram("w", [D, 3072], F32, "ExternalInput")
        bias = em.dram("bias", [128, 24], F32, "ExternalInput")
        out = em.dram("modT", [128, 24, 2], F32, "ExternalOutput")
        ct = em.sb([128, 8, 2], F32, "ct")
        cs = em.sb([128, 8, 2], F32, "cs")
        bt = em.sb([128, 24], F32, "bt")
        ot = em.sb([128, 24, 2], F32, "ot")
        em.dma(ct, cT)
        em.dma(bt, bias)
        em.act(cs, ct, AF.Silu)
        wt = [em.sb([128, 3072], F32, "w%d" % k) for k in range(8)]
        wv = w[:].rearrange("(k p) n -> k p n", p=128) if False else None
        for k in range(8):
            em.dma(wt[k], w[k * 128:(k + 1) * 128, :])
        pp = em.ps([128, 24, 2], F32, "pp")
        for j in range(24):
            for k in range(8):
                em.mm(pp[:, j, :], wt[k][:, j * 128:(j + 1) * 128], cs[:, k, :], start=(k == 0), stop=(k == 7))
        for b in range(2):
            em.tt(ot[:, :, b], pp[:, :, b], bt, ALU.add)
        em.dma(out, ot)
        em.finish()
    return nc


def run_mod(c, ada_w, ada_b):
    nc = build_mod()
    cT = np.ascontiguousarray(c.T.reshape(8, 128, 2).transpose(1, 0, 2))
    in_maps = []
    for core in range(NCORES):
        l, half = core % 4, core // 4
        in_maps.append({
            "cT": cT,
            "w": np.ascontiguousarray(ada_w[l][:, half * 3072:(half + 1) * 3072]),
            "bias": np.ascontiguousarray(ada_b[l][half * 3072:(half + 1) * 3072].reshape(24, 128).T),
        })
    res = run(nc, in_maps)
    mod = np.zeros((DEPTH, B, 128, 48), np.float32)
    for core in range(NCORES):
        l, half = core % 4, core // 4
        m = res[core]["modT"]
        for b in range(2):
            mod[l, b, :, half * 24:(half + 1) * 24] = m[:, :, b]
    return mod


def emit_ln(em, z, N, ones_bf, g, bvec, out, ps1, ps2, tmp, gb=None):
    zb, zsq, mean, rstd, t1 = tmp
    em.copy(zb, z, E='pool')
    em.act(zsq, z, AF.Square)
    for j in range(8):
        em.mm(ps1, ones_bf, zb[:, j, :], start=(j == 0), stop=(j == 7))
    for j in range(8):
        em.mm(ps2, ones_bf, zsq[:, j, :], start=(j == 0), stop=(j == 7))
    eps = LN_EPS / (ALPHA * ALPHA)
    em.ts(mean, ps1, 1.0 / D, None, op0=ALU.mult)
    em.tt(t1, mean, mean, ALU.mult)
    em.stt(rstd, ps2, 1.0 / D, t1, ALU.mult, ALU.subtract)
    em.ts(rstd, rstd, eps, None, op0=ALU.add)
    em.act(rstd, rstd, AF.Sqrt)
    em.recip(rstd, rstd)
    mb = V(mean, mean.ap.unsqueeze(1).to_broadcast([128, 8, N]))
    rb = V(rstd, rstd.ap.unsqueeze(1).to_broadcast([128, 8, N]))
    if gb is not None:
        Gt, Bt = gb
        h = 4
        em.tt(out[:, 0:h, :], z[:, 0:h, :], mb[:, 0:h, :], ALU.subtract, E='pool')
        em.tt(out[:, h:8, :], z[:, h:8, :], mb[:, h:8, :], ALU.subtract, E='dve')
        em.tt(out[:, 0:h, :], out[:, 0:h, :], rb[:, 0:h, :], ALU.mult, E='dve')
        em.tt(out[:, h:8, :], out[:, h:8, :], rb[:, h:8, :], ALU.mult, E='pool')
        em.tt(out[:, 0:h, :], out[:, 0:h, :], Gt[:, 0:h, :], ALU.mult, E='pool')
        em.tt(out[:, h:8, :], out[:, h:8, :], Gt[:, h:8, :], ALU.mult, E='dve')
        em.tt(out[:, 0:h, :], out[:, 0:h, :], Bt[:, 0:h, :], ALU.add, E='dve')
        em.tt(out[:, h:8, :], out[:, h:8, :], Bt[:, h:8, :], ALU.add, E='pool')
    else:
        h = 4
        em.tt(out[:, 0:h, :], z[:, 0:h, :], mb[:, 0:h, :], ALU.subtract, E='pool')
        em.tt(out[:, h:8, :], z[:, h:8, :], mb[:, h:8, :], ALU.subtract, E='dve')
        em.tt(out[:, 0:h, :], out[:, 0:h, :], rb[:, 0:h, :], ALU.mult, E='dve')
        em.tt(out[:, h:8, :], out[:, h:8, :], rb[:, h:8, :], ALU.mult, E='pool')
        for j in range(8):
            em.act(out[:, j, :], out[:, j, :], AF.Identity, bias=bvec[:, j:j + 1], scale=g[:, j:j + 1])


def make_gb(em, g, bvec, N):
    Gt = em.sb([128, 8, N], F32, "Gt")
    Bt = em.sb([128, 8, N], F32, "Bt")
    em.memset(Gt, 1.0)
    em.memset(Bt, 0.0)
    for j in range(8):
        em.ts(Gt[:, j, :], Gt[:, j, :], g[:, j:j + 1], None, op0=ALU.mult, E='pool')
        em.ts(Bt[:, j, :], Bt[:, j, :], bvec[:, j:j + 1], None, op0=ALU.add, E='pool')
    return Gt, Bt


def build_ffn(ntok=TPC):
    NT = 256
    nc = new_nc()
    with ExitStack() as st:
        em = Em(nc, st)
        xT = em.dram("xT", [D, ntok], F32, "ExternalInput")
        w_in = em.dram("w_in", [D, 2 * DFF], F32, "ExternalInput")
        w_out = em.dram("w_out", [DFF, D], F32, "ExternalInput")
        scal = em.dram("scal", [128, 40], F32, "ExternalInput")
        yT = em.dram("yT", [D, ntok], F32, "ExternalOutput")

        sc = em.sb([128, 40], F32, "sc")
        em.dma(sc, scal)
        sc2p = em.sb([128, 8], F32, "sc2p")
        gs = em.sb([128, 8], F32, "gs")
        em.ts(sc2p, sc[:, 0:8], 1.0, None, op0=ALU.add)
        em.ts(gs, sc[:, 16:24], 1.0, 1.0 / ALPHA, op0=ALU.add, op1=ALU.mult)
        ones_bf = em.sb([128, 128], BF16, "ones")
        em.memset(ones_bf, 1.0)

        win = [em.sb([128, 2 * DFF], BF16, "win%d" % k) for k in range(8)]
        wout = [em.sb([128, D], BF16, "wout%d" % m) for m in range(22)]
        for k in range(8):
            em.dma(win[k], w_in[k * 128:(k + 1) * 128, :], Q='pool')
        for m in range(22):
            em.dma(wout[m], w_out[m * 128:(m + 1) * 128, :], Q='pool')

        xt_r = Rot([em.sb([128, 8, NT], F32, "xt") for _ in range(2)])
        ub_r = Rot([em.sb([128, 8, NT], BF16, "ub") for _ in range(2)])
        hT = em.sb([128, 22, NT], BF16, "hT")
        sg_r = Rot([em.sb([128, NT], F32, "sg") for _ in range(3)])
        z_r = Rot([em.sb([128, 8, NT], F32, "z") for _ in range(2)])
        zb_r = Rot([em.sb([128, 8, NT], BF16, "zb") for _ in range(2)])
        zsq_r = Rot([em.sb([128, 8, NT], BF16, "zsq") for _ in range(2)])
        mean = em.sb([128, NT], F32, "mean")
        rstd = em.sb([128, NT], F32, "rstd")
        t1 = em.sb([128, NT], F32, "t1")
        pbank = Rot([em.ps([128, 512], F32, "pb") for _ in range(6)])
        ps1 = em.ps([128, 512], F32, "ps1")
        ps2 = em.ps([128, 512], F32, "ps2")

        xv = xT[:].ap.rearrange("(c p) t -> p c t", p=128)
        yv = yT[:].ap.rearrange("(c p) t -> p c t", p=128)
        def load_tile(tt):
            xt = xt_r.next()
            ub = ub_r.next()
            em.dma(xt, V(xT, xv[:, :, tt * NT:(tt + 1) * NT]))
            for c in range(8):
                em.ts(ub[:, c, :], xt[:, c, :], sc2p[:, c:c + 1], sc[:, 8 + c:9 + c], op0=ALU.mult, op1=ALU.add,
                      E=('dve' if c % 2 == 0 else 'pool'))
            return xt, ub

        nxt_tile = load_tile(0)
        for tt in range(ntok // NT):
            tsl = slice(tt * NT, (tt + 1) * NT)
            xt, ub = nxt_tile
            z, zb, zsq = z_r.next(), zb_r.next(), zsq_r.next()
            for m in range(22):
                pg = pbank.next()
                pu = pbank.next()
                for k in range(8):
                    em.mm(pg[:, 0:NT], win[k][:, m * 128:(m + 1) * 128], ub[:, k, :], start=(k == 0), stop=(k == 7))
                for k in range(8):
                    em.mm(pu[:, 0:NT], win[k][:, DFF + m * 128:DFF + (m + 1) * 128], ub[:, k, :],
                          start=(k == 0), stop=(k == 7))
                sg = sg_r.next()
                em.act(sg, pg[:, 0:NT], AF.Silu)
                em.tt(hT[:, m, :], sg, pu[:, 0:NT], ALU.mult)
            for j in range(8):
                py = pbank.next()
                for m in range(22):
                    em.mm(py[:, 0:NT], wout[m][:, j * 128:(j + 1) * 128], hT[:, m, :], start=(m == 0), stop=(m == 21))
                em.stt(z[:, j, :], py[:, 0:NT], gs[:, j:j + 1], xt[:, j, :], ALU.mult, ALU.add)
            if tt + 1 < ntok // NT:
                nxt_tile = load_tile(tt + 1)
            emit_ln(em, z, NT, ones_bf, sc[:, 24:32], sc[:, 32:40], z, ps1[:, 0:NT], ps2[:, 0:NT],
                    (zb, zsq, mean, rstd, t1))
            em.dma(V(yT, yv[:, :, tsl]), z, Q='pool')
        em.finish()
    return nc


def chunk_masks():
    i = np.arange(128)
    same = (i[:, None] // 64) == (i[None, :] // 64)
    mS = (same & (i[:, None] < i[None, :])).astype(np.float32)
    mI = (same & (i[:, None] <= i[None, :])).astype(np.float32)
    mL = mS.T.copy()
    return mS, mI, mL


def build_rwkv_scan(Tn=T, SEG=1024, PI=2):
    NP = SEG // 128
    NCH = SEG // 64
    nc = new_nc()
    with ExitStack() as st:
        em = Em(nc, st)
        din = {n: em.dram(n, [2, 128, Tn], F32, "ExternalInput") for n in ("r", "k", "kk", "a", "lw")}
        vin = em.dram("v", [Tn, 256], BF16, "ExternalInput")
        cst = em.dram("cst", [128, 128 * 5], F32, "ExternalInput")
        yT = em.dram("yT", [2, 128, Tn], F32, "ExternalOutput")

        cf = em.sb([128, 640], F32, "cf")
        em.dma(cf, cst)
        mSI = cf[:, 0:256]
        mL = cf[:, 256:384]
        If = cf[:, 384:512]
        Ib = em.sb([128, 128], BF16, "Ib")
        em.copy(Ib, cf[:, 384:512])
        reset = em.sb([128, SEG], F32, "reset")
        for q in range(SEG // 128):
            em.copy(reset[:, q * 128:(q + 1) * 128], cf[:, 512:640], E='pool')

        STh = [em.sb([64, 64], F32, "ST%d" % h) for h in range(4)]
        for h in range(4):
            em.memset(STh[h], 0.0)
        dW_r = Rot([em.sb([64, 64], F32, "dW") for _ in range(8)])

        banks = Rot([em.ps([128, 512], F32, "bk") for _ in range(8)])

        def seg_tiles(nm, dt=F32):
            return [em.sb([128, SEG], dt, nm + "%d" % hp) for hp in range(2)]
        tin = {n: seg_tiles("in_" + n) for n in din}
        cum = seg_tiles("cum")
        tmp = seg_tiles("tmp")
        tmp2 = seg_tiles("tmp2")
        kka = seg_tiles("kka")
        at = seg_tiles("at", BF16)
        bt = seg_tiles("bt", BF16)
        kt = seg_tiles("kt", BF16)
        rt = seg_tiles("rt", BF16)
        bh = seg_tiles("bh", BF16)
        kh = seg_tiles("kh", BF16)
        WC = [em.sb([128, NCH], F32, "WC%d" % hp) for hp in range(2)]
        vt = em.sb([128, NP, 256], BF16, "vt")
        yo = [em.sb([128, SEG], F32, "yo%d" % hp) for hp in range(2)]

        def rot(nm, shape, dt, n):
            return Rot([em.sb(shape, dt, nm) for _ in range(n)])
        NI = 4 * PI
        AB_r = rot("AB", [128, 256], BF16, 2 * NI)
        AK_r = rot("AK", [128, 256], BF16, 2 * NI)
        N_r = rot("N", [128, 128], BF16, 2 * NI + 4)
        Z_r = rot("Z", [128, 128], BF16, 2 * NI + 4)
        IZ_r = rot("IZ", [128, 128], BF16, 2 * NI + 4)
        X_r = rot("X", [128, 128], BF16, 2 * NI + 4)
        BK_r = rot("BK", [128, 128], BF16, NI + 4)
        Q1_r = rot("Q1", [64, 128], F32, NI + 4)
        Q2_r = rot("Q2", [64, 128], F32, NI + 4)
        MT_r = rot("MT", [64, 128], F32, NI + 4)
        G_r = rot("G", [64, 128], F32, NI + 4)

        for seg in range(Tn // SEG):
            s0 = seg * SEG
            for hp in range(2):
                for n in din:
                    em.dma(tin[n][hp], din[n][hp, :, s0:s0 + SEG])
            em.dma(vt, V(vin, vin[s0:s0 + SEG, :].ap.rearrange("(p t) c -> t p c", t=128)))
            for hp in range(2):
                r_, k_, kk_, a_, lw_ = (tin[n][hp] for n in ("r", "k", "kk", "a", "lw"))
                c_ = cum[hp]
                em.op('dve', lambda e: e.tensor_tensor_scan(out=c_.ap, data0=reset.ap, data1=lw_.ap, initial=0.0,
                                                            op0=ALU.mult, op1=ALU.add),
                      reads=[reset, lw_], writes=[c_])
                em.act(tmp[hp], c_, AF.Exp)
                em.tt(rt[hp], r_, tmp[hp], ALU.mult, E='pool')
                em.act(tmp2[hp], c_, AF.Exp, scale=-1.0)
                em.tt(kka[hp], kk_, a_, ALU.mult, E='pool')
                em.tt(bt[hp], kka[hp], tmp2[hp], ALU.mult)
                em.tt(kt[hp], k_, tmp2[hp], ALU.mult, E='pool')
                em.tt(tmp[hp], c_, lw_, ALU.subtract)
                em.act(tmp[hp], tmp[hp], AF.Exp)
                em.stt(at[hp], kk_, -1.0, tmp[hp], ALU.mult, ALU.mult)
                c3 = c_[:].re("p (c t) -> p c t", t=64)
                cC = c3[:, :, 63:64]
                em.tt(tmp2[hp][:].re("p (c t) -> p c t", t=64), cC.bc([128, NCH, 64]), c3, ALU.subtract, E='pool')
                em.act(tmp2[hp], tmp2[hp], AF.Exp)
                em.tt(bh[hp], kka[hp], tmp2[hp], ALU.mult)
                em.tt(kh[hp], k_, tmp2[hp], ALU.mult, E='pool')
                em.act(WC[hp][:].re("p (c o) -> p c o", o=1), cC, AF.Exp)

            for p0 in range(0, NP, PI):
                items = [(p, h4 // 2, slice(64 * (h4 % 2), 64 * (h4 % 2) + 64), h4)
                         for p in range(p0, min(NP, p0 + PI)) for h4 in range(4)]
                AB, AK, Nn, Z, IZ, X = {}, {}, {}, {}, {}, {}
                Q1d, Q2d, MTd, Gd = {}, {}, {}, {}
                for p, hp, ps, h4 in items:
                    key = (p, h4)
                    tsl = slice(p * 128, (p + 1) * 128)
                    pa = banks.next()
                    em.mm(pa[:, 0:128], bt[hp][ps, tsl], at[hp][ps, tsl])
                    em.mm(pa[:, 128:256], bt[hp][ps, tsl], rt[hp][ps, tsl])
                    em.mm(pa[:, 256:384], kt[hp][ps, tsl], at[hp][ps, tsl])
                    em.mm(pa[:, 384:512], kt[hp][ps, tsl], rt[hp][ps, tsl])
                    AB[key] = AB_r.next()
                    AK[key] = AK_r.next()
                    em.tt(AB[key], pa[:, 0:256], mSI, ALU.mult)
                    em.tt(AK[key], pa[:, 256:512], mSI, ALU.mult)
                    pn = banks.next()
                    em.mm(pn[:, 0:128], at[hp][ps, tsl], bt[hp][ps, tsl])
                    em.mm(pn[:, 128:192], at[hp][ps, tsl], Ib[ps, ps])
                    pn2 = banks.next()
                    em.mm(pn2[:, 0:64], AK[key][:, 0:128], vt[:, p, h4 * 64:(h4 + 1) * 64])
                    Nn[key] = N_r.next()
                    em.tt(Nn[key], pn[:, 0:128], mL, ALU.mult)
                    Z[key] = AB[key][:, 0:128]
                    IZ[key] = IZ_r.next()
                    em.tt(IZ[key], AB[key][:, 0:128], Ib, ALU.add, E='pool')
                    X[key] = X_r.next()
                    em.copy(X[key][:, 0:64], pn[:, 128:192], E='act')
                    em.copy(X[key][:, 64:128], pn2[:, 0:64], E='act')
                for j in range(6):
                    for p, hp, ps, h4 in items:
                        key = (p, h4)
                        px = banks.next()
                        em.mm(px[:, 0:128], IZ[key], X[key])
                        if j < 5:
                            em.mm(px[:, 128:256], Nn[key], Z[key])
                            if j < 4:
                                em.mm(px[:, 256:384], Z[key], Nn[key])
                        X[key] = X_r.next()
                        em.copy(X[key], px[:, 0:128], E='act')
                        if j < 5:
                            Zn = Z_r.next()
                            IZ[key] = IZ_r.next()
                            em.tt(IZ[key], px[:, 128:256], Ib, ALU.add)
                            if j < 4:
                                em.copy(Zn, px[:, 128:256], E='act')
                                Nx = N_r.next()
                                em.copy(Nx, px[:, 256:384], E='dve')
                                Nn[key] = Nx
                                Z[key] = Zn
                for p, hp, ps, h4 in items:
                    key = (p, h4)
                    tsl = slice(p * 128, (p + 1) * 128)
                    pb = banks.next()
                    em.mm(pb[:, 0:64], bh[hp][ps, tsl], Ib[ps, ps])
                    em.mm(pb[:, 64:128], kh[hp][ps, tsl], Ib[ps, ps])
                    em.mm(pb[0:64, 128:256], X[key][:, 0:64], AB[key][:, 128:256], start=True, stop=False)
                    em.mm(pb[0:64, 128:256], Ib[ps, ps], rt[hp][ps, tsl], start=False, stop=True)
                    em.mm(pb[0:64, 256:384], X[key][:, 64:128], AB[key][:, 128:256], start=True, stop=False)
                    em.mm(pb[0:64, 256:384], vt[:, p, h4 * 64:(h4 + 1) * 64], AK[key][:, 128:256], start=False, stop=True)
                    BK = BK_r.next()
                    em.copy(BK, pb[:, 0:128], E='act')
                    Q1d[key] = Q1_r.next()
                    Q2d[key] = Q2_r.next()
                    em.copy(Q1d[key], pb[0:64, 128:256], E='dve')
                    em.copy(Q2d[key], pb[0:64, 256:384], E='act')
                    pms = [banks.next(), banks.next()]
                    for c in range(2):
                        cs = slice(64 * c, 64 * c + 64)
                        pm = pms[c]
                        em.mm(pm[0:64, 0:64], X[key][cs, 0:64], BK[cs, 0:64])
                        em.mm(pm[0:64, 64:128], BK[cs, 0:64], X[key][cs, 64:128], start=True, stop=False)
                        em.mm(pm[0:64, 64:128], BK[cs, 64:128], vt[cs, p, h4 * 64:(h4 + 1) * 64], start=False, stop=True)
                    MTd[key] = MT_r.next()
                    Gd[key] = G_r.next()
                    for c in range(2):
                        ch = p * 2 + c
                        dW = dW_r.next()
                        em.ts(dW, If[ps, ps], WC[hp][ps, ch:ch + 1], None, op0=ALU.mult, E='pool')
                        em.tt(MTd[key][:, 64 * c:64 * c + 64], pms[c][0:64, 0:64], dW, ALU.add)
                        em.copy(Gd[key][:, 64 * c:64 * c + 64], pms[c][0:64, 64:128], E='dve')
                for p in range(p0, min(NP, p0 + PI)):
                    for c in range(2):
                        for h4 in range(4):
                            key = (p, h4)
                            hp, ps = h4 // 2, slice(64 * (h4 % 2), 64 * (h4 % 2) + 64)
                            pc = banks.next()
                            em.mm(pc[0:64, 0:64], STh[h4], Q1d[key][:, 64 * c:64 * c + 64])
                            em.mm(pc[0:64, 64:128], MTd[key][:, 64 * c:64 * c + 64], STh[h4])
                            em.tt(yo[hp][ps, p * 128 + 64 * c:p * 128 + 64 * c + 64], pc[0:64, 0:64],
                                  Q2d[key][:, 64 * c:64 * c + 64], ALU.add)
                            em.tt(STh[h4], pc[0:64, 64:128], Gd[key][:, 64 * c:64 * c + 64], ALU.add)
            for hp in range(2):
                em.dma(yT[hp, :, s0:s0 + SEG], yo[hp])
        em.finish()
    return nc


def build_rwkv_pre(ntok=TPC):
    NT = 256
    nc = new_nc()
    with ExitStack() as st:
        em = Em(nc, st)
        xT = em.dram("xT", [D, ntok + 1], F32, "ExternalInput")
        scal = em.dram("scal", [128, 105], F32, "ExternalInput")
        cst = em.dram("cst", [128, 128], F32, "ExternalInput")
        w_rkv = em.dram("w_rkv", [3, D, D], F32, "ExternalInput")
        w1 = em.dram("w1", [D, 64], F32, "ExternalInput")
        a1 = em.dram("a1", [D, 64], F32, "ExternalInput")
        g1 = em.dram("g1", [D, 128], F32, "ExternalInput")
        w2 = em.dram("w2", [64, D], F32, "ExternalInput")
        a2 = em.dram("a2", [64, D], F32, "ExternalInput")
        g2 = em.dram("g2", [128, D], F32, "ExternalInput")
        outs = {n: em.dram(n, [D, ntok], F32, "ExternalOutput") for n in ("r", "k", "kk", "a", "lw")}
        outs["g"] = em.dram("g", [D, ntok], BF16, "ExternalOutput")
        outs["bonus"] = em.dram("bonus", [D, ntok], BF16, "ExternalOutput")
        vtok = em.dram("vtok", [ntok, D], BF16, "ExternalOutput")

        sc = em.sb([128, 105], F32, "sc")
        em.dma(sc, scal)
        col = lambda i: sc[:, 8 * i:8 * i + 8]
        sc1p = em.sb([128, 8], F32, "sc1p")
        em.ts(sc1p, col(0), 1.0, None, op0=ALU.add)
        sh1, w0, a0, k_k, k_a, r_k = col(1), col(8), col(9), col(10), col(11), col(12)
        hv = sc[:, 104:105]
        bd_f = em.sb([128, 128], F32, "bd_f")
        em.dma(bd_f, cst)
        bd = em.sb([128, 128], BF16, "bd")
        em.copy(bd, bd_f)

        W = [[em.sb([128, D], BF16, "W%d_%d" % (i, k)) for k in range(8)] for i in range(3)]
        for i in range(3):
            for k in range(8):
                em.dma(W[i][k], w_rkv[i, k * 128:(k + 1) * 128, :], Q='pool')
        w1s = em.sb([128, 8, 64], BF16, "w1s")
        a1s = em.sb([128, 8, 64], BF16, "a1s")
        g1s = em.sb([128, 8, 128], BF16, "g1s")
        em.dma(w1s, V(w1, w1[:].ap.rearrange("(k p) n -> p k n", p=128)), Q='pool')
        em.dma(a1s, V(a1, a1[:].ap.rearrange("(k p) n -> p k n", p=128)), Q='pool')
        em.dma(g1s, V(g1, g1[:].ap.rearrange("(k p) n -> p k n", p=128)), Q='pool')
        w2s = em.sb([64, D], BF16, "w2s")
        a2s = em.sb([64, D], BF16, "a2s")
        g2s = em.sb([128, D], BF16, "g2s")
        em.dma(w2s, w2, Q='pool')
        em.dma(a2s, a2, Q='pool')
        em.dma(g2s, g2, Q='pool')

        xt_r = Rot([em.sb([128, 8, NT + 1], F32, "xt") for _ in range(2)])
        u = em.sb([128, 8, NT + 1], F32, "u")
        xx = em.sb([128, 8, NT], F32, "xx")
        lerp = [em.sb([128, 8, NT], BF16, "lerp%d" % i) for i in range(6)]
        hw = em.sb([64, NT], BF16, "hw")
        ha = em.sb([64, NT], BF16, "ha")
        hg = em.sb([128, NT], BF16, "hg")
        banks = Rot([em.ps([128, 512], F32, "bk") for _ in range(8)])

        def rot(nm, dt, n=2):
            return Rot([em.sb([128, NT], dt, nm) for _ in range(n)])
        r_r, k_r, v_r, kp_r, kk_r, a_r, lw_r = (rot(n, F32, 3) for n in ("r", "k", "v", "kp", "kk", "a", "lw"))
        g_r, bo_r, sq_r, t_r = rot("g", BF16, 3), rot("bo", BF16, 3), rot("sq", BF16, 3), rot("t", F32, 8)
        tb_r = rot("tb", BF16, 3)
        vt_r = Rot([em.sb([128, D], BF16, "vt") for _ in range(2)])

        xv = xT[:].ap.rearrange("(c p) t -> p c t", p=128)
        def load_x(tt_):
            xt_ = xt_r.next()
            em.dma(xt_, V(xT, xv[:, :, tt_ * NT:tt_ * NT + NT + 1]))
            return xt_

        xt_next = load_x(0)
        for tt in range(ntok // NT):
            t0 = tt * NT
            xt = xt_next
            for c in range(8):
                em.ts(u[:, c, :], xt[:, c, :], sc1p[:, c:c + 1], sh1[:, c:c + 1], op0=ALU.mult, op1=ALU.add,
                      E=('dve' if c % 2 == 0 else 'pool'))
            if tt + 1 < ntok // NT:
                xt_next = load_x(tt + 1)
            if tt == 0:
                em.ts(u[:, :, 0:1], u[:, :, 0:1], hv, None, op0=ALU.mult)
            for c in range(8):
                em.tt(xx[:, c, :], u[:, c, 0:NT], u[:, c, 1:NT + 1], ALU.subtract, E=('pool' if c % 2 == 0 else 'dve'))
            for i in range(6):
                for c in range(8):
                    em.stt(lerp[i][:, c, :], xx[:, c, :], sc[:, 8 * (2 + i) + c:8 * (2 + i) + c + 1], u[:, c, 1:NT + 1],
                           ALU.mult, ALU.add, E=('dve' if (c + i) % 2 == 0 else 'pool'))
            p1 = banks.next()
            for k in range(8):
                em.mm(p1[0:64, 0:NT], w1s[:, k, :], lerp[3][:, k, :], start=(k == 0), stop=(k == 7))
            em.act(hw, p1[0:64, 0:NT], AF.Tanh)
            p2 = banks.next()
            for k in range(8):
                em.mm(p2[0:64, 0:NT], a1s[:, k, :], lerp[4][:, k, :], start=(k == 0), stop=(k == 7))
            em.copy(ha, p2[0:64, 0:NT], E='dve')
            p3 = banks.next()
            for k in range(8):
                em.mm(p3[:, 0:NT], g1s[:, k, :], lerp[5][:, k, :], start=(k == 0), stop=(k == 7))
            em.act(hg, p3[:, 0:NT], AF.Sigmoid)
            for tb in range(NT // 128):
                vt = vt_r.next()
                for half in range(2):
                    pv = banks.next()
                    for k in range(8):
                        em.mm(pv[:, 0:512], lerp[2][:, k, tb * 128:(tb + 1) * 128], W[2][k][:, half * 512:(half + 1) * 512],
                              start=(k == 0), stop=(k == 7))
                    em.copy(vt[:, half * 512:(half + 1) * 512], pv[:, 0:512], E=('act' if half == 0 else 'dve'))
                em.dma(vtok[t0 + tb * 128:t0 + (tb + 1) * 128, :], vt)
            def stage1(j):
                js = slice(j * 128, (j + 1) * 128)
                jc = slice(j, j + 1)
                pr, pk, pv = banks.next(), banks.next(), banks.next()
                for (pp, i) in ((pr, 0), (pk, 1), (pv, 2)):
                    for k in range(8):
                        em.mm(pp[:, 0:NT], W[i][k][:, js], lerp[i][:, k, :], start=(k == 0), stop=(k == 7))
                r_, k_, v_ = r_r.next(), k_r.next(), v_r.next()
                em.copy(r_, pr[:, 0:NT], E='act')
                em.copy(k_, pk[:, 0:NT], E='dve')
                em.copy(v_, pv[:, 0:NT], E='act')
                pl = banks.next()
                em.mm(pl[:, 0:NT], w2s[:, js], hw)
                lw_ = lw_r.next()
                em.act(lw_, pl[:, 0:NT], AF.Sigmoid, bias=w0[:, jc])
                em.ts(lw_, lw_, -0.6065306597126334, None, op0=ALU.mult, E='pool')
                pa = banks.next()
                em.mm(pa[:, 0:NT], a2s[:, js], ha)
                a_ = a_r.next()
                em.act(a_, pa[:, 0:NT], AF.Sigmoid, bias=a0[:, jc])
                pg = banks.next()
                em.mm(pg[:, 0:NT], g2s[:, js], hg)
                g_ = g_r.next()
                em.copy(g_, pg[:, 0:NT], E='dve')
                kkr = t_r.next()
                em.ts(kkr, k_, k_k[:, jc], None, op0=ALU.mult, E='pool')
                sq = sq_r.next()
                em.act(sq, kkr, AF.Square)
                tk = t_r.next()
                em.ts(tk, a_, -1.0, k_a[:, jc], op0=ALU.add, op1=ALU.mult, E='pool')
                kp = kp_r.next()
                em.stt(kp, tk, 1.0, k_, ALU.add, ALU.mult)
                tb_ = tb_r.next()
                em.stt(tb_, r_, r_k[:, jc], kp, ALU.mult, ALU.mult)
                return dict(js=js, r_=r_, v_=v_, lw_=lw_, a_=a_, g_=g_, kkr=kkr, sq=sq, kp=kp, tb_=tb_)

            def stage2(d):
                ps_ = banks.next()
                em.mm(ps_[:, 0:NT], bd, d["sq"])
                rn = t_r.next()
                em.ts(rn, ps_[:, 0:NT], 1e-6, None, op0=ALU.add)
                em.act(rn, rn, AF.Sqrt)
                em.recip(rn, rn)
                kk_ = kk_r.next()
                em.tt(kk_, d["kkr"], rn, ALU.mult, E='pool')
                pb = banks.next()
                em.mm(pb[:, 0:NT], bd, d["tb_"])
                bo = bo_r.next()
                em.tt(bo, pb[:, 0:NT], d["v_"], ALU.mult)
                for (nm, tl) in (("r", d["r_"]), ("k", d["kp"]), ("kk", kk_), ("a", d["a_"]), ("lw", d["lw_"]),
                                 ("g", d["g_"]), ("bonus", bo)):
                    em.dma(outs[nm][d["js"], t0:t0 + NT], tl)

            prev = stage1(0)
            for j in range(8):
                nxt = stage1(j + 1) if j + 1 < 8 else None
                stage2(prev)
                prev = nxt
        em.finish()
    return nc


def build_post(mode, ntok=TPC, hscale=1.0, has_bias=True):
    NT = 256
    nc = new_nc()
    with ExitStack() as st:
        em = Em(nc, st)
        xT = em.dram("xT", [D, ntok], F32, "ExternalInput")
        w_o = em.dram("w_o", [D, D], F32, "ExternalInput")
        scal = em.dram("scal", [128, 56], F32, "ExternalInput")
        if mode == 'rwkv':
            yin = em.dram("yin", [D, ntok], F32, "ExternalInput")
            gin = em.dram("g", [D, ntok], BF16, "ExternalInput")
            bin_ = em.dram("bonus", [D, ntok], BF16, "ExternalInput")
            cst = em.dram("cst", [128, 128], F32, "ExternalInput")
        elif mode in ('gdn', 'diff'):
            yin = em.dram("yin", [D, ntok], F32, "ExternalInput")
            if mode == 'gdn':
                zin = em.dram("zs", [D, ntok], BF16, "ExternalInput")
        else:
            oin = em.dram("oT", [D, ntok], BF16, "ExternalInput")
        yT = em.dram("yT", [D, ntok], F32, "ExternalOutput")

        sc = em.sb([128, 56], F32, "sc")
        em.dma(sc, scal)
        col = lambda i: sc[:, 8 * i:8 * i + 8]
        gs = em.sb([128, 8], F32, "gs")
        em.ts(gs, col(0), 1.0, 1.0 / ALPHA, op0=ALU.add, op1=ALU.mult)
        gsb = em.sb([128, 8], F32, "gsb")
        em.tt(gsb, gs, col(3), ALU.mult)
        if hscale != 1.0:
            em.ts(sc[:, 32:33], sc[:, 32:33], float(hscale), None, op0=ALU.mult)
        ones_bf = em.sb([128, 128], BF16, "ones")
        em.memset(ones_bf, 1.0)
        if mode == 'rwkv':
            bd_f = em.sb([128, 128], F32, "bd_f")
            em.dma(bd_f, cst)
            bd = em.sb([128, 128], BF16, "bd")
            em.copy(bd, bd_f)
        Wo = [em.sb([128, D], BF16, "Wo%d" % k) for k in range(8)]
        for k in range(8):
            em.dma(Wo[k], w_o[k * 128:(k + 1) * 128, :], Q='pool')

        xt_r = Rot([em.sb([128, 8, NT], F32, "xt") for _ in range(2)])
        ot_r = Rot([em.sb([128, 8, NT], BF16, "ot") for _ in range(2)])
        z_r = Rot([em.sb([128, 8, NT], F32, "z") for _ in range(2)])
        zb_r = Rot([em.sb([128, 8, NT], BF16, "zb") for _ in range(2)])
        zsq_r = Rot([em.sb([128, 8, NT], BF16, "zsq") for _ in range(2)])
        mean = em.sb([128, NT], F32, "mean")
        rstd = em.sb([128, NT], F32, "rstd")
        t1 = em.sb([128, NT], F32, "t1")
        pbank = Rot([em.ps([128, 512], F32, "pb") for _ in range(2)])
        ps1 = em.ps([128, 512], F32, "ps1")
        ps2 = em.ps([128, 512], F32, "ps2")
        if mode != 'plain':
            pm2 = em.ps([128, 4, NT], F32, "pm2")
            pq2 = em.ps([128, 4, NT], F32, "pq2")
            hb = lambda nm, dt=F32: Rot([em.sb([128, 4, NT], dt, nm) for _ in range(2)])
            gm_h, gr_h, g1_h = hb("gm_h"), hb("gr_h"), hb("g1_h")
            ybig = em.sb([128, 8, NT], BF16, "ybig")
            ysqb = em.sb([128, 8, NT], BF16, "ysqb")
        if mode in ('gdn', 'diff'):
            yt_r = Rot([em.sb([128, 8, NT], F32, "yt") for _ in range(2)])
            zt_r = Rot([em.sb([128, 8, NT], BF16, "zt") for _ in range(2)])
            ysq_r = Rot([em.sb([128, NT], BF16, "ysq") for _ in range(2)])
            gr_r = Rot([em.sb([128, NT], F32, "gr") for _ in range(2)])
            gt1_r = Rot([em.sb([128, NT], F32, "gt1") for _ in range(2)])
        if mode == 'rwkv':
            yt_r = Rot([em.sb([128, 8, NT], F32, "yt") for _ in range(2)])
            gt_r = Rot([em.sb([128, 8, NT], BF16, "gt") for _ in range(2)])
            bt_r = Rot([em.sb([128, 8, NT], BF16, "bt") for _ in range(2)])
            yb_r = Rot([em.sb([128, NT], BF16, "yb") for _ in range(2)])
            ysq_r = Rot([em.sb([128, NT], BF16, "ysq") for _ in range(2)])
            gm_r = Rot([em.sb([128, NT], F32, "gm") for _ in range(2)])
            gr_r = Rot([em.sb([128, NT], F32, "gr") for _ in range(2)])
            gt1_r = Rot([em.sb([128, NT], F32, "gt1") for _ in range(2)])

        fm = lambda tl: tl[:].ap.rearrange("(c p) t -> p c t", p=128)
        xv, yv = fm(xT), fm(yT)
        for tt in range(ntok // NT):
            tsl = slice(tt * NT, (tt + 1) * NT)
            xt = xt_r.next()
            ot = ot_r.next()
            z, zb, zsq = z_r.next(), zb_r.next(), zsq_r.next()
            em.dma(xt, V(xT, xv[:, :, tsl]))
            if mode == 'rwkv':
                yt, gt, bt = yt_r.next(), gt_r.next(), bt_r.next()
                em.dma(yt, V(yin, fm(yin)[:, :, tsl]))
                em.dma(gt, V(gin, fm(gin)[:, :, tsl]))
                em.dma(bt, V(bin_, fm(bin_)[:, :, tsl]))
                em.copy(ybig, yt, E='pool')
                em.act(ysqb, yt, AF.Square)
                for hf in range(2):
                    hs_ = slice(4 * hf, 4 * hf + 4)
                    for j in range(4):
                        em.mm(pm2[:, j, :], bd, ybig[:, 4 * hf + j, :])
                    for j in range(4):
                        em.mm(pq2[:, j, :], bd, ysqb[:, 4 * hf + j, :])
                    gm, gr, g1 = gm_h.next(), gr_h.next(), g1_h.next()
                    em.ts(gm, pm2, 1.0 / 64, None, op0=ALU.mult)
                    em.tt(g1, gm, gm, ALU.mult, E='pool')
                    em.stt(gr, pq2, 1.0 / 64, g1, ALU.mult, ALU.subtract)
                    em.ts(gr, gr, 64e-5, None, op0=ALU.add, E='pool')
                    em.act(gr, gr, AF.Sqrt)
                    em.recip(gr, gr)
                    em.tt(g1, yt[:, hs_, :], gm, ALU.subtract, E='pool')
                    em.tt(g1, g1, gr, ALU.mult)
                    for j in range(4):
                        jj = 4 * hf + j
                        em.act(g1[:, j, :], g1[:, j, :], AF.Identity, bias=sc[:, 40 + jj:41 + jj], scale=sc[:, 32 + jj:33 + jj])
                    em.tt(g1, g1, bt[:, hs_, :], ALU.add)
                    em.tt(ot[:, hs_, :], g1, gt[:, hs_, :], ALU.mult, E='pool')
            elif mode in ('gdn', 'diff'):
                yt = yt_r.next()
                em.dma(yt, V(yin, fm(yin)[:, :, tsl]))
                if mode == 'gdn':
                    zt = zt_r.next()
                    em.dma(zt, V(zin, fm(zin)[:, :, tsl]))
                heps = 1e-6 if mode == 'gdn' else 1e-5
                em.act(ysqb, yt, AF.Square)
                for hf in range(2):
                    hs_ = slice(4 * hf, 4 * hf + 4)
                    for j in range(4):
                        em.mm(pq2[:, j, :], ones_bf, ysqb[:, 4 * hf + j, :])
                    gr, g1 = gr_h.next(), g1_h.next()
                    em.ts(gr, pq2, 1.0 / 128, heps, op0=ALU.mult, op1=ALU.add)
                    em.act(gr, gr, AF.Sqrt)
                    em.recip(gr, gr)
                    if mode == 'gdn':
                        em.stt(g1, yt[:, hs_, :], sc[:, 32:33], gr, ALU.mult, ALU.mult)
                        em.tt(ot[:, hs_, :], g1, zt[:, hs_, :], ALU.mult, E='pool')
                    else:
                        em.stt(ot[:, hs_, :], yt[:, hs_, :], sc[:, 32:33], gr, ALU.mult, ALU.mult)
            else:
                em.dma(ot, V(oin, fm(oin)[:, :, tsl]))
            for j in range(8):
                py = pbank.next()
                for k in range(8):
                    em.mm(py[:, 0:NT], Wo[k][:, j * 128:(j + 1) * 128], ot[:, k, :], start=(k == 0), stop=(k == 7))
                em.stt(z[:, j, :], py[:, 0:NT], gs[:, j:j + 1], xt[:, j, :], ALU.mult, ALU.add)
                if has_bias:
                    em.ts(z[:, j, :], z[:, j, :], gsb[:, j:j + 1], None, op0=ALU.add, E='pool')
            emit_ln(em, z, NT, ones_bf, col(1), col(2), z, ps1[:, 0:NT], ps2[:, 0:NT], (zb, zsq, mean, rstd, t1))
            em.dma(V(yT, yv[:, :, tsl]), z, Q='pool')
        em.finish()
    return nc


def gdn_level_masks():
    i = np.arange(128)
    out = []
    for li in range(7):
        m = 1 << li
        out.append((((i[:, None] // (2 * m)) == (i[None, :] // (2 * m))) &
                    ((i[:, None] // m) < (i[None, :] // m))).astype(np.float32))
    return out


def build_gdn_scan(Tn=T, SEG=1024, PI=4):
    NCK = SEG // 128
    nc = new_nc()
    with ExitStack() as st:
        em = Em(nc, st)
        qin = em.dram("q", [2, 128, Tn], BF16, "ExternalInput")
        kin = em.dram("k", [2, 128, Tn], BF16, "ExternalInput")
        vin = em.dram("v", [2, 128, Tn], BF16, "ExternalInput")
        bin_ = em.dram("beta", [2, Tn], F32, "ExternalInput")
        gin = em.dram("g", [2, Tn], F32, "ExternalInput")
        cst = em.dram("cst", [128, 512 + 7 * 128], F32, "ExternalInput")
        oT = em.dram("oT", [2, 128, Tn], F32, "ExternalOutput")

        cf = em.sb([128, 512 + 7 * 128], F32, "cf")
        em.dma(cf, cst)
        lvl = [cf[:, 512 + 128 * i:512 + 128 * (i + 1)] for i in range(7)]
        mnegI = cf[:, 0:128]
        nmS = cf[:, 128:256]
        If = cf[:, 256:384]
        Ib = em.sb([128, 128], BF16, "Ib")
        em.copy(Ib, cf[:, 256:384])
        e0 = em.sb([128, 2], F32, "e0")
        em.copy(e0[:, 0:1], cf[:, 256:257])
        em.copy(e0[:, 1:2], cf[:, 256:257])
        reset = em.sb([128, SEG], F32, "reset")
        for c in range(NCK):
            em.copy(reset[:, c * 128:(c + 1) * 128], cf[:, 384:512], E='pool')
        S = [em.sb([128, 128], F32, "S%d" % h) for h in range(2)]
        for h in range(2):
            em.memset(S[h], 0.0)
        banks = Rot([em.ps([128, 512], F32, "bk") for _ in range(8)])

        qt = [em.sb([128, SEG], BF16, "qt%d" % h) for h in range(2)]
        kt = [em.sb([128, SEG], BF16, "kt%d" % h) for h in range(2)]
        vt = [em.sb([128, SEG], BF16, "vt%d" % h) for h in range(2)]
        bb = [em.sb([128, SEG], F32, "bb%d" % h) for h in range(2)]
        gb = [em.sb([128, SEG], F32, "gb%d" % h) for h in range(2)]
        gc = [em.sb([128, SEG], F32, "gc%d" % h) for h in range(2)]
        eg = [em.sb([128, SEG], F32, "eg%d" % h) for h in range(2)]
        oo = [em.sb([128, SEG], F32, "oo%d" % h) for h in range(2)]

        def rot(nm, shape, dt, n):
            return Rot([em.sb(shape, dt, nm) for _ in range(n)])
        NI = 2 * PI
        col_r = rot("col", [128, 8], F32, NI + 2)
        DT_r = rot("DT", [128, 128], F32, 4)
        nb_r = rot("nb", [128, 128], F32, 4)
        t_r = rot("tt", [128, 128], F32, 4)
        Aqk_r = rot("Aqk", [128, 128], BF16, NI + 2)
        Z_r = rot("Z", [128, 128], F32, NI + 2)
        Zl_r = rot("Zl", [128, 128], F32, NI + 2)
        W_r = rot("W", [128, 128], F32, NI + 2)
        T_r = rot("T", [128, 128], F32, 2 * NI + 2)
        U_r = rot("U", [128, 128], F32, 2 * NI + 2)
        X0_r = rot("X0", [128, 256], F32, NI + 2)
        X_r = rot("X", [128, 256], BF16, NI + 2)
        kd_r = rot("kd", [128, 128], BF16, NI + 2)
        MT_r = rot("MT", [128, 128], F32, NI + 2)
        G_r = rot("G", [128, 128], F32, NI + 2)
        Q1_r = rot("Q1", [128, 128], F32, NI + 2)
        Q2_r = rot("Q2", [128, 128], F32, NI + 2)
        gI_r = rot("gI", [128, 128], F32, 4)

        for seg in range(Tn // SEG):
            s0 = seg * SEG
            for h in range(2):
                em.dma(qt[h], qin[h, :, s0:s0 + SEG])
                em.dma(kt[h], kin[h, :, s0:s0 + SEG])
                em.dma(vt[h], vin[h, :, s0:s0 + SEG])
                em.dma(bb[h], V(bin_, bin_[h:h + 1, s0:s0 + SEG].ap.partition_broadcast(128)))
                em.dma(gb[h], V(gin, gin[h:h + 1, s0:s0 + SEG].ap.partition_broadcast(128)))
                g_, c_ = gb[h], gc[h]
                em.op('dve', lambda e, c_=c_, g_=g_: e.tensor_tensor_scan(out=c_.ap, data0=reset.ap, data1=g_.ap,
                                                                          initial=0.0, op0=ALU.mult, op1=ALU.add),
                      reads=[reset, g_], writes=[c_])
                em.act(eg[h], c_, AF.Exp)
            for ck0 in range(0, NCK, PI):
                items = [(ck, h) for ck in range(ck0, min(NCK, ck0 + PI)) for h in range(2)]
                Z, X, cols, Aqk, kd = {}, {}, {}, {}, {}
                MTd, Gd, Q1d, Q2d = {}, {}, {}, {}
                for ck, h in items:
                    key = (ck, h)
                    tsl = slice(ck * 128, (ck + 1) * 128)
                    pcol = banks.next()
                    em.mm(pcol[:, 0:1], gc[h][:, tsl], e0[:, 0:1])
                    em.mm(pcol[:, 1:2], bb[h][:, tsl], e0[:, 0:1])
                    cl = col_r.next()
                    cols[key] = cl
                    em.copy(cl[:, 0:2], pcol[:, 0:2], E='dve')
                    em.ts(cl[:, 2:3], cl[:, 0:1], -1.0, None, op0=ALU.mult)
                    em.act(cl[:, 3:4], cl[:, 0:1], AF.Exp)
                    em.tt(cl[:, 4:5], cl[:, 3:4], cl[:, 1:2], ALU.mult)
                    em.act(cl[:, 5:6], cl[:, 2:3], AF.Exp, bias=gc[h][:, ck * 128 + 127:ck * 128 + 128])
                    em.copy(cl[:, 6:7], eg[h][:, ck * 128 + 127:ck * 128 + 128], E='pool')
                    DT = DT_r.next()
                    em.tt(DT, gc[h][:, tsl], mnegI, ALU.add, E='pool')
                    em.act(DT, DT, AF.Exp, bias=cl[:, 2:3])
                    nb = nb_r.next()
                    em.tt(nb, bb[h][:, tsl], nmS, ALU.mult, E='pool')
                    pk = banks.next()
                    em.mm(pk[:, 0:128], kt[h][:, tsl], kt[h][:, tsl])
                    em.mm(pk[:, 128:256], kt[h][:, tsl], qt[h][:, tsl])
                    em.mm(pk[:, 256:384], kt[h][:, tsl], Ib)
                    em.mm(pk[:, 384:512], vt[h][:, tsl], Ib)
                    t1 = t_r.next()
                    em.tt(t1, pk[:, 0:128], DT, ALU.mult)
                    Z[key] = Z_r.next()
                    em.tt(Z[key], t1, nb, ALU.mult, E='pool')
                    Aqk[key] = Aqk_r.next()
                    em.tt(Aqk[key], pk[:, 128:256], DT, ALU.mult)
                    X0 = X0_r.next()
                    X[key] = X0
                    em.ts(X0[:, 0:128], pk[:, 384:512], cl[:, 1:2], None, op0=ALU.mult)
                    em.ts(X0[:, 128:256], pk[:, 256:384], cl[:, 4:5], None, op0=ALU.mult)
                    kd[key] = kd_r.next()
                    em.act(kd[key], pk[:, 256:384], AF.Copy, scale=cl[:, 5:6])
                Tm, Um = {}, {}
                for ck, h in items:
                    key = (ck, h)
                    Um[key] = U_r.next()
                    em.tt(Um[key], Z[key], lvl[0], ALU.mult, E='pool')
                    em.tt(Um[key], Um[key], If, ALU.add, E='pool')
                    pt = banks.next()
                    em.mm(pt[:, 0:128], Um[key], If)
                    Tm[key] = T_r.next()
                    em.copy(Tm[key], pt[:, 0:128], E='act')
                for li in range(1, 7):
                    pws, Wts = {}, {}
                    for ck, h in items:
                        key = (ck, h)
                        Zl = Zl_r.next()
                        em.tt(Zl, Z[key], lvl[li], ALU.mult, E='pool')
                        pw = banks.next()
                        em.mm(pw[:, 0:128], Zl, Tm[key])
                        Wt = W_r.next()
                        em.copy(Wt, pw[:, 0:128], E='act')
                        pws[key], Wts[key] = pw, Wt
                    for ck, h in items:
                        key = (ck, h)
                        pw, Wt = pws[key], Wts[key]
                        if li < 6:
                            em.mm(pw[:, 128:256], Um[key], Wt)
                        em.mm(pw[:, 256:384], Wt, Um[key])
                        Un = U_r.next()
                        em.tt(Un, pw[:, 256:384], Um[key], ALU.add)
                        if li < 6:
                            Tn_ = T_r.next()
                            em.tt(Tn_, pw[:, 128:256], Tm[key], ALU.add)
                            Tm[key] = Tn_
                        Um[key] = Un
                Xbs = {}
                for ck, h in items:
                    key = (ck, h)
                    px = banks.next()
                    em.mm(px[:, 0:256], Um[key], X[key])
                    Xbs[key] = X_r.next()
                    em.copy(Xbs[key], px[:, 0:256], E='act')
                for ck, h in items:
                    key = (ck, h)
                    tsl = slice(ck * 128, (ck + 1) * 128)
                    Xb = Xbs[key]
                    cl = cols[key]
                    uc, wc = Xb[:, 0:128], Xb[:, 128:256]
                    pm = banks.next()
                    em.mm(pm[:, 0:128], wc, kd[key])
                    em.mm(pm[:, 128:256], kd[key], uc)
                    em.mm(pm[:, 256:384], wc, Aqk[key])
                    em.mm(pm[:, 384:512], uc, Aqk[key])
                    gI = gI_r.next()
                    em.ts(gI, If, cl[:, 6:7], None, op0=ALU.mult, E='pool')
                    MTd[key], Gd[key], Q1d[key], Q2d[key] = MT_r.next(), G_r.next(), Q1_r.next(), Q2_r.next()
                    em.stt(MTd[key], pm[:, 0:128], -1.0, gI, ALU.mult, ALU.add)
                    em.copy(Gd[key], pm[:, 128:256], E='act')
                    qd = t_r.next()
                    em.tt(qd, qt[h][:, tsl], eg[h][:, tsl], ALU.mult, E='pool')
                    em.stt(Q1d[key], pm[:, 256:384], -1.0, qd, ALU.mult, ALU.add)
                    em.copy(Q2d[key], pm[:, 384:512], E='act')
                for ck, h in items:
                    key = (ck, h)
                    tsl = slice(ck * 128, (ck + 1) * 128)
                    pc = banks.next()
                    em.mm(pc[:, 0:128], S[h], Q1d[key])
                    em.mm(pc[:, 128:256], MTd[key], S[h])
                    em.tt(oo[h][:, tsl], pc[:, 0:128], Q2d[key], ALU.add)
                    em.tt(S[h], pc[:, 128:256], Gd[key], ALU.add)
            for h in range(2):
                em.dma(oT[h, :, s0:s0 + SEG], oo[h])
        em.finish()
    return nc


def build_gdn_pre(ntok=TPC):
    NT = 256
    NH = NT + 3
    nc = new_nc()
    with ExitStack() as st:
        em = Em(nc, st)
        xT = em.dram("xT", [D, ntok + 3], F32, "ExternalInput")
        scal = em.dram("scal", [128, 113], F32, "ExternalInput")
        hsc = em.dram("hsc", [8, 2], F32, "ExternalInput")
        ident = em.dram("ident", [128, 128], F32, "ExternalInput")
        w_in = em.dram("w_in", [D, 4112], F32, "ExternalInput")
        outs = {n: em.dram(n, [D, ntok], BF16, "ExternalOutput") for n in ("q", "k", "v", "zs")}
        beta_o = em.dram("beta", [8, ntok], F32, "ExternalOutput")
        g_o = em.dram("g", [8, ntok], F32, "ExternalOutput")

        sc = em.sb([128, 113], F32, "sc")
        em.dma(sc, scal)
        sc1p = em.sb([128, 8], F32, "sc1p")
        em.ts(sc1p, sc[:, 0:8], 1.0, None, op0=ALU.add)
        hv = sc[:, 112:113]
        hs = em.sb([8, 2], F32, "hs")
        em.dma(hs, hsc)
        nea = em.sb([8, 1], F32, "nea")
        em.act(nea, hs[:, 0:1], AF.Exp)
        em.ts(nea, nea, -1.0, None, op0=ALU.mult)
        ones_bf = em.sb([128, 128], BF16, "ones")
        em.memset(ones_bf, 1.0)
        idf = em.sb([128, 128], F32, "idf")
        em.dma(idf, ident)
        dg = em.sb([128, 96, 128], F32, "dg")
        for jk in range(96):
            em.ts(dg[:, jk, :], idf, sc[:, 16 + jk:17 + jk], None, op0=ALU.mult, E=('dve' if jk % 2 == 0 else 'pool'))
        W = [em.sb([128, 4112], BF16, "W%d" % k) for k in range(8)]
        for k in range(8):
            em.dma(W[k], w_in[k * 128:(k + 1) * 128, :], Q='pool')

        xt_r = Rot([em.sb([128, 8, NH], F32, "xt") for _ in range(2)])
        ub = em.sb([128, 8, NH], BF16, "ub")
        banks = Rot([em.ps([128, 512], F32, "bk") for _ in range(8)])

        def rot(nm, n_, dt, n=2):
            return Rot([em.sb([128, n_], dt, nm) for _ in range(n)])
        pp_r, cv_r, s_r, sq_r, rn_r = rot("pp", NH, F32, 4), rot("cv", NT, F32, 3), rot("s", NT, F32, 3), rot("sq", NT, BF16, 3), rot("rn", NT, F32)
        ob_r = rot("ob", NT, BF16, 4)
        bt_r = Rot([em.sb([8, NT], F32, "bt") for _ in range(2)])
        gt_r = Rot([em.sb([8, NT], F32, "gt") for _ in range(2)])

        xv = xT[:].ap.rearrange("(c p) t -> p c t", p=128)
        def load_x(tt_):
            xt_ = xt_r.next()
            em.dma(xt_, V(xT, xv[:, :, tt_ * NT:tt_ * NT + NH]))
            return xt_

        xt_next = load_x(0)
        for tt in range(ntok // NT):
            t0 = tt * NT
            xt = xt_next
            for c in range(8):
                em.ts(ub[:, c, :], xt[:, c, :], sc1p[:, c:c + 1], sc[:, 8 + c:9 + c], op0=ALU.mult, op1=ALU.add,
                      E=('dve' if c % 2 == 0 else 'pool'))
            if tt == 0:
                em.ts(ub[:, :, 0:3], ub[:, :, 0:3], hv, None, op0=ALU.mult)
            if tt + 1 < ntok // NT:
                xt_next = load_x(tt + 1)
            def stage1a(j):
                pp = banks.next()
                for k in range(8):
                    em.mm(pp[:, 0:NH], W[k][:, j * 128:(j + 1) * 128], ub[:, k, :], start=(k == 0), stop=(k == 7))
                if j >= 24:
                    ob = ob_r.next()
                    em.act(ob, pp[:, 3:NT + 3], AF.Silu)
                    em.dma(outs["zs"][(j - 24) * 128:(j - 23) * 128, t0:t0 + NT], ob)
                    return None
                ps_ = pp_r.next()
                em.copy(ps_, pp[:, 0:NH], E='act')
                return dict(j=j, ps_=ps_)

            def stage1b(d):
                if d is None:
                    return None
                j, ps_ = d["j"], d["ps_"]
                pc = banks.next()
                for kk in range(4):
                    em.mm(pc[:, 0:NT], dg[:, j * 4 + kk, :], ps_[:, kk:kk + NT], start=(kk == 0), stop=(kk == 3))
                if j >= 16:
                    ob = ob_r.next()
                    em.act(ob, pc[:, 0:NT], AF.Silu)
                    em.dma(outs["v"][(j % 8) * 128:(j % 8 + 1) * 128, t0:t0 + NT], ob)
                    return None
                s_ = s_r.next()
                em.act(s_, pc[:, 0:NT], AF.Silu)
                sq = sq_r.next()
                em.act(sq, s_, AF.Square)
                return dict(j=j, s_=s_, sq=sq)

            def stage2(d):
                if d is None:
                    return
                j = d["j"]
                pn = banks.next()
                em.mm(pn[:, 0:NT], ones_bf, d["sq"])
                rn = rn_r.next()
                em.ts(rn, pn[:, 0:NT], 1e-6, None, op0=ALU.add)
                em.act(rn, rn, AF.Sqrt)
                em.recip(rn, rn)
                ob = ob_r.next()
                if j < 8:
                    em.stt(ob, d["s_"], 128.0 ** -0.5, rn, ALU.mult, ALU.mult)
                else:
                    em.tt(ob, d["s_"], rn, ALU.mult, E='pool')
                nm = "q" if j < 8 else "k"
                em.dma(outs[nm][(j % 8) * 128:(j % 8 + 1) * 128, t0:t0 + NT], ob)

            da = stage1a(0)
            db = None
            for j in range(32):
                da_next = stage1a(j + 1) if j + 1 < 32 else None
                db_new = stage1b(da)
                stage2(db)
                da, db = da_next, db_new
            stage2(db)
            pb = banks.next()
            for k in range(8):
                em.mm(pb[0:8, 0:NT], W[k][:, 4096:4104], ub[:, k, 3:NT + 3], start=(k == 0), stop=(k == 7))
            bt = bt_r.next()
            em.act(bt, pb[0:8, 0:NT], AF.Sigmoid)
            em.dma(beta_o[:, t0:t0 + NT], bt)
            pa = banks.next()
            for k in range(8):
                em.mm(pa[0:8, 0:NT], W[k][:, 4104:4112], ub[:, k, 3:NT + 3], start=(k == 0), stop=(k == 7))
            gt = gt_r.next()
            em.act(gt, pa[0:8, 0:NT], AF.Exp, bias=hs[:, 1:2])
            em.act(gt, gt, AF.Ln, bias=1.0)
            em.ts(gt, gt, nea[:, 0:1], None, op0=ALU.mult)
            em.dma(g_o[:, t0:t0 + NT], gt)
        em.finish()
    return nc


def rope_tables(pos):
    d = np.arange(128) % 64
    inv = (10000.0 ** (-(np.arange(0, 64, 2, dtype=np.float32)) / 64)).astype(np.float32)
    ang = pos.astype(np.float32)[None, :] * inv[d % 32][:, None]
    C = np.cos(ang).astype(np.float32)
    S = np.sin(ang).astype(np.float32)
    S = np.where((d < 32)[:, None], -S, S).astype(np.float32)
    return C, S


def rope_perm(ncols):
    c = np.arange(ncols)
    return (c // 64) * 64 + (c % 64 + 32) % 64


def build_diff_pre(ntok=TPC):
    NT = 256
    nc = new_nc()
    with ExitStack() as st:
        em = Em(nc, st)
        xT = em.dram("xT", [D, ntok], F32, "ExternalInput")
        scal = em.dram("scal", [128, 16], F32, "ExternalInput")
        w_in = em.dram("w_in", [D, 3 * D], F32, "ExternalInput")
        w_pm = em.dram("w_pm", [D, 2 * D], F32, "ExternalInput")
        ctab = em.dram("ctab", [128, ntok], F32, "ExternalInput")
        stab = em.dram("stab", [128, ntok], F32, "ExternalInput")
        qo = em.dram("q", [D, ntok], BF16, "ExternalOutput")
        ko = em.dram("k", [D, ntok], BF16, "ExternalOutput")
        vtok = em.dram("vtok", [ntok, D], BF16, "ExternalOutput")
        sc = em.sb([128, 16], F32, "sc")
        em.dma(sc, scal)
        sc1p = em.sb([128, 8], F32, "sc1p")
        em.ts(sc1p, sc[:, 0:8], 1.0, None, op0=ALU.add)
        W = [em.sb([128, 3 * D], BF16, "W%d" % k) for k in range(8)]
        Wp = [em.sb([128, 2 * D], BF16, "Wp%d" % k) for k in range(8)]
        for k in range(8):
            em.dma(W[k], w_in[k * 128:(k + 1) * 128, :], Q='pool')
            em.dma(Wp[k], w_pm[k * 128:(k + 1) * 128, :], Q='pool')
        xt_r = Rot([em.sb([128, 8, NT], F32, "xt") for _ in range(2)])
        ub = em.sb([128, 8, NT], BF16, "ub")
        ct_r = Rot([em.sb([128, NT], F32, "ct") for _ in range(2)])
        st_r = Rot([em.sb([128, NT], F32, "st") for _ in range(2)])
        t1_r = Rot([em.sb([128, NT], F32, "t1") for _ in range(2)])
        t2_r = Rot([em.sb([128, NT], F32, "t2") for _ in range(2)])
        ob_r = Rot([em.sb([128, NT], BF16, "ob") for _ in range(3)])
        vt_r = Rot([em.sb([128, D], BF16, "vt") for _ in range(2)])
        banks = Rot([em.ps([128, 512], F32, "bk") for _ in range(8)])
        xv = xT[:].ap.rearrange("(c p) t -> p c t", p=128)
        for tt in range(ntok // NT):
            t0 = tt * NT
            xt = xt_r.next()
            em.dma(xt, V(xT, xv[:, :, t0:t0 + NT]))
            ct, stb = ct_r.next(), st_r.next()
            em.dma(ct, ctab[:, t0:t0 + NT])
            em.dma(stb, stab[:, t0:t0 + NT])
            for c in range(8):
                em.ts(ub[:, c, :], xt[:, c, :], sc1p[:, c:c + 1], sc[:, 8 + c:9 + c], op0=ALU.mult, op1=ALU.add,
                      E=('dve' if c % 2 == 0 else 'pool'))
            for j in range(16):
                p1, p2 = banks.next(), banks.next()
                for k in range(8):
                    em.mm(p1[:, 0:NT], W[k][:, j * 128:(j + 1) * 128], ub[:, k, :], start=(k == 0), stop=(k == 7))
                for k in range(8):
                    em.mm(p2[:, 0:NT], Wp[k][:, j * 128:(j + 1) * 128], ub[:, k, :], start=(k == 0), stop=(k == 7))
                scl = 0.125 if j < 8 else 1.0
                t1, t2, ob = t1_r.next(), t2_r.next(), ob_r.next()
                em.stt(t1, p1[:, 0:NT], scl, ct, ALU.mult, ALU.mult)
                em.stt(t2, p2[:, 0:NT], scl, stb, ALU.mult, ALU.mult)
                em.tt(ob, t1, t2, ALU.add, E='pool')
                dst = qo if j < 8 else ko
                em.dma(dst[(j % 8) * 128:(j % 8 + 1) * 128, t0:t0 + NT], ob)
            for tb in range(NT // 128):
                vt = vt_r.next()
                for half in range(2):
                    pv = banks.next()
                    for k in range(8):
                        em.mm(pv[:, 0:512], ub[:, k, tb * 128:(tb + 1) * 128],
                              W[k][:, 2 * D + half * 512:2 * D + (half + 1) * 512], start=(k == 0), stop=(k == 7))
                    em.copy(vt[:, half * 512:(half + 1) * 512], pv[:, 0:512], E=('act' if half == 0 else 'dve'))
                em.dma(vtok[t0 + tb * 128:t0 + (tb + 1) * 128, :], vt)
        em.finish()
    return nc


def build_diff_attn(Tn=T, lam_init=0.0):
    NB = Tn // 128
    NG = Tn // 512
    nc = new_nc()
    with ExitStack() as st:
        em = Em(nc, st)
        qin = em.dram("q", [2, 2, 64, Tn], BF16, "ExternalInput")
        kin = em.dram("k", [2, 2, 64, Tn], BF16, "ExternalInput")
        vin = em.dram("v", [2, Tn, 128], BF16, "ExternalInput")
        lin = em.dram("lam", [4, 64], F32, "ExternalInput")
        cst = em.dram("cst", [128, 256], F32, "ExternalInput")
        oT = em.dram("oT", [2, 128, Tn], F32, "ExternalOutput")

        cf = em.sb([128, 256], F32, "cf")
        em.dma(cf, cst)
        If = cf[:, 128:256]
        tri = em.sb([128, 128], BF16, "tri")
        em.copy(tri, cf[:, 0:128])
        sel = em.sb([64, 65], BF16, "sel")
        em.memset(sel, 0.0)
        em.memset(sel[:, 64:65], 1.0)
        lt = em.sb([128, 4, 64], F32, "lt")
        em.dma(lt, V(lin, lin[:].ap.rearrange("a b -> (a b)").partition_broadcast(128)).re("p (a b) -> p a b", a=4))
        lp = em.sb([128, 2, 64], F32, "lp")
        em.tt(lp[:, 0, :], lt[:, 0, :], lt[:, 1, :], ALU.mult)
        em.tt(lp[:, 1, :], lt[:, 2, :], lt[:, 3, :], ALU.mult)
        ls = em.sb([128, 4], F32, "ls")
        em.op('dve', lambda e: e.reduce_sum(out=ls[:, 0:1].ap, in_=lp[:, 0, :].ap, axis=AX.X), reads=[lp], writes=[ls])
        em.op('dve', lambda e: e.reduce_sum(out=ls[:, 1:2].ap, in_=lp[:, 1, :].ap, axis=AX.X), reads=[lp], writes=[ls])
        em.act(ls[:, 0:2], ls[:, 0:2], AF.Exp)
        em.tt(ls[:, 2:3], ls[:, 1:2], ls[:, 0:1], ALU.subtract)
        em.ts(ls[:, 3:4], ls[:, 2:3], -float(lam_init), None, op0=ALU.add)

        kaug = [em.sb([65, Tn], BF16, "kaug%d" % c) for c in range(2)]
        vaug = em.sb([128, NB, 129], BF16, "vaug")
        qaug_r = [Rot([em.sb([65, 512], BF16, "qaug%d" % c) for _ in range(2)]) for c in range(2)]
        ksq = em.sb([64, 512], BF16, "ksq")
        kmx = em.sb([65, 40], F32, "kmx")
        km2 = [em.sb([65, 1], F32, "km2_%d" % c) for c in range(2)]
        qsq_r = Rot([em.sb([64, 512], BF16, "qsq") for _ in range(2)])
        PT_r = Rot([em.sb([128, 2, 512], BF16, "PT") for _ in range(3)])
        sb2 = Rot([em.ps([128, 2, 512], F32, "sb2_%d" % i) for i in range(2)])

        class _Half:
            def __init__(self, c):
                self.c = c

            def next(self):
                t = sb2.next()
                return t[:, self.c, :]
        sbk = [_Half(0), _Half(1)]
        obk = [[em.ps([128, 512], F32, "ob%d_%d" % (c, i)) for i in range(2)] for c in range(2)]
        rc_r = Rot([em.sb([128, 2], F32, "rc") for _ in range(4)])
        t_r = Rot([em.sb([128, 128], F32, "tf") for _ in range(3)])
        of_r = Rot([em.sb([128, 128], F32, "of") for _ in range(3)])
        ost_r = Rot([em.sb([128, 512], F32, "ost") for _ in range(2)])

        for u in range(2):
            em.dma(vaug[:, :, 0:128], V(vin, vin[u].ap.rearrange("(n p) c -> p n c", p=128)))
            em.memset(vaug[:, :, 128:129], 1.0)
            for c in range(2):
                em.dma(kaug[c][0:64, :], kin[u, c])
                em.memset(kaug[c][64:65, :], 1.0)
                for g in range(NG):
                    em.act(ksq, kaug[c][0:64, g * 512:(g + 1) * 512], AF.Square)
                    pn = sbk[c].next()
                    em.mm(pn[0:65, 0:512], sel, ksq)
                    em.op('dve', lambda e, pn=pn, g=g: e.reduce_max(out=kmx[64:65, g:g + 1].ap, in_=pn[64:65, 0:512].ap, axis=AX.X),
                          reads=[pn], writes=[kmx])
                em.op('dve', lambda e, c=c: e.reduce_max(out=km2[c][64:65, 0:1].ap, in_=kmx[64:65, 0:NG].ap, axis=AX.X),
                      reads=[kmx], writes=[km2[c]])
                em.ts(km2[c][64:65, :], km2[c][64:65, :], 1.05, None, op0=ALU.mult)
            for G in range(NG):
                qa = []
                for c in range(2):
                    q_ = qaug_r[c].next()
                    qa.append(q_)
                    em.dma(q_[0:64, :], qin[u, c, :, G * 512:(G + 1) * 512])
                    qsq = qsq_r.next()
                    em.act(qsq, q_[0:64, :], AF.Square)
                    pn = sbk[c].next()
                    em.mm(pn[0:65, 0:512], sel, qsq)
                    em.act(q_[64:65, :], pn[64:65, 0:512], AF.Sqrt, scale=km2[c][64:65, 0:1])
                    em.ts(q_[64:65, :], q_[64:65, :], -1.0, None, op0=ALU.mult)
                nkb = 4 * G + 4
                first = [[True, True], [True, True]]

                def s_mm(j):
                    m = max(0, j - 4 * G)
                    ncol = (4 - m) * 128
                    ps2 = sb2.next()
                    for c in range(2):
                        em.mm(ps2[:, c, 0:ncol], kaug[c][:, j * 128:(j + 1) * 128], qa[c][:, m * 128:512])
                    return ps2

                cur = s_mm(0)
                for j in range(nkb):
                    m = max(0, j - 4 * G)
                    ncol = (4 - m) * 128
                    PT2 = PT_r.next()
                    em.act(PT2[:, :, 0:ncol], cur[:, :, 0:ncol], AF.Exp)
                    if j >= 4 * G:
                        for c in range(2):
                            em.tt(PT2[:, c, 0:128], PT2[:, c, 0:128], tri, ALU.mult, E='pool')
                    nxt = s_mm(j + 1) if j + 1 < nkb else None
                    for c in range(2):
                        PT = PT2[:, c, :]
                        for qi in range(m, 4):
                            bk = obk[c][qi // 2]
                            o_ = bk[:, (qi % 2) * 129:(qi % 2) * 129 + 129]
                            em.mm(o_, PT[:, (qi - m) * 128:(qi - m + 1) * 128], vaug[:, j, :],
                                  start=first[c][qi // 2], stop=(j == 4 * G + qi), sgc=True)
                            first[c][qi // 2] = False
                    cur = nxt
                ost = ost_r.next()
                for qi in range(4):
                    o1 = obk[0][qi // 2][:, (qi % 2) * 129:(qi % 2) * 129 + 129]
                    o2 = obk[1][qi // 2][:, (qi % 2) * 129:(qi % 2) * 129 + 129]
                    rc = rc_r.next()
                    em.recip(rc[:, 0:1], o1[:, 128:129])
                    em.recip(rc[:, 1:2], o2[:, 128:129])
                    em.tt(rc[:, 1:2], rc[:, 1:2], ls[:, 3:4], ALU.mult)
                    t2 = t_r.next()
                    em.ts(t2, o2[:, 0:128], rc[:, 1:2], None, op0=ALU.mult)
                    of = of_r.next()
                    em.stt(of, o1[:, 0:128], rc[:, 0:1], t2, ALU.mult, ALU.add)
                    pt = sbk[qi % 2].next()
                    em.mm(pt[:, 0:128], of, If)
                    em.copy(ost[:, qi * 128:(qi + 1) * 128], pt[:, 0:128], E='act')
                em.dma(oT[u, :, G * 512:(G + 1) * 512], ost)
        em.finish()
    return nc


def build_swa(ntok=TPC):
    NT = 512
    NBL = NT // 128
    nc = new_nc()
    with ExitStack() as st:
        em = Em(nc, st)
        xT = em.dram("xT", [D, 128 + ntok], F32, "ExternalInput")
        scal = em.dram("scal", [128, 36], F32, "ExternalInput")
        w_in = em.dram("w_in", [D, 1280], F32, "ExternalInput")
        w_pm = em.dram("w_pm", [D, 1152], F32, "ExternalInput")
        bv = em.dram("bv", [1, 128], F32, "ExternalInput")
        sinks = em.dram("sinks", [1, 16], F32, "ExternalInput")
        ctab = em.dram("ctab", [128, 128 + ntok], F32, "ExternalInput")
        stab = em.dram("stab", [128, 128 + ntok], F32, "ExternalInput")
        cst = em.dram("cst", [128, 384], F32, "ExternalInput")
        oT = em.dram("oT", [D, ntok], BF16, "ExternalOutput")

        sc = em.sb([128, 36], F32, "sc")
        em.dma(sc, scal)
        sc1p = em.sb([128, 8], F32, "sc1p")
        em.ts(sc1p, sc[:, 0:8], 1.0, None, op0=ALU.add)
        hv = sc[:, 34:35]
        cf = em.sb([128, 384], F32, "cf")
        em.dma(cf, cst)
        If = cf[:, 256:384]
        mP = em.sb([128, 512], BF16, "mP")
        mC = em.sb([128, 512], BF16, "mC")
        for i in range(4):
            em.copy(mP[:, i * 128:(i + 1) * 128], cf[:, 0:128])
            em.copy(mC[:, i * 128:(i + 1) * 128], cf[:, 128:256])
        bvb = em.sb([128, 128], F32, "bvb")
        em.dma(bvb, V(bv, bv[0:1, :].ap.partition_broadcast(128)))
        esk = em.sb([128, 16], F32, "esk")
        em.dma(esk, V(sinks, sinks[0:1, :].ap.partition_broadcast(128)))
        em.act(esk, esk, AF.Exp)
        sel = em.sb([64, 65], BF16, "sel")
        em.memset(sel, 0.0)
        em.memset(sel[:, 64:65], 1.0)
        vvirt = em.sb([65, 66], BF16, "vvirt")
        em.memset(vvirt, 0.0)
        em.memset(vvirt[64:65, 65:66], 1.0)

        W = [em.sb([128, 1280], BF16, "W%d" % k) for k in range(8)]
        Wp = [em.sb([128, 1152], BF16, "Wp%d" % k) for k in range(8)]
        for k in range(8):
            em.dma(W[k], w_in[k * 128:(k + 1) * 128, :], Q='pool')
            em.dma(Wp[k], w_pm[k * 128:(k + 1) * 128, :], Q='pool')

        xt_r = Rot([em.sb([128, 8, NT], F32, "xt") for _ in range(2)])
        ub = em.sb([128, 8, NT], BF16, "ub")
        ct_r = Rot([em.sb([128, NT], F32, "ct") for _ in range(2)])
        st_r = Rot([em.sb([128, NT], F32, "st") for _ in range(2)])
        t1_r = Rot([em.sb([128, NT], F32, "t1") for _ in range(2)])
        t2_r = Rot([em.sb([128, NT], F32, "t2") for _ in range(2)])
        kaug = [em.sb([65, 128 + NT], BF16, "kaug%d" % g) for g in range(2)]
        vaug = [em.sb([128, NBL + 1, 66], BF16, "vaug%d" % g) for g in range(2)]
        for g in range(2):
            em.memset(kaug[g][64:65, :], 1.0)
            em.memset(vaug[g][:, :, 64:65], 1.0)
            em.memset(vaug[g][:, :, 65:66], 0.0)
        qaug = em.sb([65, 16, NT], BF16, "qaug")
        erow = em.sb([65, 16, NT], BF16, "erow")
        ksq = em.sb([64, 128 + NT], BF16, "ksq")
        qsq_r = Rot([em.sb([64, NT], BF16, "qsq") for _ in range(2)])
        km = em.sb([65, 4], F32, "km")
        km2 = [em.sb([65, 1], F32, "km2_%d" % g) for g in range(2)]
        PT_r = Rot([em.sb([128, 512], BF16, "PT") for _ in range(4)])
        den_r = Rot([em.sb([128, 4], F32, "den") for _ in range(4)])
        otok_r = Rot([em.sb([128, D], F32, "otok") for _ in range(2)])
        ostg_r = Rot([em.sb([128, 8, NT], BF16, "ostg") for _ in range(2)])
        banks = Rot([em.ps([128, 512], F32, "bk") for _ in range(4)])
        sbanks = Rot([em.ps([128, 512], F32, "sbk") for _ in range(4)])

        xv = xT[:].ap.rearrange("(c p) t -> p c t", p=128)
        ov = oT[:].ap.rearrange("(c p) t -> p c t", p=128)

        def project(c0, n, tile_i):
            xt = xt_r.next()
            em.dma(xt[:, :, 0:n], V(xT, xv[:, :, c0:c0 + n]))
            ct, stb = ct_r.next(), st_r.next()
            em.dma(ct[:, 0:n], ctab[:, c0:c0 + n])
            em.dma(stb[:, 0:n], stab[:, c0:c0 + n])
            for c in range(8):
                em.ts(ub[:, c, 0:n], xt[:, c, 0:n], sc1p[:, c:c + 1], sc[:, 8 + c:9 + c], op0=ALU.mult, op1=ALU.add,
                      E=('dve' if c % 2 == 0 else 'pool'))
            koff = 0 if tile_i < 0 else 128
            chunks = [8] if tile_i < 0 else list(range(9))
            for j in chunks:
                p1, p2 = banks.next(), banks.next()
                for k in range(8):
                    em.mm(p1[:, 0:n], W[k][:, j * 128:(j + 1) * 128], ub[:, k, 0:n], start=(k == 0), stop=(k == 7))
                for k in range(8):
                    em.mm(p2[:, 0:n], Wp[k][:, j * 128:(j + 1) * 128], ub[:, k, 0:n], start=(k == 0), stop=(k == 7))
                t1, t2 = t1_r.next(), t2_r.next()
                b1 = sc[:, 16 + j:17 + j] if j < 8 else sc[:, 32:33]
                b2 = sc[:, 24 + j:25 + j] if j < 8 else sc[:, 33:34]
                em.stt(t1[:, 0:n], p1[:, 0:n], b1, ct[:, 0:n], ALU.add, ALU.mult)
                em.stt(t2[:, 0:n], p2[:, 0:n], b2, stb[:, 0:n], ALU.add, ALU.mult)
                for e in range(2):
                    ps = slice(64 * e, 64 * e + 64)
                    if j < 8:
                        em.tt(qaug[0:64, 2 * j + e, 0:n], t1[ps, 0:n], t2[ps, 0:n], ALU.add, E='pool')
                    else:
                        em.stt(kaug[e][0:64, koff:koff + n], t1[ps, 0:n], 1.0, t2[ps, 0:n], ALU.mult, ALU.add, E='pool')
                        em.ts(kaug[e][0:64, koff:koff + n], kaug[e][0:64, koff:koff + n], 0.125, None, op0=ALU.mult, E='pool')
            for bl in range(n // 128):
                pv = banks.next()
                for k in range(8):
                    em.mm(pv[:, 0:128], ub[:, k, bl * 128:(bl + 1) * 128], W[k][:, 1152:1280], start=(k == 0), stop=(k == 7))
                for g in range(2):
                    vb = bl + (0 if tile_i < 0 else 1)
                    em.tt(vaug[g][:, vb, 0:64], pv[:, g * 64:(g + 1) * 64], bvb[:, g * 64:(g + 1) * 64], ALU.add)

        project(0, 128, -1)
        for tt in range(ntok // NT):
            project(128 + tt * NT, NT, tt)
            for g in range(2):
                em.act(ksq, kaug[g][0:64, :], AF.Square)
                for hf in range(2):
                    w_ = (128 + NT) // 2
                    pn = banks.next()
                    em.mm(pn[0:65, 0:w_], sel, ksq[:, hf * w_:(hf + 1) * w_])
                    em.op('dve', lambda e, pn=pn, hf=hf, w_=w_: e.reduce_max(out=km[64:65, hf:hf + 1].ap, in_=pn[64:65, 0:w_].ap, axis=AX.X),
                          reads=[pn], writes=[km])
                em.op('dve', lambda e, g=g: e.reduce_max(out=km2[g][64:65, 0:1].ap, in_=km[64:65, 0:2].ap, axis=AX.X),
                      reads=[km], writes=[km2[g]])
                em.ts(km2[g][64:65, :], km2[g][64:65, :], 1.05, None, op0=ALU.mult)
            for h in range(16):
                g = h // 8
                qsq = qsq_r.next()
                em.act(qsq, qaug[0:64, h, :], AF.Square)
                pn = banks.next()
                em.mm(pn[0:65, 0:NT], sel, qsq)
                em.act(qaug[64:65, h, :], pn[64:65, 0:NT], AF.Sqrt, scale=km2[g][64:65, 0:1])
                em.ts(qaug[64:65, h, :], qaug[64:65, h, :], -1.0, None, op0=ALU.mult)
                em.act(erow[64:65, h, :], qaug[64:65, h, :], AF.Exp)
            ostg = ostg_r.next()
            def s_mm(bl, g, hg):
                h0 = g * 8 + hg * 4
                q4 = qaug[:, h0:h0 + 4, bl * 128:(bl + 1) * 128]
                pp, pc = sbanks.next(), sbanks.next()
                em.mm(pp[:, 0:512], kaug[g][:, bl * 128:(bl + 1) * 128], q4)
                em.mm(pc[:, 0:512], kaug[g][:, (bl + 1) * 128:(bl + 2) * 128], q4)
                return pp, pc

            groups = [(bl, g, hg) for bl in range(NBL) for g in range(2) for hg in range(2)]
            s_next = s_mm(*groups[0])
            for gi, (bl, g, hg) in enumerate(groups):
                bsl = slice(bl * 128, (bl + 1) * 128)
                if g == 0 and hg == 0:
                    otok = otok_r.next()
                if True:
                    if True:
                        h0 = g * 8 + hg * 4
                        pp, pc = s_next
                        Pp, Pc = PT_r.next(), PT_r.next()
                        em.act(Pp, pp[:, 0:512], AF.Exp)
                        em.act(Pc, pc[:, 0:512], AF.Exp)
                        em.tt(Pp, Pp, mP, ALU.mult, E='pool')
                        em.tt(Pc, Pc, mC, ALU.mult, E='dve')
                        if tt == 0 and bl == 0:
                            em.ts(Pp, Pp, hv, None, op0=ALU.mult)
                        if gi + 1 < len(groups):
                            s_next = s_mm(*groups[gi + 1])
                        po = banks.next()
                        for i in range(4):
                            o_ = po[:, i * 66:(i + 1) * 66]
                            em.mm(o_, Pp[:, i * 128:(i + 1) * 128], vaug[g][:, bl, :], start=(i == 0), stop=False, sgc=True)
                            em.mm(o_, Pc[:, i * 128:(i + 1) * 128], vaug[g][:, bl + 1, :], start=False, stop=False, sgc=True)
                            em.mm(o_, erow[64:65, h0 + i, bsl], vvirt[64:65, :], start=False, stop=True, sgc=True)
                        for i in range(4):
                            h = h0 + i
                            o_ = po[:, i * 66:(i + 1) * 66]
                            den = den_r.next()
                            em.copy(den[:, 2:4], o_[:, 64:66], E='dve')
                            em.stt(den[:, 0:1], den[:, 3:4], esk[:, h:h + 1], den[:, 2:3], ALU.mult, ALU.add)
                            em.recip(den[:, 1:2], den[:, 0:1])
                            em.ts(otok[:, h * 64:(h + 1) * 64], o_[:, 0:64], den[:, 1:2], None, op0=ALU.mult)
                if g == 1 and hg == 1:
                    for j in range(8):
                        pt = banks.next()
                        em.mm(pt[:, 0:128], otok[:, j * 128:(j + 1) * 128], If)
                        em.copy(ostg[:, j, bsl], pt[:, 0:128], E='act')
            em.dma(V(oT, ov[:, :, tt * NT:(tt + 1) * NT]), ostg)
            for g in range(2):
                em.copy(kaug[g][0:64, 0:128], kaug[g][0:64, NT:NT + 128], E='pool')
                em.copy(vaug[g][:, 0, 0:64], vaug[g][:, NBL, 0:64], E='pool')
        em.finish()
    return nc


def pcol(v):
    v = np.asarray(v, np.float32)
    return np.ascontiguousarray(v.reshape(-1, 128).T)


def core_bq(c):
    return c // 4, c % 4


def halo_x(x_tok, c, h):
    b, q = core_bq(c)
    if q == 0:
        left = np.zeros((D, h), np.float32)
    else:
        left = x_tok[c - 1][:, TPC - h:]
    return np.ascontiguousarray(np.concatenate([left, x_tok[c]], axis=1))


def tok_to_rows(outs, name, b, r0, r1):
    return np.concatenate([outs[b * 4 + q][name][r0:r1, :] for q in range(4)], axis=1)


_DBG = {}


def kernel(x, c, ada_w, ada_b, ln_g, ln_b, ffn_w_in, ffn_w_out,
           rwkv_mu, rwkv_w_rkv, rwkv_w0, rwkv_w1, rwkv_w2, rwkv_a0, rwkv_a1, rwkv_a2,
           rwkv_g1, rwkv_g2, rwkv_k_k, rwkv_k_a, rwkv_r_k, rwkv_gn_g, rwkv_gn_b, rwkv_w_out,
           gdn_w_in, gdn_conv, gdn_a_log, gdn_dt_bias, gdn_norm_g, gdn_w_out,
           diff_w_in, diff_lambda, diff_subln_g, diff_w_out,
           swa_w_qkv, swa_b_qkv, swa_sinks, swa_w_out, swa_b_out, _layers=DEPTH, _debug=None):
    f32 = lambda a: np.ascontiguousarray(np.asarray(a, np.float32))
    x = f32(x)
    mod = run_mod(f32(c), f32(ada_w), f32(ada_b))
    ms = lambda l, b, w: mod[l, b][:, w * 8:(w + 1) * 8]
    xs = [np.ascontiguousarray(x[cc // 4, (cc % 4) * TPC:(cc % 4 + 1) * TPC, :].T) for cc in range(NCORES)]
    zeros8 = np.zeros((128, 8), np.float32)
    bd = np.kron(np.eye(2), np.ones((64, 64))).astype(np.float32)
    eye = np.eye(128, dtype=np.float32)
    hvcol = lambda cc: np.full((128, 1), 0.0 if cc % 4 == 0 else 1.0, np.float32)

    def post(i, mode, extra, w_o, b_out=None, cols4=None, cols5=None, hscale=1.0):
        nc = build_post(mode, TPC, hscale, has_bias=(b_out is not None))
        ims = []
        for cc in range(NCORES):
            b = cc // 4
            sc = np.concatenate([ms(i, b, 2), pcol(ln_g[i, 0]), pcol(ln_b[i, 0]),
                                 pcol(b_out) if b_out is not None else zeros8,
                                 cols4 if cols4 is not None else zeros8,
                                 cols5 if cols5 is not None else zeros8, zeros8], axis=1)
            im = {"xT": xs[cc], "w_o": f32(w_o), "scal": np.ascontiguousarray(sc)}
            im.update(extra[cc])
            ims.append(im)
        res = run(nc, ims)
        return [res[cc]["yT"] for cc in range(NCORES)]

    def ffn(i, xin):
        nc = build_ffn(TPC)
        ims = []
        for cc in range(NCORES):
            b = cc // 4
            sc = np.concatenate([ms(i, b, 4), ms(i, b, 3), ms(i, b, 5), pcol(ln_g[i, 1]), pcol(ln_b[i, 1])], axis=1)
            ims.append({"xT": xin[cc], "w_in": f32(ffn_w_in[i]), "w_out": f32(ffn_w_out[i]),
                        "scal": np.ascontiguousarray(sc)})
        res = run(nc, ims)
        return [res[cc]["yT"] for cc in range(NCORES)]

    for i in range(_layers):
        m = i % 4
        if m == 0:
            nc = build_rwkv_pre(TPC)
            ims = []
            for cc in range(NCORES):
                b = cc // 4
                sc = np.concatenate([ms(i, b, 1), ms(i, b, 0)] + [pcol(rwkv_mu[0, k]) for k in range(6)] +
                                    [pcol(rwkv_w0[0]), pcol(rwkv_a0[0]), pcol(rwkv_k_k[0]), pcol(rwkv_k_a[0]),
                                     pcol(np.asarray(rwkv_r_k[0]).reshape(-1)), hvcol(cc)], axis=1)
                ims.append({"xT": halo_x(xs, cc, 1), "scal": np.ascontiguousarray(sc), "cst": bd,
                            "w_rkv": f32(rwkv_w_rkv[0]), "w1": f32(rwkv_w1[0]), "a1": f32(rwkv_a1[0]), "g1": f32(rwkv_g1[0]),
                            "w2": f32(rwkv_w2[0]), "a2": f32(rwkv_a2[0]), "g2": f32(rwkv_g2[0])})
            pre = run(nc, ims)
            nc = build_rwkv_scan(T, 1024)
            mS, mI, mL = chunk_masks()
            reset = np.ones((128, 128), np.float32)
            reset[:, 0] = 0
            reset[:, 64] = 0
            cst = np.ascontiguousarray(np.concatenate([mS, mI, mL, eye, reset], axis=1))
            ims = []
            for cc in range(NCORES):
                b, hq = cc // 4, cc % 4
                im = {"cst": cst}
                for n in ("r", "k", "kk", "a", "lw"):
                    im[n] = np.ascontiguousarray(tok_to_rows(pre, n, b, 256 * hq, 256 * hq + 256).reshape(2, 128, T))
                im["v"] = np.ascontiguousarray(np.concatenate([pre[b * 4 + q]["vtok"][:, 256 * hq:256 * hq + 256]
                                                               for q in range(4)], axis=0))
                ims.append(im)
            sres = run(nc, ims)
            extra = []
            for cc in range(NCORES):
                b, q = cc // 4, cc % 4
                yin = np.concatenate([sres[b * 4 + hq]["yT"].reshape(256, T)[:, q * TPC:(q + 1) * TPC] for hq in range(4)], axis=0)
                extra.append({"yin": np.ascontiguousarray(yin), "g": pre[cc]["g"], "bonus": pre[cc]["bonus"], "cst": bd})
            xs = post(i, 'rwkv', extra, rwkv_w_out[0], cols4=pcol(rwkv_gn_g[0]), cols5=pcol(rwkv_gn_b[0]))
        elif m == 1:
            nc = build_gdn_pre(TPC)
            cw = np.asarray(gdn_conv[0], np.float32).reshape(4, 24, 128).transpose(2, 1, 0).reshape(128, 96)
            hsc = np.ascontiguousarray(np.stack([np.asarray(gdn_a_log[0], np.float32), np.asarray(gdn_dt_bias[0], np.float32)], axis=1))
            ims = []
            for cc in range(NCORES):
                b = cc // 4
                sc = np.concatenate([ms(i, b, 1), ms(i, b, 0), cw, hvcol(cc)], axis=1)
                ims.append({"xT": halo_x(xs, cc, 3), "scal": np.ascontiguousarray(sc), "hsc": hsc, "ident": eye,
                            "w_in": f32(gdn_w_in[0])})
            pre = run(nc, ims)
            nc = build_gdn_scan(T, 1024)
            ii = np.arange(128)
            mnegI = np.where(ii[:, None] <= ii[None, :], 0.0, -1e4).astype(np.float32)
            nmS = -(ii[:, None] < ii[None, :]).astype(np.float32)
            reset = np.ones((128, 128), np.float32)
            reset[:, 0] = 0
            cst = np.ascontiguousarray(np.concatenate([mnegI, nmS, eye, reset] + gdn_level_masks(), axis=1))
            ims = []
            for cc in range(NCORES):
                b, hq = cc // 4, cc % 4
                im = {"cst": cst}
                for n in ("q", "k", "v"):
                    im[n] = np.ascontiguousarray(tok_to_rows(pre, n, b, 256 * hq, 256 * hq + 256).reshape(2, 128, T))
                for n in ("beta", "g"):
                    im[n] = np.ascontiguousarray(tok_to_rows(pre, n, b, 2 * hq, 2 * hq + 2))
                ims.append(im)
            sres = run(nc, ims)
            extra = []
            ng = np.zeros((128, 8), np.float32)
            ng[:, 0] = np.asarray(gdn_norm_g[0], np.float32)
            for cc in range(NCORES):
                b, q = cc // 4, cc % 4
                yin = np.concatenate([sres[b * 4 + hq]["oT"].reshape(256, T)[:, q * TPC:(q + 1) * TPC] for hq in range(4)], axis=0)
                extra.append({"yin": np.ascontiguousarray(yin), "zs": pre[cc]["zs"]})
            xs = post(i, 'gdn', extra, gdn_w_out[0], cols4=ng)
        elif m == 2:
            lam_init = 0.8 - 0.6 * float(np.exp(-0.3 * i))
            nc = build_diff_pre(TPC)
            perm = rope_perm(2 * D)
            w_in = f32(diff_w_in[0])
            w_pm = np.ascontiguousarray(w_in[:, :2 * D][:, perm])
            ims = []
            for cc in range(NCORES):
                b, q = cc // 4, cc % 4
                Ct, St = rope_tables(np.arange(q * TPC, (q + 1) * TPC))
                sc = np.concatenate([ms(i, b, 1), ms(i, b, 0)], axis=1)
                ims.append({"xT": xs[cc], "scal": np.ascontiguousarray(sc), "w_in": w_in, "w_pm": w_pm, "ctab": Ct, "stab": St})
            pre = run(nc, ims)
            nc = build_diff_attn(T, lam_init)
            ii = np.arange(128)
            cst = np.ascontiguousarray(np.concatenate([(ii[:, None] <= ii[None, :]).astype(np.float32), eye], axis=1))
            ims = []
            for cc in range(NCORES):
                im = {"lam": f32(diff_lambda[0]), "cst": cst}
                qs, ks, vs = [], [], []
                for u in range(2):
                    b, h = (2 * cc + u) // 8, (2 * cc + u) % 8
                    qs.append(tok_to_rows(pre, "q", b, 128 * h, 128 * h + 128).reshape(2, 64, T))
                    ks.append(tok_to_rows(pre, "k", b, 128 * h, 128 * h + 128).reshape(2, 64, T))
                    vs.append(np.concatenate([pre[b * 4 + q]["vtok"][:, 128 * h:128 * h + 128] for q in range(4)], axis=0))
                im["q"] = np.ascontiguousarray(np.stack(qs))
                im["k"] = np.ascontiguousarray(np.stack(ks))
                im["v"] = np.ascontiguousarray(np.stack(vs))
                ims.append(im)
            ares = run(nc, ims)
            extra = []
            sg = np.zeros((128, 8), np.float32)
            sg[:, 0] = np.asarray(diff_subln_g[0], np.float32)
            for cc in range(NCORES):
                b, q = cc // 4, cc % 4
                rows = []
                for h in range(8):
                    unit = b * 8 + h
                    rows.append(ares[unit // 2]["oT"][unit % 2][:, q * TPC:(q + 1) * TPC])
                extra.append({"yin": np.ascontiguousarray(np.concatenate(rows, axis=0))})
            xs = post(i, 'diff', extra, diff_w_out[0], cols4=sg, hscale=1.0 - lam_init)
        else:
            nc = build_swa(TPC)
            perm = rope_perm(1152)
            w_in = f32(swa_w_qkv[0])
            bq = np.asarray(swa_b_qkv[0], np.float32)
            bqp = bq[:1152][perm]
            w_pm = np.ascontiguousarray(w_in[:, :1152][:, perm])
            ii = np.arange(128)
            cst = np.ascontiguousarray(np.concatenate([(ii[:, None] > ii[None, :]).astype(np.float32),
                                                       (ii[:, None] <= ii[None, :]).astype(np.float32), eye], axis=1))
            ims = []
            for cc in range(NCORES):
                b, q = cc // 4, cc % 4
                Ct, St = rope_tables(np.arange(q * TPC - 128, (q + 1) * TPC))
                sc = np.concatenate([ms(i, b, 1), ms(i, b, 0), pcol(bq[:1024]), pcol(bqp[:1024]), pcol(bq[1024:1152]),
                                     pcol(bqp[1024:1152]), hvcol(cc), np.zeros((128, 1), np.float32)], axis=1)
                ims.append({"xT": halo_x(xs, cc, 128), "scal": np.ascontiguousarray(sc), "w_in": w_in, "w_pm": w_pm,
                            "bv": np.ascontiguousarray(bq[None, 1152:]), "sinks": f32(swa_sinks[0])[None, :],
                            "ctab": Ct, "stab": St, "cst": cst})
            ares = run(nc, ims)
            extra = [{"oT": ares[cc]["oT"]} for cc in range(NCORES)]
            xs = post(i, 'plain', extra, swa_w_out[0], b_out=swa_b_out[0])
        if _debug is not None:
            _debug.append(("mix%d" % i, [a.copy() for a in xs]))
        xs = ffn(i, xs)
        if _debug is not None:
            _debug.append(("ffn%d" % i, [a.copy() for a in xs]))
    out = np.empty((B, T, D), np.float32)
    for cc in range(NCORES):
        out[cc // 4, (cc % 4) * TPC:(cc % 4 + 1) * TPC, :] = xs[cc].T
    return out
```

```python
import numpy as np
import ml_dtypes
from contextlib import ExitStack
import concourse.bass as bass
import concourse.mybir as mybir
from concourse.bass_utils import run_bass_kernel_spmd

F32 = mybir.dt.float32
BF16 = mybir.dt.bfloat16
AF = mybir.ActivationFunctionType
ALU = mybir.AluOpType
AX = mybir.AxisListType
NPBF16 = ml_dtypes.bfloat16

D = 1024
B = 2
T = 16384
DEPTH = 4
DFF = 2816
NCORES = 8
TPC = T * B // NCORES
ALPHA = (2.0 * DEPTH) ** 0.25
LN_EPS = 1e-5


class Tile:
    def __init__(self, h, name, psum=False):
        self.h = h
        self.name = name
        self.psum = psum
        self.w = None
        self.r = {}

    def __getitem__(self, idx):
        return V(self, self.h[idx])

    @property
    def ap(self):
        return self.h[:]


class V:
    def __init__(self, t, ap):
        self.t = t
        self.ap = ap

    def __getitem__(self, idx):
        return V(self.t, self.ap[idx])

    def re(self, s, **kw):
        return V(self.t, self.ap.rearrange(s, **kw))

    def bc(self, shape):
        return V(self.t, self.ap.to_broadcast(shape))


def _t(v):
    if isinstance(v, Tile):
        return v
    if isinstance(v, V):
        return v.t
    return None


def _ap(v):
    if isinstance(v, Tile):
        return v.h[:]
    if isinstance(v, V):
        return v.ap
    return v


class Em:
    NDS = 20

    def __init__(self, nc, st):
        self.nc, self.st = nc, st
        self.engs = {'pe': nc.tensor, 'dve': nc.vector, 'act': nc.scalar,
                     'pool': nc.gpsimd, 'sp': nc.sync}
        self.sems = {k: st.enter_context(nc.semaphore('sem_' + k)) for k in self.engs}
        self.cnt = {k: 0 for k in self.engs}
        self.seen = {k: {} for k in self.engs}
        self.dsem = [st.enter_context(nc.semaphore('dsem%d' % i)) for i in range(self.NDS)]
        self.dcnt = [0] * self.NDS
        self.dpool = {'sp': list(range(0, 10)), 'pool': list(range(10, 16)), 'act': list(range(16, 20))}
        self.dnext = {'sp': 0, 'pool': 0, 'act': 0}
        self.uid = 0
        self.psum_banks = None

    def sb(self, shape, dtype, name=None):
        self.uid += 1
        name = (name or 't') + '_%d' % self.uid
        h = self.st.enter_context(self.nc.sbuf_tensor(name, list(shape), dtype))
        return Tile(h, name)

    def ps(self, shape, dtype=F32, name=None):
        self.uid += 1
        name = (name or 'p') + '_%d' % self.uid
        h = self.st.enter_context(self.nc.psum_tensor(name, list(shape), dtype))
        return Tile(h, name, psum=True)

    def dram(self, name, shape, dtype, kind):
        h = self.nc.dram_tensor(name, list(shape), dtype, kind=kind)
        return Tile(h.ap(), name)

    def _sem(self, key):
        if isinstance(key, tuple):
            return self.dsem[key[1]]
        return self.sems[key]

    def _wait(self, E, dep):
        if dep is None:
            return
        key, val = dep
        if key == E and E == 'pe':
            return
        if self.seen[E].get(key, 0) >= val:
            return
        self.seen[E][key] = val
        self.engs[E].wait_ge(self._sem(key), val)

    def _deps(self, E, reads, writes):
        for v in reads:
            t = _t(v)
            if t is not None:
                self._wait(E, t.w)
                if t.psum:
                    for k, dep in list(t.r.items()):
                        if k != E:
                            self._wait(E, dep)
        for v in writes:
            t = _t(v)
            if t is not None:
                self._wait(E, t.w)
                for dep in list(t.r.values()):
                    self._wait(E, dep)

    def _mark(self, dep, reads, writes):
        for v in reads:
            t = _t(v)
            if t is not None:
                t.r[dep[0]] = dep
        for v in writes:
            t = _t(v)
            if t is not None:
                t.w = dep
                t.r = {}

    def op(self, E, fn, reads=(), writes=(), signal=True):
        self._deps(E, reads, writes)
        ins = fn(self.engs[E])
        if signal:
            self.cnt[E] += 1
            ins.then_inc(self.sems[E], 1)
            self._mark((E, self.cnt[E]), reads, writes)
        else:
            self._mark((E, self.cnt[E] + 1), reads, writes)

    def dma(self, out, in_, Q='sp', **kw):
        pool = self.dpool[Q]
        i = pool[self.dnext[Q]]
        self.dnext[Q] = (self.dnext[Q] + 1) % len(pool)
        key = ('d', i)
        if self.dcnt[i] > 0:
            self._wait(Q, (key, self.dcnt[i]))
        self._deps(Q, [in_], [out])
        ins = self.engs[Q].dma_start(out=_ap(out), in_=_ap(in_), **kw)
        self.dcnt[i] += 16
        ins.then_inc(self.dsem[i], 16)
        self._mark((key, self.dcnt[i]), [in_], [out])

    def finish(self):
        for i in range(self.NDS):
            if self.dcnt[i] > 0:
                self._wait('sp', (('d', i), self.dcnt[i]))
        for E in ('pe', 'dve', 'act', 'pool'):
            if self.cnt[E] > 0:
                self._wait('sp', (E, self.cnt[E]))

    def mm(self, out, lhsT, rhs, start=True, stop=True, sgc=False):
        kw = {'skip_group_check': True} if sgc else {}
        self.op('pe', lambda e: e.matmul(_ap(out), lhsT=_ap(lhsT), rhs=_ap(rhs), start=start, stop=stop, **kw),
                reads=[lhsT, rhs], writes=[out], signal=(stop or sgc))

    def transpose(self, out, in_, ident):
        self.op('pe', lambda e: e.transpose(_ap(out), _ap(in_), _ap(ident)),
                reads=[in_, ident], writes=[out])

    def act(self, out, in_, func, bias=None, scale=None, accum_out=None, E='act'):
        kw = {}
        rd = [in_]
        wr = [out]
        if bias is not None:
            kw['bias'] = _ap(bias)
            rd.append(bias)
        if scale is not None:
            kw['scale'] = _ap(scale)
            rd.append(scale)
        if accum_out is not None:
            kw['accum_out'] = _ap(accum_out)
            wr.append(accum_out)
        self.op('act', lambda e: e.activation(out=_ap(out), in_=_ap(in_), func=func, **kw),
                reads=rd, writes=wr)

    def tt(self, out, a, b, op, E='dve'):
        self.op(E, lambda e: e.tensor_tensor(out=_ap(out), in0=_ap(a), in1=_ap(b), op=op),
                reads=[a, b], writes=[out])

    def ts(self, out, a, s1, s2=None, op0=ALU.mult, op1=None, E='dve', accum_out=None):
        kw = {}
        wr = [out]
        if op1 is not None:
            kw['op1'] = op1
        if accum_out is not None:
            kw['accum_out'] = _ap(accum_out)
            wr.append(accum_out)
        self.op(E, lambda e: e.tensor_scalar(out=_ap(out), in0=_ap(a), scalar1=_ap(s1), scalar2=_ap(s2),
                                             op0=op0, **kw),
                reads=[a, s1, s2], writes=wr)

    def stt(self, out, a, s, b, op0, op1, E='dve'):
        E = 'dve'
        self.op(E, lambda e: e.scalar_tensor_tensor(out=_ap(out), in0=_ap(a), scalar=_ap(s), in1=_ap(b),
                                                    op0=op0, op1=op1),
                reads=[a, s, b], writes=[out])

    def copy(self, out, in_, E='dve'):
        if E == 'act':
            self.op('act', lambda e: e.copy(out=_ap(out), in_=_ap(in_)), reads=[in_], writes=[out])
        else:
            self.op(E, lambda e: e.tensor_copy(out=_ap(out), in_=_ap(in_)), reads=[in_], writes=[out])

    def memset(self, out, val, E='pool'):
        self.op(E, lambda e: e.memset(_ap(out), val), reads=[], writes=[out])

    def recip(self, out, in_):
        self.op('dve', lambda e: e.reciprocal(out=_ap(out), in_=_ap(in_)), reads=[in_], writes=[out])


class Rot:
    def __init__(self, tiles):
        self.tiles = tiles
        self.i = 0

    def next(self):
        t = self.tiles[self.i]
        self.i = (self.i + 1) % len(self.tiles)
        return t


def new_nc():
    return bass.Bass("TRN2", target_bir_lowering=False)


def run(nc, in_maps):
    import time as _time
    t0 = _time.time()
    res = run_bass_kernel_spmd(nc, in_maps, core_ids=list(range(NCORES)))
    print("[launch] %.1fs" % (_time.time() - t0), flush=True)
    return res.results


def build_mod():
    nc = new_nc()
    with ExitStack() as st:
        em = Em(nc, st)
        cT = em.dram("cT", [128, 8, 2], F32, "ExternalInput")
        w = em.dram("w", [D, 3072], F32, "ExternalInput")
        bias = em.dram("bias", [128, 24], F32, "ExternalInput")
        out = em.dram("modT", [128, 24, 2], F32, "ExternalOutput")
        ct = em.sb([128, 8, 2], F32, "ct")
        cs = em.sb([128, 8, 2], F32, "cs")
        bt = em.sb([128, 24], F32, "bt")
        ot = em.sb([128, 24, 2], F32, "ot")
        em.dma(ct, cT)
        em.dma(bt, bias)
        em.act(cs, ct, AF.Silu)
        wt = [em.sb([128, 3072], F32, "w%d" % k) for k in range(8)]
        wv = w[:].rearrange("(k p) n -> k p n", p=128) if False else None
        for k in range(8):
            em.dma(wt[k], w[k * 128:(k + 1) * 128, :])
        pp = em.ps([128, 24, 2], F32, "pp")
        for j in range(24):
            for k in range(8):
                em.mm(pp[:, j, :], wt[k][:, j * 128:(j + 1) * 128], cs[:, k, :], start=(k == 0), stop=(k == 7))
        for b in range(2):
            em.tt(ot[:, :, b], pp[:, :, b], bt, ALU.add)
        em.dma(out, ot)
        em.finish()
    return nc


def run_mod(c, ada_w, ada_b):
    nc = build_mod()
    cT = np.ascontiguousarray(c.T.reshape(8, 128, 2).transpose(1, 0, 2))
    in_maps = []
    for core in range(NCORES):
        l, half = core % 4, core // 4
        in_maps.append({
            "cT": cT,
            "w": np.ascontiguousarray(ada_w[l][:, half * 3072:(half + 1) * 3072]),
            "bias": np.ascontiguousarray(ada_b[l][half * 3072:(half + 1) * 3072].reshape(24, 128).T),
        })
    res = run(nc, in_maps)
    mod = np.zeros((DEPTH, B, 128, 48), np.float32)
    for core in range(NCORES):
        l, half = core % 4, core // 4
        m = res[core]["modT"]
        for b in range(2):
            mod[l, b, :, half * 24:(half + 1) * 24] = m[:, :, b]
    return mod


def emit_ln(em, z, N, ones_bf, g, bvec, out, ps1, ps2, tmp, gb=None):
    zb, zsq, mean, rstd, t1 = tmp
    em.copy(zb, z, E='pool')
    em.act(zsq, z, AF.Square)
    for j in range(8):
        em.mm(ps1, ones_bf, zb[:, j, :], start=(j == 0), stop=(j == 7))
    for j in range(8):
        em.mm(ps2, ones_bf, zsq[:, j, :], start=(j == 0), stop=(j == 7))
    eps = LN_EPS / (ALPHA * ALPHA)
    em.ts(mean, ps1, 1.0 / D, None, op0=ALU.mult)
    em.tt(t1, mean, mean, ALU.mult)
    em.stt(rstd, ps2, 1.0 / D, t1, ALU.mult, ALU.subtract)
    em.ts(rstd, rstd, eps, None, op0=ALU.add)
    em.act(rstd, rstd, AF.Sqrt)
    em.recip(rstd, rstd)
    mb = V(mean, mean.ap.unsqueeze(1).to_broadcast([128, 8, N]))
    rb = V(rstd, rstd.ap.unsqueeze(1).to_broadcast([128, 8, N]))
    if gb is not None:
        Gt, Bt = gb
        h = 4
        em.tt(out[:, 0:h, :], z[:, 0:h, :], mb[:, 0:h, :], ALU.subtract, E='pool')
        em.tt(out[:, h:8, :], z[:, h:8, :], mb[:, h:8, :], ALU.subtract, E='dve')
        em.tt(out[:, 0:h, :], out[:, 0:h, :], rb[:, 0:h, :], ALU.mult, E='dve')
        em.tt(out[:, h:8, :], out[:, h:8, :], rb[:, h:8, :], ALU.mult, E='pool')
        em.tt(out[:, 0:h, :], out[:, 0:h, :], Gt[:, 0:h, :], ALU.mult, E='pool')
        em.tt(out[:, h:8, :], out[:, h:8, :], Gt[:, h:8, :], ALU.mult, E='dve')
        em.tt(out[:, 0:h, :], out[:, 0:h, :], Bt[:, 0:h, :], ALU.add, E='dve')
        em.tt(out[:, h:8, :], out[:, h:8, :], Bt[:, h:8, :], ALU.add, E='pool')
    else:
        h = 4
        em.tt(out[:, 0:h, :], z[:, 0:h, :], mb[:, 0:h, :], ALU.subtract, E='pool')
        em.tt(out[:, h:8, :], z[:, h:8, :], mb[:, h:8, :], ALU.subtract, E='dve')
        em.tt(out[:, 0:h, :], out[:, 0:h, :], rb[:, 0:h, :], ALU.mult, E='dve')
        em.tt(out[:, h:8, :], out[:, h:8, :], rb[:, h:8, :], ALU.mult, E='pool')
        for j in range(8):
            em.act(out[:, j, :], out[:, j, :], AF.Identity, bias=bvec[:, j:j + 1], scale=g[:, j:j + 1])


def make_gb(em, g, bvec, N):
    Gt = em.sb([128, 8, N], F32, "Gt")
    Bt = em.sb([128, 8, N], F32, "Bt")
    em.memset(Gt, 1.0)
    em.memset(Bt, 0.0)
    for j in range(8):
        em.ts(Gt[:, j, :], Gt[:, j, :], g[:, j:j + 1], None, op0=ALU.mult, E='pool')
        em.ts(Bt[:, j, :], Bt[:, j, :], bvec[:, j:j + 1], None, op0=ALU.add, E='pool')
    return Gt, Bt


def build_ffn(ntok=TPC):
    NT = 256
    nc = new_nc()
    with ExitStack() as st:
        em = Em(nc, st)
        xT = em.dram("xT", [D, ntok], F32, "ExternalInput")
        w_in = em.dram("w_in", [D, 2 * DFF], F32, "ExternalInput")
        w_out = em.dram("w_out", [DFF, D], F32, "ExternalInput")
        scal = em.dram("scal", [128, 40], F32, "ExternalInput")
        yT = em.dram("yT", [D, ntok], F32, "ExternalOutput")

        sc = em.sb([128, 40], F32, "sc")
        em.dma(sc, scal)
        sc2p = em.sb([128, 8], F32, "sc2p")
        gs = em.sb([128, 8], F32, "gs")
        em.ts(sc2p, sc[:, 0:8], 1.0, None, op0=ALU.add)
        em.ts(gs, sc[:, 16:24], 1.0, 1.0 / ALPHA, op0=ALU.add, op1=ALU.mult)
        ones_bf = em.sb([128, 128], BF16, "ones")
        em.memset(ones_bf, 1.0)

        win = [em.sb([128, 2 * DFF], BF16, "win%d" % k) for k in range(8)]
        wout = [em.sb([128, D], BF16, "wout%d" % m) for m in range(22)]
        for k in range(8):
            em.dma(win[k], w_in[k * 128:(k + 1) * 128, :], Q='pool')
        for m in range(22):
            em.dma(wout[m], w_out[m * 128:(m + 1) * 128, :], Q='pool')

        xt_r = Rot([em.sb([128, 8, NT], F32, "xt") for _ in range(2)])
        ub_r = Rot([em.sb([128, 8, NT], BF16, "ub") for _ in range(2)])
        hT = em.sb([128, 22, NT], BF16, "hT")
        sg_r = Rot([em.sb([128, NT], F32, "sg") for _ in range(3)])
        z_r = Rot([em.sb([128, 8, NT], F32, "z") for _ in range(2)])
        zb_r = Rot([em.sb([128, 8, NT], BF16, "zb") for _ in range(2)])
        zsq_r = Rot([em.sb([128, 8, NT], BF16, "zsq") for _ in range(2)])
        mean = em.sb([128, NT], F32, "mean")
        rstd = em.sb([128, NT], F32, "rstd")
        t1 = em.sb([128, NT], F32, "t1")
        pbank = Rot([em.ps([128, 512], F32, "pb") for _ in range(6)])
        ps1 = em.ps([128, 512], F32, "ps1")
        ps2 = em.ps([128, 512], F32, "ps2")

        xv = xT[:].ap.rearrange("(c p) t -> p c t", p=128)
        yv = yT[:].ap.rearrange("(c p) t -> p c t", p=128)
        def load_tile(tt):
            xt = xt_r.next()
            ub = ub_r.next()
            em.dma(xt, V(xT, xv[:, :, tt * NT:(tt + 1) * NT]))
            for c in range(8):
                em.ts(ub[:, c, :], xt[:, c, :], sc2p[:, c:c + 1], sc[:, 8 + c:9 + c], op0=ALU.mult, op1=ALU.add,
                      E=('dve' if c % 2 == 0 else 'pool'))
            return xt, ub

        nxt_tile = load_tile(0)
        for tt in range(ntok // NT):
            tsl = slice(tt * NT, (tt + 1) * NT)
            xt, ub = nxt_tile
            z, zb, zsq = z_r.next(), zb_r.next(), zsq_r.next()
            for m in range(22):
                pg = pbank.next()
                pu = pbank.next()
                for k in range(8):
                    em.mm(pg[:, 0:NT], win[k][:, m * 128:(m + 1) * 128], ub[:, k, :], start=(k == 0), stop=(k == 7))
                for k in range(8):
                    em.mm(pu[:, 0:NT], win[k][:, DFF + m * 128:DFF + (m + 1) * 128], ub[:, k, :],
                          start=(k == 0), stop=(k == 7))
                sg = sg_r.next()
                em.act(sg, pg[:, 0:NT], AF.Silu)
                em.tt(hT[:, m, :], sg, pu[:, 0:NT], ALU.mult)
            for j in range(8):
                py = pbank.next()
                for m in range(22):
                    em.mm(py[:, 0:NT], wout[m][:, j * 128:(j + 1) * 128], hT[:, m, :], start=(m == 0), stop=(m == 21))
                em.stt(z[:, j, :], py[:, 0:NT], gs[:, j:j + 1], xt[:, j, :], ALU.mult, ALU.add)
            if tt + 1 < ntok // NT:
                nxt_tile = load_tile(tt + 1)
            emit_ln(em, z, NT, ones_bf, sc[:, 24:32], sc[:, 32:40], z, ps1[:, 0:NT], ps2[:, 0:NT],
                    (zb, zsq, mean, rstd, t1))
            em.dma(V(yT, yv[:, :, tsl]), z, Q='pool')
        em.finish()
    return nc


def chunk_masks():
    i = np.arange(128)
    same = (i[:, None] // 64) == (i[None, :] // 64)
    mS = (same & (i[:, None] < i[None, :])).astype(np.float32)
    mI = (same & (i[:, None] <= i[None, :])).astype(np.float32)
    mL = mS.T.copy()
    return mS, mI, mL


def build_rwkv_scan(Tn=T, SEG=1024, PI=2):
    NP = SEG // 128
    NCH = SEG // 64
    nc = new_nc()
    with ExitStack() as st:
        em = Em(nc, st)
        din = {n: em.dram(n, [2, 128, Tn], F32, "ExternalInput") for n in ("r", "k", "kk", "a", "lw")}
        vin = em.dram("v", [Tn, 256], BF16, "ExternalInput")
        cst = em.dram("cst", [128, 128 * 5], F32, "ExternalInput")
        yT = em.dram("yT", [2, 128, Tn], F32, "ExternalOutput")

        cf = em.sb([128, 640], F32, "cf")
        em.dma(cf, cst)
        mSI = cf[:, 0:256]
        mL = cf[:, 256:384]
        If = cf[:, 384:512]
        Ib = em.sb([128, 128], BF16, "Ib")
        em.copy(Ib, cf[:, 384:512])
        reset = em.sb([128, SEG], F32, "reset")
        for q in range(SEG // 128):
            em.copy(reset[:, q * 128:(q + 1) * 128], cf[:, 512:640], E='pool')

        STh = [em.sb([64, 64], F32, "ST%d" % h) for h in range(4)]
        for h in range(4):
            em.memset(STh[h], 0.0)
        dW_r = Rot([em.sb([64, 64], F32, "dW") for _ in range(8)])

        banks = Rot([em.ps([128, 512], F32, "bk") for _ in range(8)])

        def seg_tiles(nm, dt=F32):
            return [em.sb([128, SEG], dt, nm + "%d" % hp) for hp in range(2)]
        tin = {n: seg_tiles("in_" + n) for n in din}
        cum = seg_tiles("cum")
        tmp = seg_tiles("tmp")
        tmp2 = seg_tiles("tmp2")
        kka = seg_tiles("kka")
        at = seg_tiles("at", BF16)
        bt = seg_tiles("bt", BF16)
        kt = seg_tiles("kt", BF16)
        rt = seg_tiles("rt", BF16)
        bh = seg_tiles("bh", BF16)
        kh = seg_tiles("kh", BF16)
        WC = [em.sb([128, NCH], F32, "WC%d" % hp) for hp in range(2)]
        vt = em.sb([128, NP, 256], BF16, "vt")
        yo = [em.sb([128, SEG], F32, "yo%d" % hp) for hp in range(2)]

        def rot(nm, shape, dt, n):
            return Rot([em.sb(shape, dt, nm) for _ in range(n)])
        NI = 4 * PI
        AB_r = rot("AB", [128, 256], BF16, 2 * NI)
        AK_r = rot("AK", [128, 256], BF16, 2 * NI)
        N_r = rot("N", [128, 128], BF16, 2 * NI + 4)
        Z_r = rot("Z", [128, 128], BF16, 2 * NI + 4)
        IZ_r = rot("IZ", [128, 128], BF16, 2 * NI + 4)
        X_r = rot("X", [128, 128], BF16, 2 * NI + 4)
        BK_r = rot("BK", [128, 128], BF16, NI + 4)
        Q1_r = rot("Q1", [64, 128], F32, NI + 4)
        Q2_r = rot("Q2", [64, 128], F32, NI + 4)
        MT_r = rot("MT", [64, 128], F32, NI + 4)
        G_r = rot("G", [64, 128], F32, NI + 4)

        for seg in range(Tn // SEG):
            s0 = seg * SEG
            for hp in range(2):
                for n in din:
                    em.dma(tin[n][hp], din[n][hp, :, s0:s0 + SEG])
            em.dma(vt, V(vin, vin[s0:s0 + SEG, :].ap.rearrange("(p t) c -> t p c", t=128)))
            for hp in range(2):
                r_, k_, kk_, a_, lw_ = (tin[n][hp] for n in ("r", "k", "kk", "a", "lw"))
                c_ = cum[hp]
                em.op('dve', lambda e: e.tensor_tensor_scan(out=c_.ap, data0=reset.ap, data1=lw_.ap, initial=0.0,
                                                            op0=ALU.mult, op1=ALU.add),
                      reads=[reset, lw_], writes=[c_])
                em.act(tmp[hp], c_, AF.Exp)
                em.tt(rt[hp], r_, tmp[hp], ALU.mult, E='pool')
                em.act(tmp2[hp], c_, AF.Exp, scale=-1.0)
                em.tt(kka[hp], kk_, a_, ALU.mult, E='pool')
                em.tt(bt[hp], kka[hp], tmp2[hp], ALU.mult)
                em.tt(kt[hp], k_, tmp2[hp], ALU.mult, E='pool')
                em.tt(tmp[hp], c_, lw_, ALU.subtract)
                em.act(tmp[hp], tmp[hp], AF.Exp)
                em.stt(at[hp], kk_, -1.0, tmp[hp], ALU.mult, ALU.mult)
                c3 = c_[:].re("p (c t) -> p c t", t=64)
                cC = c3[:, :, 63:64]
                em.tt(tmp2[hp][:].re("p (c t) -> p c t", t=64), cC.bc([128, NCH, 64]), c3, ALU.subtract, E='pool')
                em.act(tmp2[hp], tmp2[hp], AF.Exp)
                em.tt(bh[hp], kka[hp], tmp2[hp], ALU.mult)
                em.tt(kh[hp], k_, tmp2[hp], ALU.mult, E='pool')
                em.act(WC[hp][:].re("p (c o) -> p c o", o=1), cC, AF.Exp)

            for p0 in range(0, NP, PI):
                items = [(p, h4 // 2, slice(64 * (h4 % 2), 64 * (h4 % 2) + 64), h4)
                         for p in range(p0, min(NP, p0 + PI)) for h4 in range(4)]
                AB, AK, Nn, Z, IZ, X = {}, {}, {}, {}, {}, {}
                Q1d, Q2d, MTd, Gd = {}, {}, {}, {}
                for p, hp, ps, h4 in items:
                    key = (p, h4)
                    tsl = slice(p * 128, (p + 1) * 128)
                    pa = banks.next()
                    em.mm(pa[:, 0:128], bt[hp][ps, tsl], at[hp][ps, tsl])
                    em.mm(pa[:, 128:256], bt[hp][ps, tsl], rt[hp][ps, tsl])
                    em.mm(pa[:, 256:384], kt[hp][ps, tsl], at[hp][ps, tsl])
                    em.mm(pa[:, 384:512], kt[hp][ps, tsl], rt[hp][ps, tsl])
                    AB[key] = AB_r.next()
                    AK[key] = AK_r.next()
                    em.tt(AB[key], pa[:, 0:256], mSI, ALU.mult)
                    em.tt(AK[key], pa[:, 256:512], mSI, ALU.mult)
                    pn = banks.next()
                    em.mm(pn[:, 0:128], at[hp][ps, tsl], bt[hp][ps, tsl])
                    em.mm(pn[:, 128:192], at[hp][ps, tsl], Ib[ps, ps])
                    pn2 = banks.next()
                    em.mm(pn2[:, 0:64], AK[key][:, 0:128], vt[:, p, h4 * 64:(h4 + 1) * 64])
                    Nn[key] = N_r.next()
                    em.tt(Nn[key], pn[:, 0:128], mL, ALU.mult)
                    Z[key] = AB[key][:, 0:128]
                    IZ[key] = IZ_r.next()
                    em.tt(IZ[key], AB[key][:, 0:128], Ib, ALU.add, E='pool')
                    X[key] = X_r.next()
                    em.copy(X[key][:, 0:64], pn[:, 128:192], E='act')
                    em.copy(X[key][:, 64:128], pn2[:, 0:64], E='act')
                for j in range(6):
                    for p, hp, ps, h4 in items:
                        key = (p, h4)
                        px = banks.next()
                        em.mm(px[:, 0:128], IZ[key], X[key])
                        if j < 5:
                            em.mm(px[:, 128:256], Nn[key], Z[key])
                            if j < 4:
                                em.mm(px[:, 256:384], Z[key], Nn[key])
                        X[key] = X_r.next()
                        em.copy(X[key], px[:, 0:128], E='act')
                        if j < 5:
                            Zn = Z_r.next()
                            IZ[key] = IZ_r.next()
                            em.tt(IZ[key], px[:, 128:256], Ib, ALU.add)
                            if j < 4:
                                em.copy(Zn, px[:, 128:256], E='act')
                                Nx = N_r.next()
                                em.copy(Nx, px[:, 256:384], E='dve')
                                Nn[key] = Nx
                                Z[key] = Zn
                for p, hp, ps, h4 in items:
                    key = (p, h4)
                    tsl = slice(p * 128, (p + 1) * 128)
                    pb = banks.next()
                    em.mm(pb[:, 0:64], bh[hp][ps, tsl], Ib[ps, ps])
                    em.mm(pb[:, 64:128], kh[hp][ps, tsl], Ib[ps, ps])
                    em.mm(pb[0:64, 128:256], X[key][:, 0:64], AB[key][:, 128:256], start=True, stop=False)
                    em.mm(pb[0:64, 128:256], Ib[ps, ps], rt[hp][ps, tsl], start=False, stop=True)
                    em.mm(pb[0:64, 256:384], X[key][:, 64:128], AB[key][:, 128:256], start=True, stop=False)
                    em.mm(pb[0:64, 256:384], vt[:, p, h4 * 64:(h4 + 1) * 64], AK[key][:, 128:256], start=False, stop=True)
                    BK = BK_r.next()
                    em.copy(BK, pb[:, 0:128], E='act')
                    Q1d[key] = Q1_r.next()
                    Q2d[key] = Q2_r.next()
                    em.copy(Q1d[key], pb[0:64, 128:256], E='dve')
                    em.copy(Q2d[key], pb[0:64, 256:384], E='act')
                    pms = [banks.next(), banks.next()]
                    for c in range(2):
                        cs = slice(64 * c, 64 * c + 64)
                        pm = pms[c]
                        em.mm(pm[0:64, 0:64], X[key][cs, 0:64], BK[cs, 0:64])
                        em.mm(pm[0:64, 64:128], BK[cs, 0:64], X[key][cs, 64:128], start=True, stop=False)
                        em.mm(pm[0:64, 64:128], BK[cs, 64:128], vt[cs, p, h4 * 64:(h4 + 1) * 64], start=False, stop=True)
                    MTd[key] = MT_r.next()
                    Gd[key] = G_r.next()
                    for c in range(2):
                        ch = p * 2 + c
                        dW = dW_r.next()
                        em.ts(dW, If[ps, ps], WC[hp][ps, ch:ch + 1], None, op0=ALU.mult, E='pool')
                        em.tt(MTd[key][:, 64 * c:64 * c + 64], pms[c][0:64, 0:64], dW, ALU.add)
                        em.copy(Gd[key][:, 64 * c:64 * c + 64], pms[c][0:64, 64:128], E='dve')
                for p in range(p0, min(NP, p0 + PI)):
                    for c in range(2):
                        for h4 in range(4):
                            key = (p, h4)
                            hp, ps = h4 // 2, slice(64 * (h4 % 2), 64 * (h4 % 2) + 64)
                            pc = banks.next()
                            em.mm(pc[0:64, 0:64], STh[h4], Q1d[key][:, 64 * c:64 * c + 64])
                            em.mm(pc[0:64, 64:128], MTd[key][:, 64 * c:64 * c + 64], STh[h4])
                            em.tt(yo[hp][ps, p * 128 + 64 * c:p * 128 + 64 * c + 64], pc[0:64, 0:64],
                                  Q2d[key][:, 64 * c:64 * c + 64], ALU.add)
                            em.tt(STh[h4], pc[0:64, 64:128], Gd[key][:, 64 * c:64 * c + 64], ALU.add)
            for hp in range(2):
                em.dma(yT[hp, :, s0:s0 + SEG], yo[hp])
        em.finish()
    return nc


def build_rwkv_pre(ntok=TPC):
    NT = 256
    nc = new_nc()
    with ExitStack() as st:
        em = Em(nc, st)
        xT = em.dram("xT", [D, ntok + 1], F32, "ExternalInput")
        scal = em.dram("scal", [128, 105], F32, "ExternalInput")
        cst = em.dram("cst", [128, 128], F32, "ExternalInput")
        w_rkv = em.dram("w_rkv", [3, D, D], F32, "ExternalInput")
        w1 = em.dram("w1", [D, 64], F32, "ExternalInput")
        a1 = em.dram("a1", [D, 64], F32, "ExternalInput")
        g1 = em.dram("g1", [D, 128], F32, "ExternalInput")
        w2 = em.dram("w2", [64, D], F32, "ExternalInput")
        a2 = em.dram("a2", [64, D], F32, "ExternalInput")
        g2 = em.dram("g2", [128, D], F32, "ExternalInput")
        outs = {n: em.dram(n, [D, ntok], F32, "ExternalOutput") for n in ("r", "k", "kk", "a", "lw")}
        outs["g"] = em.dram("g", [D, ntok], BF16, "ExternalOutput")
        outs["bonus"] = em.dram("bonus", [D, ntok], BF16, "ExternalOutput")
        vtok = em.dram("vtok", [ntok, D], BF16, "ExternalOutput")

        sc = em.sb([128, 105], F32, "sc")
        em.dma(sc, scal)
        col = lambda i: sc[:, 8 * i:8 * i + 8]
        sc1p = em.sb([128, 8], F32, "sc1p")
        em.ts(sc1p, col(0), 1.0, None, op0=ALU.add)
        sh1, w0, a0, k_k, k_a, r_k = col(1), col(8), col(9), col(10), col(11), col(12)
        hv = sc[:, 104:105]
        bd_f = em.sb([128, 128], F32, "bd_f")
        em.dma(bd_f, cst)
        bd = em.sb([128, 128], BF16, "bd")
        em.copy(bd, bd_f)

        W = [[em.sb([128, D], BF16, "W%d_%d" % (i, k)) for k in range(8)] for i in range(3)]
        for i in range(3):
            for k in range(8):
                em.dma(W[i][k], w_rkv[i, k * 128:(k + 1) * 128, :], Q='pool')
        w1s = em.sb([128, 8, 64], BF16, "w1s")
        a1s = em.sb([128, 8, 64], BF16, "a1s")
        g1s = em.sb([128, 8, 128], BF16, "g1s")
        em.dma(w1s, V(w1, w1[:].ap.rearrange("(k p) n -> p k n", p=128)), Q='pool')
        em.dma(a1s, V(a1, a1[:].ap.rearrange("(k p) n -> p k n", p=128)), Q='pool')
        em.dma(g1s, V(g1, g1[:].ap.rearrange("(k p) n -> p k n", p=128)), Q='pool')
        Wm = [[em.sb([128, D], BF16, "Wm%d_%d" % (i, k)) for k in range(8)] for i in range(3)]
        for i in range(3):
            for k in range(8):
                em.ts(Wm[i][k], W[i][k], sc[:, 8 * (2 + i) + k:8 * (2 + i) + k + 1], None, op0=ALU.mult,
                      E=('dve' if k % 2 == 0 else 'pool'))
        w1m = em.sb([128, 8, 64], BF16, "w1m")
        a1m = em.sb([128, 8, 64], BF16, "a1m")
        g1m = em.sb([128, 8, 128], BF16, "g1m")
        for k in range(8):
            em.ts(w1m[:, k, :], w1s[:, k, :], sc[:, 8 * 5 + k:8 * 5 + k + 1], None, op0=ALU.mult, E='pool')
            em.ts(a1m[:, k, :], a1s[:, k, :], sc[:, 8 * 6 + k:8 * 6 + k + 1], None, op0=ALU.mult, E='pool')
            em.ts(g1m[:, k, :], g1s[:, k, :], sc[:, 8 * 7 + k:8 * 7 + k + 1], None, op0=ALU.mult, E='pool')
        w2s = em.sb([64, D], BF16, "w2s")
        a2s = em.sb([64, D], BF16, "a2s")
        g2s = em.sb([128, D], BF16, "g2s")
        em.dma(w2s, w2, Q='pool')
        em.dma(a2s, a2, Q='pool')
        em.dma(g2s, g2, Q='pool')

        xt_r = Rot([em.sb([128, 8, NT + 1], F32, "xt") for _ in range(2)])
        u = em.sb([128, 8, NT + 1], F32, "u")
        xxb = em.sb([128, 8, NT], BF16, "xxb")
        ubf = em.sb([128, 8, NT], BF16, "ubf")

        def acc(out, wsel, wmsel, n0=0, n1=NT):
            for k in range(8):
                em.mm(out, wsel(k), ubf[:, k, n0:n1], start=(k == 0), stop=False)
            for k in range(8):
                em.mm(out, wmsel(k), xxb[:, k, n0:n1], start=False, stop=(k == 7))
        hw = em.sb([64, NT], BF16, "hw")
        ha = em.sb([64, NT], BF16, "ha")
        hg = em.sb([128, NT], BF16, "hg")
        banks = Rot([em.ps([128, 512], F32, "bk") for _ in range(8)])

        def rot(nm, dt, n=2):
            return Rot([em.sb([128, NT], dt, nm) for _ in range(n)])
        r_r, k_r, v_r, kp_r, kk_r, a_r, lw_r = (rot(n, F32, 3) for n in ("r", "k", "v", "kp", "kk", "a", "lw"))
        g_r, bo_r, sq_r, t_r = rot("g", BF16, 3), rot("bo", BF16, 3), rot("sq", BF16, 3), rot("t", F32, 8)
        tb_r = rot("tb", BF16, 3)
        vt_r = Rot([em.sb([128, D], BF16, "vt") for _ in range(2)])

        xv = xT[:].ap.rearrange("(c p) t -> p c t", p=128)
        def load_x(tt_):
            xt_ = xt_r.next()
            em.dma(xt_, V(xT, xv[:, :, tt_ * NT:tt_ * NT + NT + 1]))
            return xt_

        xt_next = load_x(0)
        for tt in range(ntok // NT):
            t0 = tt * NT
            xt = xt_next
            for c in range(8):
                em.ts(u[:, c, :], xt[:, c, :], sc1p[:, c:c + 1], sh1[:, c:c + 1], op0=ALU.mult, op1=ALU.add,
                      E=('dve' if c % 2 == 0 else 'pool'))
            if tt + 1 < ntok // NT:
                xt_next = load_x(tt + 1)
            if tt == 0:
                em.ts(u[:, :, 0:1], u[:, :, 0:1], hv, None, op0=ALU.mult)
            h4 = 4
            em.tt(xxb[:, 0:h4, :], u[:, 0:h4, 0:NT], u[:, 0:h4, 1:NT + 1], ALU.subtract, E='pool')
            em.tt(xxb[:, h4:8, :], u[:, h4:8, 0:NT], u[:, h4:8, 1:NT + 1], ALU.subtract, E='dve')
            em.copy(ubf[:, 0:h4, :], u[:, 0:h4, 1:NT + 1], E='dve')
            em.copy(ubf[:, h4:8, :], u[:, h4:8, 1:NT + 1], E='act')
            p1 = banks.next()
            acc(p1[0:64, 0:NT], lambda k: w1s[:, k, :], lambda k: w1m[:, k, :])
            em.act(hw, p1[0:64, 0:NT], AF.Tanh)
            p2 = banks.next()
            acc(p2[0:64, 0:NT], lambda k: a1s[:, k, :], lambda k: a1m[:, k, :])
            em.copy(ha, p2[0:64, 0:NT], E='dve')
            p3 = banks.next()
            acc(p3[:, 0:NT], lambda k: g1s[:, k, :], lambda k: g1m[:, k, :])
            em.act(hg, p3[:, 0:NT], AF.Sigmoid)
            for tb in range(NT // 128):
                vt = vt_r.next()
                for half in range(2):
                    pv = banks.next()
                    for k in range(8):
                        em.mm(pv[:, 0:512], ubf[:, k, tb * 128:(tb + 1) * 128], W[2][k][:, half * 512:(half + 1) * 512],
                              start=(k == 0), stop=False)
                    for k in range(8):
                        em.mm(pv[:, 0:512], xxb[:, k, tb * 128:(tb + 1) * 128], Wm[2][k][:, half * 512:(half + 1) * 512],
                              start=False, stop=(k == 7))
                    em.copy(vt[:, half * 512:(half + 1) * 512], pv[:, 0:512], E=('act' if half == 0 else 'dve'))
                em.dma(vtok[t0 + tb * 128:t0 + (tb + 1) * 128, :], vt)
            def stage1(j):
                js = slice(j * 128, (j + 1) * 128)
                jc = slice(j, j + 1)
                pr, pk, pv = banks.next(), banks.next(), banks.next()
                for (pp, i) in ((pr, 0), (pk, 1), (pv, 2)):
                    acc(pp[:, 0:NT], lambda k, i=i: W[i][k][:, js], lambda k, i=i: Wm[i][k][:, js])
                r_, k_, v_ = r_r.next(), k_r.next(), v_r.next()
                em.copy(r_, pr[:, 0:NT], E='act')
                em.copy(k_, pk[:, 0:NT], E='dve')
                em.copy(v_, pv[:, 0:NT], E='act')
                pl = banks.next()
                em.mm(pl[:, 0:NT], w2s[:, js], hw)
                lw_ = lw_r.next()
                em.act(lw_, pl[:, 0:NT], AF.Sigmoid, bias=w0[:, jc])
                em.ts(lw_, lw_, -0.6065306597126334, None, op0=ALU.mult, E='pool')
                pa = banks.next()
                em.mm(pa[:, 0:NT], a2s[:, js], ha)
                a_ = a_r.next()
                em.act(a_, pa[:, 0:NT], AF.Sigmoid, bias=a0[:, jc])
                pg = banks.next()
                em.mm(pg[:, 0:NT], g2s[:, js], hg)
                g_ = g_r.next()
                em.copy(g_, pg[:, 0:NT], E='dve')
                kkr = t_r.next()
                em.ts(kkr, k_, k_k[:, jc], None, op0=ALU.mult, E='pool')
                sq = sq_r.next()
                em.act(sq, kkr, AF.Square)
                tk = t_r.next()
                em.ts(tk, a_, -1.0, k_a[:, jc], op0=ALU.add, op1=ALU.mult, E='pool')
                kp = kp_r.next()
                em.stt(kp, tk, 1.0, k_, ALU.add, ALU.mult)
                tb_ = tb_r.next()
                em.stt(tb_, r_, r_k[:, jc], kp, ALU.mult, ALU.mult)
                return dict(js=js, r_=r_, v_=v_, lw_=lw_, a_=a_, g_=g_, kkr=kkr, sq=sq, kp=kp, tb_=tb_)

            def stage2(d):
                ps_ = banks.next()
                em.mm(ps_[:, 0:NT], bd, d["sq"])
                rn = t_r.next()
                em.ts(rn, ps_[:, 0:NT], 1e-6, None, op0=ALU.add)
                em.act(rn, rn, AF.Sqrt)
                em.recip(rn, rn)
                kk_ = kk_r.next()
                em.tt(kk_, d["kkr"], rn, ALU.mult, E='pool')
                pb = banks.next()
                em.mm(pb[:, 0:NT], bd, d["tb_"])
                bo = bo_r.next()
                em.tt(bo, pb[:, 0:NT], d["v_"], ALU.mult)
                for (nm, tl) in (("r", d["r_"]), ("k", d["kp"]), ("kk", kk_), ("a", d["a_"]), ("lw", d["lw_"]),
                                 ("g", d["g_"]), ("bonus", bo)):
                    em.dma(outs[nm][d["js"], t0:t0 + NT], tl)

            prev = stage1(0)
            for j in range(8):
                nxt = stage1(j + 1) if j + 1 < 8 else None
                stage2(prev)
                prev = nxt
        em.finish()
    return nc


def build_post(mode, ntok=TPC, hscale=1.0, has_bias=True):
    NT = 256
    nc = new_nc()
    with ExitStack() as st:
        em = Em(nc, st)
        xT = em.dram("xT", [D, ntok], F32, "ExternalInput")
        w_o = em.dram("w_o", [D, D], F32, "ExternalInput")
        scal = em.dram("scal", [128, 56], F32, "ExternalInput")
        if mode == 'rwkv':
            yin = em.dram("yin", [D, ntok], F32, "ExternalInput")
            gin = em.dram("g", [D, ntok], BF16, "ExternalInput")
            bin_ = em.dram("bonus", [D, ntok], BF16, "ExternalInput")
            cst = em.dram("cst", [128, 128], F32, "ExternalInput")
        elif mode in ('gdn', 'diff'):
            yin = em.dram("yin", [D, ntok], F32, "ExternalInput")
            if mode == 'gdn':
                zin = em.dram("zs", [D, ntok], BF16, "ExternalInput")
        else:
            oin = em.dram("oT", [D, ntok], BF16, "ExternalInput")
        yT = em.dram("yT", [D, ntok], F32, "ExternalOutput")

        sc = em.sb([128, 56], F32, "sc")
        em.dma(sc, scal)
        col = lambda i: sc[:, 8 * i:8 * i + 8]
        gs = em.sb([128, 8], F32, "gs")
        em.ts(gs, col(0), 1.0, 1.0 / ALPHA, op0=ALU.add, op1=ALU.mult)
        gsb = em.sb([128, 8], F32, "gsb")
        em.tt(gsb, gs, col(3), ALU.mult)
        if hscale != 1.0:
            em.ts(sc[:, 32:33], sc[:, 32:33], float(hscale), None, op0=ALU.mult)
        ones_bf = em.sb([128, 128], BF16, "ones")
        em.memset(ones_bf, 1.0)
        if mode == 'rwkv':
            bd_f = em.sb([128, 128], F32, "bd_f")
            em.dma(bd_f, cst)
            bd = em.sb([128, 128], BF16, "bd")
            em.copy(bd, bd_f)
        Wo = [em.sb([128, D], BF16, "Wo%d" % k) for k in range(8)]
        for k in range(8):
            em.dma(Wo[k], w_o[k * 128:(k + 1) * 128, :], Q='pool')

        xt_r = Rot([em.sb([128, 8, NT], F32, "xt") for _ in range(2)])
        ot_r = Rot([em.sb([128, 8, NT], BF16, "ot") for _ in range(2)])
        z_r = Rot([em.sb([128, 8, NT], F32, "z") for _ in range(2)])
        zb_r = Rot([em.sb([128, 8, NT], BF16, "zb") for _ in range(2)])
        zsq_r = Rot([em.sb([128, 8, NT], BF16, "zsq") for _ in range(2)])
        mean = em.sb([128, NT], F32, "mean")
        rstd = em.sb([128, NT], F32, "rstd")
        t1 = em.sb([128, NT], F32, "t1")
        pbank = Rot([em.ps([128, 512], F32, "pb") for _ in range(2)])
        ps1 = em.ps([128, 512], F32, "ps1")
        ps2 = em.ps([128, 512], F32, "ps2")
        if mode != 'plain':
            pm2 = em.ps([128, 4, NT], F32, "pm2")
            pq2 = em.ps([128, 4, NT], F32, "pq2")
            hb = lambda nm, dt=F32: Rot([em.sb([128, 4, NT], dt, nm) for _ in range(2)])
            gm_h, gr_h, g1_h = hb("gm_h"), hb("gr_h"), hb("g1_h")
            ybig = em.sb([128, 8, NT], BF16, "ybig")
            ysqb = em.sb([128, 8, NT], BF16, "ysqb")
        if mode in ('gdn', 'diff'):
            yt_r = Rot([em.sb([128, 8, NT], F32, "yt") for _ in range(2)])
            zt_r = Rot([em.sb([128, 8, NT], BF16, "zt") for _ in range(2)])
            ysq_r = Rot([em.sb([128, NT], BF16, "ysq") for _ in range(2)])
            gr_r = Rot([em.sb([128, NT], F32, "gr") for _ in range(2)])
            gt1_r = Rot([em.sb([128, NT], F32, "gt1") for _ in range(2)])
        if mode == 'rwkv':
            yt_r = Rot([em.sb([128, 8, NT], F32, "yt") for _ in range(2)])
            gt_r = Rot([em.sb([128, 8, NT], BF16, "gt") for _ in range(2)])
            bt_r = Rot([em.sb([128, 8, NT], BF16, "bt") for _ in range(2)])
            yb_r = Rot([em.sb([128, NT], BF16, "yb") for _ in range(2)])
            ysq_r = Rot([em.sb([128, NT], BF16, "ysq") for _ in range(2)])
            gm_r = Rot([em.sb([128, NT], F32, "gm") for _ in range(2)])
            gr_r = Rot([em.sb([128, NT], F32, "gr") for _ in range(2)])
            gt1_r = Rot([em.sb([128, NT], F32, "gt1") for _ in range(2)])

        fm = lambda tl: tl[:].ap.rearrange("(c p) t -> p c t", p=128)
        xv, yv = fm(xT), fm(yT)
        for tt in range(ntok // NT):
            tsl = slice(tt * NT, (tt + 1) * NT)
            xt = xt_r.next()
            ot = ot_r.next()
            z, zb, zsq = z_r.next(), zb_r.next(), zsq_r.next()
            em.dma(xt, V(xT, xv[:, :, tsl]))
            if mode == 'rwkv':
                yt, gt, bt = yt_r.next(), gt_r.next(), bt_r.next()
                em.dma(yt, V(yin, fm(yin)[:, :, tsl]))
                em.dma(gt, V(gin, fm(gin)[:, :, tsl]))
                em.dma(bt, V(bin_, fm(bin_)[:, :, tsl]))
                em.copy(ybig, yt, E='pool')
                em.act(ysqb, yt, AF.Square)
                for hf in range(2):
                    hs_ = slice(4 * hf, 4 * hf + 4)
                    for j in range(4):
                        em.mm(pm2[:, j, :], bd, ybig[:, 4 * hf + j, :])
                    for j in range(4):
                        em.mm(pq2[:, j, :], bd, ysqb[:, 4 * hf + j, :])
                    gm, gr, g1 = gm_h.next(), gr_h.next(), g1_h.next()
                    em.ts(gm, pm2, 1.0 / 64, None, op0=ALU.mult)
                    em.tt(g1, gm, gm, ALU.mult, E='pool')
                    em.stt(gr, pq2, 1.0 / 64, g1, ALU.mult, ALU.subtract)
                    em.ts(gr, gr, 64e-5, None, op0=ALU.add, E='pool')
                    em.act(gr, gr, AF.Sqrt)
                    em.recip(gr, gr)
                    em.tt(g1, yt[:, hs_, :], gm, ALU.subtract, E='pool')
                    em.tt(g1, g1, gr, ALU.mult)
                    for j in range(4):
                        jj = 4 * hf + j
                        em.act(g1[:, j, :], g1[:, j, :], AF.Identity, bias=sc[:, 40 + jj:41 + jj], scale=sc[:, 32 + jj:33 + jj])
                    em.tt(g1, g1, bt[:, hs_, :], ALU.add)
                    em.tt(ot[:, hs_, :], g1, gt[:, hs_, :], ALU.mult, E='pool')
            elif mode in ('gdn', 'diff'):
                yt = yt_r.next()
                em.dma(yt, V(yin, fm(yin)[:, :, tsl]))
                if mode == 'gdn':
                    zt = zt_r.next()
                    em.dma(zt, V(zin, fm(zin)[:, :, tsl]))
                heps = 1e-6 if mode == 'gdn' else 1e-5
                em.act(ysqb, yt, AF.Square)
                for hf in range(2):
                    hs_ = slice(4 * hf, 4 * hf + 4)
                    for j in range(4):
                        em.mm(pq2[:, j, :], ones_bf, ysqb[:, 4 * hf + j, :])
                    gr, g1 = gr_h.next(), g1_h.next()
                    em.ts(gr, pq2, 1.0 / 128, heps, op0=ALU.mult, op1=ALU.add)
                    em.act(gr, gr, AF.Sqrt)
                    em.recip(gr, gr)
                    if mode == 'gdn':
                        em.stt(g1, yt[:, hs_, :], sc[:, 32:33], gr, ALU.mult, ALU.mult)
                        em.tt(ot[:, hs_, :], g1, zt[:, hs_, :], ALU.mult, E='pool')
                    else:
                        em.stt(ot[:, hs_, :], yt[:, hs_, :], sc[:, 32:33], gr, ALU.mult, ALU.mult)
            else:
                em.dma(ot, V(oin, fm(oin)[:, :, tsl]))
            for j in range(8):
                py = pbank.next()
                for k in range(8):
                    em.mm(py[:, 0:NT], Wo[k][:, j * 128:(j + 1) * 128], ot[:, k, :], start=(k == 0), stop=(k == 7))
                em.stt(z[:, j, :], py[:, 0:NT], gs[:, j:j + 1], xt[:, j, :], ALU.mult, ALU.add)
                if has_bias:
                    em.ts(z[:, j, :], z[:, j, :], gsb[:, j:j + 1], None, op0=ALU.add, E='pool')
            emit_ln(em, z, NT, ones_bf, col(1), col(2), z, ps1[:, 0:NT], ps2[:, 0:NT], (zb, zsq, mean, rstd, t1))
            em.dma(V(yT, yv[:, :, tsl]), z, Q='pool')
        em.finish()
    return nc


def gdn_level_masks():
    i = np.arange(128)
    out = []
    for li in range(7):
        m = 1 << li
        out.append((((i[:, None] // (2 * m)) == (i[None, :] // (2 * m))) &
                    ((i[:, None] // m) < (i[None, :] // m))).astype(np.float32))
    return out


def build_gdn_scan(Tn=T, SEG=1024, PI=4):
    NCK = SEG // 128
    nc = new_nc()
    with ExitStack() as st:
        em = Em(nc, st)
        qin = em.dram("q", [2, 128, Tn], BF16, "ExternalInput")
        kin = em.dram("k", [2, 128, Tn], BF16, "ExternalInput")
        vin = em.dram("v", [2, 128, Tn], BF16, "ExternalInput")
        bin_ = em.dram("beta", [2, Tn], F32, "ExternalInput")
        gin = em.dram("g", [2, Tn], F32, "ExternalInput")
        cst = em.dram("cst", [128, 512 + 7 * 128], F32, "ExternalInput")
        oT = em.dram("oT", [2, 128, Tn], F32, "ExternalOutput")

        cf = em.sb([128, 512 + 7 * 128], F32, "cf")
        em.dma(cf, cst)
        lvl = [cf[:, 512 + 128 * i:512 + 128 * (i + 1)] for i in range(7)]
        mnegI = cf[:, 0:128]
        nmS = cf[:, 128:256]
        If = cf[:, 256:384]
        Ib = em.sb([128, 128], BF16, "Ib")
        em.copy(Ib, cf[:, 256:384])
        e0 = em.sb([128, 2], F32, "e0")
        em.copy(e0[:, 0:1], cf[:, 256:257])
        em.copy(e0[:, 1:2], cf[:, 256:257])
        reset = em.sb([128, SEG], F32, "reset")
        for c in range(NCK):
            em.copy(reset[:, c * 128:(c + 1) * 128], cf[:, 384:512], E='pool')
        S = [em.sb([128, 128], F32, "S%d" % h) for h in range(2)]
        for h in range(2):
            em.memset(S[h], 0.0)
        banks = Rot([em.ps([128, 512], F32, "bk") for _ in range(8)])

        qt = [em.sb([128, SEG], BF16, "qt%d" % h) for h in range(2)]
        kt = [em.sb([128, SEG], BF16, "kt%d" % h) for h in range(2)]
        vt = [em.sb([128, SEG], BF16, "vt%d" % h) for h in range(2)]
        bb = [em.sb([128, SEG], F32, "bb%d" % h) for h in range(2)]
        gb = [em.sb([128, SEG], F32, "gb%d" % h) for h in range(2)]
        gc = [em.sb([128, SEG], F32, "gc%d" % h) for h in range(2)]
        eg = [em.sb([128, SEG], F32, "eg%d" % h) for h in range(2)]
        oo = [em.sb([128, SEG], F32, "oo%d" % h) for h in range(2)]

        def rot(nm, shape, dt, n):
            return Rot([em.sb(shape, dt, nm) for _ in range(n)])
        NI = 2 * PI
        col_r = rot("col", [128, 8], F32, NI + 2)
        DT_r = rot("DT", [128, 128], F32, 4)
        nb_r = rot("nb", [128, 128], F32, 4)
        t_r = rot("tt", [128, 128], F32, 4)
        Aqk_r = rot("Aqk", [128, 128], BF16, NI + 2)
        Z_r = rot("Z", [128, 128], F32, NI + 2)
        Zl_r = rot("Zl", [128, 128], F32, NI + 2)
        W_r = rot("W", [128, 128], F32, NI + 2)
        T_r = rot("T", [128, 128], F32, 2 * NI + 2)
        U_r = rot("U", [128, 128], F32, 2 * NI + 2)
        X0_r = rot("X0", [128, 256], F32, NI + 2)
        X_r = rot("X", [128, 256], BF16, NI + 2)
        kd_r = rot("kd", [128, 128], BF16, NI + 2)
        MT_r = rot("MT", [128, 128], F32, NI + 2)
        G_r = rot("G", [128, 128], F32, NI + 2)
        Q1_r = rot("Q1", [128, 128], F32, NI + 2)
        Q2_r = rot("Q2", [128, 128], F32, NI + 2)
        gI_r = rot("gI", [128, 128], F32, 4)

        for seg in range(Tn // SEG):
            s0 = seg * SEG
            for h in range(2):
                em.dma(qt[h], qin[h, :, s0:s0 + SEG])
                em.dma(kt[h], kin[h, :, s0:s0 + SEG])
                em.dma(vt[h], vin[h, :, s0:s0 + SEG])
                em.dma(bb[h], V(bin_, bin_[h:h + 1, s0:s0 + SEG].ap.partition_broadcast(128)))
                em.dma(gb[h], V(gin, gin[h:h + 1, s0:s0 + SEG].ap.partition_broadcast(128)))
                g_, c_ = gb[h], gc[h]
                em.op('dve', lambda e, c_=c_, g_=g_: e.tensor_tensor_scan(out=c_.ap, data0=reset.ap, data1=g_.ap,
                                                                          initial=0.0, op0=ALU.mult, op1=ALU.add),
                      reads=[reset, g_], writes=[c_])
                em.act(eg[h], c_, AF.Exp)
            for ck0 in range(0, NCK, PI):
                items = [(ck, h) for ck in range(ck0, min(NCK, ck0 + PI)) for h in range(2)]
                Z, X, cols, Aqk, kd = {}, {}, {}, {}, {}
                MTd, Gd, Q1d, Q2d = {}, {}, {}, {}
                for ck, h in items:
                    key = (ck, h)
                    tsl = slice(ck * 128, (ck + 1) * 128)
                    pcol = banks.next()
                    em.mm(pcol[:, 0:1], gc[h][:, tsl], e0[:, 0:1])
                    em.mm(pcol[:, 1:2], bb[h][:, tsl], e0[:, 0:1])
                    cl = col_r.next()
                    cols[key] = cl
                    em.copy(cl[:, 0:2], pcol[:, 0:2], E='dve')
                    em.ts(cl[:, 2:3], cl[:, 0:1], -1.0, None, op0=ALU.mult)
                    em.act(cl[:, 3:4], cl[:, 0:1], AF.Exp)
                    em.tt(cl[:, 4:5], cl[:, 3:4], cl[:, 1:2], ALU.mult)
                    em.act(cl[:, 5:6], cl[:, 2:3], AF.Exp, bias=gc[h][:, ck * 128 + 127:ck * 128 + 128])
                    em.copy(cl[:, 6:7], eg[h][:, ck * 128 + 127:ck * 128 + 128], E='pool')
                    DT = DT_r.next()
                    em.tt(DT, gc[h][:, tsl], mnegI, ALU.add, E='pool')
                    em.act(DT, DT, AF.Exp, bias=cl[:, 2:3])
                    nb = nb_r.next()
                    em.tt(nb, bb[h][:, tsl], nmS, ALU.mult, E='pool')
                    pk = banks.next()
                    em.mm(pk[:, 0:128], kt[h][:, tsl], kt[h][:, tsl])
                    em.mm(pk[:, 128:256], kt[h][:, tsl], qt[h][:, tsl])
                    em.mm(pk[:, 256:384], kt[h][:, tsl], Ib)
                    em.mm(pk[:, 384:512], vt[h][:, tsl], Ib)
                    t1 = t_r.next()
                    em.tt(t1, pk[:, 0:128], DT, ALU.mult)
                    Z[key] = Z_r.next()
                    em.tt(Z[key], t1, nb, ALU.mult, E='pool')
                    Aqk[key] = Aqk_r.next()
                    em.tt(Aqk[key], pk[:, 128:256], DT, ALU.mult)
                    X0 = X0_r.next()
                    X[key] = X0
                    em.ts(X0[:, 0:128], pk[:, 384:512], cl[:, 1:2], None, op0=ALU.mult)
                    em.ts(X0[:, 128:256], pk[:, 256:384], cl[:, 4:5], None, op0=ALU.mult)
                    kd[key] = kd_r.next()
                    em.act(kd[key], pk[:, 256:384], AF.Copy, scale=cl[:, 5:6])
                Tm, Um = {}, {}
                for ck, h in items:
                    key = (ck, h)
                    Um[key] = U_r.next()
                    em.tt(Um[key], Z[key], lvl[0], ALU.mult, E='pool')
                    em.tt(Um[key], Um[key], If, ALU.add, E='pool')
                    pt = banks.next()
                    em.mm(pt[:, 0:128], Um[key], If)
                    Tm[key] = T_r.next()
                    em.copy(Tm[key], pt[:, 0:128], E='act')
                for li in range(1, 7):
                    pws, Wts = {}, {}
                    for ck, h in items:
                        key = (ck, h)
                        Zl = Zl_r.next()
                        em.tt(Zl, Z[key], lvl[li], ALU.mult, E='pool')
                        pw = banks.next()
                        em.mm(pw[:, 0:128], Zl, Tm[key])
                        Wt = W_r.next()
                        em.copy(Wt, pw[:, 0:128], E='act')
                        pws[key], Wts[key] = pw, Wt
                    for ck, h in items:
                        key = (ck, h)
                        pw, Wt = pws[key], Wts[key]
                        if li < 6:
                            em.mm(pw[:, 128:256], Um[key], Wt)
                        em.mm(pw[:, 256:384], Wt, Um[key])
                        Un = U_r.next()
                        em.tt(Un, pw[:, 256:384], Um[key], ALU.add)
                        if li < 6:
                            Tn_ = T_r.next()
                            em.tt(Tn_, pw[:, 128:256], Tm[key], ALU.add)
                            Tm[key] = Tn_
                        Um[key] = Un
                Xbs = {}
                for ck, h in items:
                    key = (ck, h)
                    px = banks.next()
                    em.mm(px[:, 0:256], Um[key], X[key])
                    Xbs[key] = X_r.next()
                    em.copy(Xbs[key], px[:, 0:256], E='act')
                for ck, h in items:
                    key = (ck, h)
                    tsl = slice(ck * 128, (ck + 1) * 128)
                    Xb = Xbs[key]
                    cl = cols[key]
                    uc, wc = Xb[:, 0:128], Xb[:, 128:256]
                    pm = banks.next()
                    em.mm(pm[:, 0:128], wc, kd[key])
                    em.mm(pm[:, 128:256], kd[key], uc)
                    em.mm(pm[:, 256:384], wc, Aqk[key])
                    em.mm(pm[:, 384:512], uc, Aqk[key])
                    gI = gI_r.next()
                    em.ts(gI, If, cl[:, 6:7], None, op0=ALU.mult, E='pool')
                    MTd[key], Gd[key], Q1d[key], Q2d[key] = MT_r.next(), G_r.next(), Q1_r.next(), Q2_r.next()
                    em.stt(MTd[key], pm[:, 0:128], -1.0, gI, ALU.mult, ALU.add)
                    em.copy(Gd[key], pm[:, 128:256], E='act')
                    qd = t_r.next()
                    em.tt(qd, qt[h][:, tsl], eg[h][:, tsl], ALU.mult, E='pool')
                    em.stt(Q1d[key], pm[:, 256:384], -1.0, qd, ALU.mult, ALU.add)
                    em.copy(Q2d[key], pm[:, 384:512], E='act')
                for ck, h in items:
                    key = (ck, h)
                    tsl = slice(ck * 128, (ck + 1) * 128)
                    pc = banks.next()
                    em.mm(pc[:, 0:128], S[h], Q1d[key])
                    em.mm(pc[:, 128:256], MTd[key], S[h])
                    em.tt(oo[h][:, tsl], pc[:, 0:128], Q2d[key], ALU.add)
                    em.tt(S[h], pc[:, 128:256], Gd[key], ALU.add)
            for h in range(2):
                em.dma(oT[h, :, s0:s0 + SEG], oo[h])
        em.finish()
    return nc


def build_gdn_pre(ntok=TPC):
    NT = 256
    NH = NT + 3
    nc = new_nc()
    with ExitStack() as st:
        em = Em(nc, st)
        xT = em.dram("xT", [D, ntok + 3], F32, "ExternalInput")
        scal = em.dram("scal", [128, 113], F32, "ExternalInput")
        hsc = em.dram("hsc", [8, 2], F32, "ExternalInput")
        ident = em.dram("ident", [128, 128], F32, "ExternalInput")
        w_in = em.dram("w_in", [D, 4112], F32, "ExternalInput")
        outs = {n: em.dram(n, [D, ntok], BF16, "ExternalOutput") for n in ("q", "k", "v", "zs")}
        beta_o = em.dram("beta", [8, ntok], F32, "ExternalOutput")
        g_o = em.dram("g", [8, ntok], F32, "ExternalOutput")

        sc = em.sb([128, 113], F32, "sc")
        em.dma(sc, scal)
        sc1p = em.sb([128, 8], F32, "sc1p")
        em.ts(sc1p, sc[:, 0:8], 1.0, None, op0=ALU.add)
        hv = sc[:, 112:113]
        hs = em.sb([8, 2], F32, "hs")
        em.dma(hs, hsc)
        nea = em.sb([8, 1], F32, "nea")
        em.act(nea, hs[:, 0:1], AF.Exp)
        em.ts(nea, nea, -1.0, None, op0=ALU.mult)
        ones_bf = em.sb([128, 128], BF16, "ones")
        em.memset(ones_bf, 1.0)
        idf = em.sb([128, 128], F32, "idf")
        em.dma(idf, ident)
        dg = em.sb([128, 96, 128], F32, "dg")
        for jk in range(96):
            em.ts(dg[:, jk, :], idf, sc[:, 16 + jk:17 + jk], None, op0=ALU.mult, E=('dve' if jk % 2 == 0 else 'pool'))
        W = [em.sb([128, 4112], BF16, "W%d" % k) for k in range(8)]
        for k in range(8):
            em.dma(W[k], w_in[k * 128:(k + 1) * 128, :], Q='pool')

        xt_r = Rot([em.sb([128, 8, NH], F32, "xt") for _ in range(2)])
        ub = em.sb([128, 8, NH], BF16, "ub")
        banks = Rot([em.ps([128, 512], F32, "bk") for _ in range(8)])

        def rot(nm, n_, dt, n=2):
            return Rot([em.sb([128, n_], dt, nm) for _ in range(n)])
        pp_r, cv_r, s_r, sq_r, rn_r = rot("pp", NH, F32, 4), rot("cv", NT, F32, 3), rot("s", NT, F32, 3), rot("sq", NT, BF16, 3), rot("rn", NT, F32)
        ob_r = rot("ob", NT, BF16, 4)
        bt_r = Rot([em.sb([8, NT], F32, "bt") for _ in range(2)])
        gt_r = Rot([em.sb([8, NT], F32, "gt") for _ in range(2)])

        xv = xT[:].ap.rearrange("(c p) t -> p c t", p=128)
        def load_x(tt_):
            xt_ = xt_r.next()
            em.dma(xt_, V(xT, xv[:, :, tt_ * NT:tt_ * NT + NH]))
            return xt_

        xt_next = load_x(0)
        for tt in range(ntok // NT):
            t0 = tt * NT
            xt = xt_next
            for c in range(8):
                em.ts(ub[:, c, :], xt[:, c, :], sc1p[:, c:c + 1], sc[:, 8 + c:9 + c], op0=ALU.mult, op1=ALU.add,
                      E=('dve' if c % 2 == 0 else 'pool'))
            if tt == 0:
                em.ts(ub[:, :, 0:3], ub[:, :, 0:3], hv, None, op0=ALU.mult)
            if tt + 1 < ntok // NT:
                xt_next = load_x(tt + 1)
            def stage1a(j):
                pp = banks.next()
                for k in range(8):
                    em.mm(pp[:, 0:NH], W[k][:, j * 128:(j + 1) * 128], ub[:, k, :], start=(k == 0), stop=(k == 7))
                if j >= 24:
                    ob = ob_r.next()
                    em.act(ob, pp[:, 3:NT + 3], AF.Silu)
                    em.dma(outs["zs"][(j - 24) * 128:(j - 23) * 128, t0:t0 + NT], ob)
                    return None
                ps_ = pp_r.next()
                em.copy(ps_, pp[:, 0:NH], E='act')
                return dict(j=j, ps_=ps_)

            def stage1b(d):
                if d is None:
                    return None
                j, ps_ = d["j"], d["ps_"]
                pc = banks.next()
                for kk in range(4):
                    em.mm(pc[:, 0:NT], dg[:, j * 4 + kk, :], ps_[:, kk:kk + NT], start=(kk == 0), stop=(kk == 3))
                if j >= 16:
                    ob = ob_r.next()
                    em.act(ob, pc[:, 0:NT], AF.Silu)
                    em.dma(outs["v"][(j % 8) * 128:(j % 8 + 1) * 128, t0:t0 + NT], ob)
                    return None
                s_ = s_r.next()
                em.act(s_, pc[:, 0:NT], AF.Silu)
                sq = sq_r.next()
                em.act(sq, s_, AF.Square)
                return dict(j=j, s_=s_, sq=sq)

            def stage2(d):
                if d is None:
                    return
                j = d["j"]
                pn = banks.next()
                em.mm(pn[:, 0:NT], ones_bf, d["sq"])
                rn = rn_r.next()
                em.ts(rn, pn[:, 0:NT], 1e-6, None, op0=ALU.add)
                em.act(rn, rn, AF.Sqrt)
                em.recip(rn, rn)
                ob = ob_r.next()
                if j < 8:
                    em.stt(ob, d["s_"], 128.0 ** -0.5, rn, ALU.mult, ALU.mult)
                else:
                    em.tt(ob, d["s_"], rn, ALU.mult, E='pool')
                nm = "q" if j < 8 else "k"
                em.dma(outs[nm][(j % 8) * 128:(j % 8 + 1) * 128, t0:t0 + NT], ob)

            da = stage1a(0)
            db = None
            for j in range(32):
                da_next = stage1a(j + 1) if j + 1 < 32 else None
                db_new = stage1b(da)
                stage2(db)
                da, db = da_next, db_new
            stage2(db)
            pb = banks.next()
            for k in range(8):
                em.mm(pb[0:8, 0:NT], W[k][:, 4096:4104], ub[:, k, 3:NT + 3], start=(k == 0), stop=(k == 7))
            bt = bt_r.next()
            em.act(bt, pb[0:8, 0:NT], AF.Sigmoid)
            em.dma(beta_o[:, t0:t0 + NT], bt)
            pa = banks.next()
            for k in range(8):
                em.mm(pa[0:8, 0:NT], W[k][:, 4104:4112], ub[:, k, 3:NT + 3], start=(k == 0), stop=(k == 7))
            gt = gt_r.next()
            em.act(gt, pa[0:8, 0:NT], AF.Exp, bias=hs[:, 1:2])
            em.act(gt, gt, AF.Ln, bias=1.0)
            em.ts(gt, gt, nea[:, 0:1], None, op0=ALU.mult)
            em.dma(g_o[:, t0:t0 + NT], gt)
        em.finish()
    return nc


def rope_tables(pos):
    d = np.arange(128) % 64
    inv = (10000.0 ** (-(np.arange(0, 64, 2, dtype=np.float32)) / 64)).astype(np.float32)
    ang = pos.astype(np.float32)[None, :] * inv[d % 32][:, None]
    C = np.cos(ang).astype(np.float32)
    S = np.sin(ang).astype(np.float32)
    S = np.where((d < 32)[:, None], -S, S).astype(np.float32)
    return C, S


def rope_perm(ncols):
    c = np.arange(ncols)
    return (c // 64) * 64 + (c % 64 + 32) % 64


def build_diff_pre(ntok=TPC):
    NT = 256
    nc = new_nc()
    with ExitStack() as st:
        em = Em(nc, st)
        xT = em.dram("xT", [D, ntok], F32, "ExternalInput")
        scal = em.dram("scal", [128, 16], F32, "ExternalInput")
        w_in = em.dram("w_in", [D, 3 * D], F32, "ExternalInput")
        w_pm = em.dram("w_pm", [D, 2 * D], F32, "ExternalInput")
        ctab = em.dram("ctab", [128, ntok], F32, "ExternalInput")
        stab = em.dram("stab", [128, ntok], F32, "ExternalInput")
        qo = em.dram("q", [D, ntok], BF16, "ExternalOutput")
        ko = em.dram("k", [D, ntok], BF16, "ExternalOutput")
        vtok = em.dram("vtok", [ntok, D], BF16, "ExternalOutput")
        sc = em.sb([128, 16], F32, "sc")
        em.dma(sc, scal)
        sc1p = em.sb([128, 8], F32, "sc1p")
        em.ts(sc1p, sc[:, 0:8], 1.0, None, op0=ALU.add)
        W = [em.sb([128, 3 * D], BF16, "W%d" % k) for k in range(8)]
        Wp = [em.sb([128, 2 * D], BF16, "Wp%d" % k) for k in range(8)]
        for k in range(8):
            em.dma(W[k], w_in[k * 128:(k + 1) * 128, :], Q='pool')
            em.dma(Wp[k], w_pm[k * 128:(k + 1) * 128, :], Q='pool')
        xt_r = Rot([em.sb([128, 8, NT], F32, "xt") for _ in range(2)])
        ub = em.sb([128, 8, NT], BF16, "ub")
        ct_r = Rot([em.sb([128, NT], F32, "ct") for _ in range(2)])
        st_r = Rot([em.sb([128, NT], F32, "st") for _ in range(2)])
        t1_r = Rot([em.sb([128, NT], F32, "t1") for _ in range(2)])
        t2_r = Rot([em.sb([128, NT], F32, "t2") for _ in range(2)])
        ob_r = Rot([em.sb([128, NT], BF16, "ob") for _ in range(3)])
        vt_r = Rot([em.sb([128, D], BF16, "vt") for _ in range(2)])
        banks = Rot([em.ps([128, 512], F32, "bk") for _ in range(8)])
        xv = xT[:].ap.rearrange("(c p) t -> p c t", p=128)
        for tt in range(ntok // NT):
            t0 = tt * NT
            xt = xt_r.next()
            em.dma(xt, V(xT, xv[:, :, t0:t0 + NT]))
            ct, stb = ct_r.next(), st_r.next()
            em.dma(ct, ctab[:, t0:t0 + NT])
            em.dma(stb, stab[:, t0:t0 + NT])
            for c in range(8):
                em.ts(ub[:, c, :], xt[:, c, :], sc1p[:, c:c + 1], sc[:, 8 + c:9 + c], op0=ALU.mult, op1=ALU.add,
                      E=('dve' if c % 2 == 0 else 'pool'))
            for j in range(16):
                p1, p2 = banks.next(), banks.next()
                for k in range(8):
                    em.mm(p1[:, 0:NT], W[k][:, j * 128:(j + 1) * 128], ub[:, k, :], start=(k == 0), stop=(k == 7))
                for k in range(8):
                    em.mm(p2[:, 0:NT], Wp[k][:, j * 128:(j + 1) * 128], ub[:, k, :], start=(k == 0), stop=(k == 7))
                scl = 0.125 if j < 8 else 1.0
                t1, t2, ob = t1_r.next(), t2_r.next(), ob_r.next()
                em.stt(t1, p1[:, 0:NT], scl, ct, ALU.mult, ALU.mult)
                em.stt(t2, p2[:, 0:NT], scl, stb, ALU.mult, ALU.mult)
                em.tt(ob, t1, t2, ALU.add, E='pool')
                dst = qo if j < 8 else ko
                em.dma(dst[(j % 8) * 128:(j % 8 + 1) * 128, t0:t0 + NT], ob)
            for tb in range(NT // 128):
                vt = vt_r.next()
                for half in range(2):
                    pv = banks.next()
                    for k in range(8):
                        em.mm(pv[:, 0:512], ub[:, k, tb * 128:(tb + 1) * 128],
                              W[k][:, 2 * D + half * 512:2 * D + (half + 1) * 512], start=(k == 0), stop=(k == 7))
                    em.copy(vt[:, half * 512:(half + 1) * 512], pv[:, 0:512], E=('act' if half == 0 else 'dve'))
                em.dma(vtok[t0 + tb * 128:t0 + (tb + 1) * 128, :], vt)
        em.finish()
    return nc


def build_diff_attn(Tn=T, lam_init=0.0):
    NB = Tn // 128
    NG = Tn // 512
    nc = new_nc()
    with ExitStack() as st:
        em = Em(nc, st)
        qin = em.dram("q", [2, 2, 64, Tn], BF16, "ExternalInput")
        kin = em.dram("k", [2, 2, 64, Tn], BF16, "ExternalInput")
        vin = em.dram("v", [2, Tn, 128], BF16, "ExternalInput")
        lin = em.dram("lam", [4, 64], F32, "ExternalInput")
        cst = em.dram("cst", [128, 256], F32, "ExternalInput")
        oT = em.dram("oT", [2, 128, Tn], F32, "ExternalOutput")

        cf = em.sb([128, 256], F32, "cf")
        em.dma(cf, cst)
        If = cf[:, 128:256]
        tri = em.sb([128, 128], BF16, "tri")
        em.copy(tri, cf[:, 0:128])
        sel = em.sb([64, 65], BF16, "sel")
        em.memset(sel, 0.0)
        em.memset(sel[:, 64:65], 1.0)
        lt = em.sb([128, 4, 64], F32, "lt")
        em.dma(lt, V(lin, lin[:].ap.rearrange("a b -> (a b)").partition_broadcast(128)).re("p (a b) -> p a b", a=4))
        lp = em.sb([128, 2, 64], F32, "lp")
        em.tt(lp[:, 0, :], lt[:, 0, :], lt[:, 1, :], ALU.mult)
        em.tt(lp[:, 1, :], lt[:, 2, :], lt[:, 3, :], ALU.mult)
        ls = em.sb([128, 4], F32, "ls")
        em.op('dve', lambda e: e.reduce_sum(out=ls[:, 0:1].ap, in_=lp[:, 0, :].ap, axis=AX.X), reads=[lp], writes=[ls])
        em.op('dve', lambda e: e.reduce_sum(out=ls[:, 1:2].ap, in_=lp[:, 1, :].ap, axis=AX.X), reads=[lp], writes=[ls])
        em.act(ls[:, 0:2], ls[:, 0:2], AF.Exp)
        em.tt(ls[:, 2:3], ls[:, 1:2], ls[:, 0:1], ALU.subtract)
        em.ts(ls[:, 3:4], ls[:, 2:3], -float(lam_init), None, op0=ALU.add)

        kaug = [em.sb([65, Tn], BF16, "kaug%d" % c) for c in range(2)]
        vaug = em.sb([128, NB, 129], BF16, "vaug")
        qaug_r = [Rot([em.sb([65, 512], BF16, "qaug%d" % c) for _ in range(2)]) for c in range(2)]
        ksq = em.sb([64, 512], BF16, "ksq")
        kmx = em.sb([65, 40], F32, "kmx")
        km2 = [em.sb([65, 1], F32, "km2_%d" % c) for c in range(2)]
        qsq_r = Rot([em.sb([64, 512], BF16, "qsq") for _ in range(2)])
        PT_r = Rot([em.sb([128, 2, 512], BF16, "PT") for _ in range(3)])
        sb2 = Rot([em.ps([128, 2, 512], F32, "sb2_%d" % i) for i in range(2)])

        class _Half:
            def __init__(self, c):
                self.c = c

            def next(self):
                t = sb2.next()
                return t[:, self.c, :]
        sbk = [_Half(0), _Half(1)]
        obk = [[em.ps([128, 512], F32, "ob%d_%d" % (c, i)) for i in range(2)] for c in range(2)]
        rc_r = Rot([em.sb([128, 2], F32, "rc") for _ in range(4)])
        t_r = Rot([em.sb([128, 128], F32, "tf") for _ in range(3)])
        of_r = Rot([em.sb([128, 128], F32, "of") for _ in range(3)])
        ost_r = Rot([em.sb([128, 512], F32, "ost") for _ in range(2)])

        for u in range(2):
            em.dma(vaug[:, :, 0:128], V(vin, vin[u].ap.rearrange("(n p) c -> p n c", p=128)))
            em.memset(vaug[:, :, 128:129], 1.0)
            for c in range(2):
                em.dma(kaug[c][0:64, :], kin[u, c])
                em.memset(kaug[c][64:65, :], 1.0)
                for g in range(NG):
                    em.act(ksq, kaug[c][0:64, g * 512:(g + 1) * 512], AF.Square)
                    pn = sbk[c].next()
                    em.mm(pn[0:65, 0:512], sel, ksq)
                    em.op('dve', lambda e, pn=pn, g=g: e.reduce_max(out=kmx[64:65, g:g + 1].ap, in_=pn[64:65, 0:512].ap, axis=AX.X),
                          reads=[pn], writes=[kmx])
                em.op('dve', lambda e, c=c: e.reduce_max(out=km2[c][64:65, 0:1].ap, in_=kmx[64:65, 0:NG].ap, axis=AX.X),
                      reads=[kmx], writes=[km2[c]])
                em.ts(km2[c][64:65, :], km2[c][64:65, :], 1.05, None, op0=ALU.mult)
            for G in range(NG):
                qa = []
                for c in range(2):
                    q_ = qaug_r[c].next()
                    qa.append(q_)
                    em.dma(q_[0:64, :], qin[u, c, :, G * 512:(G + 1) * 512])
                    qsq = qsq_r.next()
                    em.act(qsq, q_[0:64, :], AF.Square)
                    pn = sbk[c].next()
                    em.mm(pn[0:65, 0:512], sel, qsq)
                    em.act(q_[64:65, :], pn[64:65, 0:512], AF.Sqrt, scale=km2[c][64:65, 0:1])
                    em.ts(q_[64:65, :], q_[64:65, :], -1.0, None, op0=ALU.mult)
                nkb = 4 * G + 4
                first = [[True, True], [True, True]]

                def s_mm(j):
                    m = max(0, j - 4 * G)
                    ncol = (4 - m) * 128
                    ps2 = sb2.next()
                    for c in range(2):
                        em.mm(ps2[:, c, 0:ncol], kaug[c][:, j * 128:(j + 1) * 128], qa[c][:, m * 128:512])
                    return ps2

                cur = s_mm(0)
                for j in range(nkb):
                    m = max(0, j - 4 * G)
                    ncol = (4 - m) * 128
                    PT2 = PT_r.next()
                    em.act(PT2[:, :, 0:ncol], cur[:, :, 0:ncol], AF.Exp)
                    if j >= 4 * G:
                        for c in range(2):
                            em.tt(PT2[:, c, 0:128], PT2[:, c, 0:128], tri, ALU.mult, E='pool')
                    nxt = s_mm(j + 1) if j + 1 < nkb else None
                    for c in range(2):
                        PT = PT2[:, c, :]
                        for qi in range(m, 4):
                            bk = obk[c][qi // 2]
                            o_ = bk[:, (qi % 2) * 129:(qi % 2) * 129 + 129]
                            em.mm(o_, PT[:, (qi - m) * 128:(qi - m + 1) * 128], vaug[:, j, :],
                                  start=first[c][qi // 2], stop=(j == 4 * G + qi), sgc=True)
                            first[c][qi // 2] = False
                    cur = nxt
                ost = ost_r.next()
                for qi in range(4):
                    o1 = obk[0][qi // 2][:, (qi % 2) * 129:(qi % 2) * 129 + 129]
                    o2 = obk[1][qi // 2][:, (qi % 2) * 129:(qi % 2) * 129 + 129]
                    rc = rc_r.next()
                    em.recip(rc[:, 0:1], o1[:, 128:129])
                    em.recip(rc[:, 1:2], o2[:, 128:129])
                    em.tt(rc[:, 1:2], rc[:, 1:2], ls[:, 3:4], ALU.mult)
                    t2 = t_r.next()
                    em.ts(t2, o2[:, 0:128], rc[:, 1:2], None, op0=ALU.mult)
                    of = of_r.next()
                    em.stt(of, o1[:, 0:128], rc[:, 0:1], t2, ALU.mult, ALU.add)
                    pt = sbk[qi % 2].next()
                    em.mm(pt[:, 0:128], of, If)
                    em.copy(ost[:, qi * 128:(qi + 1) * 128], pt[:, 0:128], E='act')
                em.dma(oT[u, :, G * 512:(G + 1) * 512], ost)
        em.finish()
    return nc


def build_swa(ntok=TPC):
    NT = 512
    NBL = NT // 128
    nc = new_nc()
    with ExitStack() as st:
        em = Em(nc, st)
        xT = em.dram("xT", [D, 128 + ntok], F32, "ExternalInput")
        scal = em.dram("scal", [128, 36], F32, "ExternalInput")
        w_in = em.dram("w_in", [D, 1280], F32, "ExternalInput")
        w_pm = em.dram("w_pm", [D, 1152], F32, "ExternalInput")
        bv = em.dram("bv", [1, 128], F32, "ExternalInput")
        sinks = em.dram("sinks", [1, 16], F32, "ExternalInput")
        ctab = em.dram("ctab", [128, 128 + ntok], F32, "ExternalInput")
        stab = em.dram("stab", [128, 128 + ntok], F32, "ExternalInput")
        cst = em.dram("cst", [128, 384], F32, "ExternalInput")
        oT = em.dram("oT", [D, ntok], BF16, "ExternalOutput")

        sc = em.sb([128, 36], F32, "sc")
        em.dma(sc, scal)
        sc1p = em.sb([128, 8], F32, "sc1p")
        em.ts(sc1p, sc[:, 0:8], 1.0, None, op0=ALU.add)
        hv = sc[:, 34:35]
        cf = em.sb([128, 384], F32, "cf")
        em.dma(cf, cst)
        If = cf[:, 256:384]
        mP = em.sb([128, 512], BF16, "mP")
        mC = em.sb([128, 512], BF16, "mC")
        for i in range(4):
            em.copy(mP[:, i * 128:(i + 1) * 128], cf[:, 0:128])
            em.copy(mC[:, i * 128:(i + 1) * 128], cf[:, 128:256])
        bvb = em.sb([128, 128], F32, "bvb")
        em.dma(bvb, V(bv, bv[0:1, :].ap.partition_broadcast(128)))
        esk = em.sb([128, 16], F32, "esk")
        em.dma(esk, V(sinks, sinks[0:1, :].ap.partition_broadcast(128)))
        em.act(esk, esk, AF.Exp)
        sel = em.sb([64, 65], BF16, "sel")
        em.memset(sel, 0.0)
        em.memset(sel[:, 64:65], 1.0)
        vvirt = em.sb([65, 66], BF16, "vvirt")
        em.memset(vvirt, 0.0)
        em.memset(vvirt[64:65, 65:66], 1.0)

        W = [em.sb([128, 1280], BF16, "W%d" % k) for k in range(8)]
        Wp = [em.sb([128, 1152], BF16, "Wp%d" % k) for k in range(8)]
        for k in range(8):
            em.dma(W[k], w_in[k * 128:(k + 1) * 128, :], Q='pool')
            em.dma(Wp[k], w_pm[k * 128:(k + 1) * 128, :], Q='pool')

        xt_r = Rot([em.sb([128, 8, NT], F32, "xt") for _ in range(2)])
        ub = em.sb([128, 8, NT], BF16, "ub")
        ct_r = Rot([em.sb([128, NT], F32, "ct") for _ in range(2)])
        st_r = Rot([em.sb([128, NT], F32, "st") for _ in range(2)])
        t1_r = Rot([em.sb([128, NT], F32, "t1") for _ in range(2)])
        t2_r = Rot([em.sb([128, NT], F32, "t2") for _ in range(2)])
        kaug = [em.sb([65, 128 + NT], BF16, "kaug%d" % g) for g in range(2)]
        vaug = [em.sb([128, NBL + 1, 66], BF16, "vaug%d" % g) for g in range(2)]
        for g in range(2):
            em.memset(kaug[g][64:65, :], 1.0)
            em.memset(vaug[g][:, :, 64:65], 1.0)
            em.memset(vaug[g][:, :, 65:66], 0.0)
        qaug = em.sb([65, 16, NT], BF16, "qaug")
        erow = em.sb([65, 16, NT], BF16, "erow")
        ksq = em.sb([64, 128 + NT], BF16, "ksq")
        qsq_r = Rot([em.sb([64, NT], BF16, "qsq") for _ in range(2)])
        km = em.sb([65, 4], F32, "km")
        km2 = [em.sb([65, 1], F32, "km2_%d" % g) for g in range(2)]
        PT_r = Rot([em.sb([128, 512], BF16, "PT") for _ in range(4)])
        den_r = Rot([em.sb([128, 4], F32, "den") for _ in range(4)])
        otok_r = Rot([em.sb([128, D], F32, "otok") for _ in range(2)])
        ostg_r = Rot([em.sb([128, 8, NT], BF16, "ostg") for _ in range(2)])
        banks = Rot([em.ps([128, 512], F32, "bk") for _ in range(4)])
        sbanks = Rot([em.ps([128, 512], F32, "sbk") for _ in range(4)])

        xv = xT[:].ap.rearrange("(c p) t -> p c t", p=128)
        ov = oT[:].ap.rearrange("(c p) t -> p c t", p=128)

        def project(c0, n, tile_i):
            xt = xt_r.next()
            em.dma(xt[:, :, 0:n], V(xT, xv[:, :, c0:c0 + n]))
            ct, stb = ct_r.next(), st_r.next()
            em.dma(ct[:, 0:n], ctab[:, c0:c0 + n])
            em.dma(stb[:, 0:n], stab[:, c0:c0 + n])
            for c in range(8):
                em.ts(ub[:, c, 0:n], xt[:, c, 0:n], sc1p[:, c:c + 1], sc[:, 8 + c:9 + c], op0=ALU.mult, op1=ALU.add,
                      E=('dve' if c % 2 == 0 else 'pool'))
            koff = 0 if tile_i < 0 else 128
            chunks = [8] if tile_i < 0 else list(range(9))
            for j in chunks:
                p1, p2 = banks.next(), banks.next()
                for k in range(8):
                    em.mm(p1[:, 0:n], W[k][:, j * 128:(j + 1) * 128], ub[:, k, 0:n], start=(k == 0), stop=(k == 7))
                for k in range(8):
                    em.mm(p2[:, 0:n], Wp[k][:, j * 128:(j + 1) * 128], ub[:, k, 0:n], start=(k == 0), stop=(k == 7))
                t1, t2 = t1_r.next(), t2_r.next()
                b1 = sc[:, 16 + j:17 + j] if j < 8 else sc[:, 32:33]
                b2 = sc[:, 24 + j:25 + j] if j < 8 else sc[:, 33:34]
                em.stt(t1[:, 0:n], p1[:, 0:n], b1, ct[:, 0:n], ALU.add, ALU.mult)
                em.stt(t2[:, 0:n], p2[:, 0:n], b2, stb[:, 0:n], ALU.add, ALU.mult)
                for e in range(2):
                    ps = slice(64 * e, 64 * e + 64)
                    if j < 8:
                        em.tt(qaug[0:64, 2 * j + e, 0:n], t1[ps, 0:n], t2[ps, 0:n], ALU.add, E='pool')
                    else:
                        em.stt(kaug[e][0:64, koff:koff + n], t1[ps, 0:n], 1.0, t2[ps, 0:n], ALU.mult, ALU.add, E='pool')
                        em.ts(kaug[e][0:64, koff:koff + n], kaug[e][0:64, koff:koff + n], 0.125, None, op0=ALU.mult, E='pool')
            for bl in range(n // 128):
                pv = banks.next()
                for k in range(8):
                    em.mm(pv[:, 0:128], ub[:, k, bl * 128:(bl + 1) * 128], W[k][:, 1152:1280], start=(k == 0), stop=(k == 7))
                for g in range(2):
                    vb = bl + (0 if tile_i < 0 else 1)
                    em.tt(vaug[g][:, vb, 0:64], pv[:, g * 64:(g + 1) * 64], bvb[:, g * 64:(g + 1) * 64], ALU.add)

        project(0, 128, -1)
        for tt in range(ntok // NT):
            project(128 + tt * NT, NT, tt)
            for g in range(2):
                em.act(ksq, kaug[g][0:64, :], AF.Square)
                for hf in range(2):
                    w_ = (128 + NT) // 2
                    pn = banks.next()
                    em.mm(pn[0:65, 0:w_], sel, ksq[:, hf * w_:(hf + 1) * w_])
                    em.op('dve', lambda e, pn=pn, hf=hf, w_=w_: e.reduce_max(out=km[64:65, hf:hf + 1].ap, in_=pn[64:65, 0:w_].ap, axis=AX.X),
                          reads=[pn], writes=[km])
                em.op('dve', lambda e, g=g: e.reduce_max(out=km2[g][64:65, 0:1].ap, in_=km[64:65, 0:2].ap, axis=AX.X),
                      reads=[km], writes=[km2[g]])
                em.ts(km2[g][64:65, :], km2[g][64:65, :], 1.05, None, op0=ALU.mult)
            for h in range(16):
                g = h // 8
                qsq = qsq_r.next()
                em.act(qsq, qaug[0:64, h, :], AF.Square)
                pn = banks.next()
                em.mm(pn[0:65, 0:NT], sel, qsq)
                em.act(qaug[64:65, h, :], pn[64:65, 0:NT], AF.Sqrt, scale=km2[g][64:65, 0:1])
                em.ts(qaug[64:65, h, :], qaug[64:65, h, :], -1.0, None, op0=ALU.mult)
                em.act(erow[64:65, h, :], qaug[64:65, h, :], AF.Exp)
            ostg = ostg_r.next()
            def s_mm(bl, g, hg):
                h0 = g * 8 + hg * 4
                q4 = qaug[:, h0:h0 + 4, bl * 128:(bl + 1) * 128]
                pp, pc = sbanks.next(), sbanks.next()
                em.mm(pp[:, 0:512], kaug[g][:, bl * 128:(bl + 1) * 128], q4)
                em.mm(pc[:, 0:512], kaug[g][:, (bl + 1) * 128:(bl + 2) * 128], q4)
                return pp, pc

            groups = [(bl, g, hg) for bl in range(NBL) for g in range(2) for hg in range(2)]
            s_next = s_mm(*groups[0])
            for gi, (bl, g, hg) in enumerate(groups):
                bsl = slice(bl * 128, (bl + 1) * 128)
                if g == 0 and hg == 0:
                    otok = otok_r.next()
                if True:
                    if True:
                        h0 = g * 8 + hg * 4
                        pp, pc = s_next
                        Pp, Pc = PT_r.next(), PT_r.next()
                        em.act(Pp, pp[:, 0:512], AF.Exp)
                        em.act(Pc, pc[:, 0:512], AF.Exp)
                        em.tt(Pp, Pp, mP, ALU.mult, E='pool')
                        em.tt(Pc, Pc, mC, ALU.mult, E='dve')
                        if tt == 0 and bl == 0:
                            em.ts(Pp, Pp, hv, None, op0=ALU.mult)
                        if gi + 1 < len(groups):
                            s_next = s_mm(*groups[gi + 1])
                        po = banks.next()
                        for i in range(4):
                            o_ = po[:, i * 66:(i + 1) * 66]
                            em.mm(o_, Pp[:, i * 128:(i + 1) * 128], vaug[g][:, bl, :], start=(i == 0), stop=False, sgc=True)
                            em.mm(o_, Pc[:, i * 128:(i + 1) * 128], vaug[g][:, bl + 1, :], start=False, stop=False, sgc=True)
                            em.mm(o_, erow[64:65, h0 + i, bsl], vvirt[64:65, :], start=False, stop=True, sgc=True)
                        for i in range(4):
                            h = h0 + i
                            o_ = po[:, i * 66:(i + 1) * 66]
                            den = den_r.next()
                            em.copy(den[:, 2:4], o_[:, 64:66], E='dve')
                            em.stt(den[:, 0:1], den[:, 3:4], esk[:, h:h + 1], den[:, 2:3], ALU.mult, ALU.add)
                            em.recip(den[:, 1:2], den[:, 0:1])
                            em.ts(otok[:, h * 64:(h + 1) * 64], o_[:, 0:64], den[:, 1:2], None, op0=ALU.mult)
                if g == 1 and hg == 1:
                    for j in range(8):
                        pt = banks.next()
                        em.mm(pt[:, 0:128], otok[:, j * 128:(j + 1) * 128], If)
                        em.copy(ostg[:, j, bsl], pt[:, 0:128], E='act')
            em.dma(V(oT, ov[:, :, tt * NT:(tt + 1) * NT]), ostg)
            for g in range(2):
                em.copy(kaug[g][0:64, 0:128], kaug[g][0:64, NT:NT + 128], E='pool')
                em.copy(vaug[g][:, 0, 0:64], vaug[g][:, NBL, 0:64], E='pool')
        em.finish()
    return nc


def pcol(v):
    v = np.asarray(v, np.float32)
    return np.ascontiguousarray(v.reshape(-1, 128).T)


def core_bq(c):
    return c // 4, c % 4


def halo_x(x_tok, c, h):
    b, q = core_bq(c)
    if q == 0:
        left = np.zeros((D, h), np.float32)
    else:
        left = x_tok[c - 1][:, TPC - h:]
    return np.ascontiguousarray(np.concatenate([left, x_tok[c]], axis=1))


def tok_to_rows(outs, name, b, r0, r1):
    return np.concatenate([outs[b * 4 + q][name][r0:r1, :] for q in range(4)], axis=1)


_DBG = {}


def kernel(x, c, ada_w, ada_b, ln_g, ln_b, ffn_w_in, ffn_w_out,
           rwkv_mu, rwkv_w_rkv, rwkv_w0, rwkv_w1, rwkv_w2, rwkv_a0, rwkv_a1, rwkv_a2,
           rwkv_g1, rwkv_g2, rwkv_k_k, rwkv_k_a, rwkv_r_k, rwkv_gn_g, rwkv_gn_b, rwkv_w_out,
           gdn_w_in, gdn_conv, gdn_a_log, gdn_dt_bias, gdn_norm_g, gdn_w_out,
           diff_w_in, diff_lambda, diff_subln_g, diff_w_out,
           swa_w_qkv, swa_b_qkv, swa_sinks, swa_w_out, swa_b_out, _layers=DEPTH, _debug=None):
    f32 = lambda a: np.ascontiguousarray(np.asarray(a, np.float32))
    x = f32(x)
    mod = run_mod(f32(c), f32(ada_w), f32(ada_b))
    ms = lambda l, b, w: mod[l, b][:, w * 8:(w + 1) * 8]
    xs = [np.ascontiguousarray(x[cc // 4, (cc % 4) * TPC:(cc % 4 + 1) * TPC, :].T) for cc in range(NCORES)]
    zeros8 = np.zeros((128, 8), np.float32)
    bd = np.kron(np.eye(2), np.ones((64, 64))).astype(np.float32)
    eye = np.eye(128, dtype=np.float32)
    hvcol = lambda cc: np.full((128, 1), 0.0 if cc % 4 == 0 else 1.0, np.float32)

    def post(i, mode, extra, w_o, b_out=None, cols4=None, cols5=None, hscale=1.0):
        nc = build_post(mode, TPC, hscale, has_bias=(b_out is not None))
        ims = []
        for cc in range(NCORES):
            b = cc // 4
            sc = np.concatenate([ms(i, b, 2), pcol(ln_g[i, 0]), pcol(ln_b[i, 0]),
                                 pcol(b_out) if b_out is not None else zeros8,
                                 cols4 if cols4 is not None else zeros8,
                                 cols5 if cols5 is not None else zeros8, zeros8], axis=1)
            im = {"xT": xs[cc], "w_o": f32(w_o), "scal": np.ascontiguousarray(sc)}
            im.update(extra[cc])
            ims.append(im)
        res = run(nc, ims)
        return [res[cc]["yT"] for cc in range(NCORES)]

    def ffn(i, xin):
        nc = build_ffn(TPC)
        ims = []
        for cc in range(NCORES):
            b = cc // 4
            sc = np.concatenate([ms(i, b, 4), ms(i, b, 3), ms(i, b, 5), pcol(ln_g[i, 1]), pcol(ln_b[i, 1])], axis=1)
            ims.append({"xT": xin[cc], "w_in": f32(ffn_w_in[i]), "w_out": f32(ffn_w_out[i]),
                        "scal": np.ascontiguousarray(sc)})
        res = run(nc, ims)
        return [res[cc]["yT"] for cc in range(NCORES)]

    for i in range(_layers):
        m = i % 4
        if m == 0:
            nc = build_rwkv_pre(TPC)
            ims = []
            for cc in range(NCORES):
                b = cc // 4
                sc = np.concatenate([ms(i, b, 1), ms(i, b, 0)] + [pcol(rwkv_mu[0, k]) for k in range(6)] +
                                    [pcol(rwkv_w0[0]), pcol(rwkv_a0[0]), pcol(rwkv_k_k[0]), pcol(rwkv_k_a[0]),
                                     pcol(np.asarray(rwkv_r_k[0]).reshape(-1)), hvcol(cc)], axis=1)
                ims.append({"xT": halo_x(xs, cc, 1), "scal": np.ascontiguousarray(sc), "cst": bd,
                            "w_rkv": f32(rwkv_w_rkv[0]), "w1": f32(rwkv_w1[0]), "a1": f32(rwkv_a1[0]), "g1": f32(rwkv_g1[0]),
                            "w2": f32(rwkv_w2[0]), "a2": f32(rwkv_a2[0]), "g2": f32(rwkv_g2[0])})
            pre = run(nc, ims)
            nc = build_rwkv_scan(T, 1024)
            mS, mI, mL = chunk_masks()
            reset = np.ones((128, 128), np.float32)
            reset[:, 0] = 0
            reset[:, 64] = 0
            cst = np.ascontiguousarray(np.concatenate([mS, mI, mL, eye, reset], axis=1))
            ims = []
            for cc in range(NCORES):
                b, hq = cc // 4, cc % 4
                im = {"cst": cst}
                for n in ("r", "k", "kk", "a", "lw"):
                    im[n] = np.ascontiguousarray(tok_to_rows(pre, n, b, 256 * hq, 256 * hq + 256).reshape(2, 128, T))
                im["v"] = np.ascontiguousarray(np.concatenate([pre[b * 4 + q]["vtok"][:, 256 * hq:256 * hq + 256]
                                                               for q in range(4)], axis=0))
                ims.append(im)
            sres = run(nc, ims)
            extra = []
            for cc in range(NCORES):
                b, q = cc // 4, cc % 4
                yin = np.concatenate([sres[b * 4 + hq]["yT"].reshape(256, T)[:, q * TPC:(q + 1) * TPC] for hq in range(4)], axis=0)
                extra.append({"yin": np.ascontiguousarray(yin), "g": pre[cc]["g"], "bonus": pre[cc]["bonus"], "cst": bd})
            xs = post(i, 'rwkv', extra, rwkv_w_out[0], cols4=pcol(rwkv_gn_g[0]), cols5=pcol(rwkv_gn_b[0]))
        elif m == 1:
            nc = build_gdn_pre(TPC)
            cw = np.asarray(gdn_conv[0], np.float32).reshape(4, 24, 128).transpose(2, 1, 0).reshape(128, 96)
            hsc = np.ascontiguousarray(np.stack([np.asarray(gdn_a_log[0], np.float32), np.asarray(gdn_dt_bias[0], np.float32)], axis=1))
            ims = []
            for cc in range(NCORES):
                b = cc // 4
                sc = np.concatenate([ms(i, b, 1), ms(i, b, 0), cw, hvcol(cc)], axis=1)
                ims.append({"xT": halo_x(xs, cc, 3), "scal": np.ascontiguousarray(sc), "hsc": hsc, "ident": eye,
                            "w_in": f32(gdn_w_in[0])})
            pre = run(nc, ims)
            nc = build_gdn_scan(T, 1024)
            ii = np.arange(128)
            mnegI = np.where(ii[:, None] <= ii[None, :], 0.0, -1e4).astype(np.float32)
            nmS = -(ii[:, None] < ii[None, :]).astype(np.float32)
            reset = np.ones((128, 128), np.float32)
            reset[:, 0] = 0
            cst = np.ascontiguousarray(np.concatenate([mnegI, nmS, eye, reset] + gdn_level_masks(), axis=1))
            ims = []
            for cc in range(NCORES):
                b, hq = cc // 4, cc % 4
                im = {"cst": cst}
                for n in ("q", "k", "v"):
                    im[n] = np.ascontiguousarray(tok_to_rows(pre, n, b, 256 * hq, 256 * hq + 256).reshape(2, 128, T))
                for n in ("beta", "g"):
                    im[n] = np.ascontiguousarray(tok_to_rows(pre, n, b, 2 * hq, 2 * hq + 2))
                ims.append(im)
            sres = run(nc, ims)
            extra = []
            ng = np.zeros((128, 8), np.float32)
            ng[:, 0] = np.asarray(gdn_norm_g[0], np.float32)
            for cc in range(NCORES):
                b, q = cc // 4, cc % 4
                yin = np.concatenate([sres[b * 4 + hq]["oT"].reshape(256, T)[:, q * TPC:(q + 1) * TPC] for hq in range(4)], axis=0)
                extra.append({"yin": np.ascontiguousarray(yin), "zs": pre[cc]["zs"]})
            xs = post(i, 'gdn', extra, gdn_w_out[0], cols4=ng)
        elif m == 2:
            lam_init = 0.8 - 0.6 * float(np.exp(-0.3 * i))
            nc = build_diff_pre(TPC)
            perm = rope_perm(2 * D)
            w_in = f32(diff_w_in[0])
            w_pm = np.ascontiguousarray(w_in[:, :2 * D][:, perm])
            ims = []
            for cc in range(NCORES):
                b, q = cc // 4, cc % 4
                Ct, St = rope_tables(np.arange(q * TPC, (q + 1) * TPC))
                sc = np.concatenate([ms(i, b, 1), ms(i, b, 0)], axis=1)
                ims.append({"xT": xs[cc], "scal": np.ascontiguousarray(sc), "w_in": w_in, "w_pm": w_pm, "ctab": Ct, "stab": St})
            pre = run(nc, ims)
            nc = build_diff_attn(T, lam_init)
            ii = np.arange(128)
            cst = np.ascontiguousarray(np.concatenate([(ii[:, None] <= ii[None, :]).astype(np.float32), eye], axis=1))
            ims = []
            for cc in range(NCORES):
                im = {"lam": f32(diff_lambda[0]), "cst": cst}
                qs, ks, vs = [], [], []
                for u in range(2):
                    b, h = (2 * cc + u) // 8, (2 * cc + u) % 8
                    qs.append(tok_to_rows(pre, "q", b, 128 * h, 128 * h + 128).reshape(2, 64, T))
                    ks.append(tok_to_rows(pre, "k", b, 128 * h, 128 * h + 128).reshape(2, 64, T))
                    vs.append(np.concatenate([pre[b * 4 + q]["vtok"][:, 128 * h:128 * h + 128] for q in range(4)], axis=0))
                im["q"] = np.ascontiguousarray(np.stack(qs))
                im["k"] = np.ascontiguousarray(np.stack(ks))
                im["v"] = np.ascontiguousarray(np.stack(vs))
                ims.append(im)
            ares = run(nc, ims)
            extra = []
            sg = np.zeros((128, 8), np.float32)
            sg[:, 0] = np.asarray(diff_subln_g[0], np.float32)
            for cc in range(NCORES):
                b, q = cc // 4, cc % 4
                rows = []
                for h in range(8):
                    unit = b * 8 + h
                    rows.append(ares[unit // 2]["oT"][unit % 2][:, q * TPC:(q + 1) * TPC])
                extra.append({"yin": np.ascontiguousarray(np.concatenate(rows, axis=0))})
            xs = post(i, 'diff', extra, diff_w_out[0], cols4=sg, hscale=1.0 - lam_init)
        else:
            nc = build_swa(TPC)
            perm = rope_perm(1152)
            w_in = f32(swa_w_qkv[0])
            bq = np.asarray(swa_b_qkv[0], np.float32)
            bqp = bq[:1152][perm]
            w_pm = np.ascontiguousarray(w_in[:, :1152][:, perm])
            ii = np.arange(128)
            cst = np.ascontiguousarray(np.concatenate([(ii[:, None] > ii[None, :]).astype(np.float32),
                                                       (ii[:, None] <= ii[None, :]).astype(np.float32), eye], axis=1))
            ims = []
            for cc in range(NCORES):
                b, q = cc // 4, cc % 4
                Ct, St = rope_tables(np.arange(q * TPC - 128, (q + 1) * TPC))
                sc = np.concatenate([ms(i, b, 1), ms(i, b, 0), pcol(bq[:1024]), pcol(bqp[:1024]), pcol(bq[1024:1152]),
                                     pcol(bqp[1024:1152]), hvcol(cc), np.zeros((128, 1), np.float32)], axis=1)
                ims.append({"xT": halo_x(xs, cc, 128), "scal": np.ascontiguousarray(sc), "w_in": w_in, "w_pm": w_pm,
                            "bv": np.ascontiguousarray(bq[None, 1152:]), "sinks": f32(swa_sinks[0])[None, :],
                            "ctab": Ct, "stab": St, "cst": cst})
            ares = run(nc, ims)
            extra = [{"oT": ares[cc]["oT"]} for cc in range(NCORES)]
            xs = post(i, 'plain', extra, swa_w_out[0], b_out=swa_b_out[0])
        if _debug is not None:
            _debug.append(("mix%d" % i, [a.copy() for a in xs]))
        xs = ffn(i, xs)
        if _debug is not None:
            _debug.append(("ffn%d" % i, [a.copy() for a in xs]))
    out = np.empty((B, T, D), np.float32)
    for cc in range(NCORES):
        out[cc // 4, (cc % 4) * TPC:(cc % 4 + 1) * TPC, :] = xs[cc].T
    return out
```
